# Optimizing a Trainium2 kernel written in Bass

```python
import math
import jax
import jax.numpy as jnp
from jax import lax
import numpy as np


D_MODEL = 4096
BATCH = 2
SEQ = 8192
DEPTH = 1
DEC_BATCH = 2
DEC_SEQ = 4096
PAST_LEN = 128

MIX_WIDTH = D_MODEL
SSM_WIDTH = MIX_WIDTH // 2
SSM_GROUP = 16
SSM_GROUPS = SSM_WIDTH // SSM_GROUP
SSM_STATE = 64
ATTN_WIDTH = MIX_WIDTH - SSM_WIDTH
V_HEAD_DIM = 128
N_HEADS = ATTN_WIDTH // V_HEAD_DIM
QK_NOPE_DIM = 128
QK_ROPE_DIM = 64
Q_LORA_RANK = 896
KV_LORA_RANK = 512
IN_COLS = SSM_WIDTH + Q_LORA_RANK + KV_LORA_RANK + QK_ROPE_DIM
D_FF = 11008
CONV_WIDTH = 3
Q_BLOCK = 128
ROPE_THETA = 10000.0
EPS = 1e-6

kernel_name = 'hymba_s5_mla_convffn_encoder'


def rms_norm(x, g):
    xf = x.astype(jnp.float32)
    xf = xf * lax.rsqrt(jnp.mean(xf * xf, axis=-1, keepdims=True) + EPS)
    return (xf * g.astype(jnp.float32)).astype(x.dtype)


def rope_tables(length):
    inv = 1.0 / (ROPE_THETA ** (jnp.arange(0, QK_ROPE_DIM, 2, dtype=jnp.float32) / QK_ROPE_DIM))
    ang = jnp.arange(length, dtype=jnp.float32)[:, None] * inv[None, :]
    return jnp.cos(ang)[:, None, :], jnp.sin(ang)[:, None, :]


def apply_rope(x, cos, sin):
    xf = x.astype(jnp.float32)
    half = QK_ROPE_DIM // 2
    x1, x2 = xf[..., :half], xf[..., half:]
    return jnp.concatenate([x1 * cos - x2 * sin, x2 * cos + x1 * sin], axis=-1).astype(x.dtype)


def _ssm_combine(earlier, later):
    a_i, b_i = earlier
    a_j, b_j = later
    return a_j * a_i, a_j * b_i + b_j


def s5_mixer(u, a_re, a_im, b_re, b_im, c_re, c_im, log_dt, d_skip, glu_w):
    bsz, length, _ = u.shape
    uf = u.astype(jnp.float32).reshape(bsz, length, SSM_GROUPS, SSM_GROUP)
    lam = lax.complex(a_re.astype(jnp.float32), a_im.astype(jnp.float32))
    dt = jnp.exp(log_dt.astype(jnp.float32))[..., None]
    abar = jnp.exp(lam * dt)
    bmat = lax.complex(b_re.astype(jnp.float32), b_im.astype(jnp.float32))
    bbar = ((abar - 1.0) / lam)[..., None] * bmat
    cmat = lax.complex(c_re.astype(jnp.float32), c_im.astype(jnp.float32))
    uc = uf.astype(jnp.complex64)
    y = d_skip.astype(jnp.float32) * uf
    for direction in range(2):
        bu = jnp.einsum('blgh,gph->blgp', uc, bbar[direction])
        a_el = jnp.broadcast_to(abar[direction], bu.shape)
        _, states = lax.associative_scan(_ssm_combine, (a_el, bu), reverse=(direction == 1), axis=1)
        y = y + jnp.real(jnp.einsum('blgp,ghp->blgh', states, cmat[direction]))
    y = y.reshape(bsz, length, SSM_WIDTH).astype(u.dtype)
    g = jax.nn.gelu(y)
    return g * jax.nn.sigmoid(g @ glu_w)


def mla_mixer(q_lat, kv_lat, k_rope_raw, q_norm_g, w_q_up, kv_norm_g, w_kv_up):
    bsz, length, _ = q_lat.shape
    q = (rms_norm(q_lat, q_norm_g) @ w_q_up).reshape(bsz, length, N_HEADS, QK_NOPE_DIM + QK_ROPE_DIM)
    kv = (rms_norm(kv_lat, kv_norm_g) @ w_kv_up).reshape(bsz, length, N_HEADS, QK_NOPE_DIM + V_HEAD_DIM)
    cos, sin = rope_tables(length)
    q_nope = q[..., :QK_NOPE_DIM]
    q_rope = apply_rope(q[..., QK_NOPE_DIM:], cos, sin)
    k_nope = kv[..., :QK_NOPE_DIM]
    v = kv[..., QK_NOPE_DIM:]
    k_rope = apply_rope(k_rope_raw[:, :, None, :], cos, sin)[:, :, 0, :]
    qb = min(Q_BLOCK, length)
    nb = length // qb
    qn_blocks = q_nope.reshape(bsz, nb, qb, N_HEADS, QK_NOPE_DIM).swapaxes(0, 1)
    qr_blocks = q_rope.reshape(bsz, nb, qb, N_HEADS, QK_ROPE_DIM).swapaxes(0, 1)
    scale = (QK_NOPE_DIM + QK_ROPE_DIM) ** -0.5

    def attend(blk):
        qn, qr = blk
        s = jnp.einsum('bqhd,bkhd->bhqk', qn, k_nope) + jnp.einsum('bqhr,bkr->bhqk', qr, k_rope)
        p = jax.nn.softmax(s.astype(jnp.float32) * scale, axis=-1).astype(v.dtype)
        return jnp.einsum('bhqk,bkhd->bqhd', p, v)

    out = lax.map(attend, (qn_blocks, qr_blocks))
    return out.swapaxes(0, 1).reshape(bsz, length, ATTN_WIDTH)


def conv_ffn(h, w_up, w_gate, conv_w, conv_b, w_down):
    up = h @ w_up
    pad = jnp.pad(up, ((0, 0), (1, 1), (0, 0)))
    up = pad[:, :-2] * conv_w[0] + pad[:, 1:-1] * conv_w[1] + pad[:, 2:] * conv_w[2] + conv_b
    return (jax.nn.silu(up) * (h @ w_gate)) @ w_down


def encode(x, w_in, norm_mix_g, ssm_a_re, ssm_a_im, ssm_b_re, ssm_b_im, ssm_c_re, ssm_c_im,
           ssm_log_dt, ssm_d, ssm_glu_w, q_norm_g, w_q_up, kv_norm_g, w_kv_up, ssm_out_norm_g,
           attn_out_norm_g, w_out, norm_ffn_g, w_ffn_up, w_ffn_gate, ffn_conv_w, ffn_conv_b,
           w_ffn_down, norm_final_g):
    o1 = SSM_WIDTH
    o2 = o1 + Q_LORA_RANK
    o3 = o2 + KV_LORA_RANK
    for l in range(DEPTH):
        h = rms_norm(x, norm_mix_g[l])
        proj = h @ w_in[l]
        ssm_out = s5_mixer(proj[..., :o1], ssm_a_re[l], ssm_a_im[l], ssm_b_re[l], ssm_b_im[l],
                           ssm_c_re[l], ssm_c_im[l], ssm_log_dt[l], ssm_d[l], ssm_glu_w[l])
        attn_out = mla_mixer(proj[..., o1:o2], proj[..., o2:o3], proj[..., o3:],
                             q_norm_g[l], w_q_up[l], kv_norm_g[l], w_kv_up[l])
        merged = jnp.concatenate([rms_norm(ssm_out, ssm_out_norm_g[l]),
                                  rms_norm(attn_out, attn_out_norm_g[l])], axis=-1)
        x = x + merged @ w_out[l]
        x = x + conv_ffn(rms_norm(x, norm_ffn_g[l]), w_ffn_up[l], w_ffn_gate[l],
                         ffn_conv_w[l], ffn_conv_b[l], w_ffn_down[l])
    return rms_norm(x, norm_final_g)


def setup_inputs(seed: int = 0) -> dict:
    key = jax.random.key(seed)
    ks = jax.random.split(key, 32)
    f32 = jnp.float32

    def nrm(k, shape, scale):
        return jax.random.normal(k, shape, f32) * scale

    def gain(k, shape):
        return 1.0 + 0.02 * jax.random.normal(k, shape, f32)

    a_re = -0.5 + 0.01 * jax.random.normal(ks[4], (DEPTH, 2, SSM_GROUPS, SSM_STATE), f32)
    a_im = math.pi * jnp.arange(SSM_STATE, dtype=f32) + 0.01 * jax.random.normal(ks[5], (DEPTH, 2, SSM_GROUPS, SSM_STATE), f32)
    log_dt = jax.random.uniform(ks[10], (DEPTH, 2, SSM_GROUPS), f32, math.log(0.001), math.log(0.1))
    return {
        'x_prompt': jax.random.normal(ks[0], (BATCH, SEQ, D_MODEL), f32),
        'x_sample': jax.random.normal(ks[1], (DEC_BATCH, DEC_SEQ, D_MODEL), f32),
        'w_in': nrm(ks[2], (DEPTH, D_MODEL, IN_COLS), D_MODEL ** -0.5),
        'norm_mix_g': gain(ks[3], (DEPTH, D_MODEL)),
        'ssm_a_re': a_re,
        'ssm_a_im': a_im,
        'ssm_b_re': nrm(ks[6], (DEPTH, 2, SSM_GROUPS, SSM_STATE, SSM_GROUP), (2 * SSM_GROUP) ** -0.5),
        'ssm_b_im': nrm(ks[7], (DEPTH, 2, SSM_GROUPS, SSM_STATE, SSM_GROUP), (2 * SSM_GROUP) ** -0.5),
        'ssm_c_re': nrm(ks[8], (DEPTH, 2, SSM_GROUPS, SSM_GROUP, SSM_STATE), (2 * SSM_STATE) ** -0.5),
        'ssm_c_im': nrm(ks[9], (DEPTH, 2, SSM_GROUPS, SSM_GROUP, SSM_STATE), (2 * SSM_STATE) ** -0.5),
        'ssm_log_dt': log_dt,
        'ssm_d': nrm(ks[11], (DEPTH, SSM_GROUPS, SSM_GROUP), 1.0),
        'ssm_glu_w': nrm(ks[12], (DEPTH, SSM_WIDTH, SSM_WIDTH), SSM_WIDTH ** -0.5),
        'q_norm_g': gain(ks[13], (DEPTH, Q_LORA_RANK)),
        'w_q_up': nrm(ks[14], (DEPTH, Q_LORA_RANK, N_HEADS * (QK_NOPE_DIM + QK_ROPE_DIM)), Q_LORA_RANK ** -0.5),
        'kv_norm_g': gain(ks[15], (DEPTH, KV_LORA_RANK)),
        'w_kv_up': nrm(ks[16], (DEPTH, KV_LORA_RANK, N_HEADS * (QK_NOPE_DIM + V_HEAD_DIM)), KV_LORA_RANK ** -0.5),
        'ssm_out_norm_g': gain(ks[17], (DEPTH, SSM_WIDTH)),
        'attn_out_norm_g': gain(ks[18], (DEPTH, ATTN_WIDTH)),
        'w_out': nrm(ks[19], (DEPTH, MIX_WIDTH, D_MODEL), MIX_WIDTH ** -0.5),
        'norm_ffn_g': gain(ks[20], (DEPTH, D_MODEL)),
        'w_ffn_up': nrm(ks[21], (DEPTH, D_MODEL, D_FF), D_MODEL ** -0.5),
        'w_ffn_gate': nrm(ks[22], (DEPTH, D_MODEL, D_FF), D_MODEL ** -0.5),
        'ffn_conv_w': nrm(ks[23], (DEPTH, CONV_WIDTH, D_FF), CONV_WIDTH ** -0.5),
        'ffn_conv_b': nrm(ks[24], (DEPTH, D_FF), 0.02),
        'w_ffn_down': nrm(ks[25], (DEPTH, D_FF, D_MODEL), D_FF ** -0.5),
        'norm_final_g': gain(ks[26], (D_MODEL,)),
    }


def reference(x_prompt, x_sample, w_in, norm_mix_g, ssm_a_re, ssm_a_im, ssm_b_re, ssm_b_im,
              ssm_c_re, ssm_c_im, ssm_log_dt, ssm_d, ssm_glu_w, q_norm_g, w_q_up, kv_norm_g,
              w_kv_up, ssm_out_norm_g, attn_out_norm_g, w_out, norm_ffn_g, w_ffn_up, w_ffn_gate,
              ffn_conv_w, ffn_conv_b, w_ffn_down, norm_final_g):
    y_prompt = encode(x_prompt, w_in, norm_mix_g, ssm_a_re, ssm_a_im, ssm_b_re, ssm_b_im, ssm_c_re,
                      ssm_c_im, ssm_log_dt, ssm_d, ssm_glu_w, q_norm_g, w_q_up, kv_norm_g, w_kv_up,
                      ssm_out_norm_g, attn_out_norm_g, w_out, norm_ffn_g, w_ffn_up, w_ffn_gate,
                      ffn_conv_w, ffn_conv_b, w_ffn_down, norm_final_g)
    y_sample = encode(x_sample, w_in, norm_mix_g, ssm_a_re, ssm_a_im, ssm_b_re, ssm_b_im, ssm_c_re,
                      ssm_c_im, ssm_log_dt, ssm_d, ssm_glu_w, q_norm_g, w_q_up, kv_norm_g, w_kv_up,
                      ssm_out_norm_g, attn_out_norm_g, w_out, norm_ffn_g, w_ffn_up, w_ffn_gate,
                      ffn_conv_w, ffn_conv_b, w_ffn_down, norm_final_g)
    return (y_prompt, y_sample)
```

```python
from contextlib import ExitStack
import numpy as np
from concourse.bass_utils import run_bass_kernel_spmd
import concourse.bass as bass
import concourse.mybir as mybir

F32 = mybir.dt.float32
BF16 = mybir.dt.bfloat16
I32 = mybir.dt.int32
ALU = mybir.AluOpType
AF = mybir.ActivationFunctionType
TWO_PI = 6.283185307179586
PI = 3.141592653589793


class Buf:
    __slots__ = ("w", "r", "name")

    def __init__(self, name=""):
        self.w = None
        self.r = {}
        self.name = name


class Prog:
    NDMA = 12

    def __init__(self, nc, phase_sem):
        self.nc = nc
        self.phase_sem = phase_sem
        self.phase_idx = 0
        self.sems = {}
        self.cnt = {}
        self.dma_rr = {"sync": 0, "act": 0, "pool": 0}
        for eng in ("act", "pool", "dve", "pe"):
            self._sem(eng)
        for qn in ("sync", "act", "pool"):
            for i in range(self.NDMA):
                self._sem(f"d{qn}{i}")
        self._reset()

    def _reset(self):
        self.q = {k: [] for k in ("sync", "act", "pool", "dve", "pe")}

    def _sem(self, name):
        if name not in self.sems:
            self.sems[name] = self.nc.alloc_semaphore(f"s_{name}")
            self.cnt[name] = 0
        return self.sems[name]

    def op(self, eng, fn, reads=(), writes=(), waits=(), dma=False, track=True):
        w = {}

        def add(ev):
            if ev is None:
                return
            s, c = ev
            if w.get(s, 0) < c:
                w[s] = c
        for ev in waits:
            add(ev)
        for b in reads:
            add(b.w)
        for b in writes:
            add(b.w)
            for s, c in b.r.items():
                add((s, c))
        ev = None
        inc = 0
        sname = None
        if track:
            if dma:
                i = self.dma_rr[eng]
                self.dma_rr[eng] = (i + 1) % self.NDMA
                sname = f"d{eng}{i}"
                if self.cnt[sname] > 0:
                    add((sname, self.cnt[sname]))
                inc = 16
            else:
                sname = eng
                inc = 1
            self.cnt[sname] += inc
            ev = (sname, self.cnt[sname])
        self.q[eng].append((fn, tuple(w.items()), sname, inc))
        if ev is not None:
            for b in reads:
                if b.r.get(ev[0], 0) < ev[1]:
                    b.r[ev[0]] = ev[1]
            for b in writes:
                b.w = ev
                b.r = {}
        return ev

    def mm_group(self, psbuf, mms, extra_reads=()):
        n = len(mms)
        allreads = []
        for i, (fn, reads) in enumerate(mms):
            allreads.extend(reads)
            last = i == n - 1
            w = {}

            def add(ev, w=w):
                if ev is None:
                    return
                s, c = ev
                if w.get(s, 0) < c:
                    w[s] = c
            for b in reads:
                add(b.w)
            if i == 0:
                add(psbuf.w)
                for s, c in psbuf.r.items():
                    add((s, c))
            if last:
                self.cnt["pe"] += 1
                ev = ("pe", self.cnt["pe"])
                self.q["pe"].append((fn, tuple(w.items()), "pe", 1))
            else:
                self.q["pe"].append((fn, tuple(w.items()), None, 0))
        for b in allreads:
            if b.r.get("pe", 0) < ev[1]:
                b.r["pe"] = ev[1]
        psbuf.w = ev
        psbuf.r = {}
        return ev

    def mm(self, fn, reads=(), ps=None, first=False, last=False):
        w = {}

        def add(ev):
            if ev is None:
                return
            s_, c = ev
            if w.get(s_, 0) < c:
                w[s_] = c
        for b_ in reads:
            add(b_.w)
        if first and ps is not None:
            add(ps.w)
            for s_, c in ps.r.items():
                add((s_, c))
        if last:
            self.cnt["pe"] += 1
            ev = ("pe", self.cnt["pe"])
            self.q["pe"].append((fn, tuple(w.items()), "pe", 1))
            for b_ in reads:
                if b_.r.get("pe", 0) < ev[1]:
                    b_.r["pe"] = ev[1]
            ps.w = ev
            ps.r = {}
            return ev
        self.cnt["pe"] += 1
        ev = ("pe", self.cnt["pe"])
        self.q["pe"].append((fn, tuple(w.items()), "pe", 1))
        for b_ in reads:
            if b_.r.get("pe", 0) < ev[1]:
                b_.r["pe"] = ev[1]
        return None

    def flush(self):
        nc = self.nc
        pidx = self.phase_idx
        with nc.Block() as block:
            def run(engname):
                def body(e):
                    seen = {}
                    if pidx > 0:
                        e.wait_ge(self.phase_sem, pidx)
                    for fn, waits, sname, inc in self.q[engname]:
                        for (s, v) in waits:
                            if seen.get(s, 0) >= v:
                                continue
                            seen[s] = v
                            e.wait_ge(self.sems[s], v)
                        ins = fn(e)
                        if sname is not None:
                            ins.then_inc(self.sems[sname], inc)
                    if engname == "sync":
                        for s, c in self.cnt.items():
                            if c > 0 and seen.get(s, 0) < c:
                                e.wait_ge(self.sems[s], c)
                        e.nop().then_inc(self.phase_sem, 1)
                return body
            block.sync(run("sync"))
            block.scalar(run("act"))
            block.gpsimd(run("pool"))
            block.vector(run("dve"))
            block.tensor(run("pe"))
        self.phase_idx += 1
        self._reset()


def splits(n, maxw):
    k = (n + maxw - 1) // maxw
    base, rem = divmod(n, k)
    out = []
    lo = 0
    for i in range(k):
        wdt = base + (1 if i < rem else 0)
        out.append((lo, lo + wdt))
        lo += wdt
    return out


class Cfg:
    def __init__(self, D, G, H, QL, KVL, DFF, Lp, Ls):
        self.D, self.G, self.H, self.QL, self.KVL, self.DFF = D, G, H, QL, KVL, DFF
        self.SW = 16 * G
        self.AW = 128 * H
        self.KD = D // 128
        self.UT = self.SW // 128
        self.QT = QL // 128
        self.KT = KVL // 128
        self.HT = H
        self.MT = self.UT + self.HT
        self.FT = DFF // 128
        self.INC = self.SW + QL + KVL + 64
        self.L = [Lp, Ls]
        self.T = [Lp // 4, Ls // 4]
        self.QKD = 192
        self.EPS = 1e-6


FULL = Cfg(4096, 128, 16, 896, 512, 11008, 8192, 4096)


class B:
    pass


def own_ranges(L, T, lo, hi):
    out = []
    a, b = max(lo, 0), min(hi, T + 1)
    if a < b:
        out.append((a, b, a + 1))
    if lo <= L - 1 < hi:
        out.append((L - 1, L, 0))
    return out


def declare(nc, cfg, debug):
    b = B()
    b.nc, b.cfg = nc, cfg
    D = cfg.D
    kind_s = "ExternalOutput" if debug else "Internal"

    def inp(name, shape, dt=F32):
        return nc.dram_tensor(name, list(shape), dt, kind="ExternalInput").ap()

    def scr(name, shape, dt):
        return nc.dram_tensor(name, list(shape), dt, kind=kind_s).ap()
    b.x = [inp("xp", [cfg.L[0], D]), inp("xs", [cfg.L[1], D])]
    b.pos = [inp("posp", [1, cfg.L[0]]), inp("poss", [1, cfg.L[1]])]
    b.masks = inp("masks", [1, 32])
    b.w_in = inp("w_in", [D, cfg.INC])
    b.g_mix = inp("g_mix", [1, D])
    b.g_ffn = inp("g_ffn", [1, D])
    b.g_fin = inp("g_fin", [1, D])
    b.gq = inp("gq", [128, cfg.QT])
    b.gkv = inp("gkv", [128, cfg.KT])
    b.gmo = inp("gmo", [128, cfg.MT])
    b.w_q_up = inp("w_q_up", [cfg.QL, cfg.H * 192])
    b.w_kv_up = inp("w_kv_up", [cfg.KVL, cfg.H * 256])
    b.w_out = inp("w_out", [cfg.SW + cfg.AW, D])
    b.w_up = inp("w_up", [D, cfg.DFF])
    b.w_gate = inp("w_gate", [D, cfg.DFF])
    b.w_down = inp("w_down", [cfg.DFF, D])
    b.convw = inp("convw", [128, 3, cfg.FT])
    b.convb = inp("convb", [128, cfg.FT])
    b.glu_w = inp("glu_w", [cfg.SW, cfg.SW])
    b.invf = inp("invf", [64, 1])
    b.sgn = inp("sgn", [64, 1])
    b.ident_in = inp("ident", [128, 128])
    b.log_dt = inp("log_dt", [1, 2 * cfg.G])
    b.sel_in = inp("sel", [128, 64, 128])
    b.selo_in = inp("selo", [128, 64, 128])
    b.cst = inp("cst", [128, 8, 128])
    b.sig = inp("sigv", [128, 4])
    b.evec = inp("evec", [1, 64])
    b.are_h = inp("are_h", [128, 2 * cfg.G])
    b.aim_h = inp("aim_h", [128, 2 * cfg.G])
    b.bx1_h = inp("bx1_h", [128, 2 * cfg.G, 16])
    b.bx2_h = inp("bx2_h", [128, 2 * cfg.G, 16])
    b.cx1_h = inp("cx1_h", [128, 2 * cfg.G, 16])
    b.cx2_h = inp("cx2_h", [128, 2 * cfg.G, 16])
    b.dsk_h = inp("dsk_h", [128, cfg.G])
    b.y = [nc.dram_tensor("yp", [cfg.T[0], D], F32, kind="ExternalOutput").ap(),
           nc.dram_tensor("ys", [cfg.T[1], D], F32, kind="ExternalOutput").ap()]
    b.uT, b.kvn, b.kr, b.rtab, b.qn, b.mraw, b.x1, b.h2T, b.x2, b.yT = [], [], [], [], [], [], [], [], [], []
    for s in range(2):
        L, T = cfg.L[s], cfg.T[s]
        b.uT.append(scr(f"uT{s}", [cfg.SW, L], BF16))
        b.kvn.append(scr(f"kvn{s}", [cfg.KVL, L], BF16))
        b.kr.append(scr(f"kr{s}", [64, L], BF16))
        b.rtab.append(scr(f"rtab{s}", [2, 64, L], F32))
        b.qn.append(scr(f"qn{s}", [cfg.QL, T + 2], BF16))
        b.mraw.append(scr(f"mraw{s}", [cfg.SW + cfg.AW, T + 2], BF16))
        b.x1.append(scr(f"x1{s}", [T + 2, D], F32))
        b.h2T.append(scr(f"h2T{s}", [D, T + 2], BF16))
        b.x2.append(scr(f"x2{s}", [T, D], F32))
        b.yT.append(scr(f"yT{s}", [cfg.SW, T + 2], F32))
    b.phase_sem = nc.alloc_semaphore("phase")
    b.P = Prog(nc, b.phase_sem)
    return b


def bcast_rows(ap_row, nparts, n, off=0):
    return bass.AP(ap_row.tensor, ap_row.offset + off, [[0, nparts], [1, n]])


def range_reduce(P, eng, ang, tmpf, tmpi, angB, tmpB, n):
    P.op(eng, lambda e: e.tensor_scalar(out=tmpf, in0=ang, scalar1=1.0 / TWO_PI, scalar2=None, op0=ALU.mult), reads=[angB], writes=[tmpB])
    P.op(eng, lambda e: e.tensor_copy(out=tmpi, in_=tmpf), reads=[tmpB], writes=[tmpB])
    P.op(eng, lambda e: e.tensor_copy(out=tmpf, in_=tmpi), reads=[tmpB], writes=[tmpB])
    P.op(eng, lambda e: e.scalar_tensor_tensor(out=ang, in0=tmpf, scalar=-TWO_PI, in1=ang, op0=ALU.mult, op1=ALU.add), reads=[tmpB], writes=[angB])
    P.op(eng, lambda e: e.tensor_scalar(out=tmpf, in0=ang, scalar1=PI, scalar2=TWO_PI, op0=ALU.is_gt, op1=ALU.mult), reads=[angB], writes=[tmpB])
    P.op(eng, lambda e: e.tensor_tensor(out=ang, in0=ang, in1=tmpf, op=ALU.subtract), reads=[tmpB], writes=[angB])
    P.op(eng, lambda e: e.tensor_scalar(out=tmpf, in0=ang, scalar1=-PI, scalar2=TWO_PI, op0=ALU.is_lt, op1=ALU.mult), reads=[angB], writes=[tmpB])
    P.op(eng, lambda e: e.tensor_tensor(out=ang, in0=ang, in1=tmpf, op=ALU.add), reads=[tmpB], writes=[angB])


def phase_A(b, seq):
    nc, cfg, P = b.nc, b.cfg, b.P
    L, T, D, KD = cfg.L[seq], cfg.T[seq], cfg.D, cfg.KD
    GS = 512
    NG = L // GS
    x = b.x[seq]
    o1, o2, o3 = cfg.SW, cfg.SW + cfg.QL, cfg.SW + cfg.QL + cfg.KVL
    NLAT = max(cfg.QT, cfg.KT)
    with ExitStack() as st:
        def sb(name, shape, dt):
            return st.enter_context(nc.sbuf_tensor(f"A{seq}_{name}", list(shape), dt))

        def pst(name, shape, dt):
            return st.enter_context(nc.psum_tensor(f"A{seq}_{name}", list(shape), dt))
        xt = [sb(f"xt{i}", [128, D], F32) for i in range(2)]
        xtB = [Buf() for _ in range(2)]
        xn = [sb(f"xn{i}", [128, D], BF16) for i in range(2)]
        xnB = [Buf() for _ in range(2)]
        ss = sb("ss", [128, 8], F32)
        ssB = Buf()
        hT = sb("hT", [128, KD, GS], BF16)
        hTB = [Buf() for _ in range(4)]
        gbc = sb("gbc", [128, D], F32)
        gbcB = Buf()
        wt = [sb(f"wt{i}", [128, KD, 128], BF16) for i in range(3)]
        wtB = [Buf() for _ in range(3)]
        wtr = sb("wtr", [128, KD, 64], BF16)
        wtrs = sb("wtrs", [128, KD, 64], BF16)
        wtrB = Buf()
        ident = sb("ident", [128, 128], BF16)
        ones = sb("ones", [128, 128], BF16)
        cB = Buf()
        gq = sb("gq", [128, cfg.QT], F32)
        gkv = sb("gkv", [128, cfg.KT], F32)
        invf = sb("invf", [64, 1], F32)
        sgn = sb("sgn", [64, 1], F32)
        stg = [sb(f"stg{i}", [128, GS], BF16) for i in range(4)]
        stgB = [Buf() for _ in range(4)]
        ltmp = sb("ltmp", [128, NLAT, GS], F32)
        ltmpB = [Buf() for _ in range(NLAT)]
        lsq = sb("lsq", [128, NLAT, GS], BF16)
        lsqB = [Buf() for _ in range(NLAT)]
        rk = sb("rk", [128, GS], F32)
        rkB = Buf()
        posb = sb("posb", [64, GS], F32)
        posB = Buf()
        ang = sb("ang", [64, GS], F32)
        angB = Buf()
        rtf = sb("rtf", [64, GS], F32)
        rti = sb("rti", [64, GS], I32)
        rtB = Buf()
        cosT = sb("cosT", [64, GS], F32)
        sinT = sb("sinT", [64, GS], F32)
        cosB, sinB = Buf(), Buf()
        t1 = sb("t1", [64, GS], F32)
        t2 = sb("t2", [64, GS], F32)
        t1B, t2B = Buf(), Buf()
        tp = [pst(f"tp{i}", [128, 8, 128], BF16) for i in range(2)]
        tpB = [Buf() for _ in range(2)]
        acc = [pst(f"acc{i}", [128, GS], F32) for i in range(3)]
        accB = [Buf() for _ in range(3)]
        ssp = pst("ssp", [128, GS], F32)
        sspB = Buf()

        P.op("pool", lambda e: e.dma_start(out=ident[:], in_=b.ident_in), writes=[cB], dma=True)
        P.op("pool", lambda e: e.memset(ones[:], 1.0), writes=[cB])
        P.op("sync", lambda e: e.dma_start(out=gbc[:], in_=bcast_rows(b.g_mix, 128, D)), writes=[gbcB], dma=True)
        P.op("sync", lambda e: e.dma_start(out=gq[:], in_=b.gq), writes=[cB], dma=True)
        P.op("sync", lambda e: e.dma_start(out=gkv[:], in_=b.gkv), writes=[cB], dma=True)
        P.op("sync", lambda e: e.dma_start(out=invf[:], in_=b.invf), writes=[cB], dma=True)
        P.op("sync", lambda e: e.dma_start(out=sgn[:], in_=b.sgn), writes=[cB], dma=True)
        wv = b.w_in.rearrange("(k p) c -> p k c", p=128)
        P.op("pool", lambda e: e.dma_start(out=wtr[:], in_=wv[:, :, o3:o3 + 64]), writes=[wtrB], dma=True)
        P.op("pool", lambda e: e.dma_start(out=wtrs[:, :, 0:32], in_=wv[:, :, o3 + 32:o3 + 64]), writes=[wtrB], dma=True)
        P.op("pool", lambda e: e.dma_start(out=wtrs[:, :, 32:64], in_=wv[:, :, o3:o3 + 32]), writes=[wtrB], dma=True)

        cnt = {"tile": 0, "wt": 0, "acc": 0, "stg": 0, "ev": 0}

        def evac_eng():
            cnt["ev"] += 1
            return "act" if cnt["ev"] % 2 else "dve"

        def copy_op(eng, out, in_):
            if eng == "act":
                return lambda e: e.activation(out=out, in_=in_, func=AF.Copy)
            return lambda e: e.tensor_copy(out=out, in_=in_)

        def load_w(c0, M):
            i = cnt["wt"] % 3
            cnt["wt"] += 1
            P.op("pool", lambda e: e.dma_start(out=wt[i][:, :, 0:M], in_=wv[:, :, c0:c0 + M]), writes=[wtB[i]], dma=True)
            return i

        def proj(wtile, wB, M, lo, hi):
            a = cnt["acc"] % 3
            cnt["acc"] += 1
            N = hi - lo
            mms = []
            for k in range(KD):
                mms.append((lambda e, k=k: e.matmul(acc[a][0:M, 0:N], lhsT=wtile[:, k, 0:M], rhs=hT[:, k, lo:hi],
                                                    start=(k == 0), stop=(k == KD - 1)), [wB] + hTB if k == 0 else []))
            P.mm_group(accB[a], mms)
            return a

        def store(dst, src_fn_eng, N, M=128):
            i = cnt["stg"] % 4
            cnt["stg"] += 1
            src_fn_eng(stg[i][0:M, 0:N], stgB[i])
            P.op("sync", lambda e: e.dma_start(out=dst, in_=stg[i][0:M, 0:N], allow_slow_non_contiguous=(N < 16)), reads=[stgB[i]], dma=True)

        def latent(kind, ntile, c_base, gain, nfeat, lo, hi, dst, dcol):
            N = hi - lo
            for t in range(ntile):
                wi = load_w(c_base + t * 128, 128)
                a = proj(wt[wi], wtB[wi], 128, lo, hi)
                P.op("act", lambda e, a=a, t=t: e.activation(out=ltmp[:, t, 0:N], in_=acc[a][:, 0:N], func=AF.Copy), reads=[accB[a]], writes=[ltmpB[t]])
                P.op("act", lambda e, a=a, t=t: e.activation(out=lsq[:, t, 0:N], in_=acc[a][:, 0:N], func=AF.Square), reads=[accB[a]], writes=[lsqB[t]])
            P.mm_group(sspB, [(lambda e, t=t: e.matmul(ssp[:, 0:N], lhsT=ones[:], rhs=lsq[:, t, 0:N], start=(t == 0), stop=(t == ntile - 1)), [lsqB[t], cB]) for t in range(ntile)])
            P.op("act", lambda e: e.activation(out=rk[:, 0:N], in_=ssp[:, 0:N], func=AF.Ln, scale=1.0 / nfeat, bias=cfg.EPS), reads=[sspB], writes=[rkB])
            P.op("act", lambda e: e.activation(out=rk[:, 0:N], in_=rk[:, 0:N], func=AF.Exp, scale=-0.5), reads=[rkB], writes=[rkB])
            for t in range(ntile):
                def prod(sap, sB, t=t):
                    P.op("dve", lambda e: e.scalar_tensor_tensor(out=sap, in0=ltmp[:, t, 0:N], scalar=gain[:, t:t + 1], in1=rk[:, 0:N], op0=ALU.mult, op1=ALU.mult), reads=[ltmpB[t], rkB, cB], writes=[sB])
                store(dst[t * 128:(t + 1) * 128, dcol:dcol + N], prod, N)

        for g in range(NG):
            for tt in range(4):
                ti = cnt["tile"]
                cnt["tile"] += 1
                bi = ti % 2
                r0 = g * GS + tt * 128
                P.op("sync", lambda e, bi=bi, r0=r0: e.dma_start(out=xt[bi][:], in_=x[r0:r0 + 128, :]), writes=[xtB[bi]], dma=True)
                P.op("act", lambda e, bi=bi: e.activation(out=xn[bi][:], in_=xt[bi][:], func=AF.Square, accum_out=ss[:, 0:1]), reads=[xtB[bi]], writes=[xnB[bi], ssB])
                P.op("act", lambda e: e.activation(out=ss[:, 1:2], in_=ss[:, 0:1], func=AF.Ln, scale=1.0 / D, bias=cfg.EPS), reads=[ssB], writes=[ssB])
                P.op("act", lambda e: e.activation(out=ss[:, 2:3], in_=ss[:, 1:2], func=AF.Exp, scale=-0.5), reads=[ssB], writes=[ssB])
                P.op("dve", lambda e, bi=bi: e.scalar_tensor_tensor(out=xn[bi][:], in0=xt[bi][:], scalar=ss[:, 2:3], in1=gbc[:], op0=ALU.mult, op1=ALU.mult), reads=[xtB[bi], ssB, gbcB], writes=[xnB[bi]])
                TB = min(8, KD)
                for j in range(KD // TB):
                    pb = (ti * (KD // TB) + j) % 2
                    P.mm_group(tpB[pb], [(lambda e, pb=pb, bi=bi, k=k: e.transpose(out=tp[pb][:, k % TB, :], in_=xn[bi][:, k * 128:(k + 1) * 128], identity=ident[:]), [xnB[bi], cB]) for k in range(j * TB, j * TB + TB)])
                    eng = "act"
                    P.op(eng, copy_op(eng, hT[:, j * TB:j * TB + TB, tt * 128:(tt + 1) * 128], tp[pb][:, 0:TB, :]), reads=[tpB[pb]], writes=[hTB[tt]])
            for j in range(cfg.UT):
                wi = load_w(j * 128, 128)
                a = proj(wt[wi], wtB[wi], 128, 0, GS)
                eng = evac_eng()

                def prod(sap, sB, a=a, eng=eng):
                    P.op(eng, copy_op(eng, sap, acc[a][:, 0:GS]), reads=[accB[a]], writes=[sB])
                store(b.uT[seq][j * 128:(j + 1) * 128, g * GS:(g + 1) * GS], prod, GS)
            latent("kv", cfg.KT, o2, gkv, cfg.KVL, 0, GS, b.kvn[seq], g * GS)
            P.op("sync", lambda e, g=g: e.dma_start(out=posb[:], in_=bcast_rows(b.pos[seq], 64, GS, off=g * GS)), writes=[posB], dma=True)
            for which in range(2):
                if which == 0:
                    P.op("dve", lambda e: e.tensor_scalar(out=ang[:], in0=posb[:], scalar1=invf[:, 0:1], scalar2=None, op0=ALU.mult), reads=[posB, cB], writes=[angB])
                else:
                    P.op("dve", lambda e: e.tensor_scalar(out=ang[:], in0=posb[:], scalar1=invf[:, 0:1], scalar2=PI / 2, op0=ALU.mult, op1=ALU.add), reads=[posB, cB], writes=[angB])
                range_reduce(P, "dve", ang[:], rtf[:], rti[:], angB, rtB, GS)
                if which == 0:
                    P.op("act", lambda e: e.activation(out=sinT[:], in_=ang[:], func=AF.Sin), reads=[angB], writes=[sinB])
                    P.op("dve", lambda e: e.tensor_scalar(out=sinT[:], in0=sinT[:], scalar1=sgn[:, 0:1], scalar2=None, op0=ALU.mult), reads=[sinB, cB], writes=[sinB])
                else:
                    P.op("act", lambda e: e.activation(out=cosT[:], in_=ang[:], func=AF.Sin), reads=[angB], writes=[cosB])
            P.op("sync", lambda e, g=g: e.dma_start(out=b.rtab[seq][0, :, g * GS:(g + 1) * GS], in_=cosT[:]), reads=[cosB], dma=True)
            P.op("sync", lambda e, g=g: e.dma_start(out=b.rtab[seq][1, :, g * GS:(g + 1) * GS], in_=sinT[:]), reads=[sinB], dma=True)
            a1 = proj(wtr, wtrB, 64, 0, GS)
            a2 = proj(wtrs, wtrB, 64, 0, GS)
            P.op("dve", lambda e, a1=a1: e.tensor_tensor(out=t1[:], in0=acc[a1][0:64, :], in1=cosT[:], op=ALU.mult), reads=[accB[a1], cosB], writes=[t1B])
            P.op("dve", lambda e, a2=a2: e.tensor_tensor(out=t2[:], in0=acc[a2][0:64, :], in1=sinT[:], op=ALU.mult), reads=[accB[a2], sinB], writes=[t2B])

            def prod(sap, sB):
                P.op("dve", lambda e: e.tensor_tensor(out=sap, in0=t1[:], in1=t2[:], op=ALU.add), reads=[t1B, t2B], writes=[sB])
            store(b.kr[seq][:, g * GS:(g + 1) * GS], prod, GS, M=64)
            for (tlo, thi, clo) in own_ranges(L, T, g * GS, (g + 1) * GS):
                latent("q", cfg.QT, o1, gq, cfg.QL, tlo - g * GS, thi - g * GS, b.qn[seq], clo)
        P.flush()


def copy_fn(eng, out, in_, scale=None):
    if eng == "act":
        if scale is None:
            return lambda e: e.activation(out=out, in_=in_, func=AF.Copy)
        return lambda e: e.activation(out=out, in_=in_, func=AF.Copy, scale=scale)
    if scale is None:
        return lambda e: e.tensor_copy(out=out, in_=in_)
    return lambda e: e.tensor_scalar(out=out, in0=in_, scalar1=scale, scalar2=None, op0=ALU.mult)


def phase_C(b):
    for seq in range(2):
        attn_seq(b, seq)


def attn_seq(b, seq):
    nc, cfg, P = b.nc, b.cfg, b.P
    L, T = cfg.L[seq], cfg.T[seq]
    TC = T + 2
    KT, QT, H = cfg.KT, cfg.QT, cfg.H
    NKT, NKB = L // 128, L // 512
    qtiles = splits(TC, 512)
    SCALE = 192.0 ** -0.5
    with ExitStack() as st:
        def sb(name, shape, dt):
            return st.enter_context(nc.sbuf_tensor(f"C{seq}_{name}", list(shape), dt))

        def pst(name, shape, dt):
            return st.enter_context(nc.psum_tensor(f"C{seq}_{name}", list(shape), dt))
        kvnT = sb("kvnT", [128, KT, L], BF16)
        krT = sb("krT", [64, L], BF16)
        qnT = sb("qnT", [128, QT, TC], BF16)
        cosq = sb("cosq", [64, TC], BF16)
        sinq = sb("sinq", [64, TC], BF16)
        inB = Buf()
        Kh = sb("Kh", [128, L], BF16)
        Vh = sb("Vh", [128, L], BF16)
        KhB = [Buf() for _ in range(NKB)]
        VhB = [Buf() for _ in range(NKB)]
        qhn = sb("qhn", [128, TC], BF16)
        qhr = sb("qhr", [64, TC], BF16)
        qhB = [Buf() for _ in qtiles]
        wkv = [sb(f"wkv{i}", [128, KT, 256], BF16) for i in range(2)]
        wq = [sb(f"wq{i}", [128, QT, 192], BF16) for i in range(2)]
        wqs = [sb(f"wqs{i}", [128, QT, 64], BF16) for i in range(2)]
        wB = [Buf() for _ in range(2)]
        PT = [sb(f"PT{i}", [128, 512], BF16) for i in range(3)]
        PTB = [Buf() for _ in range(3)]
        rden = sb("rden", [128, 512], F32)
        rdB = Buf()
        ostg = [sb(f"ostg{i}", [128, 512], BF16) for i in range(2)]
        ostgB = [Buf() for _ in range(2)]
        t1 = sb("t1", [64, 512], F32)
        t2 = sb("t2", [64, 512], F32)
        t1B, t2B = Buf(), Buf()
        ones = sb("ones", [128, 128], BF16)
        cB = Buf()
        S = [pst(f"S{i}", [128, 512], F32) for i in range(2)]
        SB = [Buf() for _ in range(2)]
        O = [pst(f"O{i}", [128, 512], F32) for i in range(2)]
        OB = [Buf() for _ in range(2)]
        DEN = [pst(f"DEN{i}", [128, 512], F32) for i in range(2)]
        DENB = [Buf() for _ in range(2)]
        acc = [pst(f"acc{i}", [128, 512], F32) for i in range(2)]
        accB = [Buf() for _ in range(2)]
        cnt = {"acc": 0, "ev": 0, "s": 0, "pt": 0, "o": 0, "st": 0}

        def evac_eng():
            cnt["ev"] += 1
            return "act" if cnt["ev"] % 2 else "dve"

        P.op("pool", lambda e: e.memset(ones[:], 1.0), writes=[cB])
        P.op("sync", lambda e: e.dma_start(out=kvnT[:], in_=b.kvn[seq].rearrange("(k p) l -> p k l", p=128)), writes=[inB], dma=True)
        P.op("sync", lambda e: e.dma_start(out=krT[:], in_=b.kr[seq]), writes=[inB], dma=True)
        P.op("sync", lambda e: e.dma_start(out=qnT[:], in_=b.qn[seq].rearrange("(k p) c -> p k c", p=128)), writes=[inB], dma=True)
        for tab, dst in ((0, cosq), (1, sinq)):
            P.op("pool", lambda e, tab=tab, dst=dst: e.dma_start(out=dst[:, 0:1], in_=b.rtab[seq][tab, :, L - 1:L], allow_slow_non_contiguous=True), writes=[inB], dma=True)
            P.op("pool", lambda e, tab=tab, dst=dst: e.dma_start(out=dst[:, 1:TC], in_=b.rtab[seq][tab, :, 0:T + 1]), writes=[inB], dma=True)
        wkvv = b.w_kv_up.rearrange("(k p) c -> p k c", p=128)
        wqv = b.w_q_up.rearrange("(k p) c -> p k c", p=128)
        for h in range(H):
            wi = h % 2
            P.op("pool", lambda e, wi=wi, h=h: e.dma_start(out=wkv[wi][:], in_=wkvv[:, :, h * 256:(h + 1) * 256]), writes=[wB[wi]], dma=True)
            P.op("pool", lambda e, wi=wi, h=h: e.dma_start(out=wq[wi][:], in_=wqv[:, :, h * 192:(h + 1) * 192]), writes=[wB[wi]], dma=True)
            P.op("pool", lambda e, wi=wi, h=h: e.dma_start(out=wqs[wi][:, :, 0:32], in_=wqv[:, :, h * 192 + 160:h * 192 + 192]), writes=[wB[wi]], dma=True)
            P.op("pool", lambda e, wi=wi, h=h: e.dma_start(out=wqs[wi][:, :, 32:64], in_=wqv[:, :, h * 192 + 128:h * 192 + 160]), writes=[wB[wi]], dma=True)
            for kb in range(NKB):
                a = cnt["acc"] % 2
                cnt["acc"] += 1
                P.mm_group(accB[a], [(lambda e, a=a, kt=kt, kb=kb, wi=wi: e.matmul(acc[a][:, 0:512], lhsT=wkv[wi][:, kt, 0:128], rhs=kvnT[:, kt, kb * 512:(kb + 1) * 512], start=(kt == 0), stop=(kt == KT - 1)), [wB[wi], inB]) for kt in range(KT)])
                eng = evac_eng()
                P.op(eng, copy_fn(eng, Kh[:, kb * 512:(kb + 1) * 512], acc[a][:, 0:512]), reads=[accB[a]], writes=[KhB[kb]])
                a = cnt["acc"] % 2
                cnt["acc"] += 1
                mms = []
                for i in range(4):
                    for kt in range(KT):
                        k0 = kb * 512 + i * 128
                        mms.append((lambda e, a=a, kt=kt, i=i, k0=k0, wi=wi: e.matmul(acc[a][:, i * 128:(i + 1) * 128], lhsT=kvnT[:, kt, k0:k0 + 128], rhs=wkv[wi][:, kt, 128:256], start=(kt == 0), stop=(kt == KT - 1)), [wB[wi], inB]))
                P.mm_group(accB[a], mms)
                eng = evac_eng()
                P.op(eng, copy_fn(eng, Vh[:, kb * 512:(kb + 1) * 512], acc[a][:, 0:512]), reads=[accB[a]], writes=[VhB[kb]])
            for qi, (c0, c1) in enumerate(qtiles):
                N = c1 - c0
                a = cnt["acc"] % 2
                cnt["acc"] += 1
                P.mm_group(accB[a], [(lambda e, a=a, qt=qt, wi=wi, c0=c0, c1=c1, N=N: e.matmul(acc[a][:, 0:N], lhsT=wq[wi][:, qt, 0:128], rhs=qnT[:, qt, c0:c1], start=(qt == 0), stop=(qt == QT - 1)), [wB[wi], inB]) for qt in range(QT)])
                P.op("act", copy_fn("act", qhn[:, c0:c1], acc[a][:, 0:N], scale=SCALE), reads=[accB[a]], writes=[qhB[qi]])
                a1 = cnt["acc"] % 2
                cnt["acc"] += 1
                P.mm_group(accB[a1], [(lambda e, a1=a1, qt=qt, wi=wi, c0=c0, c1=c1, N=N: e.matmul(acc[a1][0:64, 0:N], lhsT=wq[wi][:, qt, 128:192], rhs=qnT[:, qt, c0:c1], start=(qt == 0), stop=(qt == QT - 1)), [wB[wi], inB]) for qt in range(QT)])
                P.op("dve", lambda e, a1=a1, c0=c0, c1=c1, N=N: e.scalar_tensor_tensor(out=t1[:, 0:N], in0=acc[a1][0:64, 0:N], scalar=SCALE, in1=cosq[:, c0:c1], op0=ALU.mult, op1=ALU.mult), reads=[accB[a1], inB], writes=[t1B])
                a2 = cnt["acc"] % 2
                cnt["acc"] += 1
                P.mm_group(accB[a2], [(lambda e, a2=a2, qt=qt, wi=wi, c0=c0, c1=c1, N=N: e.matmul(acc[a2][0:64, 0:N], lhsT=wqs[wi][:, qt, 0:64], rhs=qnT[:, qt, c0:c1], start=(qt == 0), stop=(qt == QT - 1)), [wB[wi], inB]) for qt in range(QT)])
                P.op("dve", lambda e, a2=a2, c0=c0, c1=c1, N=N: e.scalar_tensor_tensor(out=t2[:, 0:N], in0=acc[a2][0:64, 0:N], scalar=SCALE, in1=sinq[:, c0:c1], op0=ALU.mult, op1=ALU.mult), reads=[accB[a2], inB], writes=[t2B])
                P.op("dve", lambda e, c0=c0, c1=c1, N=N: e.tensor_tensor(out=qhr[:, c0:c1], in0=t1[:, 0:N], in1=t2[:, 0:N], op=ALU.add), reads=[t1B, t2B], writes=[qhB[qi]])
            for qi, (c0, c1) in enumerate(qtiles):
                N = c1 - c0
                ob = cnt["o"] % 2
                cnt["o"] += 1
                for kt in range(NKT):
                    si = cnt["s"] % 2
                    cnt["s"] += 1
                    pi = cnt["pt"] % 3
                    cnt["pt"] += 1
                    kb = kt // 4
                    k0 = kt * 128
                    P.mm_group(SB[si], [
                        (lambda e, si=si, k0=k0, c0=c0, c1=c1, N=N: e.matmul(S[si][:, 0:N], lhsT=Kh[:, k0:k0 + 128], rhs=qhn[:, c0:c1], start=True, stop=False), [KhB[kb], qhB[qi]]),
                        (lambda e, si=si, k0=k0, c0=c0, c1=c1, N=N: e.matmul(S[si][:, 0:N], lhsT=krT[:, k0:k0 + 128], rhs=qhr[:, c0:c1], start=False, stop=True), [inB]),
                    ])
                    P.op("act", lambda e, si=si, pi=pi, N=N: e.activation(out=PT[pi][:, 0:N], in_=S[si][:, 0:N], func=AF.Exp), reads=[SB[si]], writes=[PTB[pi]])
                    first, last = kt == 0, kt == NKT - 1
                    P.mm(lambda e, ob=ob, pi=pi, k0=k0, N=N, first=first, last=last: e.matmul(O[ob][:, 0:N], lhsT=Vh[:, k0:k0 + 128], rhs=PT[pi][:, 0:N], start=first, stop=last), reads=[VhB[kb], PTB[pi]], ps=OB[ob], first=first, last=last)
                    P.mm(lambda e, ob=ob, pi=pi, N=N, first=first, last=last: e.matmul(DEN[ob][:, 0:N], lhsT=ones[:], rhs=PT[pi][:, 0:N], start=first, stop=last), reads=[cB, PTB[pi]], ps=DENB[ob], first=first, last=last)
                P.op("dve", lambda e, ob=ob, N=N: e.reciprocal(out=rden[:, 0:N], in_=DEN[ob][:, 0:N]), reads=[DENB[ob]], writes=[rdB])
                oi = cnt["st"] % 2
                cnt["st"] += 1
                P.op("dve", lambda e, ob=ob, oi=oi, N=N: e.tensor_tensor(out=ostg[oi][:, 0:N], in0=O[ob][:, 0:N], in1=rden[:, 0:N], op=ALU.mult), reads=[OB[ob], rdB], writes=[ostgB[oi]])
                r0 = cfg.SW + h * 128
                P.op("sync", lambda e, oi=oi, r0=r0, c0=c0, c1=c1, N=N: e.dma_start(out=b.mraw[seq][r0:r0 + 128, c0:c1], in_=ostg[oi][:, 0:N]), reads=[ostgB[oi]], dma=True)
        P.flush()


def phase_D(b):
    nc, cfg, P = b.nc, b.cfg, b.P
    D, MT, UT = cfg.D, cfg.MT, cfg.UT
    NCG = D // 512
    for seq in range(2):
        L, T = cfg.L[seq], cfg.T[seq]
        TC = T + 2
        for gi, (g0, g1) in enumerate(splits(TC, 1026)):
            n = g1 - g0
            with ExitStack() as st:
                def sb(name, shape, dt):
                    return st.enter_context(nc.sbuf_tensor(f"D{seq}{gi}_{name}", list(shape), dt))

                def pst(name, shape, dt):
                    return st.enter_context(nc.psum_tensor(f"D{seq}{gi}_{name}", list(shape), dt))
                mr = sb("mr", [128, MT, n], BF16)
                mrB = [Buf() for _ in range(MT)]
                sq = [sb(f"sq{i}", [128, 512], BF16) for i in range(2)]
                sqB = [Buf() for _ in range(2)]
                rs = sb("rs", [128, 2, n], F32)
                rsB = [Buf() for _ in range(2)]
                wo = [sb(f"wo{i}", [128, MT, 512], BF16) for i in range(2)]
                woB = [Buf() for _ in range(2)]
                xr = [sb(f"xr{i}", [128, 512], F32) for i in range(2)]
                xrB = [Buf() for _ in range(2)]
                xo = [sb(f"xo{i}", [128, 512], F32) for i in range(2)]
                xoB = [Buf() for _ in range(2)]
                gmo = sb("gmo", [128, MT], F32)
                ones = sb("ones", [128, 128], BF16)
                cB = Buf()
                ssp = pst("ssp", [128, 512], F32)
                sspB = Buf()
                acc = [pst(f"acc{i}", [128, 512], F32) for i in range(3)]
                accB = [Buf() for _ in range(3)]
                cnt = {"sq": 0, "acc": 0, "x": 0}
                P.op("pool", lambda e: e.memset(ones[:], 1.0), writes=[cB])
                P.op("sync", lambda e: e.dma_start(out=gmo[:], in_=b.gmo), writes=[cB], dma=True)
                mv = b.mraw[seq].rearrange("(k p) c -> p k c", p=128)
                for k in range(MT):
                    P.op("sync", lambda e, k=k: e.dma_start(out=mr[:, k, :], in_=mv[:, k, g0:g1]), writes=[mrB[k]], dma=True)
                for half, (k0, k1) in enumerate(((0, UT), (UT, MT))):
                    width = (k1 - k0) * 128
                    for (c0, c1) in splits(n, 512):
                        N = c1 - c0
                        for k in range(k0, k1):
                            i = cnt["sq"] % 2
                            cnt["sq"] += 1
                            P.op("act", lambda e, i=i, k=k, c0=c0, c1=c1, N=N: e.activation(out=sq[i][:, 0:N], in_=mr[:, k, c0:c1], func=AF.Square), reads=[mrB[k]], writes=[sqB[i]])
                            P.mm(lambda e, i=i, N=N, k=k, k0=k0, k1=k1: e.matmul(ssp[:, 0:N], lhsT=ones[:], rhs=sq[i][:, 0:N], start=(k == k0), stop=(k == k1 - 1)), reads=[sqB[i], cB], ps=sspB, first=(k == k0), last=(k == k1 - 1))
                        P.op("act", lambda e, half=half, c0=c0, c1=c1, N=N, width=width: e.activation(out=rs[:, half, c0:c1], in_=ssp[:, 0:N], func=AF.Ln, scale=1.0 / width, bias=cfg.EPS), reads=[sspB], writes=[rsB[half]])
                        P.op("act", lambda e, half=half, c0=c0, c1=c1: e.activation(out=rs[:, half, c0:c1], in_=rs[:, half, c0:c1], func=AF.Exp, scale=-0.5), reads=[rsB[half]], writes=[rsB[half]])
                for k in range(MT):
                    half = 0 if k < UT else 1
                    P.op("dve", lambda e, k=k, half=half: e.scalar_tensor_tensor(out=mr[:, k, :], in0=mr[:, k, :], scalar=gmo[:, k:k + 1], in1=rs[:, half, :], op0=ALU.mult, op1=ALU.mult), reads=[rsB[half], cB], writes=[mrB[k]])
                wov = b.w_out.rearrange("(k p) c -> p k c", p=128)
                for cgi in range(NCG):
                    wi = cgi % 2
                    P.op("pool", lambda e, wi=wi, cgi=cgi: e.dma_start(out=wo[wi][:], in_=wov[:, :, cgi * 512:(cgi + 1) * 512]), writes=[woB[wi]], dma=True)
                    for (t0, t1_) in splits(n, 128):
                        m = t1_ - t0
                        ca, cb = g0 + t0, g0 + t1_
                        a = cnt["acc"] % 3
                        cnt["acc"] += 1
                        P.mm_group(accB[a], [(lambda e, a=a, k=k, wi=wi, t0=t0, t1_=t1_, m=m: e.matmul(acc[a][0:m, 0:512], lhsT=mr[:, k, t0:t1_], rhs=wo[wi][:, k, :], start=(k == 0), stop=(k == MT - 1)), [mrB[k], woB[wi]]) for k in range(MT)])
                        xi = cnt["x"] % 2
                        cnt["x"] += 1
                        cs = slice(cgi * 512, (cgi + 1) * 512)
                        if ca == 0:
                            P.op("sync", lambda e, xi=xi, cs=cs: e.dma_start(out=xr[xi][0:1, :], in_=b.x[seq][L - 1:L, cs]), writes=[xrB[xi]], dma=True)
                            if m > 1:
                                P.op("sync", lambda e, xi=xi, cs=cs, m=m, cb=cb: e.dma_start(out=xr[xi][1:m, :], in_=b.x[seq][0:cb - 1, cs]), writes=[xrB[xi]], dma=True)
                        else:
                            P.op("sync", lambda e, xi=xi, cs=cs, m=m, ca=ca, cb=cb: e.dma_start(out=xr[xi][0:m, :], in_=b.x[seq][ca - 1:cb - 1, cs]), writes=[xrB[xi]], dma=True)
                        P.op("dve", lambda e, xi=xi, a=a, m=m: e.tensor_tensor(out=xo[xi][0:m, :], in0=acc[a][0:m, 0:512], in1=xr[xi][0:m, :], op=ALU.add), reads=[accB[a], xrB[xi]], writes=[xoB[xi]])
                        P.op("sync", lambda e, xi=xi, m=m, ca=ca, cb=cb, cs=cs: e.dma_start(out=b.x1[seq][ca:cb, cs], in_=xo[xi][0:m, :]), reads=[xoB[xi]], dma=True)
                P.flush()


def phase_G(b):
    nc, cfg, P = b.nc, b.cfg, b.P
    D, KD = cfg.D, cfg.KD
    TB = min(8, KD)
    with ExitStack() as st:
        def sb(name, shape, dt):
            return st.enter_context(nc.sbuf_tensor(f"G_{name}", list(shape), dt))

        def pst(name, shape, dt):
            return st.enter_context(nc.psum_tensor(f"G_{name}", list(shape), dt))
        xt = [sb(f"xt{i}", [128, D], F32) for i in range(2)]
        xtB = [Buf() for _ in range(2)]
        xn = [sb(f"xn{i}", [128, D], BF16) for i in range(2)]
        xnB = [Buf() for _ in range(2)]
        ss = sb("ss", [128, 8], F32)
        ssB = Buf()
        gbc = sb("gbc", [128, D], F32)
        ident = sb("ident", [128, 128], BF16)
        cB = Buf()
        hs = [sb(f"hs{i}", [128, KD, 128], BF16) for i in range(2)]
        hsB = [Buf() for _ in range(2)]
        tp = [pst(f"tp{i}", [128, 8, 128], BF16) for i in range(2)]
        tpB = [Buf() for _ in range(2)]
        P.op("pool", lambda e: e.dma_start(out=ident[:], in_=b.ident_in), writes=[cB], dma=True)
        P.op("sync", lambda e: e.dma_start(out=gbc[:], in_=bcast_rows(b.g_ffn, 128, D)), writes=[cB], dma=True)
        ti = 0
        pbc = 0
        for seq in range(2):
            TC = cfg.T[seq] + 2
            hv = b.h2T[seq].rearrange("(k p) c -> p k c", p=128)
            for (r0, r1) in splits(TC, 128):
                m = r1 - r0
                bi = ti % 2
                ti += 1
                P.op("sync", lambda e, bi=bi, r0=r0, r1=r1, m=m, x1s=b.x1[seq]: e.dma_start(out=xt[bi][0:m, :], in_=x1s[r0:r1, :]), writes=[xtB[bi]], dma=True)
                P.op("act", lambda e, bi=bi, m=m: e.activation(out=xn[bi][0:m, :], in_=xt[bi][0:m, :], func=AF.Square, accum_out=ss[0:m, 0:1]), reads=[xtB[bi]], writes=[xnB[bi], ssB])
                P.op("act", lambda e, m=m: e.activation(out=ss[0:m, 1:2], in_=ss[0:m, 0:1], func=AF.Ln, scale=1.0 / D, bias=cfg.EPS), reads=[ssB], writes=[ssB])
                P.op("act", lambda e, m=m: e.activation(out=ss[0:m, 2:3], in_=ss[0:m, 1:2], func=AF.Exp, scale=-0.5), reads=[ssB], writes=[ssB])
                P.op("dve", lambda e, bi=bi, m=m: e.scalar_tensor_tensor(out=xn[bi][0:m, :], in0=xt[bi][0:m, :], scalar=ss[0:m, 2:3], in1=gbc[0:m, :], op0=ALU.mult, op1=ALU.mult), reads=[xtB[bi], ssB, cB], writes=[xnB[bi]])
                for j in range(KD // TB):
                    pb = pbc % 2
                    pbc += 1
                    P.mm_group(tpB[pb], [(lambda e, pb=pb, bi=bi, k=k, m=m: e.transpose(out=tp[pb][:, k % TB, 0:m], in_=xn[bi][0:m, k * 128:(k + 1) * 128], identity=ident[0:m, 0:m]), [xnB[bi], cB]) for k in range(j * TB, j * TB + TB)])
                    P.op("act", copy_fn("act", hs[bi][:, j * TB:j * TB + TB, 0:m], tp[pb][:, 0:TB, 0:m]), reads=[tpB[pb]], writes=[hsB[bi]])
                P.op("sync", lambda e, bi=bi, r0=r0, r1=r1, m=m, hv=hv: e.dma_start(out=hv[:, :, r0:r1], in_=hs[bi][:, :, 0:m], allow_slow_non_contiguous=(m < 16)), reads=[hsB[bi]], dma=True)
        P.flush()


def phase_E(b):
    nc, cfg, P = b.nc, b.cfg, b.P
    D, KD, FT = cfg.D, cfg.KD, cfg.FT
    NCG = D // 512
    GW = 512
    for seq in range(2):
        T = cfg.T[seq]
        NGR = T // GW
        for gi in range(NGR):
            a0 = 1 + gi * GW
            with ExitStack() as st:
                def sb(name, shape, dt):
                    return st.enter_context(nc.sbuf_tensor(f"E{seq}{gi}_{name}", list(shape), dt))

                def pst(name, shape, dt):
                    return st.enter_context(nc.psum_tensor(f"E{seq}{gi}_{name}", list(shape), dt))
                h2 = sb("h2", [128, KD, GW + 2], BF16)
                h2B = Buf()
                actT = sb("actT", [128, FT, GW], BF16)
                actB = [Buf() for _ in range(FT)]
                cw = sb("cw", [128, 3, FT], F32)
                cbias = sb("cbias", [128, FT], F32)
                mk = sb("mk", [128, 32], F32)
                cB = Buf()
                pb = [pst(f"pb{i}", [128, 512], F32) for i in range(8)]
                pbB = [Buf() for _ in range(8)]
                x2B = [[Buf() for _ in range(NCG)] for _ in range(4)]
                st1 = ExitStack()

                def sb1(name, shape, dt):
                    return st1.enter_context(nc.sbuf_tensor(f"E{seq}{gi}_{name}", list(shape), dt))
                wu = [sb1(f"wu{i}", [128, KD, 128], BF16) for i in range(2)]
                wg = [sb1(f"wg{i}", [128, KD, 128], BF16) for i in range(2)]
                wuB = [Buf() for _ in range(2)]
                wgB = [Buf() for _ in range(2)]
                upf = [sb1(f"upf{i}", [128, GW + 2], F32) for i in range(2)]
                upfB = [Buf() for _ in range(2)]
                c1 = sb1("c1", [128, GW], F32)
                c1B = Buf()
                sil = sb1("sil", [128, GW], F32)
                silB = Buf()
                P.op("sync", lambda e: e.dma_start(out=cw[:], in_=b.convw), writes=[cB], dma=True)
                P.op("sync", lambda e: e.dma_start(out=cbias[:], in_=b.convb), writes=[cB], dma=True)
                P.op("sync", lambda e: e.dma_start(out=mk[:], in_=bcast_rows(b.masks, 128, 32)), writes=[cB], dma=True)
                hv = b.h2T[seq].rearrange("(k p) c -> p k c", p=128)
                P.op("sync", lambda e, a0=a0, hv=hv: e.dma_start(out=h2[:], in_=hv[:, :, a0 - 1:a0 + GW + 1]), writes=[h2B], dma=True)
                wuv = b.w_up.rearrange("(k p) c -> p k c", p=128)
                wgv = b.w_gate.rearrange("(k p) c -> p k c", p=128)
                halves = splits(GW + 2, 512)
                for f in range(FT):
                    wi = f % 2
                    P.op("pool", lambda e, wi=wi, f=f: e.dma_start(out=wu[wi][:], in_=wuv[:, :, f * 128:(f + 1) * 128]), writes=[wuB[wi]], dma=True)
                    P.op("pool", lambda e, wi=wi, f=f: e.dma_start(out=wg[wi][:], in_=wgv[:, :, f * 128:(f + 1) * 128]), writes=[wgB[wi]], dma=True)
                    ui = f % 2
                    for hi_, (lo, hi) in enumerate(halves):
                        pi = (f % 2) * 2 + hi_
                        N = hi - lo
                        P.mm_group(pbB[pi], [(lambda e, pi=pi, k=k, wi=wi, lo=lo, hi=hi, N=N: e.matmul(pb[pi][:, 0:N], lhsT=wu[wi][:, k, :], rhs=h2[:, k, lo:hi], start=(k == 0), stop=(k == KD - 1)), [wuB[wi], h2B]) for k in range(KD)])
                        P.op("act", copy_fn("act", upf[ui][:, lo:hi], pb[pi][:, 0:N]), reads=[pbB[pi]], writes=[upfB[ui]])
                    gp = 4 + f % 2
                    P.mm_group(pbB[gp], [(lambda e, gp=gp, k=k, wi=wi: e.matmul(pb[gp][:, 0:GW], lhsT=wg[wi][:, k, :], rhs=h2[:, k, 1:GW + 1], start=(k == 0), stop=(k == KD - 1)), [wgB[wi], h2B]) for k in range(KD)])
                    if gi == 0:
                        P.op("dve", lambda e, ui=ui: e.tensor_scalar(out=upf[ui][:, 0:1], in0=upf[ui][:, 0:1], scalar1=mk[:, 3:4], scalar2=None, op0=ALU.mult), reads=[cB], writes=[upfB[ui]])
                    if gi == NGR - 1:
                        P.op("dve", lambda e, ui=ui: e.tensor_scalar(out=upf[ui][:, GW + 1:GW + 2], in0=upf[ui][:, GW + 1:GW + 2], scalar1=mk[:, 0:1], scalar2=None, op0=ALU.mult), reads=[cB], writes=[upfB[ui]])
                    P.op("dve", lambda e, ui=ui, f=f: e.tensor_scalar(out=c1[:], in0=upf[ui][:, 1:GW + 1], scalar1=cw[:, 1, f:f + 1], scalar2=cbias[:, f:f + 1], op0=ALU.mult, op1=ALU.add), reads=[upfB[ui], cB], writes=[c1B])
                    P.op("dve", lambda e, ui=ui, f=f: e.scalar_tensor_tensor(out=c1[:], in0=upf[ui][:, 0:GW], scalar=cw[:, 0, f:f + 1], in1=c1[:], op0=ALU.mult, op1=ALU.add), reads=[upfB[ui], cB], writes=[c1B])
                    P.op("dve", lambda e, ui=ui, f=f: e.scalar_tensor_tensor(out=c1[:], in0=upf[ui][:, 2:GW + 2], scalar=cw[:, 2, f:f + 1], in1=c1[:], op0=ALU.mult, op1=ALU.add), reads=[upfB[ui], cB], writes=[c1B])
                    P.op("act", lambda e: e.activation(out=sil[:], in_=c1[:], func=AF.Silu), reads=[c1B], writes=[silB])
                    P.op("dve", lambda e, gp=gp, f=f: e.tensor_tensor(out=actT[:, f, :], in0=pb[gp][:, 0:GW], in1=sil[:], op=ALU.mult), reads=[pbB[gp], silB], writes=[actB[f]])
                P.flush()
                st1.close()
                st2 = ExitStack()

                def sb2(name, shape, dt):
                    return st2.enter_context(nc.sbuf_tensor(f"E{seq}{gi}_{name}", list(shape), dt))
                wd = [sb2(f"wd{i}", [128, 512], BF16) for i in range(6)]
                wdB = [Buf() for _ in range(6)]
                xr = [sb2(f"xr{i}", [128, 512], F32) for i in range(2)]
                xrB = [Buf() for _ in range(2)]
                xo = [sb2(f"xo{i}", [128, 512], F32) for i in range(2)]
                xoB = [Buf() for _ in range(2)]
                junk = sb2("junk", [128, 512], BF16)
                junkB = Buf()
                ssq = sb2("ssq", [128, 4, NCG + 4], F32)
                ssqB = [Buf() for _ in range(4)]
                gfs = [sb2(f"gfs{i}", [128, 512], F32) for i in range(2)]
                gfsB = [Buf() for _ in range(2)]
                wdc = 0
                xc = 0
                for cgi in range(NCG):
                    st_ = (cgi % 2) * 4
                    cs = slice(cgi * 512, (cgi + 1) * 512)
                    for f in range(FT):
                        wi = wdc % 6
                        wdc += 1
                        P.op("pool", lambda e, wi=wi, f=f, cs=cs: e.dma_start(out=wd[wi][:], in_=b.w_down[f * 128:(f + 1) * 128, cs]), writes=[wdB[wi]], dma=True)
                        for i in range(4):
                            P.mm(lambda e, i=i, f=f, wi=wi, st_=st_: e.matmul(pb[st_ + i][:, 0:512], lhsT=actT[:, f, i * 128:(i + 1) * 128], rhs=wd[wi][:], start=(f == 0), stop=(f == FT - 1)), reads=[wdB[wi], actB[f]], ps=pbB[st_ + i], first=(f == 0), last=(f == FT - 1))
                    for i in range(4):
                        xi = xc % 2
                        xc += 1
                        ca = a0 + i * 128
                        P.op("sync", lambda e, xi=xi, ca=ca, cs=cs: e.dma_start(out=xr[xi][:], in_=b.x1[seq][ca:ca + 128, cs]), writes=[xrB[xi]], dma=True)
                        P.op("dve", lambda e, xi=xi, i=i, st_=st_: e.tensor_tensor(out=xo[xi][:], in0=pb[st_ + i][:, 0:512], in1=xr[xi][:], op=ALU.add), reads=[pbB[st_ + i], xrB[xi]], writes=[xoB[xi]])
                        P.op("act", lambda e, xi=xi, i=i, cgi=cgi: e.activation(out=junk[:], in_=xo[xi][:], func=AF.Square, accum_out=ssq[:, i, cgi:cgi + 1]), reads=[xoB[xi]], writes=[junkB, ssqB[i]])
                        P.op("sync", lambda e, xi=xi, ca=ca, cs=cs: e.dma_start(out=b.x2[seq][ca - 1:ca + 127, cs], in_=xo[xi][:]), reads=[xoB[xi]], writes=[x2B[i][cgi]], dma=True)
                for i in range(4):
                    P.op("dve", lambda e, i=i: e.tensor_reduce(out=ssq[:, i, NCG:NCG + 1], in_=ssq[:, i, 0:NCG], axis=mybir.AxisListType.X, op=ALU.add), reads=[ssqB[i]], writes=[ssqB[i]])
                    P.op("act", lambda e, i=i: e.activation(out=ssq[:, i, NCG + 1:NCG + 2], in_=ssq[:, i, NCG:NCG + 1], func=AF.Ln, scale=1.0 / D, bias=cfg.EPS), reads=[ssqB[i]], writes=[ssqB[i]])
                    P.op("act", lambda e, i=i: e.activation(out=ssq[:, i, NCG + 2:NCG + 3], in_=ssq[:, i, NCG + 1:NCG + 2], func=AF.Exp, scale=-0.5), reads=[ssqB[i]], writes=[ssqB[i]])
                    for cgi in range(NCG):
                        xi = xc % 2
                        xc += 1
                        cs = slice(cgi * 512, (cgi + 1) * 512)
                        r0 = a0 - 1 + i * 128
                        P.op("sync", lambda e, xi=xi, r0=r0, cs=cs: e.dma_start(out=xr[xi][:], in_=b.x2[seq][r0:r0 + 128, cs]), reads=[x2B[i][cgi]], writes=[xrB[xi]], dma=True)
                        P.op("sync", lambda e, xi=xi, cgi=cgi: e.dma_start(out=gfs[xi][:], in_=bcast_rows(b.g_fin, 128, 512, off=cgi * 512)), writes=[gfsB[xi]], dma=True)
                        P.op("dve", lambda e, xi=xi, i=i, cs=cs: e.scalar_tensor_tensor(out=xo[xi][:], in0=xr[xi][:], scalar=ssq[:, i, NCG + 2:NCG + 3], in1=gfs[xi][:], op0=ALU.mult, op1=ALU.mult), reads=[xrB[xi], ssqB[i], gfsB[xi]], writes=[xoB[xi]])
                        P.op("sync", lambda e, xi=xi, r0=r0, cs=cs: e.dma_start(out=b.y[seq][r0:r0 + 128, cs], in_=xo[xi][:]), reads=[xoB[xi]], dma=True)
                P.flush()
                st2.close()


def MM(out, lhsT, rhs, start, stop):
    return lambda e: e.matmul(out, lhsT=lhsT, rhs=rhs, start=start, stop=stop)


def rot_exps(cfg):
    s = set()
    for k in range(1, 8):
        s.add(8 * k)
        s.add(64 * k)
    for T in cfg.T:
        rq = T // 512
        for k in range(1, rq):
            s.add(512 * k)
        s.add(T)
        s.add(2 * T)
    return sorted(s)


def exp_vector(cfg):
    ev = list(range(0, 8)) + list(range(7, -1, -1)) + list(range(1, 9)) + list(range(8, 0, -1)) + [-i for i in range(8)]
    ev += rot_exps(cfg)
    assert len(ev) <= 64
    return ev


def sb3(t, off, dims):
    row = 1
    for d in list(t.shape)[1:]:
        row *= d
    return bass.AP(t, off, [[row, 128]] + [list(d) for d in dims])


def phase_B(b):
    nc, cfg, P = b.nc, b.cfg, b.P
    G, UT = cfg.G, cfg.UT
    ROT = rot_exps(cfg)
    NR = len(ROT)
    ROTI = {e: i for i, e in enumerate(ROT)}
    EV = exp_vector(cfg)
    NE = len(EV)
    RC0 = 40
    with ExitStack() as st:
        def sb(name, shape, dt):
            return st.enter_context(nc.sbuf_tensor(f"B_{name}", list(shape), dt))

        def pst(name, shape, dt):
            return st.enter_context(nc.psum_tensor(f"B_{name}", list(shape), dt))
        sel = sb("sel", [128, 64, 128], BF16)
        selo = sb("selo", [128, 64, 128], BF16)
        cstf = sb("cstf", [128, 4, 128], F32)
        identb = sb("identb", [128, 128], BF16)
        sig = sb("sig", [128, 4], F32)
        evb = sb("evb", [128, 64], F32)
        mk = sb("mk", [128, 32], F32)
        cB = Buf()
        are = sb("are", [128, 16], F32)
        aim = sb("aim", [128, 16], F32)
        ldt = sb("ldt", [128, 16], F32)
        sm = sb("sm", [128, 12, 16], F32)
        Bx1 = sb("Bx1", [128, 16, 16], F32)
        Bx2 = sb("Bx2", [128, 16, 16], F32)
        Cx1 = sb("Cx1", [128, 16, 16], F32)
        Cx2 = sb("Cx2", [128, 16, 16], F32)
        M1B = sb("M1B", [128, 16, 16], F32)
        M2B = sb("M2B", [128, 16, 16], F32)
        tb1 = sb("tb1", [128, 16, 16], F32)
        tb2 = sb("tb2", [128, 16, 16], F32)
        dvec = sb("dvec", [128, 8], F32)
        LRt = sb("LRt", [128, 16, NE], F32)
        ANG = sb("ANG", [128, 16, NE], F32)
        ANGf = sb("ANGf", [128, 16, NE], F32)
        ANGi = sb("ANGi", [128, 16, NE], I32)
        PT1 = sb("PT1", [128, 16, NE], F32)
        PT2 = sb("PT2", [128, 16, NE], F32)
        NPT2 = sb("NPT2", [128, 16, NE], F32)
        tabB = Buf()
        inB = Buf()
        gtA = sb("gtA", [128, 8, 16], F32)
        gtB = sb("gtB", [128, 8, 16], F32)
        gtAB, gtBB = Buf(), Buf()
        Wxn = sb("Wxn", [128, 8, 16], BF16)
        WxnB = Buf()
        Phi = sb("Phi", [128, 2, 128], BF16)
        Psi = sb("Psi", [128, 2, 128], BF16)
        PhB = Buf()
        ttA = sb("ttA", [128, 128], F32)
        ttBt = sb("ttBt", [128, 128], F32)
        ttAB, ttBB = Buf(), Buf()
        rtmp = [sb(f"rtmp{i}", [128, 128], F32) for i in range(2)]
        rtmpB = [Buf() for _ in range(2)]
        WxT = [[sb(f"WxT{p}{d}", [128, 128], BF16) for d in range(2)] for p in range(2)]
        Wy = [[sb(f"Wy{p}{d}", [128, 8, 16], BF16) for d in range(2)] for p in range(2)]
        rot = [[sb(f"rot{p}{d}", [128, NR, 128], BF16) for d in range(2)] for p in range(2)]
        TT = [sb(f"TT{p}", [128, 128], BF16) for p in range(2)]
        matB = [Buf() for _ in range(2)]
        Lmax = max(cfg.L)
        NCmax = Lmax // 8
        uTt = [sb(f"uTt{s}", [128, cfg.L[s]], BF16) for s in range(2)]
        uTB = [Buf() for _ in range(2)]
        U = [sb(f"U{s}", [128, 8, cfg.L[s] // 8], BF16) for s in range(2)]
        UB = [[Buf() for _ in range(8)] for _ in range(2)]
        Xl = [sb("X0", [128, NCmax], BF16), sb("X1", [128, NCmax // 8], BF16), sb("X2", [128, NCmax // 64], BF16), sb("X3", [128, 4], BF16)]
        XB = [Buf() for _ in range(4)]
        Pl = [None, sb("P1", [128, NCmax // 8], BF16), sb("P2", [128, NCmax // 64], BF16), sb("P3", [128, 4], BF16)]
        P0 = [sb(f"P0_{d}", [128, NCmax], BF16) for d in range(2)]
        PB = [Buf() for _ in range(4)]
        P0B = [Buf() for _ in range(2)]
        Xm = sb("Xm", [128, 3, 4], BF16)
        XmB = Buf()
        NYmax = max(cfg.T) // 8 + 2
        Ys = [sb(f"Ys{s}", [128, 8, cfg.T[s] // 8 + 2], BF16) for s in range(2)]
        YsB = [[Buf() for _ in range(8)] for _ in range(2)]
        yst = [sb(f"yst{s}", [128, cfg.T[s] + 2], F32) for s in range(2)]
        ystB = [Buf() for _ in range(2)]
        pA = [pst(f"pA{i}", [128, 512], F32) for i in range(2)]
        pAB = [Buf() for _ in range(2)]
        pS = [pst(f"pS{i}", [128, 512], F32) for i in range(2)]
        pSB = [Buf() for _ in range(2)]
        pC = [pst(f"pC{i}", [128, 512], F32) for i in range(2)]
        pCB = [Buf() for _ in range(2)]
        pY = pst("pY", [128, 512], F32)
        pYB = Buf()
        pT = pst("pT", [128, 8, 128], BF16)
        pTB = Buf()
        cnt = {"a": 0, "s": 0, "ev": 0, "rt": 0}

        def evac_eng():
            cnt["ev"] += 1
            return "act" if cnt["ev"] % 2 else "dve"

        P.op("pool", lambda e: e.dma_start(out=sel[:], in_=b.sel_in), writes=[cB], dma=True)
        P.op("pool", lambda e: e.dma_start(out=selo[:], in_=b.selo_in), writes=[cB], dma=True)
        P.op("sync", lambda e: e.dma_start(out=cstf[:], in_=b.cst[:, 0:4, :]), writes=[cB], dma=True)
        P.op("pool", lambda e: e.dma_start(out=identb[:], in_=b.ident_in), writes=[cB], dma=True)
        P.op("sync", lambda e: e.dma_start(out=sig[:], in_=b.sig), writes=[cB], dma=True)
        P.op("sync", lambda e: e.dma_start(out=evb[:], in_=bcast_rows(b.evec, 128, 64)), writes=[cB], dma=True)
        P.op("sync", lambda e: e.dma_start(out=mk[:], in_=bcast_rows(b.masks, 128, 32)), writes=[cB], dma=True)
        pswap, maskf, maskb, identf = cstf[:, 0, :], cstf[:, 1, :], cstf[:, 2, :], cstf[:, 3, :]
        SIG, NSIG = sig[:, 0:1], sig[:, 1:2]

        def bc_e(tab):
            return tab[:, :].unsqueeze(2).broadcast_to([128, 16, NE])

        evbc = evb[:, 0:NE].unsqueeze(1).broadcast_to([128, 16, NE])

        def dv(fn, reads, writes):
            return P.op("dve", fn, reads=reads, writes=writes)

        for k in range(UT):
            g0 = k * 8
            for d in range(2):
                P.op("sync", lambda e, d=d, g0=g0: e.dma_start(out=are[:, d * 8:(d + 1) * 8], in_=b.are_h[:, d * G + g0:d * G + g0 + 8]), writes=[inB], dma=True)
                P.op("sync", lambda e, d=d, g0=g0: e.dma_start(out=aim[:, d * 8:(d + 1) * 8], in_=b.aim_h[:, d * G + g0:d * G + g0 + 8]), writes=[inB], dma=True)
                P.op("sync", lambda e, d=d, g0=g0: e.dma_start(out=ldt[:, d * 8:(d + 1) * 8], in_=bcast_rows(b.log_dt, 128, 8, off=d * G + g0)), writes=[inB], dma=True)
                for (dst, src) in ((Bx1, b.bx1_h), (Bx2, b.bx2_h), (Cx1, b.cx1_h), (Cx2, b.cx2_h)):
                    P.op("sync", lambda e, d=d, g0=g0, dst=dst, src=src: e.dma_start(out=dst[:, d * 8:(d + 1) * 8, :], in_=src[:, d * G + g0:d * G + g0 + 8, :]), writes=[inB], dma=True)
            P.op("sync", lambda e, g0=g0: e.dma_start(out=dvec[:], in_=b.dsk_h[:, g0:g0 + 8]), writes=[inB], dma=True)
            DT, LR, TH = sm[:, 0, :], sm[:, 1, :], sm[:, 2, :]
            P.op("act", lambda e: e.activation(out=DT, in_=ldt[:], func=AF.Exp), reads=[inB], writes=[tabB])
            dv(lambda e: e.tensor_tensor(out=LR, in0=are[:], in1=DT, op=ALU.mult), [inB, tabB], [tabB])
            dv(lambda e: e.tensor_tensor(out=TH, in0=aim[:], in1=DT, op=ALU.mult), [inB, tabB], [tabB])
            dv(lambda e: e.tensor_tensor(out=LRt[:], in0=bc_e(sm[:, 1, :]), in1=evbc, op=ALU.mult), [tabB, cB], [tabB])
            P.op("act", lambda e: e.activation(out=LRt[:], in_=LRt[:], func=AF.Exp), reads=[tabB], writes=[tabB])
            for which in range(2):
                dv(lambda e: e.tensor_tensor(out=ANG[:], in0=bc_e(sm[:, 2, :]), in1=evbc, op=ALU.mult), [tabB, cB], [tabB])
                if which == 1:
                    dv(lambda e: e.tensor_scalar(out=ANG[:], in0=ANG[:], scalar1=PI / 2, scalar2=None, op0=ALU.add), [tabB], [tabB])
                range_reduce(P, "dve", ANG[:], ANGf[:], ANGi[:], tabB, tabB, 16 * NE)
                P.op("act", lambda e: e.activation(out=ANG[:], in_=ANG[:], func=AF.Sin), reads=[tabB], writes=[tabB])
                if which == 0:
                    dv(lambda e: e.scalar_tensor_tensor(out=PT2[:], in0=LRt[:], scalar=SIG, in1=ANG[:], op0=ALU.mult, op1=ALU.mult), [tabB, cB], [tabB])
                    dv(lambda e: e.tensor_scalar(out=NPT2[:], in0=PT2[:], scalar1=-1.0, scalar2=None, op0=ALU.mult), [tabB], [tabB])
                else:
                    dv(lambda e: e.tensor_tensor(out=PT1[:], in0=LRt[:], in1=ANG[:], op=ALU.mult), [tabB], [tabB])
            i1 = 16
            NRr, NI, DEN, ZR, ZI, T0, T1 = (sm[:, j, :] for j in range(3, 10))
            dv(lambda e: e.tensor_scalar(out=NRr, in0=PT1[:, :, i1], scalar1=-1.0, scalar2=None, op0=ALU.add), [tabB], [tabB])
            dv(lambda e: e.tensor_scalar(out=NI, in0=PT2[:, :, i1], scalar1=SIG, scalar2=None, op0=ALU.mult), [tabB, cB], [tabB])
            dv(lambda e: e.tensor_tensor(out=DEN, in0=are[:], in1=are[:], op=ALU.mult), [inB], [tabB])
            dv(lambda e: e.tensor_tensor(out=T0, in0=aim[:], in1=aim[:], op=ALU.mult), [inB], [tabB])
            dv(lambda e: e.tensor_tensor(out=DEN, in0=DEN, in1=T0, op=ALU.add), [tabB], [tabB])
            dv(lambda e: e.reciprocal(out=DEN, in_=DEN), [tabB], [tabB])
            dv(lambda e: e.tensor_tensor(out=T0, in0=NRr, in1=are[:], op=ALU.mult), [tabB, inB], [tabB])
            dv(lambda e: e.tensor_tensor(out=T1, in0=NI, in1=aim[:], op=ALU.mult), [tabB, inB], [tabB])
            dv(lambda e: e.tensor_tensor(out=T0, in0=T0, in1=T1, op=ALU.add), [tabB], [tabB])
            dv(lambda e: e.tensor_tensor(out=ZR, in0=T0, in1=DEN, op=ALU.mult), [tabB], [tabB])
            dv(lambda e: e.tensor_tensor(out=T0, in0=NI, in1=are[:], op=ALU.mult), [tabB, inB], [tabB])
            dv(lambda e: e.tensor_tensor(out=T1, in0=NRr, in1=aim[:], op=ALU.mult), [tabB, inB], [tabB])
            dv(lambda e: e.tensor_tensor(out=T0, in0=T0, in1=T1, op=ALU.subtract), [tabB], [tabB])
            dv(lambda e: e.tensor_tensor(out=ZI, in0=T0, in1=DEN, op=ALU.mult), [tabB], [tabB])
            dv(lambda e: e.tensor_scalar(out=ZI, in0=ZI, scalar1=SIG, scalar2=None, op0=ALU.mult), [tabB, cB], [tabB])

            def bc_h(v):
                return v.unsqueeze(2).broadcast_to([128, 16, 16])
            dv(lambda e: e.tensor_tensor(out=tb1[:], in0=Bx1[:], in1=bc_h(ZR), op=ALU.mult), [tabB, inB], [tabB])
            dv(lambda e: e.tensor_tensor(out=tb2[:], in0=Bx2[:], in1=bc_h(ZI), op=ALU.mult), [tabB, inB], [tabB])
            dv(lambda e: e.tensor_tensor(out=M1B[:], in0=tb1[:], in1=tb2[:], op=ALU.add), [tabB], [tabB])
            dv(lambda e: e.tensor_tensor(out=tb1[:], in0=Bx2[:], in1=bc_h(ZR), op=ALU.mult), [tabB, inB], [tabB])
            dv(lambda e: e.tensor_tensor(out=tb2[:], in0=Bx1[:], in1=bc_h(ZI), op=ALU.mult), [tabB, inB], [tabB])
            dv(lambda e: e.tensor_tensor(out=M2B[:], in0=tb1[:], in1=tb2[:], op=ALU.subtract), [tabB], [tabB])
            dv(lambda e: e.tensor_scalar(out=Cx1[:], in0=Cx1[:], scalar1=NSIG, scalar2=None, op0=ALU.mult), [inB, cB], [tabB, inB])
            dv(lambda e: e.tensor_scalar(out=Cx2[:], in0=Cx2[:], scalar1=NSIG, scalar2=None, op0=ALU.mult), [inB, cB], [tabB, inB])
            for s in range(2):
                L = cfg.L[s]
                NC = L // 8
                P.op("sync", lambda e, s=s, k=k: e.dma_start(out=uTt[s][:], in_=b.uT[s][k * 128:(k + 1) * 128, :]), writes=[uTB[s]], dma=True)
                for gl in range(8):
                    for (c0, c1) in splits(NC, 512):
                        N = c1 - c0
                        a = cnt["a"] % 2
                        cnt["a"] += 1
                        P.mm_group(pAB[a], [(MM(pA[a][:, 0:N], sel[:, gl * 8 + t, :], uTt[s][:, 8 * c0 + t:8 * c1:8], t == 0, t == 7), [uTB[s], cB]) for t in range(8)])
                        eng = evac_eng()
                        P.op(eng, copy_fn(eng, U[s][:, gl, c0:c1], pA[a][:, 0:N]), reads=[pAB[a]], writes=[UB[s][gl]])
            for gl in range(8):
                par = gl % 2

                def outer(out3, j0, T1tab, T2tab, gd, outB, extra_reads=()):
                    p1 = PT1[:, gd, j0:j0 + 8].unsqueeze(2).broadcast_to([128, 8, 16])
                    p2 = PT2[:, gd, j0:j0 + 8].unsqueeze(2).broadcast_to([128, 8, 16])
                    m1 = T1tab[:, gd, :].unsqueeze(1).broadcast_to([128, 8, 16])
                    m2 = T2tab[:, gd, :].unsqueeze(1).broadcast_to([128, 8, 16])
                    dv(lambda e: e.tensor_tensor(out=gtA[:], in0=p1, in1=m1, op=ALU.mult), [tabB], [gtAB])
                    dv(lambda e: e.tensor_tensor(out=gtB[:], in0=p2, in1=m2, op=ALU.mult), [tabB], [gtBB])
                    dv(lambda e: e.tensor_tensor(out=out3, in0=gtA[:], in1=gtB[:], op=ALU.add), [gtAB, gtBB], [outB])
                for d in range(2):
                    gd = d * 8 + gl
                    outer(Wxn[:], 8 if d == 0 else 0, M1B, M2B, gd, WxnB)
                    P.mm_group(pTB, [(lambda e: e.transpose(out=pT[:, 0, :], in_=Wxn[:].rearrange("p a b -> p (a b)"), identity=identb[:]), [WxnB, cB])])
                    P.op("act", copy_fn("act", WxT[par][d][:], pT[:, 0, :]), reads=[pTB], writes=[matB[par]])
                    outer(Wy[par][d][:], 16 if d == 0 else 24, Cx1, Cx2, gd, matB[par])
                    outer(Phi[:, d, :].rearrange("p (a b) -> p a b", a=8), 0 if d == 0 else 32, Cx1, Cx2, gd, PhB)
                    outer(Psi[:, d, :].rearrange("p (a b) -> p a b", a=8), 32 if d == 0 else 0, M1B, M2B, gd, PhB)
                    for jr in range(NR):
                        ri = cnt["rt"] % 2
                        cnt["rt"] += 1
                        col = RC0 + jr
                        P.op("dve", lambda e, ri=ri, gd=gd, col=col: e.tensor_scalar(out=rtmp[ri][:], in0=identf, scalar1=PT1[:, gd, col:col + 1], scalar2=None, op0=ALU.mult), reads=[tabB, cB], writes=[rtmpB[ri]])
                        P.op("dve", lambda e, ri=ri, gd=gd, col=col, par=par, d=d, jr=jr: e.scalar_tensor_tensor(out=rot[par][d][:, jr, :], in0=pswap, scalar=NPT2[:, gd, col:col + 1], in1=rtmp[ri][:], op0=ALU.mult, op1=ALU.add), reads=[tabB, cB, rtmpB[ri]], writes=[matB[par]])
                a = cnt["a"] % 2
                cnt["a"] += 1
                P.mm_group(pAB[a], [(MM(pA[a][:, 0:128], Psi[:, 0, :], Phi[:, 0, :], True, True), [PhB]), (MM(pA[a][:, 128:256], Psi[:, 1, :], Phi[:, 1, :], True, True), [PhB])])
                dv(lambda e, a=a: e.tensor_tensor(out=ttA[:], in0=pA[a][:, 0:128], in1=maskf, op=ALU.mult), [pAB[a], cB], [ttAB])
                dv(lambda e, a=a: e.tensor_tensor(out=ttBt[:], in0=pA[a][:, 128:256], in1=maskb, op=ALU.mult), [pAB[a], cB], [ttBB])
                dv(lambda e: e.tensor_tensor(out=ttA[:], in0=ttA[:], in1=ttBt[:], op=ALU.add), [ttBB], [ttAB])
                dv(lambda e, gl=gl, par=par: e.scalar_tensor_tensor(out=TT[par][:], in0=identf, scalar=dvec[:, gl:gl + 1], in1=ttA[:], op0=ALU.mult, op1=ALU.add), [ttAB, inB, cB], [matB[par]])

                for s in range(2):
                    L, T = cfg.L[s], cfg.T[s]
                    NC = L // 8
                    rq = T // 512
                    radices = [8, 8] + ([rq] if rq > 1 else [])
                    nlev = len(radices)
                    n = [NC]
                    for r_ in radices:
                        n.append(n[-1] // r_)
                    assert n[-1] == 4
                    ue = [8]
                    for r_ in radices:
                        ue.append(ue[-1] * r_)
                    assert ue[-1] == T
                    for d in range(2):
                        fwd = d == 0
                        mB = matB[par]

                        def rotap(ex, par=par, d=d):
                            if ex == 0:
                                return identb[:]
                            return rot[par][d][:, ROTI[ex], :]
                        Xs = [Xl[0], Xl[1], Xl[2], Xl[3]]
                        Xlev = [Xs[i] for i in range(nlev)] + [Xl[3]]
                        XBl = [XB[i] for i in range(nlev)] + [XB[3]]
                        Plev = [P0[d]] + [Pl[i] for i in range(1, nlev)] + [Pl[3]]
                        PBl = [P0B[d]] + [PB[i] for i in range(1, nlev)] + [PB[3]]
                        for (c0, c1) in splits(NC, 512):
                            N = c1 - c0
                            si = cnt["s"] % 2
                            cnt["s"] += 1
                            P.mm_group(pSB[si], [(MM(pS[si][:, 0:N], WxT[par][d][:], U[s][:, gl, c0:c1], True, True), [mB, UB[s][gl]])])
                            eng = evac_eng()
                            P.op(eng, copy_fn(eng, Xlev[0][:, c0:c1], pS[si][:, 0:N]), reads=[pSB[si]], writes=[XBl[0]])
                        for lev, rad in enumerate(radices):
                            nn = n[lev + 1]
                            si = cnt["s"] % 2
                            cnt["s"] += 1
                            mms = []
                            for r in range(rad):
                                kk = (rad - 1 - r) if fwd else r
                                mms.append((MM(pS[si][:, 0:nn], rotap(ue[lev] * kk), Xlev[lev][:, r:n[lev]:rad], r == 0, r == rad - 1), [mB, XBl[lev], cB]))
                            P.mm_group(pSB[si], mms)
                            eng = evac_eng()
                            P.op(eng, copy_fn(eng, Xlev[lev + 1][:, 0:nn], pS[si][:, 0:nn]), reads=[pSB[si]], writes=[XBl[lev + 1]])
                        X4 = Xlev[nlev]
                        mb0 = 4 if fwd else 16
                        for i in range(1, 4):
                            mo = mb0 + (i - 1) * 4
                            if fwd:
                                dv(lambda e, i=i, mo=mo, X4=X4: e.tensor_tensor(out=Xm[:, i - 1, i:4], in0=X4[:, 0:4 - i], in1=mk[:, mo + i:mo + 4], op=ALU.mult), [XBl[nlev], cB], [XmB])
                                dv(lambda e, i=i, mo=mo, X4=X4: e.tensor_tensor(out=Xm[:, i - 1, 0:i], in0=X4[:, 4 - i:4], in1=mk[:, mo:mo + i], op=ALU.mult), [XBl[nlev], cB], [XmB])
                            else:
                                dv(lambda e, i=i, mo=mo, X4=X4: e.tensor_tensor(out=Xm[:, i - 1, 0:4 - i], in0=X4[:, i:4], in1=mk[:, mo:mo + 4 - i], op=ALU.mult), [XBl[nlev], cB], [XmB])
                                dv(lambda e, i=i, mo=mo, X4=X4: e.tensor_tensor(out=Xm[:, i - 1, 4 - i:4], in0=X4[:, 0:i], in1=mk[:, mo + 4 - i:mo + 4], op=ALU.mult), [XBl[nlev], cB], [XmB])
                        si = cnt["s"] % 2
                        cnt["s"] += 1
                        P.mm_group(pSB[si], [(MM(pS[si][:, 0:4], rotap(T * (i - 1)), Xm[:, i - 1, :], i == 1, i == 3), [mB, XmB, cB]) for i in range(1, 4)])
                        eng = evac_eng()
                        P.op(eng, copy_fn(eng, Plev[nlev][:, 0:4], pS[si][:, 0:4]), reads=[pSB[si]], writes=[PBl[nlev]])
                        for lev in range(nlev - 1, -1, -1):
                            rad = radices[lev]
                            nn = n[lev + 1]
                            big = rad * nn > 512
                            if big:
                                banks, bB = pC, pCB
                                per = 4
                            else:
                                si = cnt["s"] % 2
                                cnt["s"] += 1
                                banks, bB = [pS[si]], [pSB[si]]
                                per = rad
                            mms_by_bank = {}
                            for r in range(rad):
                                bk = r // per
                                o0 = (r % per) * nn
                                terms = [(rotap(ue[lev] * (r if fwd else rad - 1 - r)), Plev[lev + 1][:, 0:nn], PBl[lev + 1])]
                                rr = range(0, r) if fwd else range(r + 1, rad)
                                for r2 in rr:
                                    kk = (r - 1 - r2) if fwd else (r2 - 1 - r)
                                    terms.append((rotap(ue[lev] * kk), Xlev[lev][:, r2:n[lev]:rad], XBl[lev]))
                                for ti_, (lh, rh, rb) in enumerate(terms):
                                    mms_by_bank.setdefault(bk, []).append((MM(banks[bk][:, o0:o0 + nn], lh, rh, ti_ == 0, ti_ == len(terms) - 1), [mB, rb, cB]))
                            for bk, mms in mms_by_bank.items():
                                P.mm_group(bB[bk], mms)
                                r0 = bk * per
                                cnt_r = min(per, rad - r0)
                                dst = sb3(Plev[lev], r0, [[1, cnt_r], [rad, nn]])
                                src = banks[bk][:, 0:cnt_r * nn].rearrange("p (r j) -> p r j", r=cnt_r)
                                eng = evac_eng()
                                P.op(eng, copy_fn(eng, dst, src), reads=[bB[bk]], writes=[PBl[lev]])
                    NYo = T // 8 + 1
                    terms = [(TT[par][:], lambda c0, c1, s=s, gl=gl: U[s][:, gl, c0:c1], UB[s][gl]),
                             (Wy[par][0][:].rearrange("p a b -> p (a b)"), lambda c0, c1: P0[0][:, c0:c1], P0B[0]),
                             (Wy[par][1][:].rearrange("p a b -> p (a b)"), lambda c0, c1: P0[1][:, c0:c1], P0B[1])]
                    mms = []
                    for ti_, (lh, rf, rb) in enumerate(terms):
                        mms.append((MM(pY[:, 0:1], lh, rf(NC - 1, NC), ti_ == 0, ti_ == 2), [matB[par], rb]))
                    for ti_, (lh, rf, rb) in enumerate(terms):
                        mms.append((MM(pY[:, 1:1 + NYo], lh, rf(0, NYo), ti_ == 0, ti_ == 2), [matB[par], rb]))
                    P.mm_group(pYB, mms)
                    eng = evac_eng()
                    P.op(eng, copy_fn(eng, Ys[s][:, gl, 0:NYo + 1], pY[:, 0:NYo + 1]), reads=[pYB], writes=[YsB[s][gl]])
            for s in range(2):
                T = cfg.T[s]
                NY = T // 8 + 2
                for t in range(8):
                    a = cnt["a"] % 2
                    cnt["a"] += 1
                    P.mm_group(pAB[a], [(MM(pA[a][:, 0:NY], selo[:, gl * 8 + t, :], Ys[s][:, gl, 0:NY], gl == 0, gl == 7), [YsB[s][gl], cB]) for gl in range(8)])
                    dst = sb3(yst[s], 1 + t, [[8, T // 8]])
                    P.op("act", copy_fn("act", dst, pA[a][:, 1:1 + T // 8]), reads=[pAB[a]], writes=[ystB[s]])
                    if t == 7:
                        P.op("act", copy_fn("act", yst[s][:, 0:1], pA[a][:, 0:1]), reads=[pAB[a]], writes=[ystB[s]])
                    if t == 0:
                        P.op("act", copy_fn("act", yst[s][:, T + 1:T + 2], pA[a][:, T // 8 + 1:T // 8 + 2]), reads=[pAB[a]], writes=[ystB[s]])
                P.op("sync", lambda e, s=s, k=k: e.dma_start(out=b.yT[s][k * 128:(k + 1) * 128, :], in_=yst[s][:]), reads=[ystB[s]], dma=True)
        P.flush()


def phase_H(b):
    nc, cfg, P = b.nc, b.cfg, b.P
    UT = cfg.UT
    with ExitStack() as st:
        def sb(name, shape, dt):
            return st.enter_context(nc.sbuf_tensor(f"H_{name}", list(shape), dt))

        def pst(name, shape, dt):
            return st.enter_context(nc.psum_tensor(f"H_{name}", list(shape), dt))
        yv = sb("yv", [128, UT, 512], F32)
        yB = Buf()
        gT = sb("gT", [128, UT, 512], BF16)
        gB = Buf()
        w1 = sb("w1", [128, 512], F32)
        w2 = sb("w2", [128, 512], F32)
        w1B, w2B = Buf(), Buf()
        sg = [sb(f"sg{i}", [128, 512], F32) for i in range(2)]
        sgB = [Buf() for _ in range(2)]
        wt = [sb(f"wt{i}", [128, UT, 128], BF16) for i in range(2)]
        wtB = [Buf() for _ in range(2)]
        stg = [sb(f"stg{i}", [128, 512], BF16) for i in range(2)]
        stgB = [Buf() for _ in range(2)]
        acc = [pst(f"acc{i}", [128, 512], F32) for i in range(2)]
        accB = [Buf() for _ in range(2)]
        gv = b.glu_w.rearrange("(k p) c -> p k c", p=128)
        c = 0
        for seq in range(2):
            TC = cfg.T[seq] + 2
            yTv = b.yT[seq].rearrange("(k p) c -> p k c", p=128)
            mrs = b.mraw[seq]
            for (c0, c1) in splits(TC, 512):
                N = c1 - c0
                P.op("sync", lambda e, c0=c0, c1=c1, N=N, yTv=yTv: e.dma_start(out=yv[:, :, 0:N], in_=yTv[:, :, c0:c1]), writes=[yB], dma=True)
                for k in range(UT):
                    yk = yv[:, k, 0:N]
                    P.op("dve", lambda e, yk=yk, N=N: e.tensor_tensor(out=w1[:, 0:N], in0=yk, in1=yk, op=ALU.mult), reads=[yB], writes=[w1B])
                    P.op("dve", lambda e, N=N: e.tensor_scalar(out=w1[:, 0:N], in0=w1[:, 0:N], scalar1=0.044715, scalar2=1.0, op0=ALU.mult, op1=ALU.add), reads=[w1B], writes=[w1B])
                    P.op("dve", lambda e, yk=yk, N=N: e.tensor_tensor(out=w1[:, 0:N], in0=w1[:, 0:N], in1=yk, op=ALU.mult), reads=[w1B, yB], writes=[w1B])
                    P.op("act", lambda e, N=N: e.activation(out=w2[:, 0:N], in_=w1[:, 0:N], func=AF.Sigmoid, scale=1.5957691216057308), reads=[w1B], writes=[w2B])
                    P.op("dve", lambda e, yk=yk, N=N, k=k: e.tensor_tensor(out=gT[:, k, 0:N], in0=w2[:, 0:N], in1=yk, op=ALU.mult), reads=[w2B, yB], writes=[gB])
                for j in range(UT):
                    wi = c % 2
                    c += 1
                    P.op("pool", lambda e, wi=wi, j=j: e.dma_start(out=wt[wi][:], in_=gv[:, :, j * 128:(j + 1) * 128]), writes=[wtB[wi]], dma=True)
                    P.mm_group(accB[wi], [(MM(acc[wi][:, 0:N], wt[wi][:, k, :], gT[:, k, 0:N], k == 0, k == UT - 1), [wtB[wi], gB]) for k in range(UT)])
                    P.op("act", lambda e, wi=wi, N=N: e.activation(out=sg[wi][:, 0:N], in_=acc[wi][:, 0:N], func=AF.Sigmoid), reads=[accB[wi]], writes=[sgB[wi]])
                    P.op("dve", lambda e, wi=wi, N=N, j=j: e.tensor_tensor(out=stg[wi][:, 0:N], in0=sg[wi][:, 0:N], in1=gT[:, j, 0:N], op=ALU.mult), reads=[sgB[wi], gB], writes=[stgB[wi]])
                    P.op("sync", lambda e, wi=wi, N=N, j=j, c0=c0, c1=c1, mrs=mrs: e.dma_start(out=mrs[j * 128:(j + 1) * 128, c0:c1], in_=stg[wi][:, 0:N]), reads=[stgB[wi]], dma=True)
        P.flush()


ROPE_THETA = 10000.0

def consts(cfg):
    c = {}
    c["ident"] = np.eye(128, dtype=np.float32)
    inv = 1.0 / (ROPE_THETA ** (np.arange(0, 64, 2, dtype=np.float32) / 64)).astype(np.float32)
    c["invf"] = np.concatenate([inv, inv]).reshape(64, 1).astype(np.float32)
    c["sgn"] = np.concatenate([-np.ones(32), np.ones(32)]).reshape(64, 1).astype(np.float32)
    sel = np.zeros((128, 64, 128), np.float32)
    for gl in range(8):
        for t in range(8):
            for h in range(16):
                sel[gl * 16 + h, gl * 8 + t, t * 16 + h] = 1.0
    c["sel"] = sel
    c["selo"] = np.ascontiguousarray(sel.transpose(2, 1, 0))
    cst = np.zeros((128, 8, 128), np.float32)
    for k in range(128):
        cst[k, 0, (k + 64) % 128] = 1.0
    s_idx = np.arange(128) // 16
    cst[:, 1, :] = (s_idx[None, :] >= s_idx[:, None]).astype(np.float32)
    cst[:, 2, :] = (s_idx[None, :] <= s_idx[:, None]).astype(np.float32)
    cst[:, 3, :] = np.eye(128)
    c["cst"] = cst
    sig = np.zeros((128, 4), np.float32)
    sig[:64, 0] = -1.0; sig[64:, 0] = 1.0
    sig[:, 1] = -sig[:, 0]
    c["sigv"] = sig
    return c

def host_prepare(cfg, inputs):
    f = lambda a: np.ascontiguousarray(np.asarray(a, dtype=np.float32))
    cs = consts(cfg)
    shared = dict(cs)
    shared["w_in"] = f(inputs["w_in"][0])
    shared["g_mix"] = f(inputs["norm_mix_g"][0]).reshape(1, -1)
    shared["g_ffn"] = f(inputs["norm_ffn_g"][0]).reshape(1, -1)
    shared["g_fin"] = f(inputs["norm_final_g"]).reshape(1, -1)
    shared["gq"] = f(np.asarray(inputs["q_norm_g"][0]).reshape(cfg.QT, 128).T)
    shared["gkv"] = f(np.asarray(inputs["kv_norm_g"][0]).reshape(cfg.KT, 128).T)
    gmo = np.concatenate([np.asarray(inputs["ssm_out_norm_g"][0]), np.asarray(inputs["attn_out_norm_g"][0])])
    shared["gmo"] = f(gmo.reshape(cfg.MT, 128).T)
    shared["w_q_up"] = f(inputs["w_q_up"][0])
    shared["w_kv_up"] = f(inputs["w_kv_up"][0])
    shared["w_out"] = f(inputs["w_out"][0])
    shared["w_up"] = f(inputs["w_ffn_up"][0])
    shared["w_gate"] = f(inputs["w_ffn_gate"][0])
    shared["w_down"] = f(inputs["w_ffn_down"][0])
    cw = np.asarray(inputs["ffn_conv_w"][0])
    shared["convw"] = f(cw.reshape(3, cfg.FT, 128).transpose(2, 0, 1))
    shared["convb"] = f(np.asarray(inputs["ffn_conv_b"][0]).reshape(cfg.FT, 128).T)
    shared["glu_w"] = f(inputs["ssm_glu_w"][0])
    G = cfg.G
    are = np.asarray(inputs["ssm_a_re"][0]); aim = np.asarray(inputs["ssm_a_im"][0])
    def st2(x):
        y = x.transpose(2, 0, 1).reshape(64, 2 * G)
        return f(np.concatenate([y, y], 0))
    shared["are_h"] = st2(are); shared["aim_h"] = st2(aim)
    bre = np.asarray(inputs["ssm_b_re"][0]); bim = np.asarray(inputs["ssm_b_im"][0])
    br = bre.transpose(2, 0, 1, 3).reshape(64, 2 * G, 16); bi = bim.transpose(2, 0, 1, 3).reshape(64, 2 * G, 16)
    shared["bx1_h"] = f(np.concatenate([br, bi], 0)); shared["bx2_h"] = f(np.concatenate([bi, br], 0))
    cre = np.asarray(inputs["ssm_c_re"][0]); cim = np.asarray(inputs["ssm_c_im"][0])
    cr = cre.transpose(3, 0, 1, 2).reshape(64, 2 * G, 16); ci = cim.transpose(3, 0, 1, 2).reshape(64, 2 * G, 16)
    shared["cx1_h"] = f(np.concatenate([cr, ci], 0)); shared["cx2_h"] = f(np.concatenate([ci, cr], 0))
    dsk = np.asarray(inputs["ssm_d"][0])
    shared["dsk_h"] = f(np.tile(dsk.T, (8, 1)))
    shared["log_dt"] = f(inputs["ssm_log_dt"][0]).reshape(1, -1)
    ev = exp_vector(cfg)
    evv = np.zeros((1, 64), np.float32); evv[0, :len(ev)] = ev
    shared["evec"] = evv
    maps = []
    xp = np.asarray(inputs["x_prompt"]); xs = np.asarray(inputs["x_sample"])
    for c in range(8):
        si, q = c // 4, c % 4
        m = dict(shared)
        for nm, xx, s in (("p", xp, 0), ("s", xs, 1)):
            L, T = cfg.L[s], cfg.T[s]
            m["x" + nm] = f(np.roll(xx[si], -q * T, axis=0))
            m["pos" + nm] = ((np.arange(L) + q * T) % L).astype(np.float32).reshape(1, L)
        mk = np.zeros((1, 32), np.float32)
        lm = np.array([0.0 if j == (3 - q) else 1.0 for j in range(4)], np.float32)
        mk[0, 0:4] = lm
        for j in range(4):
            for i in range(1, 4):
                mk[0, 4 + (i - 1) * 4 + j] = np.prod([lm[(j - l) % 4] for l in range(1, i + 1)])
                mk[0, 16 + (i - 1) * 4 + j] = np.prod([lm[(j + l) % 4] for l in range(0, i)])
        m["masks"] = mk
        maps.append(m)
    return maps


_PHASES = "ABHCDGE"


def build_program(cfg, debug=False):
    nc = bass.Bass("TRN2", target_bir_lowering=False)
    b = declare(nc, cfg, debug=debug)
    g = globals()
    for ph in _PHASES:
        if ph == "A":
            phase_A(b, 0)
            phase_A(b, 1)
        else:
            g["phase_" + ph](b)
    return nc


def kernel(**inputs):
    cfg = FULL
    nc = build_program(cfg)
    maps = host_prepare(cfg, inputs)
    res = run_bass_kernel_spmd(nc, maps, core_ids=list(range(8)))
    yp = np.zeros((2, cfg.L[0], cfg.D), np.float32)
    ys = np.zeros((2, cfg.L[1], cfg.D), np.float32)
    for c in range(8):
        si, q = c // 4, c % 4
        r = res.results[c]
        yp[si, q * cfg.T[0]:(q + 1) * cfg.T[0]] = np.asarray(r["yp"])
        ys[si, q * cfg.T[1]:(q + 1) * cfg.T[1]] = np.asarray(r["ys"])
    return (yp, ys)
```

```python
from contextlib import ExitStack
import numpy as np
from concourse.bass_utils import run_bass_kernel_spmd
import concourse.bass as bass
import concourse.mybir as mybir

F32 = mybir.dt.float32
BF16 = mybir.dt.bfloat16
I32 = mybir.dt.int32
ALU = mybir.AluOpType
AF = mybir.ActivationFunctionType
TWO_PI = 6.283185307179586
PI = 3.141592653589793


class Buf:
    __slots__ = ("w", "r", "name")

    def __init__(self, name=""):
        self.w = None
        self.r = {}
        self.name = name


class Prog:
    NDMA = 12

    def __init__(self, nc, phase_sem):
        self.nc = nc
        self.phase_sem = phase_sem
        self.phase_idx = 0
        self.sems = {}
        self.cnt = {}
        self.dma_rr = {"sync": 0, "act": 0, "pool": 0}
        for eng in ("act", "pool", "dve", "pe"):
            self._sem(eng)
        for qn in ("sync", "act", "pool"):
            for i in range(self.NDMA):
                self._sem(f"d{qn}{i}")
        self._reset()

    def _reset(self):
        self.q = {k: [] for k in ("sync", "act", "pool", "dve", "pe")}

    def _sem(self, name):
        if name not in self.sems:
            self.sems[name] = self.nc.alloc_semaphore(f"s_{name}")
            self.cnt[name] = 0
        return self.sems[name]

    def op(self, eng, fn, reads=(), writes=(), waits=(), dma=False, track=True):
        w = {}

        def add(ev):
            if ev is None:
                return
            s, c = ev
            if w.get(s, 0) < c:
                w[s] = c
        for ev in waits:
            add(ev)
        for b in reads:
            add(b.w)
        for b in writes:
            add(b.w)
            for s, c in b.r.items():
                add((s, c))
        ev = None
        inc = 0
        sname = None
        if track:
            if dma:
                i = self.dma_rr[eng]
                self.dma_rr[eng] = (i + 1) % self.NDMA
                sname = f"d{eng}{i}"
                if self.cnt[sname] > 0:
                    add((sname, self.cnt[sname]))
                inc = 16
            else:
                sname = eng
                inc = 1
            self.cnt[sname] += inc
            ev = (sname, self.cnt[sname])
        self.q[eng].append((fn, tuple(w.items()), sname, inc))
        if ev is not None:
            for b in reads:
                if b.r.get(ev[0], 0) < ev[1]:
                    b.r[ev[0]] = ev[1]
            for b in writes:
                b.w = ev
                b.r = {}
        return ev

    def mm_group(self, psbuf, mms, extra_reads=()):
        n = len(mms)
        allreads = []
        for i, (fn, reads) in enumerate(mms):
            allreads.extend(reads)
            last = i == n - 1
            w = {}

            def add(ev, w=w):
                if ev is None:
                    return
                s, c = ev
                if w.get(s, 0) < c:
                    w[s] = c
            for b in reads:
                add(b.w)
            if i == 0:
                add(psbuf.w)
                for s, c in psbuf.r.items():
                    add((s, c))
            if last:
                self.cnt["pe"] += 1
                ev = ("pe", self.cnt["pe"])
                self.q["pe"].append((fn, tuple(w.items()), "pe", 1))
            else:
                self.q["pe"].append((fn, tuple(w.items()), None, 0))
        for b in allreads:
            if b.r.get("pe", 0) < ev[1]:
                b.r["pe"] = ev[1]
        psbuf.w = ev
        psbuf.r = {}
        return ev

    def mm(self, fn, reads=(), ps=None, first=False, last=False):
        w = {}

        def add(ev):
            if ev is None:
                return
            s_, c = ev
            if w.get(s_, 0) < c:
                w[s_] = c
        for b_ in reads:
            add(b_.w)
        if first and ps is not None:
            add(ps.w)
            for s_, c in ps.r.items():
                add((s_, c))
        if last:
            self.cnt["pe"] += 1
            ev = ("pe", self.cnt["pe"])
            self.q["pe"].append((fn, tuple(w.items()), "pe", 1))
            for b_ in reads:
                if b_.r.get("pe", 0) < ev[1]:
                    b_.r["pe"] = ev[1]
            ps.w = ev
            ps.r = {}
            return ev
        self.cnt["pe"] += 1
        ev = ("pe", self.cnt["pe"])
        self.q["pe"].append((fn, tuple(w.items()), "pe", 1))
        for b_ in reads:
            if b_.r.get("pe", 0) < ev[1]:
                b_.r["pe"] = ev[1]
        return None

    def flush(self):
        nc = self.nc
        pidx = self.phase_idx
        with nc.Block() as block:
            def run(engname):
                def body(e):
                    seen = {}
                    if pidx > 0:
                        e.wait_ge(self.phase_sem, pidx)
                    for fn, waits, sname, inc in self.q[engname]:
                        for (s, v) in waits:
                            if seen.get(s, 0) >= v:
                                continue
                            seen[s] = v
                            e.wait_ge(self.sems[s], v)
                        ins = fn(e)
                        if sname is not None:
                            ins.then_inc(self.sems[sname], inc)
                    if engname == "sync":
                        for s, c in self.cnt.items():
                            if c > 0 and seen.get(s, 0) < c:
                                e.wait_ge(self.sems[s], c)
                        e.nop().then_inc(self.phase_sem, 1)
                return body
            block.sync(run("sync"))
            block.scalar(run("act"))
            block.gpsimd(run("pool"))
            block.vector(run("dve"))
            block.tensor(run("pe"))
        self.phase_idx += 1
        self._reset()


def splits(n, maxw):
    k = (n + maxw - 1) // maxw
    base, rem = divmod(n, k)
    out = []
    lo = 0
    for i in range(k):
        wdt = base + (1 if i < rem else 0)
        out.append((lo, lo + wdt))
        lo += wdt
    return out


class Cfg:
    def __init__(self, D, G, H, QL, KVL, DFF, Lp, Ls):
        self.D, self.G, self.H, self.QL, self.KVL, self.DFF = D, G, H, QL, KVL, DFF
        self.SW = 16 * G
        self.AW = 128 * H
        self.KD = D // 128
        self.UT = self.SW // 128
        self.QT = QL // 128
        self.KT = KVL // 128
        self.HT = H
        self.MT = self.UT + self.HT
        self.FT = DFF // 128
        self.INC = self.SW + QL + KVL + 64
        self.L = [Lp, Ls]
        self.T = [Lp // 4, Ls // 4]
        self.QKD = 192
        self.EPS = 1e-6


FULL = Cfg(4096, 128, 16, 896, 512, 11008, 8192, 4096)


class B:
    pass


def own_ranges(L, T, lo, hi):
    out = []
    a, b = max(lo, 0), min(hi, T + 1)
    if a < b:
        out.append((a, b, a + 1))
    if lo <= L - 1 < hi:
        out.append((L - 1, L, 0))
    return out


def declare(nc, cfg, debug):
    b = B()
    b.nc, b.cfg = nc, cfg
    D = cfg.D
    kind_s = "ExternalOutput" if debug else "Internal"

    def inp(name, shape, dt=F32):
        return nc.dram_tensor(name, list(shape), dt, kind="ExternalInput").ap()

    def scr(name, shape, dt):
        return nc.dram_tensor(name, list(shape), dt, kind=kind_s).ap()
    b.x = [inp("xp", [cfg.L[0], D]), inp("xs", [cfg.L[1], D])]
    b.pos = [inp("posp", [1, cfg.L[0]]), inp("poss", [1, cfg.L[1]])]
    b.masks = inp("masks", [1, 32])
    b.w_in = inp("w_in", [D, cfg.INC])
    b.g_mix = inp("g_mix", [1, D])
    b.g_ffn = inp("g_ffn", [1, D])
    b.g_fin = inp("g_fin", [1, D])
    b.gq = inp("gq", [128, cfg.QT])
    b.gkv = inp("gkv", [128, cfg.KT])
    b.gmo = inp("gmo", [128, cfg.MT])
    b.w_q_up = inp("w_q_up", [cfg.QL, cfg.H * 192])
    b.w_kv_up = inp("w_kv_up", [cfg.KVL, cfg.H * 256])
    b.w_out = inp("w_out", [cfg.SW + cfg.AW, D])
    b.w_up = inp("w_up", [D, cfg.DFF])
    b.w_gate = inp("w_gate", [D, cfg.DFF])
    b.w_down = inp("w_down", [cfg.DFF, D])
    b.convw = inp("convw", [128, 3, cfg.FT])
    b.convb = inp("convb", [128, cfg.FT])
    b.glu_w = inp("glu_w", [cfg.SW, cfg.SW])
    b.invf = inp("invf", [64, 1])
    b.sgn = inp("sgn", [64, 1])
    b.ident_in = inp("ident", [128, 128])
    b.log_dt = inp("log_dt", [1, 2 * cfg.G])
    b.sel_in = inp("sel", [128, 64, 128])
    b.selo_in = inp("selo", [128, 64, 128])
    b.cst = inp("cst", [128, 8, 128])
    b.sig = inp("sigv", [128, 4])
    b.evec = inp("evec", [1, 64])
    b.are_h = inp("are_h", [128, 2 * cfg.G])
    b.aim_h = inp("aim_h", [128, 2 * cfg.G])
    b.bx1_h = inp("bx1_h", [128, 2 * cfg.G, 16])
    b.bx2_h = inp("bx2_h", [128, 2 * cfg.G, 16])
    b.cx1_h = inp("cx1_h", [128, 2 * cfg.G, 16])
    b.cx2_h = inp("cx2_h", [128, 2 * cfg.G, 16])
    b.dsk_h = inp("dsk_h", [128, cfg.G])
    b.y = [nc.dram_tensor("yp", [cfg.T[0], D], F32, kind="ExternalOutput").ap(),
           nc.dram_tensor("ys", [cfg.T[1], D], F32, kind="ExternalOutput").ap()]
    b.uT, b.kvn, b.kr, b.rtab, b.qn, b.mraw, b.x1, b.h2T, b.x2, b.yT = [], [], [], [], [], [], [], [], [], []
    for s in range(2):
        L, T = cfg.L[s], cfg.T[s]
        b.uT.append(scr(f"uT{s}", [cfg.SW, L], BF16))
        b.kvn.append(scr(f"kvn{s}", [cfg.KVL, L], BF16))
        b.kr.append(scr(f"kr{s}", [64, L], BF16))
        b.rtab.append(scr(f"rtab{s}", [2, 64, L], F32))
        b.qn.append(scr(f"qn{s}", [cfg.QL, T + 2], BF16))
        b.mraw.append(scr(f"mraw{s}", [cfg.SW + cfg.AW, T + 2], BF16))
        b.x1.append(scr(f"x1{s}", [T + 2, D], F32))
        b.h2T.append(scr(f"h2T{s}", [D, T + 2], BF16))
        b.x2.append(scr(f"x2{s}", [T, D], F32))
        b.yT.append(scr(f"yT{s}", [cfg.SW, T + 2], F32))
    b.phase_sem = nc.alloc_semaphore("phase")
    b.P = Prog(nc, b.phase_sem)
    return b


def bcast_rows(ap_row, nparts, n, off=0):
    return bass.AP(ap_row.tensor, ap_row.offset + off, [[0, nparts], [1, n]])


def range_reduce(P, eng, ang, tmpf, tmpi, angB, tmpB, n):
    P.op(eng, lambda e: e.tensor_scalar(out=tmpf, in0=ang, scalar1=1.0 / TWO_PI, scalar2=None, op0=ALU.mult), reads=[angB], writes=[tmpB])
    P.op(eng, lambda e: e.tensor_copy(out=tmpi, in_=tmpf), reads=[tmpB], writes=[tmpB])
    P.op(eng, lambda e: e.tensor_copy(out=tmpf, in_=tmpi), reads=[tmpB], writes=[tmpB])
    P.op(eng, lambda e: e.scalar_tensor_tensor(out=ang, in0=tmpf, scalar=-TWO_PI, in1=ang, op0=ALU.mult, op1=ALU.add), reads=[tmpB], writes=[angB])
    P.op(eng, lambda e: e.tensor_scalar(out=tmpf, in0=ang, scalar1=PI, scalar2=TWO_PI, op0=ALU.is_gt, op1=ALU.mult), reads=[angB], writes=[tmpB])
    P.op(eng, lambda e: e.tensor_tensor(out=ang, in0=ang, in1=tmpf, op=ALU.subtract), reads=[tmpB], writes=[angB])
    P.op(eng, lambda e: e.tensor_scalar(out=tmpf, in0=ang, scalar1=-PI, scalar2=TWO_PI, op0=ALU.is_lt, op1=ALU.mult), reads=[angB], writes=[tmpB])
    P.op(eng, lambda e: e.tensor_tensor(out=ang, in0=ang, in1=tmpf, op=ALU.add), reads=[tmpB], writes=[angB])


def phase_A(b, seq):
    nc, cfg, P = b.nc, b.cfg, b.P
    L, T, D, KD = cfg.L[seq], cfg.T[seq], cfg.D, cfg.KD
    GS = 512
    NG = L // GS
    x = b.x[seq]
    o1, o2, o3 = cfg.SW, cfg.SW + cfg.QL, cfg.SW + cfg.QL + cfg.KVL
    NLAT = max(cfg.QT, cfg.KT)
    with ExitStack() as st:
        def sb(name, shape, dt):
            return st.enter_context(nc.sbuf_tensor(f"A{seq}_{name}", list(shape), dt))

        def pst(name, shape, dt):
            return st.enter_context(nc.psum_tensor(f"A{seq}_{name}", list(shape), dt))
        xt = [sb(f"xt{i}", [128, D], F32) for i in range(2)]
        xtB = [Buf() for _ in range(2)]
        xn = [sb(f"xn{i}", [128, D], BF16) for i in range(2)]
        xnB = [Buf() for _ in range(2)]
        ss = sb("ss", [128, 8], F32)
        ssB = Buf()
        hT = sb("hT", [128, KD, GS], BF16)
        hTB = [Buf() for _ in range(4)]
        gbc = sb("gbc", [128, D], F32)
        gbcB = Buf()
        wt = [sb(f"wt{i}", [128, KD, 128], BF16) for i in range(3)]
        wtB = [Buf() for _ in range(3)]
        wtr = sb("wtr", [128, KD, 64], BF16)
        wtrs = sb("wtrs", [128, KD, 64], BF16)
        wtrB = Buf()
        ident = sb("ident", [128, 128], BF16)
        ones = sb("ones", [128, 128], BF16)
        cB = Buf()
        gq = sb("gq", [128, cfg.QT], F32)
        gkv = sb("gkv", [128, cfg.KT], F32)
        invf = sb("invf", [64, 1], F32)
        sgn = sb("sgn", [64, 1], F32)
        stg = [sb(f"stg{i}", [128, GS], BF16) for i in range(4)]
        stgB = [Buf() for _ in range(4)]
        ltmp = sb("ltmp", [128, NLAT, GS], F32)
        ltmpB = [Buf() for _ in range(NLAT)]
        lsq = sb("lsq", [128, NLAT, GS], BF16)
        lsqB = [Buf() for _ in range(NLAT)]
        rk = sb("rk", [128, GS], F32)
        rkB = Buf()
        posb = sb("posb", [64, GS], F32)
        posB = Buf()
        ang = sb("ang", [64, GS], F32)
        angB = Buf()
        rtf = sb("rtf", [64, GS], F32)
        rti = sb("rti", [64, GS], I32)
        rtB = Buf()
        cosT = sb("cosT", [64, GS], F32)
        sinT = sb("sinT", [64, GS], F32)
        cosB, sinB = Buf(), Buf()
        t1 = sb("t1", [64, GS], F32)
        t2 = sb("t2", [64, GS], F32)
        t1B, t2B = Buf(), Buf()
        tp = [pst(f"tp{i}", [128, 8, 128], BF16) for i in range(2)]
        tpB = [Buf() for _ in range(2)]
        acc = [pst(f"acc{i}", [128, GS], F32) for i in range(3)]
        accB = [Buf() for _ in range(3)]
        ssp = pst("ssp", [128, GS], F32)
        sspB = Buf()

        P.op("pool", lambda e: e.dma_start(out=ident[:], in_=b.ident_in), writes=[cB], dma=True)
        P.op("pool", lambda e: e.memset(ones[:], 1.0), writes=[cB])
        P.op("sync", lambda e: e.dma_start(out=gbc[:], in_=bcast_rows(b.g_mix, 128, D)), writes=[gbcB], dma=True)
        P.op("sync", lambda e: e.dma_start(out=gq[:], in_=b.gq), writes=[cB], dma=True)
        P.op("sync", lambda e: e.dma_start(out=gkv[:], in_=b.gkv), writes=[cB], dma=True)
        P.op("sync", lambda e: e.dma_start(out=invf[:], in_=b.invf), writes=[cB], dma=True)
        P.op("sync", lambda e: e.dma_start(out=sgn[:], in_=b.sgn), writes=[cB], dma=True)
        wv = b.w_in.rearrange("(k p) c -> p k c", p=128)
        P.op("pool", lambda e: e.dma_start(out=wtr[:], in_=wv[:, :, o3:o3 + 64]), writes=[wtrB], dma=True)
        P.op("pool", lambda e: e.dma_start(out=wtrs[:, :, 0:32], in_=wv[:, :, o3 + 32:o3 + 64]), writes=[wtrB], dma=True)
        P.op("pool", lambda e: e.dma_start(out=wtrs[:, :, 32:64], in_=wv[:, :, o3:o3 + 32]), writes=[wtrB], dma=True)

        cnt = {"tile": 0, "wt": 0, "acc": 0, "stg": 0, "ev": 0}

        def evac_eng():
            cnt["ev"] += 1
            return "act" if cnt["ev"] % 2 else "dve"

        def copy_op(eng, out, in_):
            if eng == "act":
                return lambda e: e.activation(out=out, in_=in_, func=AF.Copy)
            return lambda e: e.tensor_copy(out=out, in_=in_)

        def load_w(c0, M):
            i = cnt["wt"] % 3
            cnt["wt"] += 1
            P.op("pool", lambda e: e.dma_start(out=wt[i][:, :, 0:M], in_=wv[:, :, c0:c0 + M]), writes=[wtB[i]], dma=True)
            return i

        def proj(wtile, wB, M, lo, hi):
            a = cnt["acc"] % 3
            cnt["acc"] += 1
            N = hi - lo
            mms = []
            for k in range(KD):
                mms.append((lambda e, k=k: e.matmul(acc[a][0:M, 0:N], lhsT=wtile[:, k, 0:M], rhs=hT[:, k, lo:hi],
                                                    start=(k == 0), stop=(k == KD - 1)), [wB] + hTB if k == 0 else []))
            P.mm_group(accB[a], mms)
            return a

        def store(dst, src_fn_eng, N, M=128):
            i = cnt["stg"] % 4
            cnt["stg"] += 1
            src_fn_eng(stg[i][0:M, 0:N], stgB[i])
            P.op("sync", lambda e: e.dma_start(out=dst, in_=stg[i][0:M, 0:N], allow_slow_non_contiguous=(N < 16)), reads=[stgB[i]], dma=True)

        def latent(kind, ntile, c_base, gain, nfeat, lo, hi, dst, dcol):
            N = hi - lo
            for t in range(ntile):
                wi = load_w(c_base + t * 128, 128)
                a = proj(wt[wi], wtB[wi], 128, lo, hi)
                P.op("act", lambda e, a=a, t=t: e.activation(out=ltmp[:, t, 0:N], in_=acc[a][:, 0:N], func=AF.Copy), reads=[accB[a]], writes=[ltmpB[t]])
                P.op("act", lambda e, a=a, t=t: e.activation(out=lsq[:, t, 0:N], in_=acc[a][:, 0:N], func=AF.Square), reads=[accB[a]], writes=[lsqB[t]])
            P.mm_group(sspB, [(lambda e, t=t: e.matmul(ssp[:, 0:N], lhsT=ones[:], rhs=lsq[:, t, 0:N], start=(t == 0), stop=(t == ntile - 1)), [lsqB[t], cB]) for t in range(ntile)])
            P.op("act", lambda e: e.activation(out=rk[:, 0:N], in_=ssp[:, 0:N], func=AF.Ln, scale=1.0 / nfeat, bias=cfg.EPS), reads=[sspB], writes=[rkB])
            P.op("act", lambda e: e.activation(out=rk[:, 0:N], in_=rk[:, 0:N], func=AF.Exp, scale=-0.5), reads=[rkB], writes=[rkB])
            for t in range(ntile):
                def prod(sap, sB, t=t):
                    P.op("dve", lambda e: e.scalar_tensor_tensor(out=sap, in0=ltmp[:, t, 0:N], scalar=gain[:, t:t + 1], in1=rk[:, 0:N], op0=ALU.mult, op1=ALU.mult), reads=[ltmpB[t], rkB, cB], writes=[sB])
                store(dst[t * 128:(t + 1) * 128, dcol:dcol + N], prod, N)

        for g in range(NG):
            for tt in range(4):
                ti = cnt["tile"]
                cnt["tile"] += 1
                bi = ti % 2
                r0 = g * GS + tt * 128
                P.op("sync", lambda e, bi=bi, r0=r0: e.dma_start(out=xt[bi][:], in_=x[r0:r0 + 128, :]), writes=[xtB[bi]], dma=True)
                P.op("act", lambda e, bi=bi: e.activation(out=xn[bi][:], in_=xt[bi][:], func=AF.Square, accum_out=ss[:, 0:1]), reads=[xtB[bi]], writes=[xnB[bi], ssB])
                P.op("act", lambda e: e.activation(out=ss[:, 1:2], in_=ss[:, 0:1], func=AF.Ln, scale=1.0 / D, bias=cfg.EPS), reads=[ssB], writes=[ssB])
                P.op("act", lambda e: e.activation(out=ss[:, 2:3], in_=ss[:, 1:2], func=AF.Exp, scale=-0.5), reads=[ssB], writes=[ssB])
                P.op("dve", lambda e, bi=bi: e.scalar_tensor_tensor(out=xn[bi][:], in0=xt[bi][:], scalar=ss[:, 2:3], in1=gbc[:], op0=ALU.mult, op1=ALU.mult), reads=[xtB[bi], ssB, gbcB], writes=[xnB[bi]])
                TB = min(8, KD)
                for j in range(KD // TB):
                    pb = (ti * (KD // TB) + j) % 2
                    P.mm_group(tpB[pb], [(lambda e, pb=pb, bi=bi, k=k: e.transpose(out=tp[pb][:, k % TB, :], in_=xn[bi][:, k * 128:(k + 1) * 128], identity=ident[:]), [xnB[bi], cB]) for k in range(j * TB, j * TB + TB)])
                    eng = "act"
                    P.op(eng, copy_op(eng, hT[:, j * TB:j * TB + TB, tt * 128:(tt + 1) * 128], tp[pb][:, 0:TB, :]), reads=[tpB[pb]], writes=[hTB[tt]])
            for j in range(cfg.UT):
                wi = load_w(j * 128, 128)
                a = proj(wt[wi], wtB[wi], 128, 0, GS)
                eng = evac_eng()

                def prod(sap, sB, a=a, eng=eng):
                    P.op(eng, copy_op(eng, sap, acc[a][:, 0:GS]), reads=[accB[a]], writes=[sB])
                store(b.uT[seq][j * 128:(j + 1) * 128, g * GS:(g + 1) * GS], prod, GS)
            latent("kv", cfg.KT, o2, gkv, cfg.KVL, 0, GS, b.kvn[seq], g * GS)
            P.op("sync", lambda e, g=g: e.dma_start(out=posb[:], in_=bcast_rows(b.pos[seq], 64, GS, off=g * GS)), writes=[posB], dma=True)
            for which in range(2):
                if which == 0:
                    P.op("dve", lambda e: e.tensor_scalar(out=ang[:], in0=posb[:], scalar1=invf[:, 0:1], scalar2=None, op0=ALU.mult), reads=[posB, cB], writes=[angB])
                else:
                    P.op("dve", lambda e: e.tensor_scalar(out=ang[:], in0=posb[:], scalar1=invf[:, 0:1], scalar2=PI / 2, op0=ALU.mult, op1=ALU.add), reads=[posB, cB], writes=[angB])
                range_reduce(P, "dve", ang[:], rtf[:], rti[:], angB, rtB, GS)
                if which == 0:
                    P.op("act", lambda e: e.activation(out=sinT[:], in_=ang[:], func=AF.Sin), reads=[angB], writes=[sinB])
                    P.op("dve", lambda e: e.tensor_scalar(out=sinT[:], in0=sinT[:], scalar1=sgn[:, 0:1], scalar2=None, op0=ALU.mult), reads=[sinB, cB], writes=[sinB])
                else:
                    P.op("act", lambda e: e.activation(out=cosT[:], in_=ang[:], func=AF.Sin), reads=[angB], writes=[cosB])
            P.op("sync", lambda e, g=g: e.dma_start(out=b.rtab[seq][0, :, g * GS:(g + 1) * GS], in_=cosT[:]), reads=[cosB], dma=True)
            P.op("sync", lambda e, g=g: e.dma_start(out=b.rtab[seq][1, :, g * GS:(g + 1) * GS], in_=sinT[:]), reads=[sinB], dma=True)
            a1 = proj(wtr, wtrB, 64, 0, GS)
            a2 = proj(wtrs, wtrB, 64, 0, GS)
            P.op("dve", lambda e, a1=a1: e.tensor_tensor(out=t1[:], in0=acc[a1][0:64, :], in1=cosT[:], op=ALU.mult), reads=[accB[a1], cosB], writes=[t1B])
            P.op("dve", lambda e, a2=a2: e.tensor_tensor(out=t2[:], in0=acc[a2][0:64, :], in1=sinT[:], op=ALU.mult), reads=[accB[a2], sinB], writes=[t2B])

            def prod(sap, sB):
                P.op("dve", lambda e: e.tensor_tensor(out=sap, in0=t1[:], in1=t2[:], op=ALU.add), reads=[t1B, t2B], writes=[sB])
            store(b.kr[seq][:, g * GS:(g + 1) * GS], prod, GS, M=64)
            for (tlo, thi, clo) in own_ranges(L, T, g * GS, (g + 1) * GS):
                latent("q", cfg.QT, o1, gq, cfg.QL, tlo - g * GS, thi - g * GS, b.qn[seq], clo)
        P.flush()


def copy_fn(eng, out, in_, scale=None):
    if eng == "act":
        if scale is None:
            return lambda e: e.activation(out=out, in_=in_, func=AF.Copy)
        return lambda e: e.activation(out=out, in_=in_, func=AF.Copy, scale=scale)
    if scale is None:
        return lambda e: e.tensor_copy(out=out, in_=in_)
    return lambda e: e.tensor_scalar(out=out, in0=in_, scalar1=scale, scalar2=None, op0=ALU.mult)


def phase_C(b):
    for seq in range(2):
        attn_seq(b, seq)


def attn_seq(b, seq):
    nc, cfg, P = b.nc, b.cfg, b.P
    L, T = cfg.L[seq], cfg.T[seq]
    TC = T + 2
    KT, QT, H = cfg.KT, cfg.QT, cfg.H
    NKT, NKB = L // 128, L // 512
    qtiles = splits(TC, 512)
    SCALE = 192.0 ** -0.5
    with ExitStack() as st:
        def sb(name, shape, dt):
            return st.enter_context(nc.sbuf_tensor(f"C{seq}_{name}", list(shape), dt))

        def pst(name, shape, dt):
            return st.enter_context(nc.psum_tensor(f"C{seq}_{name}", list(shape), dt))
        kvnT = sb("kvnT", [128, KT, L], BF16)
        krT = sb("krT", [64, L], BF16)
        qnT = sb("qnT", [128, QT, TC], BF16)
        cosq = sb("cosq", [64, TC], BF16)
        sinq = sb("sinq", [64, TC], BF16)
        inB = Buf()
        Kh = sb("Kh", [128, L], BF16)
        Vh = sb("Vh", [128, L], BF16)
        KhB = [Buf() for _ in range(NKB)]
        VhB = [Buf() for _ in range(NKB)]
        qhn = sb("qhn", [128, TC], BF16)
        qhr = sb("qhr", [64, TC], BF16)
        qhB = [Buf() for _ in qtiles]
        wkv = [sb(f"wkv{i}", [128, KT, 256], BF16) for i in range(2)]
        wq = [sb(f"wq{i}", [128, QT, 192], BF16) for i in range(2)]
        wqs = [sb(f"wqs{i}", [128, QT, 64], BF16) for i in range(2)]
        wB = [Buf() for _ in range(2)]
        PT = [sb(f"PT{i}", [128, 512], BF16) for i in range(3)]
        PTB = [Buf() for _ in range(3)]
        rden = sb("rden", [128, 512], F32)
        rdB = Buf()
        ostg = [sb(f"ostg{i}", [128, 512], BF16) for i in range(2)]
        ostgB = [Buf() for _ in range(2)]
        t1 = sb("t1", [64, 512], F32)
        t2 = sb("t2", [64, 512], F32)
        t1B, t2B = Buf(), Buf()
        ones = sb("ones", [128, 128], BF16)
        cB = Buf()
        S = [pst(f"S{i}", [128, 512], F32) for i in range(2)]
        SB = [Buf() for _ in range(2)]
        O = [pst(f"O{i}", [128, 512], F32) for i in range(2)]
        OB = [Buf() for _ in range(2)]
        DEN = [pst(f"DEN{i}", [128, 512], F32) for i in range(2)]
        DENB = [Buf() for _ in range(2)]
        acc = [pst(f"acc{i}", [128, 512], F32) for i in range(2)]
        accB = [Buf() for _ in range(2)]
        cnt = {"acc": 0, "ev": 0, "s": 0, "pt": 0, "o": 0, "st": 0}

        def evac_eng():
            cnt["ev"] += 1
            return "act" if cnt["ev"] % 2 else "dve"

        P.op("pool", lambda e: e.memset(ones[:], 1.0), writes=[cB])
        P.op("sync", lambda e: e.dma_start(out=kvnT[:], in_=b.kvn[seq].rearrange("(k p) l -> p k l", p=128)), writes=[inB], dma=True)
        P.op("sync", lambda e: e.dma_start(out=krT[:], in_=b.kr[seq]), writes=[inB], dma=True)
        P.op("sync", lambda e: e.dma_start(out=qnT[:], in_=b.qn[seq].rearrange("(k p) c -> p k c", p=128)), writes=[inB], dma=True)
        for tab, dst in ((0, cosq), (1, sinq)):
            P.op("pool", lambda e, tab=tab, dst=dst: e.dma_start(out=dst[:, 0:1], in_=b.rtab[seq][tab, :, L - 1:L], allow_slow_non_contiguous=True), writes=[inB], dma=True)
            P.op("pool", lambda e, tab=tab, dst=dst: e.dma_start(out=dst[:, 1:TC], in_=b.rtab[seq][tab, :, 0:T + 1]), writes=[inB], dma=True)
        wkvv = b.w_kv_up.rearrange("(k p) c -> p k c", p=128)
        wqv = b.w_q_up.rearrange("(k p) c -> p k c", p=128)
        for h in range(H):
            wi = h % 2
            P.op("pool", lambda e, wi=wi, h=h: e.dma_start(out=wkv[wi][:], in_=wkvv[:, :, h * 256:(h + 1) * 256]), writes=[wB[wi]], dma=True)
            P.op("pool", lambda e, wi=wi, h=h: e.dma_start(out=wq[wi][:], in_=wqv[:, :, h * 192:(h + 1) * 192]), writes=[wB[wi]], dma=True)
            P.op("pool", lambda e, wi=wi, h=h: e.dma_start(out=wqs[wi][:, :, 0:32], in_=wqv[:, :, h * 192 + 160:h * 192 + 192]), writes=[wB[wi]], dma=True)
            P.op("pool", lambda e, wi=wi, h=h: e.dma_start(out=wqs[wi][:, :, 32:64], in_=wqv[:, :, h * 192 + 128:h * 192 + 160]), writes=[wB[wi]], dma=True)
            for kb in range(NKB):
                a = cnt["acc"] % 2
                cnt["acc"] += 1
                P.mm_group(accB[a], [(lambda e, a=a, kt=kt, kb=kb, wi=wi: e.matmul(acc[a][:, 0:512], lhsT=wkv[wi][:, kt, 0:128], rhs=kvnT[:, kt, kb * 512:(kb + 1) * 512], start=(kt == 0), stop=(kt == KT - 1)), [wB[wi], inB]) for kt in range(KT)])
                eng = evac_eng()
                P.op(eng, copy_fn(eng, Kh[:, kb * 512:(kb + 1) * 512], acc[a][:, 0:512]), reads=[accB[a]], writes=[KhB[kb]])
                a = cnt["acc"] % 2
                cnt["acc"] += 1
                mms = []
                for i in range(4):
                    for kt in range(KT):
                        k0 = kb * 512 + i * 128
                        mms.append((lambda e, a=a, kt=kt, i=i, k0=k0, wi=wi: e.matmul(acc[a][:, i * 128:(i + 1) * 128], lhsT=kvnT[:, kt, k0:k0 + 128], rhs=wkv[wi][:, kt, 128:256], start=(kt == 0), stop=(kt == KT - 1)), [wB[wi], inB]))
                P.mm_group(accB[a], mms)
                eng = evac_eng()
                P.op(eng, copy_fn(eng, Vh[:, kb * 512:(kb + 1) * 512], acc[a][:, 0:512]), reads=[accB[a]], writes=[VhB[kb]])
            for qi, (c0, c1) in enumerate(qtiles):
                N = c1 - c0
                a = cnt["acc"] % 2
                cnt["acc"] += 1
                P.mm_group(accB[a], [(lambda e, a=a, qt=qt, wi=wi, c0=c0, c1=c1, N=N: e.matmul(acc[a][:, 0:N], lhsT=wq[wi][:, qt, 0:128], rhs=qnT[:, qt, c0:c1], start=(qt == 0), stop=(qt == QT - 1)), [wB[wi], inB]) for qt in range(QT)])
                P.op("act", copy_fn("act", qhn[:, c0:c1], acc[a][:, 0:N], scale=SCALE), reads=[accB[a]], writes=[qhB[qi]])
                a1 = cnt["acc"] % 2
                cnt["acc"] += 1
                P.mm_group(accB[a1], [(lambda e, a1=a1, qt=qt, wi=wi, c0=c0, c1=c1, N=N: e.matmul(acc[a1][0:64, 0:N], lhsT=wq[wi][:, qt, 128:192], rhs=qnT[:, qt, c0:c1], start=(qt == 0), stop=(qt == QT - 1)), [wB[wi], inB]) for qt in range(QT)])
                P.op("dve", lambda e, a1=a1, c0=c0, c1=c1, N=N: e.scalar_tensor_tensor(out=t1[:, 0:N], in0=acc[a1][0:64, 0:N], scalar=SCALE, in1=cosq[:, c0:c1], op0=ALU.mult, op1=ALU.mult), reads=[accB[a1], inB], writes=[t1B])
                a2 = cnt["acc"] % 2
                cnt["acc"] += 1
                P.mm_group(accB[a2], [(lambda e, a2=a2, qt=qt, wi=wi, c0=c0, c1=c1, N=N: e.matmul(acc[a2][0:64, 0:N], lhsT=wqs[wi][:, qt, 0:64], rhs=qnT[:, qt, c0:c1], start=(qt == 0), stop=(qt == QT - 1)), [wB[wi], inB]) for qt in range(QT)])
                P.op("dve", lambda e, a2=a2, c0=c0, c1=c1, N=N: e.scalar_tensor_tensor(out=t2[:, 0:N], in0=acc[a2][0:64, 0:N], scalar=SCALE, in1=sinq[:, c0:c1], op0=ALU.mult, op1=ALU.mult), reads=[accB[a2], inB], writes=[t2B])
                P.op("dve", lambda e, c0=c0, c1=c1, N=N: e.tensor_tensor(out=qhr[:, c0:c1], in0=t1[:, 0:N], in1=t2[:, 0:N], op=ALU.add), reads=[t1B, t2B], writes=[qhB[qi]])
            for qi, (c0, c1) in enumerate(qtiles):
                N = c1 - c0
                ob = cnt["o"] % 2
                cnt["o"] += 1
                sis = [(cnt["s"] + kt) % 2 for kt in range(NKT)]
                pis = [(cnt["pt"] + kt) % 3 for kt in range(NKT)]
                cnt["s"] += NKT
                cnt["pt"] += NKT

                def emit_S(kt, N=N, c0=c0, c1=c1, qi=qi, sis=sis):
                    si, k0, kb = sis[kt], kt * 128, kt // 4
                    P.mm_group(SB[si], [
                        (lambda e, si=si, k0=k0: e.matmul(S[si][:, 0:N], lhsT=Kh[:, k0:k0 + 128], rhs=qhn[:, c0:c1], start=True, stop=False), [KhB[kb], qhB[qi]]),
                        (lambda e, si=si, k0=k0: e.matmul(S[si][:, 0:N], lhsT=krT[:, k0:k0 + 128], rhs=qhr[:, c0:c1], start=False, stop=True), [inB]),
                    ])
                emit_S(0)
                for kt in range(NKT):
                    if kt + 1 < NKT:
                        emit_S(kt + 1)
                    si, pi, k0, kb = sis[kt], pis[kt], kt * 128, kt // 4
                    P.op("act", lambda e, si=si, pi=pi, N=N: e.activation(out=PT[pi][:, 0:N], in_=S[si][:, 0:N], func=AF.Exp), reads=[SB[si]], writes=[PTB[pi]])
                    first, last = kt == 0, kt == NKT - 1
                    P.mm(lambda e, pi=pi, k0=k0, first=first, last=last, N=N, ob=ob: e.matmul(O[ob][:, 0:N], lhsT=Vh[:, k0:k0 + 128], rhs=PT[pi][:, 0:N], start=first, stop=last), reads=[VhB[kb], PTB[pi]], ps=OB[ob], first=first, last=last)
                    P.mm(lambda e, pi=pi, first=first, last=last, N=N, ob=ob: e.matmul(DEN[ob][:, 0:N], lhsT=ones[:], rhs=PT[pi][:, 0:N], start=first, stop=last), reads=[cB, PTB[pi]], ps=DENB[ob], first=first, last=last)
                P.op("dve", lambda e, ob=ob, N=N: e.reciprocal(out=rden[:, 0:N], in_=DEN[ob][:, 0:N]), reads=[DENB[ob]], writes=[rdB])
                oi = cnt["st"] % 2
                cnt["st"] += 1
                P.op("dve", lambda e, ob=ob, oi=oi, N=N: e.tensor_tensor(out=ostg[oi][:, 0:N], in0=O[ob][:, 0:N], in1=rden[:, 0:N], op=ALU.mult), reads=[OB[ob], rdB], writes=[ostgB[oi]])
                r0 = cfg.SW + h * 128
                P.op("sync", lambda e, oi=oi, r0=r0, c0=c0, c1=c1, N=N: e.dma_start(out=b.mraw[seq][r0:r0 + 128, c0:c1], in_=ostg[oi][:, 0:N]), reads=[ostgB[oi]], dma=True)
        P.flush()


def phase_D(b):
    nc, cfg, P = b.nc, b.cfg, b.P
    D, MT, UT = cfg.D, cfg.MT, cfg.UT
    NCG = D // 512
    for seq in range(2):
        L, T = cfg.L[seq], cfg.T[seq]
        TC = T + 2
        for gi, (g0, g1) in enumerate(splits(TC, 1026)):
            n = g1 - g0
            with ExitStack() as st:
                def sb(name, shape, dt):
                    return st.enter_context(nc.sbuf_tensor(f"D{seq}{gi}_{name}", list(shape), dt))

                def pst(name, shape, dt):
                    return st.enter_context(nc.psum_tensor(f"D{seq}{gi}_{name}", list(shape), dt))
                mr = sb("mr", [128, MT, n], BF16)
                mrB = [Buf() for _ in range(MT)]
                sq = [sb(f"sq{i}", [128, 512], BF16) for i in range(2)]
                sqB = [Buf() for _ in range(2)]
                rs = sb("rs", [128, 2, n], F32)
                rsB = [Buf() for _ in range(2)]
                wo = [sb(f"wo{i}", [128, MT, 512], BF16) for i in range(2)]
                woB = [Buf() for _ in range(2)]
                xr = [sb(f"xr{i}", [128, 512], F32) for i in range(2)]
                xrB = [Buf() for _ in range(2)]
                xo = [sb(f"xo{i}", [128, 512], F32) for i in range(2)]
                xoB = [Buf() for _ in range(2)]
                gmo = sb("gmo", [128, MT], F32)
                ones = sb("ones", [128, 128], BF16)
                cB = Buf()
                ssp = pst("ssp", [128, 512], F32)
                sspB = Buf()
                acc = [pst(f"acc{i}", [128, 512], F32) for i in range(3)]
                accB = [Buf() for _ in range(3)]
                cnt = {"sq": 0, "acc": 0, "x": 0}
                P.op("pool", lambda e: e.memset(ones[:], 1.0), writes=[cB])
                P.op("sync", lambda e: e.dma_start(out=gmo[:], in_=b.gmo), writes=[cB], dma=True)
                mv = b.mraw[seq].rearrange("(k p) c -> p k c", p=128)
                for k in range(MT):
                    P.op("sync", lambda e, k=k: e.dma_start(out=mr[:, k, :], in_=mv[:, k, g0:g1]), writes=[mrB[k]], dma=True)
                for half, (k0, k1) in enumerate(((0, UT), (UT, MT))):
                    width = (k1 - k0) * 128
                    for (c0, c1) in splits(n, 512):
                        N = c1 - c0
                        for k in range(k0, k1):
                            i = cnt["sq"] % 2
                            cnt["sq"] += 1
                            P.op("act", lambda e, i=i, k=k, c0=c0, c1=c1, N=N: e.activation(out=sq[i][:, 0:N], in_=mr[:, k, c0:c1], func=AF.Square), reads=[mrB[k]], writes=[sqB[i]])
                            P.mm(lambda e, i=i, N=N, k=k, k0=k0, k1=k1: e.matmul(ssp[:, 0:N], lhsT=ones[:], rhs=sq[i][:, 0:N], start=(k == k0), stop=(k == k1 - 1)), reads=[sqB[i], cB], ps=sspB, first=(k == k0), last=(k == k1 - 1))
                        P.op("act", lambda e, half=half, c0=c0, c1=c1, N=N, width=width: e.activation(out=rs[:, half, c0:c1], in_=ssp[:, 0:N], func=AF.Ln, scale=1.0 / width, bias=cfg.EPS), reads=[sspB], writes=[rsB[half]])
                        P.op("act", lambda e, half=half, c0=c0, c1=c1: e.activation(out=rs[:, half, c0:c1], in_=rs[:, half, c0:c1], func=AF.Exp, scale=-0.5), reads=[rsB[half]], writes=[rsB[half]])
                for k in range(MT):
                    half = 0 if k < UT else 1
                    P.op("dve", lambda e, k=k, half=half: e.scalar_tensor_tensor(out=mr[:, k, :], in0=mr[:, k, :], scalar=gmo[:, k:k + 1], in1=rs[:, half, :], op0=ALU.mult, op1=ALU.mult), reads=[rsB[half], cB], writes=[mrB[k]])
                wov = b.w_out.rearrange("(k p) c -> p k c", p=128)
                for cgi in range(NCG):
                    wi = cgi % 2
                    P.op("pool", lambda e, wi=wi, cgi=cgi: e.dma_start(out=wo[wi][:], in_=wov[:, :, cgi * 512:(cgi + 1) * 512]), writes=[woB[wi]], dma=True)
                    for (t0, t1_) in splits(n, 128):
                        m = t1_ - t0
                        ca, cb = g0 + t0, g0 + t1_
                        a = cnt["acc"] % 3
                        cnt["acc"] += 1
                        P.mm_group(accB[a], [(lambda e, a=a, k=k, wi=wi, t0=t0, t1_=t1_, m=m: e.matmul(acc[a][0:m, 0:512], lhsT=mr[:, k, t0:t1_], rhs=wo[wi][:, k, :], start=(k == 0), stop=(k == MT - 1)), [mrB[k], woB[wi]]) for k in range(MT)])
                        xi = cnt["x"] % 2
                        cnt["x"] += 1
                        cs = slice(cgi * 512, (cgi + 1) * 512)
                        if ca == 0:
                            P.op("sync", lambda e, xi=xi, cs=cs: e.dma_start(out=xr[xi][0:1, :], in_=b.x[seq][L - 1:L, cs]), writes=[xrB[xi]], dma=True)
                            if m > 1:
                                P.op("sync", lambda e, xi=xi, cs=cs, m=m, cb=cb: e.dma_start(out=xr[xi][1:m, :], in_=b.x[seq][0:cb - 1, cs]), writes=[xrB[xi]], dma=True)
                        else:
                            P.op("sync", lambda e, xi=xi, cs=cs, m=m, ca=ca, cb=cb: e.dma_start(out=xr[xi][0:m, :], in_=b.x[seq][ca - 1:cb - 1, cs]), writes=[xrB[xi]], dma=True)
                        P.op("dve", lambda e, xi=xi, a=a, m=m: e.tensor_tensor(out=xo[xi][0:m, :], in0=acc[a][0:m, 0:512], in1=xr[xi][0:m, :], op=ALU.add), reads=[accB[a], xrB[xi]], writes=[xoB[xi]])
                        P.op("sync", lambda e, xi=xi, m=m, ca=ca, cb=cb, cs=cs: e.dma_start(out=b.x1[seq][ca:cb, cs], in_=xo[xi][0:m, :]), reads=[xoB[xi]], dma=True)
                P.flush()


def phase_G(b):
    nc, cfg, P = b.nc, b.cfg, b.P
    D, KD = cfg.D, cfg.KD
    TB = min(8, KD)
    with ExitStack() as st:
        def sb(name, shape, dt):
            return st.enter_context(nc.sbuf_tensor(f"G_{name}", list(shape), dt))

        def pst(name, shape, dt):
            return st.enter_context(nc.psum_tensor(f"G_{name}", list(shape), dt))
        xt = [sb(f"xt{i}", [128, D], F32) for i in range(2)]
        xtB = [Buf() for _ in range(2)]
        xn = [sb(f"xn{i}", [128, D], BF16) for i in range(2)]
        xnB = [Buf() for _ in range(2)]
        ss = sb("ss", [128, 8], F32)
        ssB = Buf()
        gbc = sb("gbc", [128, D], F32)
        ident = sb("ident", [128, 128], BF16)
        cB = Buf()
        hs = [sb(f"hs{i}", [128, KD, 128], BF16) for i in range(2)]
        hsB = [Buf() for _ in range(2)]
        tp = [pst(f"tp{i}", [128, 8, 128], BF16) for i in range(2)]
        tpB = [Buf() for _ in range(2)]
        P.op("pool", lambda e: e.dma_start(out=ident[:], in_=b.ident_in), writes=[cB], dma=True)
        P.op("sync", lambda e: e.dma_start(out=gbc[:], in_=bcast_rows(b.g_ffn, 128, D)), writes=[cB], dma=True)
        ti = 0
        pbc = 0
        for seq in range(2):
            TC = cfg.T[seq] + 2
            hv = b.h2T[seq].rearrange("(k p) c -> p k c", p=128)
            for (r0, r1) in splits(TC, 128):
                m = r1 - r0
                bi = ti % 2
                ti += 1
                P.op("sync", lambda e, bi=bi, r0=r0, r1=r1, m=m, x1s=b.x1[seq]: e.dma_start(out=xt[bi][0:m, :], in_=x1s[r0:r1, :]), writes=[xtB[bi]], dma=True)
                P.op("act", lambda e, bi=bi, m=m: e.activation(out=xn[bi][0:m, :], in_=xt[bi][0:m, :], func=AF.Square, accum_out=ss[0:m, 0:1]), reads=[xtB[bi]], writes=[xnB[bi], ssB])
                P.op("act", lambda e, m=m: e.activation(out=ss[0:m, 1:2], in_=ss[0:m, 0:1], func=AF.Ln, scale=1.0 / D, bias=cfg.EPS), reads=[ssB], writes=[ssB])
                P.op("act", lambda e, m=m: e.activation(out=ss[0:m, 2:3], in_=ss[0:m, 1:2], func=AF.Exp, scale=-0.5), reads=[ssB], writes=[ssB])
                P.op("dve", lambda e, bi=bi, m=m: e.scalar_tensor_tensor(out=xn[bi][0:m, :], in0=xt[bi][0:m, :], scalar=ss[0:m, 2:3], in1=gbc[0:m, :], op0=ALU.mult, op1=ALU.mult), reads=[xtB[bi], ssB, cB], writes=[xnB[bi]])
                for j in range(KD // TB):
                    pb = pbc % 2
                    pbc += 1
                    P.mm_group(tpB[pb], [(lambda e, pb=pb, bi=bi, k=k, m=m: e.transpose(out=tp[pb][:, k % TB, 0:m], in_=xn[bi][0:m, k * 128:(k + 1) * 128], identity=ident[0:m, 0:m]), [xnB[bi], cB]) for k in range(j * TB, j * TB + TB)])
                    P.op("act", copy_fn("act", hs[bi][:, j * TB:j * TB + TB, 0:m], tp[pb][:, 0:TB, 0:m]), reads=[tpB[pb]], writes=[hsB[bi]])
                P.op("sync", lambda e, bi=bi, r0=r0, r1=r1, m=m, hv=hv: e.dma_start(out=hv[:, :, r0:r1], in_=hs[bi][:, :, 0:m], allow_slow_non_contiguous=(m < 16)), reads=[hsB[bi]], dma=True)
        P.flush()


def phase_E(b):
    nc, cfg, P = b.nc, b.cfg, b.P
    D, KD, FT = cfg.D, cfg.KD, cfg.FT
    NCG = D // 512
    GW = 512
    for seq in range(2):
        T = cfg.T[seq]
        NGR = T // GW
        for gi in range(NGR):
            a0 = 1 + gi * GW
            with ExitStack() as st:
                def sb(name, shape, dt):
                    return st.enter_context(nc.sbuf_tensor(f"E{seq}{gi}_{name}", list(shape), dt))

                def pst(name, shape, dt):
                    return st.enter_context(nc.psum_tensor(f"E{seq}{gi}_{name}", list(shape), dt))
                h2 = sb("h2", [128, KD, GW + 2], BF16)
                h2B = Buf()
                actT = sb("actT", [128, FT, GW], BF16)
                actB = [Buf() for _ in range(FT)]
                cw = sb("cw", [128, 3, FT], F32)
                cbias = sb("cbias", [128, FT], F32)
                mk = sb("mk", [128, 32], F32)
                cB = Buf()
                pb = [pst(f"pb{i}", [128, 512], F32) for i in range(8)]
                pbB = [Buf() for _ in range(8)]
                x2B = [[Buf() for _ in range(NCG)] for _ in range(4)]
                st1 = ExitStack()

                def sb1(name, shape, dt):
                    return st1.enter_context(nc.sbuf_tensor(f"E{seq}{gi}_{name}", list(shape), dt))
                wu = [sb1(f"wu{i}", [128, KD, 128], BF16) for i in range(2)]
                wg = [sb1(f"wg{i}", [128, KD, 128], BF16) for i in range(2)]
                wuB = [Buf() for _ in range(2)]
                wgB = [Buf() for _ in range(2)]
                upf = [sb1(f"upf{i}", [128, GW + 2], F32) for i in range(2)]
                upfB = [Buf() for _ in range(2)]
                c1 = sb1("c1", [128, GW], F32)
                c1B = Buf()
                sil = sb1("sil", [128, GW], F32)
                silB = Buf()
                P.op("sync", lambda e: e.dma_start(out=cw[:], in_=b.convw), writes=[cB], dma=True)
                P.op("sync", lambda e: e.dma_start(out=cbias[:], in_=b.convb), writes=[cB], dma=True)
                P.op("sync", lambda e: e.dma_start(out=mk[:], in_=bcast_rows(b.masks, 128, 32)), writes=[cB], dma=True)
                hv = b.h2T[seq].rearrange("(k p) c -> p k c", p=128)
                P.op("sync", lambda e, a0=a0, hv=hv: e.dma_start(out=h2[:], in_=hv[:, :, a0 - 1:a0 + GW + 1]), writes=[h2B], dma=True)
                wuv = b.w_up.rearrange("(k p) c -> p k c", p=128)
                wgv = b.w_gate.rearrange("(k p) c -> p k c", p=128)
                halves = splits(GW + 2, 512)
                for f in range(FT):
                    wi = f % 2
                    P.op("pool", lambda e, wi=wi, f=f: e.dma_start(out=wu[wi][:], in_=wuv[:, :, f * 128:(f + 1) * 128]), writes=[wuB[wi]], dma=True)
                    P.op("pool", lambda e, wi=wi, f=f: e.dma_start(out=wg[wi][:], in_=wgv[:, :, f * 128:(f + 1) * 128]), writes=[wgB[wi]], dma=True)
                    ui = f % 2
                    for hi_, (lo, hi) in enumerate(halves):
                        pi = (f % 2) * 2 + hi_
                        N = hi - lo
                        P.mm_group(pbB[pi], [(lambda e, pi=pi, k=k, wi=wi, lo=lo, hi=hi, N=N: e.matmul(pb[pi][:, 0:N], lhsT=wu[wi][:, k, :], rhs=h2[:, k, lo:hi], start=(k == 0), stop=(k == KD - 1)), [wuB[wi], h2B]) for k in range(KD)])
                        P.op("act", copy_fn("act", upf[ui][:, lo:hi], pb[pi][:, 0:N]), reads=[pbB[pi]], writes=[upfB[ui]])
                    gp = 4 + f % 2
                    P.mm_group(pbB[gp], [(lambda e, gp=gp, k=k, wi=wi: e.matmul(pb[gp][:, 0:GW], lhsT=wg[wi][:, k, :], rhs=h2[:, k, 1:GW + 1], start=(k == 0), stop=(k == KD - 1)), [wgB[wi], h2B]) for k in range(KD)])
                    if gi == 0:
                        P.op("dve", lambda e, ui=ui: e.tensor_scalar(out=upf[ui][:, 0:1], in0=upf[ui][:, 0:1], scalar1=mk[:, 3:4], scalar2=None, op0=ALU.mult), reads=[cB], writes=[upfB[ui]])
                    if gi == NGR - 1:
                        P.op("dve", lambda e, ui=ui: e.tensor_scalar(out=upf[ui][:, GW + 1:GW + 2], in0=upf[ui][:, GW + 1:GW + 2], scalar1=mk[:, 0:1], scalar2=None, op0=ALU.mult), reads=[cB], writes=[upfB[ui]])
                    P.op("dve", lambda e, ui=ui, f=f: e.tensor_scalar(out=c1[:], in0=upf[ui][:, 1:GW + 1], scalar1=cw[:, 1, f:f + 1], scalar2=cbias[:, f:f + 1], op0=ALU.mult, op1=ALU.add), reads=[upfB[ui], cB], writes=[c1B])
                    P.op("dve", lambda e, ui=ui, f=f: e.scalar_tensor_tensor(out=c1[:], in0=upf[ui][:, 0:GW], scalar=cw[:, 0, f:f + 1], in1=c1[:], op0=ALU.mult, op1=ALU.add), reads=[upfB[ui], cB], writes=[c1B])
                    P.op("dve", lambda e, ui=ui, f=f: e.scalar_tensor_tensor(out=c1[:], in0=upf[ui][:, 2:GW + 2], scalar=cw[:, 2, f:f + 1], in1=c1[:], op0=ALU.mult, op1=ALU.add), reads=[upfB[ui], cB], writes=[c1B])
                    P.op("act", lambda e: e.activation(out=sil[:], in_=c1[:], func=AF.Silu), reads=[c1B], writes=[silB])
                    P.op("dve", lambda e, gp=gp, f=f: e.tensor_tensor(out=actT[:, f, :], in0=pb[gp][:, 0:GW], in1=sil[:], op=ALU.mult), reads=[pbB[gp], silB], writes=[actB[f]])
                P.flush()
                st1.close()
                st2 = ExitStack()

                def sb2(name, shape, dt):
                    return st2.enter_context(nc.sbuf_tensor(f"E{seq}{gi}_{name}", list(shape), dt))
                wd = [sb2(f"wd{i}", [128, 512], BF16) for i in range(12)]
                wdB = [Buf() for _ in range(12)]
                xr = [sb2(f"xr{i}", [128, 512], F32) for i in range(2)]
                xrB = [Buf() for _ in range(2)]
                xo = [sb2(f"xo{i}", [128, 512], F32) for i in range(2)]
                xoB = [Buf() for _ in range(2)]
                junk = sb2("junk", [128, 512], BF16)
                junkB = Buf()
                ssq = sb2("ssq", [128, 4, NCG + 4], F32)
                ssqB = [Buf() for _ in range(4)]
                gfs = [sb2(f"gfs{i}", [128, 512], F32) for i in range(2)]
                gfsB = [Buf() for _ in range(2)]
                wdc = 0
                xc = 0
                for cgi in range(NCG):
                    st_ = (cgi % 2) * 4
                    cs = slice(cgi * 512, (cgi + 1) * 512)
                    for f in range(FT):
                        wi = wdc % 12
                        wdc += 1
                        P.op("pool", lambda e, wi=wi, f=f, cs=cs: e.dma_start(out=wd[wi][:], in_=b.w_down[f * 128:(f + 1) * 128, cs]), writes=[wdB[wi]], dma=True)
                        for i in range(4):
                            P.mm(lambda e, i=i, f=f, wi=wi, st_=st_: e.matmul(pb[st_ + i][:, 0:512], lhsT=actT[:, f, i * 128:(i + 1) * 128], rhs=wd[wi][:], start=(f == 0), stop=(f == FT - 1)), reads=[wdB[wi], actB[f]], ps=pbB[st_ + i], first=(f == 0), last=(f == FT - 1))
                    for i in range(4):
                        xi = xc % 2
                        xc += 1
                        ca = a0 + i * 128
                        P.op("sync", lambda e, xi=xi, ca=ca, cs=cs: e.dma_start(out=xr[xi][:], in_=b.x1[seq][ca:ca + 128, cs]), writes=[xrB[xi]], dma=True)
                        P.op("dve", lambda e, xi=xi, i=i, st_=st_: e.tensor_tensor(out=xo[xi][:], in0=pb[st_ + i][:, 0:512], in1=xr[xi][:], op=ALU.add), reads=[pbB[st_ + i], xrB[xi]], writes=[xoB[xi]])
                        P.op("act", lambda e, xi=xi, i=i, cgi=cgi: e.activation(out=junk[:], in_=xo[xi][:], func=AF.Square, accum_out=ssq[:, i, cgi:cgi + 1]), reads=[xoB[xi]], writes=[junkB, ssqB[i]])
                        P.op("sync", lambda e, xi=xi, ca=ca, cs=cs: e.dma_start(out=b.x2[seq][ca - 1:ca + 127, cs], in_=xo[xi][:]), reads=[xoB[xi]], writes=[x2B[i][cgi]], dma=True)
                for i in range(4):
                    P.op("dve", lambda e, i=i: e.tensor_reduce(out=ssq[:, i, NCG:NCG + 1], in_=ssq[:, i, 0:NCG], axis=mybir.AxisListType.X, op=ALU.add), reads=[ssqB[i]], writes=[ssqB[i]])
                    P.op("act", lambda e, i=i: e.activation(out=ssq[:, i, NCG + 1:NCG + 2], in_=ssq[:, i, NCG:NCG + 1], func=AF.Ln, scale=1.0 / D, bias=cfg.EPS), reads=[ssqB[i]], writes=[ssqB[i]])
                    P.op("act", lambda e, i=i: e.activation(out=ssq[:, i, NCG + 2:NCG + 3], in_=ssq[:, i, NCG + 1:NCG + 2], func=AF.Exp, scale=-0.5), reads=[ssqB[i]], writes=[ssqB[i]])
                    for cgi in range(NCG):
                        xi = xc % 2
                        xc += 1
                        cs = slice(cgi * 512, (cgi + 1) * 512)
                        r0 = a0 - 1 + i * 128
                        P.op("sync", lambda e, xi=xi, r0=r0, cs=cs: e.dma_start(out=xr[xi][:], in_=b.x2[seq][r0:r0 + 128, cs]), reads=[x2B[i][cgi]], writes=[xrB[xi]], dma=True)
                        P.op("sync", lambda e, xi=xi, cgi=cgi: e.dma_start(out=gfs[xi][:], in_=bcast_rows(b.g_fin, 128, 512, off=cgi * 512)), writes=[gfsB[xi]], dma=True)
                        P.op("dve", lambda e, xi=xi, i=i, cs=cs: e.scalar_tensor_tensor(out=xo[xi][:], in0=xr[xi][:], scalar=ssq[:, i, NCG + 2:NCG + 3], in1=gfs[xi][:], op0=ALU.mult, op1=ALU.mult), reads=[xrB[xi], ssqB[i], gfsB[xi]], writes=[xoB[xi]])
                        P.op("sync", lambda e, xi=xi, r0=r0, cs=cs: e.dma_start(out=b.y[seq][r0:r0 + 128, cs], in_=xo[xi][:]), reads=[xoB[xi]], dma=True)
                P.flush()
                st2.close()


def MM(out, lhsT, rhs, start, stop):
    return lambda e: e.matmul(out, lhsT=lhsT, rhs=rhs, start=start, stop=stop)


def rot_exps(cfg):
    s = set()
    for k in range(1, 8):
        s.add(8 * k)
        s.add(64 * k)
    for T in cfg.T:
        rq = T // 512
        for k in range(1, rq):
            s.add(512 * k)
        s.add(T)
        s.add(2 * T)
    return sorted(s)


def exp_vector(cfg):
    ev = list(range(0, 8)) + list(range(7, -1, -1)) + list(range(1, 9)) + list(range(8, 0, -1)) + [-i for i in range(8)]
    ev += rot_exps(cfg)
    assert len(ev) <= 64
    return ev


def sb3(t, off, dims):
    row = 1
    for d in list(t.shape)[1:]:
        row *= d
    return bass.AP(t, off, [[row, 128]] + [list(d) for d in dims])


def phase_B(b):
    nc, cfg, P = b.nc, b.cfg, b.P
    G, UT = cfg.G, cfg.UT
    ROT = rot_exps(cfg)
    NR = len(ROT)
    ROTI = {e: i for i, e in enumerate(ROT)}
    EV = exp_vector(cfg)
    NE = len(EV)
    RC0 = 40
    with ExitStack() as st:
        def sb(name, shape, dt):
            return st.enter_context(nc.sbuf_tensor(f"B_{name}", list(shape), dt))

        def pst(name, shape, dt):
            return st.enter_context(nc.psum_tensor(f"B_{name}", list(shape), dt))
        sel = sb("sel", [128, 64, 128], BF16)
        selo = sb("selo", [128, 64, 128], BF16)
        cstf = sb("cstf", [128, 4, 128], F32)
        identb = sb("identb", [128, 128], BF16)
        sig = sb("sig", [128, 4], F32)
        evb = sb("evb", [128, 64], F32)
        mk = sb("mk", [128, 32], F32)
        cB = Buf()
        are = sb("are", [128, 16], F32)
        aim = sb("aim", [128, 16], F32)
        ldt = sb("ldt", [128, 16], F32)
        sm = sb("sm", [128, 12, 16], F32)
        Bx1 = sb("Bx1", [128, 16, 16], F32)
        Bx2 = sb("Bx2", [128, 16, 16], F32)
        Cx1 = sb("Cx1", [128, 16, 16], F32)
        Cx2 = sb("Cx2", [128, 16, 16], F32)
        M1B = sb("M1B", [128, 16, 16], F32)
        M2B = sb("M2B", [128, 16, 16], F32)
        tb1 = sb("tb1", [128, 16, 16], F32)
        tb2 = sb("tb2", [128, 16, 16], F32)
        dvec = sb("dvec", [128, 8], F32)
        LRt = sb("LRt", [128, 16, NE], F32)
        ANG = sb("ANG", [128, 16, NE], F32)
        ANGf = sb("ANGf", [128, 16, NE], F32)
        ANGi = sb("ANGi", [128, 16, NE], I32)
        PT1 = sb("PT1", [128, 16, NE], F32)
        PT2 = sb("PT2", [128, 16, NE], F32)
        NPT2 = sb("NPT2", [128, 16, NE], F32)
        tabB = Buf()
        inB = Buf()
        gtA = [sb(f"gtA{i}", [128, 8, 16], F32) for i in range(4)]
        gtB = [sb(f"gtB{i}", [128, 8, 16], F32) for i in range(4)]
        gtAB = [Buf() for _ in range(4)]
        gtBB = [Buf() for _ in range(4)]
        r2t = sb("r2t", [128, NR, 128], BF16)
        r2B = Buf()
        Wxn = sb("Wxn", [128, 8, 16], BF16)
        WxnB = Buf()
        Phi = sb("Phi", [128, 2, 128], BF16)
        Psi = sb("Psi", [128, 2, 128], BF16)
        PhB = Buf()
        ttA = sb("ttA", [128, 128], F32)
        ttBt = sb("ttBt", [128, 128], F32)
        ttAB, ttBB = Buf(), Buf()
        WxT = [[sb(f"WxT{p}{d}", [128, 128], BF16) for d in range(2)] for p in range(2)]
        Wy = [[sb(f"Wy{p}{d}", [128, 8, 16], BF16) for d in range(2)] for p in range(2)]
        rot = [[sb(f"rot{p}{d}", [128, NR, 128], BF16) for d in range(2)] for p in range(2)]
        TT = [sb(f"TT{p}", [128, 128], BF16) for p in range(2)]
        matB = [Buf() for _ in range(2)]
        Lmax = max(cfg.L)
        NCmax = Lmax // 8
        uTt = [sb(f"uTt{s}", [128, cfg.L[s]], BF16) for s in range(2)]
        uTB = [Buf() for _ in range(2)]
        U = [sb(f"U{s}", [128, 8, cfg.L[s] // 8], BF16) for s in range(2)]
        UB = [[Buf() for _ in range(8)] for _ in range(2)]
        XL = [[[sb(f"X{s_}{d_}{l_}", [128, max(4, (cfg.L[s_] // 8) // (8 ** l_))], BF16) for l_ in range(3)] + [sb(f"X{s_}{d_}q", [128, 4], BF16)] for d_ in range(2)] for s_ in range(2)]
        XLB = [[[Buf() for _ in range(4)] for d_ in range(2)] for s_ in range(2)]
        PL = [[[sb(f"P{s_}{d_}{l_}", [128, max(4, (cfg.L[s_] // 8) // (8 ** l_))], BF16) for l_ in range(3)] + [sb(f"P{s_}{d_}q", [128, 4], BF16)] for d_ in range(2)] for s_ in range(2)]
        PLB = [[[Buf() for _ in range(4)] for d_ in range(2)] for s_ in range(2)]
        XM = [[sb(f"Xm{s_}{d_}", [128, 3, 4], BF16) for d_ in range(2)] for s_ in range(2)]
        XMB = [[Buf() for d_ in range(2)] for s_ in range(2)]
        NYmax = max(cfg.T) // 8 + 2
        Ys = [sb(f"Ys{s}", [128, 8, cfg.T[s] // 8 + 2], BF16) for s in range(2)]
        YsB = [[Buf() for _ in range(8)] for _ in range(2)]
        yst = [sb(f"yst{s}", [128, cfg.T[s] + 2], F32) for s in range(2)]
        ystB = [Buf() for _ in range(2)]
        pA = [pst(f"pA{i}", [128, 512], F32) for i in range(2)]
        pAB = [Buf() for _ in range(2)]
        pS = [pst(f"pS{i}", [128, 512], F32) for i in range(2)]
        pSB = [Buf() for _ in range(2)]
        pC = [pst(f"pC{i}", [128, 512], F32) for i in range(2)]
        pCB = [Buf() for _ in range(2)]
        pY = pst("pY", [128, 512], F32)
        pYB = Buf()
        pT = pst("pT", [128, 8, 128], BF16)
        pTB = Buf()
        cnt = {"a": 0, "s": 0, "ev": 0, "rt": 0}

        def evac_eng():
            cnt["ev"] += 1
            return "act" if cnt["ev"] % 2 else "dve"

        P.op("pool", lambda e: e.dma_start(out=sel[:], in_=b.sel_in), writes=[cB], dma=True)
        P.op("pool", lambda e: e.dma_start(out=selo[:], in_=b.selo_in), writes=[cB], dma=True)
        P.op("sync", lambda e: e.dma_start(out=cstf[:], in_=b.cst[:, 0:4, :]), writes=[cB], dma=True)
        P.op("pool", lambda e: e.dma_start(out=identb[:], in_=b.ident_in), writes=[cB], dma=True)
        P.op("sync", lambda e: e.dma_start(out=sig[:], in_=b.sig), writes=[cB], dma=True)
        P.op("sync", lambda e: e.dma_start(out=evb[:], in_=bcast_rows(b.evec, 128, 64)), writes=[cB], dma=True)
        P.op("sync", lambda e: e.dma_start(out=mk[:], in_=bcast_rows(b.masks, 128, 32)), writes=[cB], dma=True)
        pswap, maskf, maskb, identf = cstf[:, 0, :], cstf[:, 1, :], cstf[:, 2, :], cstf[:, 3, :]
        SIG, NSIG = sig[:, 0:1], sig[:, 1:2]

        def bc_e(tab):
            return tab[:, :].unsqueeze(2).broadcast_to([128, 16, NE])

        evbc = evb[:, 0:NE].unsqueeze(1).broadcast_to([128, 16, NE])

        def dv(fn, reads, writes):
            return P.op("dve", fn, reads=reads, writes=writes)

        for k in range(UT):
            g0 = k * 8
            for d in range(2):
                P.op("sync", lambda e, d=d, g0=g0: e.dma_start(out=are[:, d * 8:(d + 1) * 8], in_=b.are_h[:, d * G + g0:d * G + g0 + 8]), writes=[inB], dma=True)
                P.op("sync", lambda e, d=d, g0=g0: e.dma_start(out=aim[:, d * 8:(d + 1) * 8], in_=b.aim_h[:, d * G + g0:d * G + g0 + 8]), writes=[inB], dma=True)
                P.op("sync", lambda e, d=d, g0=g0: e.dma_start(out=ldt[:, d * 8:(d + 1) * 8], in_=bcast_rows(b.log_dt, 128, 8, off=d * G + g0)), writes=[inB], dma=True)
                for (dst, src) in ((Bx1, b.bx1_h), (Bx2, b.bx2_h), (Cx1, b.cx1_h), (Cx2, b.cx2_h)):
                    P.op("sync", lambda e, d=d, g0=g0, dst=dst, src=src: e.dma_start(out=dst[:, d * 8:(d + 1) * 8, :], in_=src[:, d * G + g0:d * G + g0 + 8, :]), writes=[inB], dma=True)
            P.op("sync", lambda e, g0=g0: e.dma_start(out=dvec[:], in_=b.dsk_h[:, g0:g0 + 8]), writes=[inB], dma=True)
            DT, LR, TH = sm[:, 0, :], sm[:, 1, :], sm[:, 2, :]
            P.op("act", lambda e: e.activation(out=DT, in_=ldt[:], func=AF.Exp), reads=[inB], writes=[tabB])
            dv(lambda e: e.tensor_tensor(out=LR, in0=are[:], in1=DT, op=ALU.mult), [inB, tabB], [tabB])
            dv(lambda e: e.tensor_tensor(out=TH, in0=aim[:], in1=DT, op=ALU.mult), [inB, tabB], [tabB])
            dv(lambda e: e.tensor_tensor(out=LRt[:], in0=bc_e(sm[:, 1, :]), in1=evbc, op=ALU.mult), [tabB, cB], [tabB])
            P.op("act", lambda e: e.activation(out=LRt[:], in_=LRt[:], func=AF.Exp), reads=[tabB], writes=[tabB])
            for which in range(2):
                dv(lambda e: e.tensor_tensor(out=ANG[:], in0=bc_e(sm[:, 2, :]), in1=evbc, op=ALU.mult), [tabB, cB], [tabB])
                if which == 1:
                    dv(lambda e: e.tensor_scalar(out=ANG[:], in0=ANG[:], scalar1=PI / 2, scalar2=None, op0=ALU.add), [tabB], [tabB])
                range_reduce(P, "dve", ANG[:], ANGf[:], ANGi[:], tabB, tabB, 16 * NE)
                P.op("act", lambda e: e.activation(out=ANG[:], in_=ANG[:], func=AF.Sin), reads=[tabB], writes=[tabB])
                if which == 0:
                    dv(lambda e: e.scalar_tensor_tensor(out=PT2[:], in0=LRt[:], scalar=SIG, in1=ANG[:], op0=ALU.mult, op1=ALU.mult), [tabB, cB], [tabB])
                    dv(lambda e: e.tensor_scalar(out=NPT2[:], in0=PT2[:], scalar1=-1.0, scalar2=None, op0=ALU.mult), [tabB], [tabB])
                else:
                    dv(lambda e: e.tensor_tensor(out=PT1[:], in0=LRt[:], in1=ANG[:], op=ALU.mult), [tabB], [tabB])
            i1 = 16
            NRr, NI, DEN, ZR, ZI, T0, T1 = (sm[:, j, :] for j in range(3, 10))
            dv(lambda e: e.tensor_scalar(out=NRr, in0=PT1[:, :, i1], scalar1=-1.0, scalar2=None, op0=ALU.add), [tabB], [tabB])
            dv(lambda e: e.tensor_scalar(out=NI, in0=PT2[:, :, i1], scalar1=SIG, scalar2=None, op0=ALU.mult), [tabB, cB], [tabB])
            dv(lambda e: e.tensor_tensor(out=DEN, in0=are[:], in1=are[:], op=ALU.mult), [inB], [tabB])
            dv(lambda e: e.tensor_tensor(out=T0, in0=aim[:], in1=aim[:], op=ALU.mult), [inB], [tabB])
            dv(lambda e: e.tensor_tensor(out=DEN, in0=DEN, in1=T0, op=ALU.add), [tabB], [tabB])
            dv(lambda e: e.reciprocal(out=DEN, in_=DEN), [tabB], [tabB])
            dv(lambda e: e.tensor_tensor(out=T0, in0=NRr, in1=are[:], op=ALU.mult), [tabB, inB], [tabB])
            dv(lambda e: e.tensor_tensor(out=T1, in0=NI, in1=aim[:], op=ALU.mult), [tabB, inB], [tabB])
            dv(lambda e: e.tensor_tensor(out=T0, in0=T0, in1=T1, op=ALU.add), [tabB], [tabB])
            dv(lambda e: e.tensor_tensor(out=ZR, in0=T0, in1=DEN, op=ALU.mult), [tabB], [tabB])
            dv(lambda e: e.tensor_tensor(out=T0, in0=NI, in1=are[:], op=ALU.mult), [tabB, inB], [tabB])
            dv(lambda e: e.tensor_tensor(out=T1, in0=NRr, in1=aim[:], op=ALU.mult), [tabB, inB], [tabB])
            dv(lambda e: e.tensor_tensor(out=T0, in0=T0, in1=T1, op=ALU.subtract), [tabB], [tabB])
            dv(lambda e: e.tensor_tensor(out=ZI, in0=T0, in1=DEN, op=ALU.mult), [tabB], [tabB])
            dv(lambda e: e.tensor_scalar(out=ZI, in0=ZI, scalar1=SIG, scalar2=None, op0=ALU.mult), [tabB, cB], [tabB])

            def bc_h(v):
                return v.unsqueeze(2).broadcast_to([128, 16, 16])
            dv(lambda e: e.tensor_tensor(out=tb1[:], in0=Bx1[:], in1=bc_h(ZR), op=ALU.mult), [tabB, inB], [tabB])
            dv(lambda e: e.tensor_tensor(out=tb2[:], in0=Bx2[:], in1=bc_h(ZI), op=ALU.mult), [tabB, inB], [tabB])
            dv(lambda e: e.tensor_tensor(out=M1B[:], in0=tb1[:], in1=tb2[:], op=ALU.add), [tabB], [tabB])
            dv(lambda e: e.tensor_tensor(out=tb1[:], in0=Bx2[:], in1=bc_h(ZR), op=ALU.mult), [tabB, inB], [tabB])
            dv(lambda e: e.tensor_tensor(out=tb2[:], in0=Bx1[:], in1=bc_h(ZI), op=ALU.mult), [tabB, inB], [tabB])
            dv(lambda e: e.tensor_tensor(out=M2B[:], in0=tb1[:], in1=tb2[:], op=ALU.subtract), [tabB], [tabB])
            dv(lambda e: e.tensor_scalar(out=Cx1[:], in0=Cx1[:], scalar1=NSIG, scalar2=None, op0=ALU.mult), [inB, cB], [tabB, inB])
            dv(lambda e: e.tensor_scalar(out=Cx2[:], in0=Cx2[:], scalar1=NSIG, scalar2=None, op0=ALU.mult), [inB, cB], [tabB, inB])
            for s in range(2):
                L = cfg.L[s]
                NC = L // 8
                P.op("sync", lambda e, s=s, k=k: e.dma_start(out=uTt[s][:], in_=b.uT[s][k * 128:(k + 1) * 128, :]), writes=[uTB[s]], dma=True)
                for gl in range(8):
                    for (c0, c1) in splits(NC, 512):
                        N = c1 - c0
                        a = cnt["a"] % 2
                        cnt["a"] += 1
                        P.mm_group(pAB[a], [(MM(pA[a][:, 0:N], sel[:, gl * 8 + t, :], uTt[s][:, 8 * c0 + t:8 * c1:8], t == 0, t == 7), [uTB[s], cB]) for t in range(8)])
                        eng = evac_eng()
                        P.op(eng, copy_fn(eng, U[s][:, gl, c0:c1], pA[a][:, 0:N]), reads=[pAB[a]], writes=[UB[s][gl]])
            for gl in range(8):
                par = gl % 2

                def outer_mul(slot, j0, T1tab, T2tab, gd):
                    p1 = PT1[:, gd, j0:j0 + 8].unsqueeze(2).broadcast_to([128, 8, 16])
                    p2 = PT2[:, gd, j0:j0 + 8].unsqueeze(2).broadcast_to([128, 8, 16])
                    m1 = T1tab[:, gd, :].unsqueeze(1).broadcast_to([128, 8, 16])
                    m2 = T2tab[:, gd, :].unsqueeze(1).broadcast_to([128, 8, 16])
                    dv(lambda e: e.tensor_tensor(out=gtA[slot][:], in0=p1, in1=m1, op=ALU.mult), [tabB], [gtAB[slot]])
                    dv(lambda e: e.tensor_tensor(out=gtB[slot][:], in0=p2, in1=m2, op=ALU.mult), [tabB], [gtBB[slot]])

                def outer_add(slot, out3, outB):
                    dv(lambda e: e.tensor_tensor(out=out3, in0=gtA[slot][:], in1=gtB[slot][:], op=ALU.add), [gtAB[slot], gtBB[slot]], [outB])
                for d in range(2):
                    gd = d * 8 + gl
                    outer_mul(0, 8 if d == 0 else 0, M1B, M2B, gd)
                    outer_mul(1, 16 if d == 0 else 24, Cx1, Cx2, gd)
                    outer_mul(2, 0 if d == 0 else 32, Cx1, Cx2, gd)
                    outer_mul(3, 32 if d == 0 else 0, M1B, M2B, gd)
                    idb = cstf[:, 3, :].unsqueeze(1).broadcast_to([128, NR, 128])
                    psb = cstf[:, 0, :].unsqueeze(1).broadcast_to([128, NR, 128])
                    s1b = PT1[:, gd, RC0:RC0 + NR].unsqueeze(2).broadcast_to([128, NR, 128])
                    s2b = NPT2[:, gd, RC0:RC0 + NR].unsqueeze(2).broadcast_to([128, NR, 128])
                    dv(lambda e, par=par, d=d, idb=idb, s1b=s1b: e.tensor_tensor(out=rot[par][d][:], in0=idb, in1=s1b, op=ALU.mult), [tabB, cB], [matB[par]])
                    dv(lambda e, psb=psb, s2b=s2b: e.tensor_tensor(out=r2t[:], in0=psb, in1=s2b, op=ALU.mult), [tabB, cB], [r2B])
                    outer_add(0, Wxn[:], WxnB)
                    P.mm_group(pTB, [(lambda e: e.transpose(out=pT[:, 0, :], in_=Wxn[:].rearrange("p a b -> p (a b)"), identity=identb[:]), [WxnB, cB])])
                    P.op("act", copy_fn("act", WxT[par][d][:], pT[:, 0, :]), reads=[pTB], writes=[matB[par]])
                    outer_add(1, Wy[par][d][:], matB[par])
                    outer_add(2, Phi[:, d, :].rearrange("p (a b) -> p a b", a=8), PhB)
                    outer_add(3, Psi[:, d, :].rearrange("p (a b) -> p a b", a=8), PhB)
                    dv(lambda e, par=par, d=d: e.tensor_tensor(out=rot[par][d][:], in0=rot[par][d][:], in1=r2t[:], op=ALU.add), [r2B], [matB[par]])
                a = cnt["a"] % 2
                cnt["a"] += 1
                P.mm_group(pAB[a], [(MM(pA[a][:, 0:128], Psi[:, 0, :], Phi[:, 0, :], True, True), [PhB]), (MM(pA[a][:, 128:256], Psi[:, 1, :], Phi[:, 1, :], True, True), [PhB])])
                dv(lambda e, a=a: e.tensor_tensor(out=ttA[:], in0=pA[a][:, 0:128], in1=maskf, op=ALU.mult), [pAB[a], cB], [ttAB])
                dv(lambda e, a=a: e.tensor_tensor(out=ttBt[:], in0=pA[a][:, 128:256], in1=maskb, op=ALU.mult), [pAB[a], cB], [ttBB])
                dv(lambda e: e.tensor_tensor(out=ttA[:], in0=ttA[:], in1=ttBt[:], op=ALU.add), [ttBB], [ttAB])
                dv(lambda e, gl=gl, par=par: e.scalar_tensor_tensor(out=TT[par][:], in0=identf, scalar=dvec[:, gl:gl + 1], in1=ttA[:], op0=ALU.mult, op1=ALU.add), [ttAB, inB, cB], [matB[par]])

                ring = [pS[0], pS[1], pA[0], pA[1]]
                ringB = [pSB[0], pSB[1], pAB[0], pAB[1]]

                def tree_gen(s, d, gl=gl, par=par):
                    L, T = cfg.L[s], cfg.T[s]
                    NC = L // 8
                    rq = T // 512
                    radices = [8, 8] + ([rq] if rq > 1 else [])
                    nlev = len(radices)
                    n = [NC]
                    for r_ in radices:
                        n.append(n[-1] // r_)
                    assert n[-1] == 4
                    ue = [8]
                    for r_ in radices:
                        ue.append(ue[-1] * r_)
                    assert ue[-1] == T
                    fwd = d == 0
                    mB = matB[par]

                    def rotap(ex):
                        if ex == 0:
                            return identb[:]
                        return rot[par][d][:, ROTI[ex], :]
                    Xlev = [XL[s][d][i] for i in range(nlev)] + [XL[s][d][3]]
                    XBl = [XLB[s][d][i] for i in range(nlev)] + [XLB[s][d][3]]
                    Plev = [PL[s][d][i] for i in range(nlev)] + [PL[s][d][3]]
                    PBl = [PLB[s][d][i] for i in range(nlev)] + [PLB[s][d][3]]
                    Xm, XmB = XM[s][d], XMB[s][d]

                    def bank():
                        si = cnt["s"] % 4
                        cnt["s"] += 1
                        return ring[si], ringB[si]
                    for (c0, c1) in splits(NC, 512):
                        N = c1 - c0
                        bk_, bkB = bank()
                        P.mm_group(bkB, [(MM(bk_[:, 0:N], WxT[par][d][:], U[s][:, gl, c0:c1], True, True), [mB, UB[s][gl]])])
                        eng = evac_eng()
                        P.op(eng, copy_fn(eng, Xlev[0][:, c0:c1], bk_[:, 0:N]), reads=[bkB], writes=[XBl[0]])
                        yield
                    for lev, rad in enumerate(radices):
                        nn = n[lev + 1]
                        bk_, bkB = bank()
                        mms = []
                        for r in range(rad):
                            kk = (rad - 1 - r) if fwd else r
                            mms.append((MM(bk_[:, 0:nn], rotap(ue[lev] * kk), Xlev[lev][:, r:n[lev]:rad], r == 0, r == rad - 1), [mB, XBl[lev], cB]))
                        P.mm_group(bkB, mms)
                        eng = evac_eng()
                        P.op(eng, copy_fn(eng, Xlev[lev + 1][:, 0:nn], bk_[:, 0:nn]), reads=[bkB], writes=[XBl[lev + 1]])
                        yield
                    X4 = Xlev[nlev]
                    mb0 = 4 if fwd else 16
                    for i in range(1, 4):
                        mo = mb0 + (i - 1) * 4
                        if fwd:
                            dv(lambda e, i=i, mo=mo: e.tensor_tensor(out=Xm[:, i - 1, i:4], in0=X4[:, 0:4 - i], in1=mk[:, mo + i:mo + 4], op=ALU.mult), [XBl[nlev], cB], [XmB])
                            dv(lambda e, i=i, mo=mo: e.tensor_tensor(out=Xm[:, i - 1, 0:i], in0=X4[:, 4 - i:4], in1=mk[:, mo:mo + i], op=ALU.mult), [XBl[nlev], cB], [XmB])
                        else:
                            dv(lambda e, i=i, mo=mo: e.tensor_tensor(out=Xm[:, i - 1, 0:4 - i], in0=X4[:, i:4], in1=mk[:, mo:mo + 4 - i], op=ALU.mult), [XBl[nlev], cB], [XmB])
                            dv(lambda e, i=i, mo=mo: e.tensor_tensor(out=Xm[:, i - 1, 4 - i:4], in0=X4[:, 0:i], in1=mk[:, mo + 4 - i:mo + 4], op=ALU.mult), [XBl[nlev], cB], [XmB])
                    yield
                    bk_, bkB = bank()
                    P.mm_group(bkB, [(MM(bk_[:, 0:4], rotap(T * (i - 1)), Xm[:, i - 1, :], i == 1, i == 3), [mB, XmB, cB]) for i in range(1, 4)])
                    eng = evac_eng()
                    P.op(eng, copy_fn(eng, Plev[nlev][:, 0:4], bk_[:, 0:4]), reads=[bkB], writes=[PBl[nlev]])
                    yield
                    for lev in range(nlev - 1, -1, -1):
                        rad = radices[lev]
                        nn = n[lev + 1]
                        big = rad * nn > 512
                        if big:
                            banks, bB = pC, pCB
                            per = 4
                        else:
                            bk_, bkB = bank()
                            banks, bB = [bk_], [bkB]
                            per = rad
                        mms_by_bank = {}
                        for r in range(rad):
                            bk = r // per
                            o0 = (r % per) * nn
                            terms = [(rotap(ue[lev] * (r if fwd else rad - 1 - r)), Plev[lev + 1][:, 0:nn], PBl[lev + 1])]
                            rr = range(0, r) if fwd else range(r + 1, rad)
                            for r2 in rr:
                                kk = (r - 1 - r2) if fwd else (r2 - 1 - r)
                                terms.append((rotap(ue[lev] * kk), Xlev[lev][:, r2:n[lev]:rad], XBl[lev]))
                            for ti_, (lh, rh, rb) in enumerate(terms):
                                mms_by_bank.setdefault(bk, []).append((MM(banks[bk][:, o0:o0 + nn], lh, rh, ti_ == 0, ti_ == len(terms) - 1), [mB, rb, cB]))
                        for bk, mms in mms_by_bank.items():
                            P.mm_group(bB[bk], mms)
                            r0 = bk * per
                            cnt_r = min(per, rad - r0)
                            dst = sb3(Plev[lev], r0, [[1, cnt_r], [rad, nn]])
                            src = banks[bk][:, 0:cnt_r * nn].rearrange("p (r j) -> p r j", r=cnt_r)
                            eng = evac_eng()
                            P.op(eng, copy_fn(eng, dst, src), reads=[bB[bk]], writes=[PBl[lev]])
                            yield

                gens = [tree_gen(s_, d_) for s_ in range(2) for d_ in range(2)]
                while gens:
                    for g_ in list(gens):
                        try:
                            next(g_)
                        except StopIteration:
                            gens.remove(g_)
                for s in range(2):
                    L, T = cfg.L[s], cfg.T[s]
                    NC = L // 8
                    NYo = T // 8 + 1
                    terms = [(TT[par][:], lambda c0, c1, s=s, gl=gl: U[s][:, gl, c0:c1], UB[s][gl]),
                             (Wy[par][0][:].rearrange("p a b -> p (a b)"), lambda c0, c1, s=s: PL[s][0][0][:, c0:c1], PLB[s][0][0]),
                             (Wy[par][1][:].rearrange("p a b -> p (a b)"), lambda c0, c1, s=s: PL[s][1][0][:, c0:c1], PLB[s][1][0])]
                    mms = []
                    for ti_, (lh, rf, rb) in enumerate(terms):
                        mms.append((MM(pY[:, 0:1], lh, rf(NC - 1, NC), ti_ == 0, ti_ == 2), [matB[par], rb]))
                    for ti_, (lh, rf, rb) in enumerate(terms):
                        mms.append((MM(pY[:, 1:1 + NYo], lh, rf(0, NYo), ti_ == 0, ti_ == 2), [matB[par], rb]))
                    P.mm_group(pYB, mms)
                    eng = evac_eng()
                    P.op(eng, copy_fn(eng, Ys[s][:, gl, 0:NYo + 1], pY[:, 0:NYo + 1]), reads=[pYB], writes=[YsB[s][gl]])
            for s in range(2):
                T = cfg.T[s]
                NY = T // 8 + 2
                for t in range(8):
                    a = cnt["a"] % 2
                    cnt["a"] += 1
                    P.mm_group(pAB[a], [(MM(pA[a][:, 0:NY], selo[:, gl * 8 + t, :], Ys[s][:, gl, 0:NY], gl == 0, gl == 7), [YsB[s][gl], cB]) for gl in range(8)])
                    dst = sb3(yst[s], 1 + t, [[8, T // 8]])
                    P.op("act", copy_fn("act", dst, pA[a][:, 1:1 + T // 8]), reads=[pAB[a]], writes=[ystB[s]])
                    if t == 7:
                        P.op("act", copy_fn("act", yst[s][:, 0:1], pA[a][:, 0:1]), reads=[pAB[a]], writes=[ystB[s]])
                    if t == 0:
                        P.op("act", copy_fn("act", yst[s][:, T + 1:T + 2], pA[a][:, T // 8 + 1:T // 8 + 2]), reads=[pAB[a]], writes=[ystB[s]])
                P.op("sync", lambda e, s=s, k=k: e.dma_start(out=b.yT[s][k * 128:(k + 1) * 128, :], in_=yst[s][:]), reads=[ystB[s]], dma=True)
        P.flush()


def phase_H(b):
    nc, cfg, P = b.nc, b.cfg, b.P
    UT = cfg.UT
    with ExitStack() as st:
        def sb(name, shape, dt):
            return st.enter_context(nc.sbuf_tensor(f"H_{name}", list(shape), dt))

        def pst(name, shape, dt):
            return st.enter_context(nc.psum_tensor(f"H_{name}", list(shape), dt))
        yv = sb("yv", [128, UT, 512], F32)
        yB = Buf()
        gT = sb("gT", [128, UT, 512], BF16)
        gB = Buf()
        w1 = sb("w1", [128, 512], F32)
        w2 = sb("w2", [128, 512], F32)
        w1B, w2B = Buf(), Buf()
        sg = [sb(f"sg{i}", [128, 512], F32) for i in range(2)]
        sgB = [Buf() for _ in range(2)]
        wt = [sb(f"wt{i}", [128, UT, 128], BF16) for i in range(2)]
        wtB = [Buf() for _ in range(2)]
        stg = [sb(f"stg{i}", [128, 512], BF16) for i in range(2)]
        stgB = [Buf() for _ in range(2)]
        acc = [pst(f"acc{i}", [128, 512], F32) for i in range(2)]
        accB = [Buf() for _ in range(2)]
        gv = b.glu_w.rearrange("(k p) c -> p k c", p=128)
        c = 0
        for seq in range(2):
            TC = cfg.T[seq] + 2
            yTv = b.yT[seq].rearrange("(k p) c -> p k c", p=128)
            mrs = b.mraw[seq]
            for (c0, c1) in splits(TC, 512):
                N = c1 - c0
                P.op("sync", lambda e, c0=c0, c1=c1, N=N, yTv=yTv: e.dma_start(out=yv[:, :, 0:N], in_=yTv[:, :, c0:c1]), writes=[yB], dma=True)
                for k in range(UT):
                    yk = yv[:, k, 0:N]
                    P.op("dve", lambda e, yk=yk, N=N: e.tensor_tensor(out=w1[:, 0:N], in0=yk, in1=yk, op=ALU.mult), reads=[yB], writes=[w1B])
                    P.op("dve", lambda e, N=N: e.tensor_scalar(out=w1[:, 0:N], in0=w1[:, 0:N], scalar1=0.044715, scalar2=1.0, op0=ALU.mult, op1=ALU.add), reads=[w1B], writes=[w1B])
                    P.op("dve", lambda e, yk=yk, N=N: e.tensor_tensor(out=w1[:, 0:N], in0=w1[:, 0:N], in1=yk, op=ALU.mult), reads=[w1B, yB], writes=[w1B])
                    P.op("act", lambda e, N=N: e.activation(out=w2[:, 0:N], in_=w1[:, 0:N], func=AF.Sigmoid, scale=1.5957691216057308), reads=[w1B], writes=[w2B])
                    P.op("dve", lambda e, yk=yk, N=N, k=k: e.tensor_tensor(out=gT[:, k, 0:N], in0=w2[:, 0:N], in1=yk, op=ALU.mult), reads=[w2B, yB], writes=[gB])
                for j in range(UT):
                    wi = c % 2
                    c += 1
                    P.op("pool", lambda e, wi=wi, j=j: e.dma_start(out=wt[wi][:], in_=gv[:, :, j * 128:(j + 1) * 128]), writes=[wtB[wi]], dma=True)
                    P.mm_group(accB[wi], [(MM(acc[wi][:, 0:N], wt[wi][:, k, :], gT[:, k, 0:N], k == 0, k == UT - 1), [wtB[wi], gB]) for k in range(UT)])
                    P.op("act", lambda e, wi=wi, N=N: e.activation(out=sg[wi][:, 0:N], in_=acc[wi][:, 0:N], func=AF.Sigmoid), reads=[accB[wi]], writes=[sgB[wi]])
                    P.op("dve", lambda e, wi=wi, N=N, j=j: e.tensor_tensor(out=stg[wi][:, 0:N], in0=sg[wi][:, 0:N], in1=gT[:, j, 0:N], op=ALU.mult), reads=[sgB[wi], gB], writes=[stgB[wi]])
                    P.op("sync", lambda e, wi=wi, N=N, j=j, c0=c0, c1=c1, mrs=mrs: e.dma_start(out=mrs[j * 128:(j + 1) * 128, c0:c1], in_=stg[wi][:, 0:N]), reads=[stgB[wi]], dma=True)
        P.flush()


ROPE_THETA = 10000.0

def consts(cfg):
    c = {}
    c["ident"] = np.eye(128, dtype=np.float32)
    inv = 1.0 / (ROPE_THETA ** (np.arange(0, 64, 2, dtype=np.float32) / 64)).astype(np.float32)
    c["invf"] = np.concatenate([inv, inv]).reshape(64, 1).astype(np.float32)
    c["sgn"] = np.concatenate([-np.ones(32), np.ones(32)]).reshape(64, 1).astype(np.float32)
    sel = np.zeros((128, 64, 128), np.float32)
    for gl in range(8):
        for t in range(8):
            for h in range(16):
                sel[gl * 16 + h, gl * 8 + t, t * 16 + h] = 1.0
    c["sel"] = sel
    c["selo"] = np.ascontiguousarray(sel.transpose(2, 1, 0))
    cst = np.zeros((128, 8, 128), np.float32)
    for k in range(128):
        cst[k, 0, (k + 64) % 128] = 1.0
    s_idx = np.arange(128) // 16
    cst[:, 1, :] = (s_idx[None, :] >= s_idx[:, None]).astype(np.float32)
    cst[:, 2, :] = (s_idx[None, :] <= s_idx[:, None]).astype(np.float32)
    cst[:, 3, :] = np.eye(128)
    c["cst"] = cst
    sig = np.zeros((128, 4), np.float32)
    sig[:64, 0] = -1.0; sig[64:, 0] = 1.0
    sig[:, 1] = -sig[:, 0]
    c["sigv"] = sig
    return c

def host_prepare(cfg, inputs):
    f = lambda a: np.ascontiguousarray(np.asarray(a, dtype=np.float32))
    cs = consts(cfg)
    shared = dict(cs)
    shared["w_in"] = f(inputs["w_in"][0])
    shared["g_mix"] = f(inputs["norm_mix_g"][0]).reshape(1, -1)
    shared["g_ffn"] = f(inputs["norm_ffn_g"][0]).reshape(1, -1)
    shared["g_fin"] = f(inputs["norm_final_g"]).reshape(1, -1)
    shared["gq"] = f(np.asarray(inputs["q_norm_g"][0]).reshape(cfg.QT, 128).T)
    shared["gkv"] = f(np.asarray(inputs["kv_norm_g"][0]).reshape(cfg.KT, 128).T)
    gmo = np.concatenate([np.asarray(inputs["ssm_out_norm_g"][0]), np.asarray(inputs["attn_out_norm_g"][0])])
    shared["gmo"] = f(gmo.reshape(cfg.MT, 128).T)
    shared["w_q_up"] = f(inputs["w_q_up"][0])
    shared["w_kv_up"] = f(inputs["w_kv_up"][0])
    shared["w_out"] = f(inputs["w_out"][0])
    shared["w_up"] = f(inputs["w_ffn_up"][0])
    shared["w_gate"] = f(inputs["w_ffn_gate"][0])
    shared["w_down"] = f(inputs["w_ffn_down"][0])
    cw = np.asarray(inputs["ffn_conv_w"][0])
    shared["convw"] = f(cw.reshape(3, cfg.FT, 128).transpose(2, 0, 1))
    shared["convb"] = f(np.asarray(inputs["ffn_conv_b"][0]).reshape(cfg.FT, 128).T)
    shared["glu_w"] = f(inputs["ssm_glu_w"][0])
    G = cfg.G
    are = np.asarray(inputs["ssm_a_re"][0]); aim = np.asarray(inputs["ssm_a_im"][0])
    def st2(x):
        y = x.transpose(2, 0, 1).reshape(64, 2 * G)
        return f(np.concatenate([y, y], 0))
    shared["are_h"] = st2(are); shared["aim_h"] = st2(aim)
    bre = np.asarray(inputs["ssm_b_re"][0]); bim = np.asarray(inputs["ssm_b_im"][0])
    br = bre.transpose(2, 0, 1, 3).reshape(64, 2 * G, 16); bi = bim.transpose(2, 0, 1, 3).reshape(64, 2 * G, 16)
    shared["bx1_h"] = f(np.concatenate([br, bi], 0)); shared["bx2_h"] = f(np.concatenate([bi, br], 0))
    cre = np.asarray(inputs["ssm_c_re"][0]); cim = np.asarray(inputs["ssm_c_im"][0])
    cr = cre.transpose(3, 0, 1, 2).reshape(64, 2 * G, 16); ci = cim.transpose(3, 0, 1, 2).reshape(64, 2 * G, 16)
    shared["cx1_h"] = f(np.concatenate([cr, ci], 0)); shared["cx2_h"] = f(np.concatenate([ci, cr], 0))
    dsk = np.asarray(inputs["ssm_d"][0])
    shared["dsk_h"] = f(np.tile(dsk.T, (8, 1)))
    shared["log_dt"] = f(inputs["ssm_log_dt"][0]).reshape(1, -1)
    ev = exp_vector(cfg)
    evv = np.zeros((1, 64), np.float32); evv[0, :len(ev)] = ev
    shared["evec"] = evv
    maps = []
    xp = np.asarray(inputs["x_prompt"]); xs = np.asarray(inputs["x_sample"])
    for c in range(8):
        si, q = c // 4, c % 4
        m = dict(shared)
        for nm, xx, s in (("p", xp, 0), ("s", xs, 1)):
            L, T = cfg.L[s], cfg.T[s]
            m["x" + nm] = f(np.roll(xx[si], -q * T, axis=0))
            m["pos" + nm] = ((np.arange(L) + q * T) % L).astype(np.float32).reshape(1, L)
        mk = np.zeros((1, 32), np.float32)
        lm = np.array([0.0 if j == (3 - q) else 1.0 for j in range(4)], np.float32)
        mk[0, 0:4] = lm
        for j in range(4):
            for i in range(1, 4):
                mk[0, 4 + (i - 1) * 4 + j] = np.prod([lm[(j - l) % 4] for l in range(1, i + 1)])
                mk[0, 16 + (i - 1) * 4 + j] = np.prod([lm[(j + l) % 4] for l in range(0, i)])
        m["masks"] = mk
        maps.append(m)
    return maps


_PHASES = "ABHCDGE"


def build_program(cfg, debug=False):
    nc = bass.Bass("TRN2", target_bir_lowering=False)
    b = declare(nc, cfg, debug=debug)
    g = globals()
    for ph in _PHASES:
        if ph == "A":
            phase_A(b, 0)
            phase_A(b, 1)
        else:
            g["phase_" + ph](b)
    return nc


def kernel(**inputs):
    cfg = FULL
    nc = build_program(cfg)
    maps = host_prepare(cfg, inputs)
    res = run_bass_kernel_spmd(nc, maps, core_ids=list(range(8)))
    yp = np.zeros((2, cfg.L[0], cfg.D), np.float32)
    ys = np.zeros((2, cfg.L[1], cfg.D), np.float32)
    for c in range(8):
        si, q = c // 4, c % 4
        r = res.results[c]
        yp[si, q * cfg.T[0]:(q + 1) * cfg.T[0]] = np.asarray(r["yp"])
        ys[si, q * cfg.T[1]:(q + 1) * cfg.T[1]] = np.asarray(r["ys"])
    return (yp, ys)
```

```python
from contextlib import ExitStack
import numpy as np
from concourse.bass_utils import run_bass_kernel_spmd
import concourse.bass as bass
import concourse.mybir as mybir

F32 = mybir.dt.float32
BF16 = mybir.dt.bfloat16
I32 = mybir.dt.int32
ALU = mybir.AluOpType
AF = mybir.ActivationFunctionType
TWO_PI = 6.283185307179586
PI = 3.141592653589793


class Buf:
    __slots__ = ("w", "r", "name")

    def __init__(self, name=""):
        self.w = None
        self.r = {}
        self.name = name


class Prog:
    NDMA = 12

    def __init__(self, nc, phase_sem):
        self.nc = nc
        self.phase_sem = phase_sem
        self.phase_idx = 0
        self.sems = {}
        self.cnt = {}
        self.dma_rr = {"sync": 0, "act": 0, "pool": 0}
        for eng in ("act", "pool", "dve", "pe"):
            self._sem(eng)
        for qn in ("sync", "act", "pool"):
            for i in range(self.NDMA):
                self._sem(f"d{qn}{i}")
        self._reset()

    def _reset(self):
        self.q = {k: [] for k in ("sync", "act", "pool", "dve", "pe")}

    def _sem(self, name):
        if name not in self.sems:
            self.sems[name] = self.nc.alloc_semaphore(f"s_{name}")
            self.cnt[name] = 0
        return self.sems[name]

    def op(self, eng, fn, reads=(), writes=(), waits=(), dma=False, track=True):
        w = {}

        def add(ev):
            if ev is None:
                return
            s, c = ev
            if w.get(s, 0) < c:
                w[s] = c
        for ev in waits:
            add(ev)
        for b in reads:
            add(b.w)
        for b in writes:
            add(b.w)
            for s, c in b.r.items():
                add((s, c))
        ev = None
        inc = 0
        sname = None
        if track:
            if dma:
                i = self.dma_rr[eng]
                self.dma_rr[eng] = (i + 1) % self.NDMA
                sname = f"d{eng}{i}"
                if self.cnt[sname] > 0:
                    add((sname, self.cnt[sname]))
                inc = 16
            else:
                sname = eng
                inc = 1
            self.cnt[sname] += inc
            ev = (sname, self.cnt[sname])
        self.q[eng].append((fn, tuple(w.items()), sname, inc))
        if ev is not None:
            for b in reads:
                if b.r.get(ev[0], 0) < ev[1]:
                    b.r[ev[0]] = ev[1]
            for b in writes:
                b.w = ev
                b.r = {}
        return ev

    def mm_group(self, psbuf, mms, extra_reads=()):
        n = len(mms)
        allreads = []
        for i, (fn, reads) in enumerate(mms):
            allreads.extend(reads)
            last = i == n - 1
            w = {}

            def add(ev, w=w):
                if ev is None:
                    return
                s, c = ev
                if w.get(s, 0) < c:
                    w[s] = c
            for b in reads:
                add(b.w)
            if i == 0:
                add(psbuf.w)
                for s, c in psbuf.r.items():
                    add((s, c))
            if last:
                self.cnt["pe"] += 1
                ev = ("pe", self.cnt["pe"])
                self.q["pe"].append((fn, tuple(w.items()), "pe", 1))
            else:
                self.q["pe"].append((fn, tuple(w.items()), None, 0))
        for b in allreads:
            if b.r.get("pe", 0) < ev[1]:
                b.r["pe"] = ev[1]
        psbuf.w = ev
        psbuf.r = {}
        return ev

    def mm(self, fn, reads=(), ps=None, first=False, last=False):
        w = {}

        def add(ev):
            if ev is None:
                return
            s_, c = ev
            if w.get(s_, 0) < c:
                w[s_] = c
        for b_ in reads:
            add(b_.w)
        if first and ps is not None:
            add(ps.w)
            for s_, c in ps.r.items():
                add((s_, c))
        if last:
            self.cnt["pe"] += 1
            ev = ("pe", self.cnt["pe"])
            self.q["pe"].append((fn, tuple(w.items()), "pe", 1))
            for b_ in reads:
                if b_.r.get("pe", 0) < ev[1]:
                    b_.r["pe"] = ev[1]
            ps.w = ev
            ps.r = {}
            return ev
        self.cnt["pe"] += 1
        ev = ("pe", self.cnt["pe"])
        self.q["pe"].append((fn, tuple(w.items()), "pe", 1))
        for b_ in reads:
            if b_.r.get("pe", 0) < ev[1]:
                b_.r["pe"] = ev[1]
        return None

    def flush(self):
        nc = self.nc
        pidx = self.phase_idx
        with nc.Block() as block:
            def run(engname):
                def body(e):
                    seen = {}
                    if pidx > 0:
                        e.wait_ge(self.phase_sem, pidx)
                    for fn, waits, sname, inc in self.q[engname]:
                        for (s, v) in waits:
                            if seen.get(s, 0) >= v:
                                continue
                            seen[s] = v
                            e.wait_ge(self.sems[s], v)
                        ins = fn(e)
                        if sname is not None:
                            ins.then_inc(self.sems[sname], inc)
                    if engname == "sync":
                        for s, c in self.cnt.items():
                            if c > 0 and seen.get(s, 0) < c:
                                e.wait_ge(self.sems[s], c)
                        e.nop().then_inc(self.phase_sem, 1)
                return body
            block.sync(run("sync"))
            block.scalar(run("act"))
            block.gpsimd(run("pool"))
            block.vector(run("dve"))
            block.tensor(run("pe"))
        self.phase_idx += 1
        self._reset()


def splits(n, maxw):
    k = (n + maxw - 1) // maxw
    base, rem = divmod(n, k)
    out = []
    lo = 0
    for i in range(k):
        wdt = base + (1 if i < rem else 0)
        out.append((lo, lo + wdt))
        lo += wdt
    return out


class Cfg:
    def __init__(self, D, G, H, QL, KVL, DFF, Lp, Ls):
        self.D, self.G, self.H, self.QL, self.KVL, self.DFF = D, G, H, QL, KVL, DFF
        self.SW = 16 * G
        self.AW = 128 * H
        self.KD = D // 128
        self.UT = self.SW // 128
        self.QT = QL // 128
        self.KT = KVL // 128
        self.HT = H
        self.MT = self.UT + self.HT
        self.FT = DFF // 128
        self.INC = self.SW + QL + KVL + 64
        self.L = [Lp, Ls]
        self.T = [Lp // 4, Ls // 4]
        self.QKD = 192
        self.EPS = 1e-6


FULL = Cfg(4096, 128, 16, 896, 512, 11008, 8192, 4096)


class B:
    pass


def own_ranges(L, T, lo, hi):
    out = []
    a, b = max(lo, 0), min(hi, T + 1)
    if a < b:
        out.append((a, b, a + 1))
    if lo <= L - 1 < hi:
        out.append((L - 1, L, 0))
    return out


def declare(nc, cfg, debug):
    b = B()
    b.nc, b.cfg = nc, cfg
    D = cfg.D
    kind_s = "ExternalOutput" if debug else "Internal"

    def inp(name, shape, dt=F32):
        return nc.dram_tensor(name, list(shape), dt, kind="ExternalInput").ap()

    def scr(name, shape, dt):
        return nc.dram_tensor(name, list(shape), dt, kind=kind_s).ap()
    b.x = [inp("xp", [cfg.L[0], D]), inp("xs", [cfg.L[1], D])]
    b.pos = [inp("posp", [1, cfg.L[0]]), inp("poss", [1, cfg.L[1]])]
    b.masks = inp("masks", [1, 32])
    b.w_in = inp("w_in", [D, cfg.INC])
    b.g_mix = inp("g_mix", [1, D])
    b.g_ffn = inp("g_ffn", [1, D])
    b.g_fin = inp("g_fin", [1, D])
    b.gq = inp("gq", [128, cfg.QT])
    b.gkv = inp("gkv", [128, cfg.KT])
    b.gmo = inp("gmo", [128, cfg.MT])
    b.w_q_up = inp("w_q_up", [cfg.QL, cfg.H * 192])
    b.w_kv_up = inp("w_kv_up", [cfg.KVL, cfg.H * 256])
    b.w_out = inp("w_out", [cfg.SW + cfg.AW, D])
    b.w_up = inp("w_up", [D, cfg.DFF])
    b.w_gate = inp("w_gate", [D, cfg.DFF])
    b.w_down = inp("w_down", [cfg.DFF, D])
    b.convw = inp("convw", [128, 3, cfg.FT])
    b.convb = inp("convb", [128, cfg.FT])
    b.glu_w = inp("glu_w", [cfg.SW, cfg.SW])
    b.invf = inp("invf", [64, 1])
    b.sgn = inp("sgn", [64, 1])
    b.ident_in = inp("ident", [128, 128])
    b.log_dt = inp("log_dt", [1, 2 * cfg.G])
    b.sel_in = inp("sel", [128, 64, 128])
    b.selo_in = inp("selo", [128, 64, 128])
    b.cst = inp("cst", [128, 8, 128])
    b.sig = inp("sigv", [128, 4])
    b.evec = inp("evec", [1, 64])
    b.are_h = inp("are_h", [128, 2 * cfg.G])
    b.aim_h = inp("aim_h", [128, 2 * cfg.G])
    b.bx1_h = inp("bx1_h", [128, 2 * cfg.G, 16])
    b.bx2_h = inp("bx2_h", [128, 2 * cfg.G, 16])
    b.cx1_h = inp("cx1_h", [128, 2 * cfg.G, 16])
    b.cx2_h = inp("cx2_h", [128, 2 * cfg.G, 16])
    b.dsk_h = inp("dsk_h", [128, cfg.G])
    b.y = [nc.dram_tensor("yp", [cfg.T[0], D], F32, kind="ExternalOutput").ap(),
           nc.dram_tensor("ys", [cfg.T[1], D], F32, kind="ExternalOutput").ap()]
    b.uT, b.kvn, b.kr, b.rtab, b.qn, b.mraw, b.x1, b.h2T, b.x2, b.yT = [], [], [], [], [], [], [], [], [], []
    for s in range(2):
        L, T = cfg.L[s], cfg.T[s]
        b.uT.append(scr(f"uT{s}", [cfg.SW, L], BF16))
        b.kvn.append(scr(f"kvn{s}", [cfg.KVL, L], BF16))
        b.kr.append(scr(f"kr{s}", [64, L], BF16))
        b.rtab.append(scr(f"rtab{s}", [2, 64, L], F32))
        b.qn.append(scr(f"qn{s}", [cfg.QL, T + 2], BF16))
        b.mraw.append(scr(f"mraw{s}", [cfg.SW + cfg.AW, T + 2], BF16))
        b.x1.append(scr(f"x1{s}", [T + 2, D], F32))
        b.h2T.append(scr(f"h2T{s}", [D, T + 2], BF16))
        b.x2.append(scr(f"x2{s}", [T, D], F32))
        b.yT.append(scr(f"yT{s}", [cfg.SW, T + 2], F32))
    b.phase_sem = nc.alloc_semaphore("phase")
    b.P = Prog(nc, b.phase_sem)
    return b


def bcast_rows(ap_row, nparts, n, off=0):
    return bass.AP(ap_row.tensor, ap_row.offset + off, [[0, nparts], [1, n]])


def range_reduce(P, eng, ang, tmpf, tmpi, angB, tmpB, n):
    P.op(eng, lambda e: e.tensor_scalar(out=tmpf, in0=ang, scalar1=1.0 / TWO_PI, scalar2=None, op0=ALU.mult), reads=[angB], writes=[tmpB])
    P.op(eng, lambda e: e.tensor_copy(out=tmpi, in_=tmpf), reads=[tmpB], writes=[tmpB])
    P.op(eng, lambda e: e.tensor_copy(out=tmpf, in_=tmpi), reads=[tmpB], writes=[tmpB])
    P.op(eng, lambda e: e.scalar_tensor_tensor(out=ang, in0=tmpf, scalar=-TWO_PI, in1=ang, op0=ALU.mult, op1=ALU.add), reads=[tmpB], writes=[angB])
    P.op(eng, lambda e: e.tensor_scalar(out=tmpf, in0=ang, scalar1=PI, scalar2=TWO_PI, op0=ALU.is_gt, op1=ALU.mult), reads=[angB], writes=[tmpB])
    P.op(eng, lambda e: e.tensor_tensor(out=ang, in0=ang, in1=tmpf, op=ALU.subtract), reads=[tmpB], writes=[angB])
    P.op(eng, lambda e: e.tensor_scalar(out=tmpf, in0=ang, scalar1=-PI, scalar2=TWO_PI, op0=ALU.is_lt, op1=ALU.mult), reads=[angB], writes=[tmpB])
    P.op(eng, lambda e: e.tensor_tensor(out=ang, in0=ang, in1=tmpf, op=ALU.add), reads=[tmpB], writes=[angB])


def phase_A(b, seq):
    nc, cfg, P = b.nc, b.cfg, b.P
    L, T, D, KD = cfg.L[seq], cfg.T[seq], cfg.D, cfg.KD
    GS = 512
    NG = L // GS
    x = b.x[seq]
    o1, o2, o3 = cfg.SW, cfg.SW + cfg.QL, cfg.SW + cfg.QL + cfg.KVL
    NLAT = max(cfg.QT, cfg.KT)
    with ExitStack() as st:
        def sb(name, shape, dt):
            return st.enter_context(nc.sbuf_tensor(f"A{seq}_{name}", list(shape), dt))

        def pst(name, shape, dt):
            return st.enter_context(nc.psum_tensor(f"A{seq}_{name}", list(shape), dt))
        xt = [sb(f"xt{i}", [128, D], F32) for i in range(2)]
        xtB = [Buf() for _ in range(2)]
        xn = [sb(f"xn{i}", [128, D], BF16) for i in range(2)]
        xnB = [Buf() for _ in range(2)]
        ss = sb("ss", [128, 8], F32)
        ssB = Buf()
        hT = sb("hT", [128, KD, GS], BF16)
        hTB = [Buf() for _ in range(4)]
        gbc = sb("gbc", [128, D], F32)
        gbcB = Buf()
        wt = [sb(f"wt{i}", [128, KD, 128], BF16) for i in range(3)]
        wtB = [Buf() for _ in range(3)]
        wtr = sb("wtr", [128, KD, 64], BF16)
        wtrs = sb("wtrs", [128, KD, 64], BF16)
        wtrB = Buf()
        ident = sb("ident", [128, 128], BF16)
        ones = sb("ones", [128, 128], BF16)
        cB = Buf()
        gq = sb("gq", [128, cfg.QT], F32)
        gkv = sb("gkv", [128, cfg.KT], F32)
        invf = sb("invf", [64, 1], F32)
        sgn = sb("sgn", [64, 1], F32)
        stg = [sb(f"stg{i}", [128, GS], BF16) for i in range(4)]
        stgB = [Buf() for _ in range(4)]
        ltmp = sb("ltmp", [128, NLAT, GS], F32)
        ltmpB = [Buf() for _ in range(NLAT)]
        lsq = sb("lsq", [128, NLAT, GS], BF16)
        lsqB = [Buf() for _ in range(NLAT)]
        rk = sb("rk", [128, GS], F32)
        rkB = Buf()
        posb = sb("posb", [64, GS], F32)
        posB = Buf()
        ang = sb("ang", [64, GS], F32)
        angB = Buf()
        rtf = sb("rtf", [64, GS], F32)
        rti = sb("rti", [64, GS], I32)
        rtB = Buf()
        cosT = sb("cosT", [64, GS], F32)
        sinT = sb("sinT", [64, GS], F32)
        cosB, sinB = Buf(), Buf()
        t1 = sb("t1", [64, GS], F32)
        t2 = sb("t2", [64, GS], F32)
        t1B, t2B = Buf(), Buf()
        tp = [pst(f"tp{i}", [128, 8, 128], BF16) for i in range(2)]
        tpB = [Buf() for _ in range(2)]
        acc = [pst(f"acc{i}", [128, GS], F32) for i in range(3)]
        accB = [Buf() for _ in range(3)]
        ssp = pst("ssp", [128, GS], F32)
        sspB = Buf()

        P.op("pool", lambda e: e.dma_start(out=ident[:], in_=b.ident_in), writes=[cB], dma=True)
        P.op("pool", lambda e: e.memset(ones[:], 1.0), writes=[cB])
        P.op("sync", lambda e: e.dma_start(out=gbc[:], in_=bcast_rows(b.g_mix, 128, D)), writes=[gbcB], dma=True)
        P.op("sync", lambda e: e.dma_start(out=gq[:], in_=b.gq), writes=[cB], dma=True)
        P.op("sync", lambda e: e.dma_start(out=gkv[:], in_=b.gkv), writes=[cB], dma=True)
        P.op("sync", lambda e: e.dma_start(out=invf[:], in_=b.invf), writes=[cB], dma=True)
        P.op("sync", lambda e: e.dma_start(out=sgn[:], in_=b.sgn), writes=[cB], dma=True)
        wv = b.w_in.rearrange("(k p) c -> p k c", p=128)
        P.op("pool", lambda e: e.dma_start(out=wtr[:], in_=wv[:, :, o3:o3 + 64]), writes=[wtrB], dma=True)
        P.op("pool", lambda e: e.dma_start(out=wtrs[:, :, 0:32], in_=wv[:, :, o3 + 32:o3 + 64]), writes=[wtrB], dma=True)
        P.op("pool", lambda e: e.dma_start(out=wtrs[:, :, 32:64], in_=wv[:, :, o3:o3 + 32]), writes=[wtrB], dma=True)

        cnt = {"tile": 0, "wt": 0, "acc": 0, "stg": 0, "ev": 0}

        def evac_eng():
            cnt["ev"] += 1
            return "act" if cnt["ev"] % 2 else "dve"

        def copy_op(eng, out, in_):
            if eng == "act":
                return lambda e: e.activation(out=out, in_=in_, func=AF.Copy)
            return lambda e: e.tensor_copy(out=out, in_=in_)

        def load_w(c0, M):
            i = cnt["wt"] % 3
            cnt["wt"] += 1
            P.op("pool", lambda e: e.dma_start(out=wt[i][:, :, 0:M], in_=wv[:, :, c0:c0 + M]), writes=[wtB[i]], dma=True)
            return i

        def proj(wtile, wB, M, lo, hi):
            a = cnt["acc"] % 3
            cnt["acc"] += 1
            N = hi - lo
            mms = []
            for k in range(KD):
                mms.append((lambda e, k=k: e.matmul(acc[a][0:M, 0:N], lhsT=wtile[:, k, 0:M], rhs=hT[:, k, lo:hi],
                                                    start=(k == 0), stop=(k == KD - 1)), [wB] + hTB if k == 0 else []))
            P.mm_group(accB[a], mms)
            return a

        def store(dst, src_fn_eng, N, M=128):
            i = cnt["stg"] % 4
            cnt["stg"] += 1
            src_fn_eng(stg[i][0:M, 0:N], stgB[i])
            P.op("sync", lambda e: e.dma_start(out=dst, in_=stg[i][0:M, 0:N], allow_slow_non_contiguous=(N < 16)), reads=[stgB[i]], dma=True)

        def latent(kind, ntile, c_base, gain, nfeat, lo, hi, dst, dcol):
            N = hi - lo
            for t in range(ntile):
                wi = load_w(c_base + t * 128, 128)
                a = proj(wt[wi], wtB[wi], 128, lo, hi)
                P.op("act", lambda e, a=a, t=t: e.activation(out=ltmp[:, t, 0:N], in_=acc[a][:, 0:N], func=AF.Copy), reads=[accB[a]], writes=[ltmpB[t]])
                P.op("act", lambda e, a=a, t=t: e.activation(out=lsq[:, t, 0:N], in_=acc[a][:, 0:N], func=AF.Square), reads=[accB[a]], writes=[lsqB[t]])
            P.mm_group(sspB, [(lambda e, t=t: e.matmul(ssp[:, 0:N], lhsT=ones[:], rhs=lsq[:, t, 0:N], start=(t == 0), stop=(t == ntile - 1)), [lsqB[t], cB]) for t in range(ntile)])
            P.op("act", lambda e: e.activation(out=rk[:, 0:N], in_=ssp[:, 0:N], func=AF.Ln, scale=1.0 / nfeat, bias=cfg.EPS), reads=[sspB], writes=[rkB])
            P.op("act", lambda e: e.activation(out=rk[:, 0:N], in_=rk[:, 0:N], func=AF.Exp, scale=-0.5), reads=[rkB], writes=[rkB])
            for t in range(ntile):
                def prod(sap, sB, t=t):
                    P.op("dve", lambda e: e.scalar_tensor_tensor(out=sap, in0=ltmp[:, t, 0:N], scalar=gain[:, t:t + 1], in1=rk[:, 0:N], op0=ALU.mult, op1=ALU.mult), reads=[ltmpB[t], rkB, cB], writes=[sB])
                store(dst[t * 128:(t + 1) * 128, dcol:dcol + N], prod, N)

        for g in range(NG):
            for tt in range(4):
                ti = cnt["tile"]
                cnt["tile"] += 1
                bi = ti % 2
                r0 = g * GS + tt * 128
                P.op("sync", lambda e, bi=bi, r0=r0: e.dma_start(out=xt[bi][:], in_=x[r0:r0 + 128, :]), writes=[xtB[bi]], dma=True)
                P.op("act", lambda e, bi=bi: e.activation(out=xn[bi][:], in_=xt[bi][:], func=AF.Square, accum_out=ss[:, 0:1]), reads=[xtB[bi]], writes=[xnB[bi], ssB])
                P.op("act", lambda e: e.activation(out=ss[:, 1:2], in_=ss[:, 0:1], func=AF.Ln, scale=1.0 / D, bias=cfg.EPS), reads=[ssB], writes=[ssB])
                P.op("act", lambda e: e.activation(out=ss[:, 2:3], in_=ss[:, 1:2], func=AF.Exp, scale=-0.5), reads=[ssB], writes=[ssB])
                P.op("dve", lambda e, bi=bi: e.scalar_tensor_tensor(out=xn[bi][:], in0=xt[bi][:], scalar=ss[:, 2:3], in1=gbc[:], op0=ALU.mult, op1=ALU.mult), reads=[xtB[bi], ssB, gbcB], writes=[xnB[bi]])
                TB = min(8, KD)
                for j in range(KD // TB):
                    pb = (ti * (KD // TB) + j) % 2
                    P.mm_group(tpB[pb], [(lambda e, pb=pb, bi=bi, k=k: e.transpose(out=tp[pb][:, k % TB, :], in_=xn[bi][:, k * 128:(k + 1) * 128], identity=ident[:]), [xnB[bi], cB]) for k in range(j * TB, j * TB + TB)])
                    eng = "act"
                    P.op(eng, copy_op(eng, hT[:, j * TB:j * TB + TB, tt * 128:(tt + 1) * 128], tp[pb][:, 0:TB, :]), reads=[tpB[pb]], writes=[hTB[tt]])
            for j in range(cfg.UT):
                wi = load_w(j * 128, 128)
                a = proj(wt[wi], wtB[wi], 128, 0, GS)
                eng = evac_eng()

                def prod(sap, sB, a=a, eng=eng):
                    P.op(eng, copy_op(eng, sap, acc[a][:, 0:GS]), reads=[accB[a]], writes=[sB])
                store(b.uT[seq][j * 128:(j + 1) * 128, g * GS:(g + 1) * GS], prod, GS)
            latent("kv", cfg.KT, o2, gkv, cfg.KVL, 0, GS, b.kvn[seq], g * GS)
            P.op("sync", lambda e, g=g: e.dma_start(out=posb[:], in_=bcast_rows(b.pos[seq], 64, GS, off=g * GS)), writes=[posB], dma=True)
            for which in range(2):
                if which == 0:
                    P.op("dve", lambda e: e.tensor_scalar(out=ang[:], in0=posb[:], scalar1=invf[:, 0:1], scalar2=None, op0=ALU.mult), reads=[posB, cB], writes=[angB])
                else:
                    P.op("dve", lambda e: e.tensor_scalar(out=ang[:], in0=posb[:], scalar1=invf[:, 0:1], scalar2=PI / 2, op0=ALU.mult, op1=ALU.add), reads=[posB, cB], writes=[angB])
                range_reduce(P, "dve", ang[:], rtf[:], rti[:], angB, rtB, GS)
                if which == 0:
                    P.op("act", lambda e: e.activation(out=sinT[:], in_=ang[:], func=AF.Sin), reads=[angB], writes=[sinB])
                    P.op("dve", lambda e: e.tensor_scalar(out=sinT[:], in0=sinT[:], scalar1=sgn[:, 0:1], scalar2=None, op0=ALU.mult), reads=[sinB, cB], writes=[sinB])
                else:
                    P.op("act", lambda e: e.activation(out=cosT[:], in_=ang[:], func=AF.Sin), reads=[angB], writes=[cosB])
            P.op("sync", lambda e, g=g: e.dma_start(out=b.rtab[seq][0, :, g * GS:(g + 1) * GS], in_=cosT[:]), reads=[cosB], dma=True)
            P.op("sync", lambda e, g=g: e.dma_start(out=b.rtab[seq][1, :, g * GS:(g + 1) * GS], in_=sinT[:]), reads=[sinB], dma=True)
            a1 = proj(wtr, wtrB, 64, 0, GS)
            a2 = proj(wtrs, wtrB, 64, 0, GS)
            P.op("dve", lambda e, a1=a1: e.tensor_tensor(out=t1[:], in0=acc[a1][0:64, :], in1=cosT[:], op=ALU.mult), reads=[accB[a1], cosB], writes=[t1B])
            P.op("dve", lambda e, a2=a2: e.tensor_tensor(out=t2[:], in0=acc[a2][0:64, :], in1=sinT[:], op=ALU.mult), reads=[accB[a2], sinB], writes=[t2B])

            def prod(sap, sB):
                P.op("dve", lambda e: e.tensor_tensor(out=sap, in0=t1[:], in1=t2[:], op=ALU.add), reads=[t1B, t2B], writes=[sB])
            store(b.kr[seq][:, g * GS:(g + 1) * GS], prod, GS, M=64)
            for (tlo, thi, clo) in own_ranges(L, T, g * GS, (g + 1) * GS):
                latent("q", cfg.QT, o1, gq, cfg.QL, tlo - g * GS, thi - g * GS, b.qn[seq], clo)
        P.flush()


def copy_fn(eng, out, in_, scale=None):
    if eng == "act":
        if scale is None:
            return lambda e: e.activation(out=out, in_=in_, func=AF.Copy)
        return lambda e: e.activation(out=out, in_=in_, func=AF.Copy, scale=scale)
    if scale is None:
        return lambda e: e.tensor_copy(out=out, in_=in_)
    return lambda e: e.tensor_scalar(out=out, in0=in_, scalar1=scale, scalar2=None, op0=ALU.mult)


def phase_C(b):
    for seq in range(2):
        attn_seq(b, seq)


def attn_seq(b, seq):
    nc, cfg, P = b.nc, b.cfg, b.P
    L, T = cfg.L[seq], cfg.T[seq]
    TC = T + 2
    KT, QT, H = cfg.KT, cfg.QT, cfg.H
    NKT, NKB = L // 128, L // 512
    qtiles = splits(TC, 512)
    SCALE = 192.0 ** -0.5
    with ExitStack() as st:
        def sb(name, shape, dt):
            return st.enter_context(nc.sbuf_tensor(f"C{seq}_{name}", list(shape), dt))

        def pst(name, shape, dt):
            return st.enter_context(nc.psum_tensor(f"C{seq}_{name}", list(shape), dt))
        kvnT = sb("kvnT", [128, KT, L], BF16)
        krT = sb("krT", [64, L], BF16)
        qnT = sb("qnT", [128, QT, TC], BF16)
        cosq = sb("cosq", [64, TC], BF16)
        sinq = sb("sinq", [64, TC], BF16)
        inB = Buf()
        Kh = sb("Kh", [128, L], BF16)
        Vh = sb("Vh", [128, L], BF16)
        KhB = [Buf() for _ in range(NKB)]
        VhB = [Buf() for _ in range(NKB)]
        qhn = sb("qhn", [128, TC], BF16)
        qhr = sb("qhr", [64, TC], BF16)
        qhB = [Buf() for _ in qtiles]
        wkv = [sb(f"wkv{i}", [128, KT, 256], BF16) for i in range(2)]
        wq = [sb(f"wq{i}", [128, QT, 192], BF16) for i in range(2)]
        wqs = [sb(f"wqs{i}", [128, QT, 64], BF16) for i in range(2)]
        wB = [Buf() for _ in range(2)]
        PT = [sb(f"PT{i}", [128, 512], BF16) for i in range(3)]
        PTB = [Buf() for _ in range(3)]
        rden = sb("rden", [128, 512], F32)
        rdB = Buf()
        ostg = [sb(f"ostg{i}", [128, 512], BF16) for i in range(2)]
        ostgB = [Buf() for _ in range(2)]
        t1 = sb("t1", [64, 512], F32)
        t2 = sb("t2", [64, 512], F32)
        t1B, t2B = Buf(), Buf()
        ones = sb("ones", [128, 128], BF16)
        cB = Buf()
        S = [pst(f"S{i}", [128, 512], F32) for i in range(2)]
        SB = [Buf() for _ in range(2)]
        O = [pst(f"O{i}", [128, 512], F32) for i in range(2)]
        OB = [Buf() for _ in range(2)]
        DEN = [pst(f"DEN{i}", [128, 512], F32) for i in range(2)]
        DENB = [Buf() for _ in range(2)]
        acc = [pst(f"acc{i}", [128, 512], F32) for i in range(2)]
        accB = [Buf() for _ in range(2)]
        cnt = {"acc": 0, "ev": 0, "s": 0, "pt": 0, "o": 0, "st": 0}

        def evac_eng():
            cnt["ev"] += 1
            return "act" if cnt["ev"] % 2 else "dve"

        P.op("pool", lambda e: e.memset(ones[:], 1.0), writes=[cB])
        P.op("sync", lambda e: e.dma_start(out=kvnT[:], in_=b.kvn[seq].rearrange("(k p) l -> p k l", p=128)), writes=[inB], dma=True)
        P.op("sync", lambda e: e.dma_start(out=krT[:], in_=b.kr[seq]), writes=[inB], dma=True)
        P.op("sync", lambda e: e.dma_start(out=qnT[:], in_=b.qn[seq].rearrange("(k p) c -> p k c", p=128)), writes=[inB], dma=True)
        for tab, dst in ((0, cosq), (1, sinq)):
            P.op("pool", lambda e, tab=tab, dst=dst: e.dma_start(out=dst[:, 0:1], in_=b.rtab[seq][tab, :, L - 1:L], allow_slow_non_contiguous=True), writes=[inB], dma=True)
            P.op("pool", lambda e, tab=tab, dst=dst: e.dma_start(out=dst[:, 1:TC], in_=b.rtab[seq][tab, :, 0:T + 1]), writes=[inB], dma=True)
        wkvv = b.w_kv_up.rearrange("(k p) c -> p k c", p=128)
        wqv = b.w_q_up.rearrange("(k p) c -> p k c", p=128)
        for h in range(H):
            wi = h % 2
            P.op("pool", lambda e, wi=wi, h=h: e.dma_start(out=wkv[wi][:], in_=wkvv[:, :, h * 256:(h + 1) * 256]), writes=[wB[wi]], dma=True)
            P.op("pool", lambda e, wi=wi, h=h: e.dma_start(out=wq[wi][:], in_=wqv[:, :, h * 192:(h + 1) * 192]), writes=[wB[wi]], dma=True)
            P.op("pool", lambda e, wi=wi, h=h: e.dma_start(out=wqs[wi][:, :, 0:32], in_=wqv[:, :, h * 192 + 160:h * 192 + 192]), writes=[wB[wi]], dma=True)
            P.op("pool", lambda e, wi=wi, h=h: e.dma_start(out=wqs[wi][:, :, 32:64], in_=wqv[:, :, h * 192 + 128:h * 192 + 160]), writes=[wB[wi]], dma=True)
            for kb in range(NKB):
                a = cnt["acc"] % 2
                cnt["acc"] += 1
                P.mm_group(accB[a], [(lambda e, a=a, kt=kt, kb=kb, wi=wi: e.matmul(acc[a][:, 0:512], lhsT=wkv[wi][:, kt, 0:128], rhs=kvnT[:, kt, kb * 512:(kb + 1) * 512], start=(kt == 0), stop=(kt == KT - 1)), [wB[wi], inB]) for kt in range(KT)])
                eng = evac_eng()
                P.op(eng, copy_fn(eng, Kh[:, kb * 512:(kb + 1) * 512], acc[a][:, 0:512]), reads=[accB[a]], writes=[KhB[kb]])
                a = cnt["acc"] % 2
                cnt["acc"] += 1
                mms = []
                for i in range(4):
                    for kt in range(KT):
                        k0 = kb * 512 + i * 128
                        mms.append((lambda e, a=a, kt=kt, i=i, k0=k0, wi=wi: e.matmul(acc[a][:, i * 128:(i + 1) * 128], lhsT=kvnT[:, kt, k0:k0 + 128], rhs=wkv[wi][:, kt, 128:256], start=(kt == 0), stop=(kt == KT - 1)), [wB[wi], inB]))
                P.mm_group(accB[a], mms)
                eng = evac_eng()
                P.op(eng, copy_fn(eng, Vh[:, kb * 512:(kb + 1) * 512], acc[a][:, 0:512]), reads=[accB[a]], writes=[VhB[kb]])
            for qi, (c0, c1) in enumerate(qtiles):
                N = c1 - c0
                a = cnt["acc"] % 2
                cnt["acc"] += 1
                P.mm_group(accB[a], [(lambda e, a=a, qt=qt, wi=wi, c0=c0, c1=c1, N=N: e.matmul(acc[a][:, 0:N], lhsT=wq[wi][:, qt, 0:128], rhs=qnT[:, qt, c0:c1], start=(qt == 0), stop=(qt == QT - 1)), [wB[wi], inB]) for qt in range(QT)])
                P.op("act", copy_fn("act", qhn[:, c0:c1], acc[a][:, 0:N], scale=SCALE), reads=[accB[a]], writes=[qhB[qi]])
                a1 = cnt["acc"] % 2
                cnt["acc"] += 1
                P.mm_group(accB[a1], [(lambda e, a1=a1, qt=qt, wi=wi, c0=c0, c1=c1, N=N: e.matmul(acc[a1][0:64, 0:N], lhsT=wq[wi][:, qt, 128:192], rhs=qnT[:, qt, c0:c1], start=(qt == 0), stop=(qt == QT - 1)), [wB[wi], inB]) for qt in range(QT)])
                P.op("dve", lambda e, a1=a1, c0=c0, c1=c1, N=N: e.scalar_tensor_tensor(out=t1[:, 0:N], in0=acc[a1][0:64, 0:N], scalar=SCALE, in1=cosq[:, c0:c1], op0=ALU.mult, op1=ALU.mult), reads=[accB[a1], inB], writes=[t1B])
                a2 = cnt["acc"] % 2
                cnt["acc"] += 1
                P.mm_group(accB[a2], [(lambda e, a2=a2, qt=qt, wi=wi, c0=c0, c1=c1, N=N: e.matmul(acc[a2][0:64, 0:N], lhsT=wqs[wi][:, qt, 0:64], rhs=qnT[:, qt, c0:c1], start=(qt == 0), stop=(qt == QT - 1)), [wB[wi], inB]) for qt in range(QT)])
                P.op("dve", lambda e, a2=a2, c0=c0, c1=c1, N=N: e.scalar_tensor_tensor(out=t2[:, 0:N], in0=acc[a2][0:64, 0:N], scalar=SCALE, in1=sinq[:, c0:c1], op0=ALU.mult, op1=ALU.mult), reads=[accB[a2], inB], writes=[t2B])
                P.op("dve", lambda e, c0=c0, c1=c1, N=N: e.tensor_tensor(out=qhr[:, c0:c1], in0=t1[:, 0:N], in1=t2[:, 0:N], op=ALU.add), reads=[t1B, t2B], writes=[qhB[qi]])
            for qi, (c0, c1) in enumerate(qtiles):
                N = c1 - c0
                ob = cnt["o"] % 2
                cnt["o"] += 1
                sis = [(cnt["s"] + kt) % 2 for kt in range(NKT)]
                pis = [(cnt["pt"] + kt) % 3 for kt in range(NKT)]
                cnt["s"] += NKT
                cnt["pt"] += NKT

                def emit_S(kt, N=N, c0=c0, c1=c1, qi=qi, sis=sis):
                    si, k0, kb = sis[kt], kt * 128, kt // 4
                    P.mm_group(SB[si], [
                        (lambda e, si=si, k0=k0: e.matmul(S[si][:, 0:N], lhsT=Kh[:, k0:k0 + 128], rhs=qhn[:, c0:c1], start=True, stop=False), [KhB[kb], qhB[qi]]),
                        (lambda e, si=si, k0=k0: e.matmul(S[si][:, 0:N], lhsT=krT[:, k0:k0 + 128], rhs=qhr[:, c0:c1], start=False, stop=True), [inB]),
                    ])
                emit_S(0)
                for kt in range(NKT):
                    if kt + 1 < NKT:
                        emit_S(kt + 1)
                    si, pi, k0, kb = sis[kt], pis[kt], kt * 128, kt // 4
                    P.op("act", lambda e, si=si, pi=pi, N=N: e.activation(out=PT[pi][:, 0:N], in_=S[si][:, 0:N], func=AF.Exp), reads=[SB[si]], writes=[PTB[pi]])
                    first, last = kt == 0, kt == NKT - 1
                    P.mm(lambda e, pi=pi, k0=k0, first=first, last=last, N=N, ob=ob: e.matmul(O[ob][:, 0:N], lhsT=Vh[:, k0:k0 + 128], rhs=PT[pi][:, 0:N], start=first, stop=last), reads=[VhB[kb], PTB[pi]], ps=OB[ob], first=first, last=last)
                    P.mm(lambda e, pi=pi, first=first, last=last, N=N, ob=ob: e.matmul(DEN[ob][:, 0:N], lhsT=ones[:], rhs=PT[pi][:, 0:N], start=first, stop=last), reads=[cB, PTB[pi]], ps=DENB[ob], first=first, last=last)
                P.op("dve", lambda e, ob=ob, N=N: e.reciprocal(out=rden[:, 0:N], in_=DEN[ob][:, 0:N]), reads=[DENB[ob]], writes=[rdB])
                oi = cnt["st"] % 2
                cnt["st"] += 1
                P.op("dve", lambda e, ob=ob, oi=oi, N=N: e.tensor_tensor(out=ostg[oi][:, 0:N], in0=O[ob][:, 0:N], in1=rden[:, 0:N], op=ALU.mult), reads=[OB[ob], rdB], writes=[ostgB[oi]])
                r0 = cfg.SW + h * 128
                P.op("sync", lambda e, oi=oi, r0=r0, c0=c0, c1=c1, N=N: e.dma_start(out=b.mraw[seq][r0:r0 + 128, c0:c1], in_=ostg[oi][:, 0:N]), reads=[ostgB[oi]], dma=True)
        P.flush()


def phase_D(b):
    nc, cfg, P = b.nc, b.cfg, b.P
    D, MT, UT = cfg.D, cfg.MT, cfg.UT
    NCG = D // 512
    for seq in range(2):
        L, T = cfg.L[seq], cfg.T[seq]
        TC = T + 2
        for gi, (g0, g1) in enumerate(splits(TC, 1026)):
            n = g1 - g0
            with ExitStack() as st:
                def sb(name, shape, dt):
                    return st.enter_context(nc.sbuf_tensor(f"D{seq}{gi}_{name}", list(shape), dt))

                def pst(name, shape, dt):
                    return st.enter_context(nc.psum_tensor(f"D{seq}{gi}_{name}", list(shape), dt))
                mr = sb("mr", [128, MT, n], BF16)
                mrB = [Buf() for _ in range(MT)]
                sq = [sb(f"sq{i}", [128, 512], BF16) for i in range(2)]
                sqB = [Buf() for _ in range(2)]
                rs = sb("rs", [128, 2, n], F32)
                rsB = [Buf() for _ in range(2)]
                wo = [sb(f"wo{i}", [128, MT, 512], BF16) for i in range(2)]
                woB = [Buf() for _ in range(2)]
                xr = [sb(f"xr{i}", [128, 512], F32) for i in range(2)]
                xrB = [Buf() for _ in range(2)]
                xo = [sb(f"xo{i}", [128, 512], F32) for i in range(2)]
                xoB = [Buf() for _ in range(2)]
                gmo = sb("gmo", [128, MT], F32)
                ones = sb("ones", [128, 128], BF16)
                cB = Buf()
                ssp = pst("ssp", [128, 512], F32)
                sspB = Buf()
                acc = [pst(f"acc{i}", [128, 512], F32) for i in range(3)]
                accB = [Buf() for _ in range(3)]
                cnt = {"sq": 0, "acc": 0, "x": 0}
                P.op("pool", lambda e: e.memset(ones[:], 1.0), writes=[cB])
                P.op("sync", lambda e: e.dma_start(out=gmo[:], in_=b.gmo), writes=[cB], dma=True)
                mv = b.mraw[seq].rearrange("(k p) c -> p k c", p=128)
                for k in range(MT):
                    P.op("sync", lambda e, k=k: e.dma_start(out=mr[:, k, :], in_=mv[:, k, g0:g1]), writes=[mrB[k]], dma=True)
                for half, (k0, k1) in enumerate(((0, UT), (UT, MT))):
                    width = (k1 - k0) * 128
                    for (c0, c1) in splits(n, 512):
                        N = c1 - c0
                        for k in range(k0, k1):
                            i = cnt["sq"] % 2
                            cnt["sq"] += 1
                            P.op("act", lambda e, i=i, k=k, c0=c0, c1=c1, N=N: e.activation(out=sq[i][:, 0:N], in_=mr[:, k, c0:c1], func=AF.Square), reads=[mrB[k]], writes=[sqB[i]])
                            P.mm(lambda e, i=i, N=N, k=k, k0=k0, k1=k1: e.matmul(ssp[:, 0:N], lhsT=ones[:], rhs=sq[i][:, 0:N], start=(k == k0), stop=(k == k1 - 1)), reads=[sqB[i], cB], ps=sspB, first=(k == k0), last=(k == k1 - 1))
                        P.op("act", lambda e, half=half, c0=c0, c1=c1, N=N, width=width: e.activation(out=rs[:, half, c0:c1], in_=ssp[:, 0:N], func=AF.Ln, scale=1.0 / width, bias=cfg.EPS), reads=[sspB], writes=[rsB[half]])
                        P.op("act", lambda e, half=half, c0=c0, c1=c1: e.activation(out=rs[:, half, c0:c1], in_=rs[:, half, c0:c1], func=AF.Exp, scale=-0.5), reads=[rsB[half]], writes=[rsB[half]])
                for k in range(MT):
                    half = 0 if k < UT else 1
                    P.op("dve", lambda e, k=k, half=half: e.scalar_tensor_tensor(out=mr[:, k, :], in0=mr[:, k, :], scalar=gmo[:, k:k + 1], in1=rs[:, half, :], op0=ALU.mult, op1=ALU.mult), reads=[rsB[half], cB], writes=[mrB[k]])
                wov = b.w_out.rearrange("(k p) c -> p k c", p=128)
                for cgi in range(NCG):
                    wi = cgi % 2
                    P.op("pool", lambda e, wi=wi, cgi=cgi: e.dma_start(out=wo[wi][:], in_=wov[:, :, cgi * 512:(cgi + 1) * 512]), writes=[woB[wi]], dma=True)
                    for (t0, t1_) in splits(n, 128):
                        m = t1_ - t0
                        ca, cb = g0 + t0, g0 + t1_
                        a = cnt["acc"] % 3
                        cnt["acc"] += 1
                        P.mm_group(accB[a], [(lambda e, a=a, k=k, wi=wi, t0=t0, t1_=t1_, m=m: e.matmul(acc[a][0:m, 0:512], lhsT=mr[:, k, t0:t1_], rhs=wo[wi][:, k, :], start=(k == 0), stop=(k == MT - 1)), [mrB[k], woB[wi]]) for k in range(MT)])
                        xi = cnt["x"] % 2
                        cnt["x"] += 1
                        cs = slice(cgi * 512, (cgi + 1) * 512)
                        if ca == 0:
                            P.op("sync", lambda e, xi=xi, cs=cs: e.dma_start(out=xr[xi][0:1, :], in_=b.x[seq][L - 1:L, cs]), writes=[xrB[xi]], dma=True)
                            if m > 1:
                                P.op("sync", lambda e, xi=xi, cs=cs, m=m, cb=cb: e.dma_start(out=xr[xi][1:m, :], in_=b.x[seq][0:cb - 1, cs]), writes=[xrB[xi]], dma=True)
                        else:
                            P.op("sync", lambda e, xi=xi, cs=cs, m=m, ca=ca, cb=cb: e.dma_start(out=xr[xi][0:m, :], in_=b.x[seq][ca - 1:cb - 1, cs]), writes=[xrB[xi]], dma=True)
                        P.op("dve", lambda e, xi=xi, a=a, m=m: e.tensor_tensor(out=xo[xi][0:m, :], in0=acc[a][0:m, 0:512], in1=xr[xi][0:m, :], op=ALU.add), reads=[accB[a], xrB[xi]], writes=[xoB[xi]])
                        P.op("sync", lambda e, xi=xi, m=m, ca=ca, cb=cb, cs=cs: e.dma_start(out=b.x1[seq][ca:cb, cs], in_=xo[xi][0:m, :]), reads=[xoB[xi]], dma=True)
                P.flush()


def phase_G(b):
    nc, cfg, P = b.nc, b.cfg, b.P
    D, KD = cfg.D, cfg.KD
    TB = min(8, KD)
    with ExitStack() as st:
        def sb(name, shape, dt):
            return st.enter_context(nc.sbuf_tensor(f"G_{name}", list(shape), dt))

        def pst(name, shape, dt):
            return st.enter_context(nc.psum_tensor(f"G_{name}", list(shape), dt))
        xt = [sb(f"xt{i}", [128, D], F32) for i in range(2)]
        xtB = [Buf() for _ in range(2)]
        xn = [sb(f"xn{i}", [128, D], BF16) for i in range(2)]
        xnB = [Buf() for _ in range(2)]
        ss = sb("ss", [128, 8], F32)
        ssB = Buf()
        gbc = sb("gbc", [128, D], F32)
        ident = sb("ident", [128, 128], BF16)
        cB = Buf()
        hs = [sb(f"hs{i}", [128, KD, 128], BF16) for i in range(2)]
        hsB = [Buf() for _ in range(2)]
        tp = [pst(f"tp{i}", [128, 8, 128], BF16) for i in range(2)]
        tpB = [Buf() for _ in range(2)]
        P.op("pool", lambda e: e.dma_start(out=ident[:], in_=b.ident_in), writes=[cB], dma=True)
        P.op("sync", lambda e: e.dma_start(out=gbc[:], in_=bcast_rows(b.g_ffn, 128, D)), writes=[cB], dma=True)
        ti = 0
        pbc = 0
        for seq in range(2):
            TC = cfg.T[seq] + 2
            hv = b.h2T[seq].rearrange("(k p) c -> p k c", p=128)
            for (r0, r1) in splits(TC, 128):
                m = r1 - r0
                bi = ti % 2
                ti += 1
                P.op("sync", lambda e, bi=bi, r0=r0, r1=r1, m=m, x1s=b.x1[seq]: e.dma_start(out=xt[bi][0:m, :], in_=x1s[r0:r1, :]), writes=[xtB[bi]], dma=True)
                P.op("act", lambda e, bi=bi, m=m: e.activation(out=xn[bi][0:m, :], in_=xt[bi][0:m, :], func=AF.Square, accum_out=ss[0:m, 0:1]), reads=[xtB[bi]], writes=[xnB[bi], ssB])
                P.op("act", lambda e, m=m: e.activation(out=ss[0:m, 1:2], in_=ss[0:m, 0:1], func=AF.Ln, scale=1.0 / D, bias=cfg.EPS), reads=[ssB], writes=[ssB])
                P.op("act", lambda e, m=m: e.activation(out=ss[0:m, 2:3], in_=ss[0:m, 1:2], func=AF.Exp, scale=-0.5), reads=[ssB], writes=[ssB])
                P.op("dve", lambda e, bi=bi, m=m: e.scalar_tensor_tensor(out=xn[bi][0:m, :], in0=xt[bi][0:m, :], scalar=ss[0:m, 2:3], in1=gbc[0:m, :], op0=ALU.mult, op1=ALU.mult), reads=[xtB[bi], ssB, cB], writes=[xnB[bi]])
                for j in range(KD // TB):
                    pb = pbc % 2
                    pbc += 1
                    P.mm_group(tpB[pb], [(lambda e, pb=pb, bi=bi, k=k, m=m: e.transpose(out=tp[pb][:, k % TB, 0:m], in_=xn[bi][0:m, k * 128:(k + 1) * 128], identity=ident[0:m, 0:m]), [xnB[bi], cB]) for k in range(j * TB, j * TB + TB)])
                    P.op("act", copy_fn("act", hs[bi][:, j * TB:j * TB + TB, 0:m], tp[pb][:, 0:TB, 0:m]), reads=[tpB[pb]], writes=[hsB[bi]])
                P.op("sync", lambda e, bi=bi, r0=r0, r1=r1, m=m, hv=hv: e.dma_start(out=hv[:, :, r0:r1], in_=hs[bi][:, :, 0:m], allow_slow_non_contiguous=(m < 16)), reads=[hsB[bi]], dma=True)
        P.flush()


def phase_E(b):
    nc, cfg, P = b.nc, b.cfg, b.P
    D, KD, FT = cfg.D, cfg.KD, cfg.FT
    NCG = D // 512
    GW = 512
    for seq in range(2):
        T = cfg.T[seq]
        NGR = T // GW
        for gi in range(NGR):
            a0 = 1 + gi * GW
            with ExitStack() as st:
                def sb(name, shape, dt):
                    return st.enter_context(nc.sbuf_tensor(f"E{seq}{gi}_{name}", list(shape), dt))

                def pst(name, shape, dt):
                    return st.enter_context(nc.psum_tensor(f"E{seq}{gi}_{name}", list(shape), dt))
                h2 = sb("h2", [128, KD, GW + 2], BF16)
                h2B = Buf()
                actT = sb("actT", [128, FT, GW], BF16)
                actB = [Buf() for _ in range(FT)]
                cw = sb("cw", [128, 3, FT], F32)
                cbias = sb("cbias", [128, FT], F32)
                mk = sb("mk", [128, 32], F32)
                cB = Buf()
                pb = [pst(f"pb{i}", [128, 512], F32) for i in range(8)]
                pbB = [Buf() for _ in range(8)]
                x2B = [[Buf() for _ in range(NCG)] for _ in range(4)]
                st1 = ExitStack()

                def sb1(name, shape, dt):
                    return st1.enter_context(nc.sbuf_tensor(f"E{seq}{gi}_{name}", list(shape), dt))
                wu = [sb1(f"wu{i}", [128, KD, 128], BF16) for i in range(2)]
                wg = [sb1(f"wg{i}", [128, KD, 128], BF16) for i in range(2)]
                wuB = [Buf() for _ in range(2)]
                wgB = [Buf() for _ in range(2)]
                upf = [sb1(f"upf{i}", [128, GW + 2], F32) for i in range(2)]
                upfB = [Buf() for _ in range(2)]
                c1 = sb1("c1", [128, GW], F32)
                c1B = Buf()
                sil = sb1("sil", [128, GW], F32)
                silB = Buf()
                P.op("sync", lambda e: e.dma_start(out=cw[:], in_=b.convw), writes=[cB], dma=True)
                P.op("sync", lambda e: e.dma_start(out=cbias[:], in_=b.convb), writes=[cB], dma=True)
                P.op("sync", lambda e: e.dma_start(out=mk[:], in_=bcast_rows(b.masks, 128, 32)), writes=[cB], dma=True)
                hv = b.h2T[seq].rearrange("(k p) c -> p k c", p=128)
                P.op("sync", lambda e, a0=a0, hv=hv: e.dma_start(out=h2[:], in_=hv[:, :, a0 - 1:a0 + GW + 1]), writes=[h2B], dma=True)
                wuv = b.w_up.rearrange("(k p) c -> p k c", p=128)
                wgv = b.w_gate.rearrange("(k p) c -> p k c", p=128)
                halves = splits(GW + 2, 512)
                for f in range(FT):
                    wi = f % 2
                    P.op("pool", lambda e, wi=wi, f=f: e.dma_start(out=wu[wi][:], in_=wuv[:, :, f * 128:(f + 1) * 128]), writes=[wuB[wi]], dma=True)
                    P.op("pool", lambda e, wi=wi, f=f: e.dma_start(out=wg[wi][:], in_=wgv[:, :, f * 128:(f + 1) * 128]), writes=[wgB[wi]], dma=True)
                    ui = f % 2
                    for hi_, (lo, hi) in enumerate(halves):
                        pi = (f % 2) * 2 + hi_
                        N = hi - lo
                        P.mm_group(pbB[pi], [(lambda e, pi=pi, k=k, wi=wi, lo=lo, hi=hi, N=N: e.matmul(pb[pi][:, 0:N], lhsT=wu[wi][:, k, :], rhs=h2[:, k, lo:hi], start=(k == 0), stop=(k == KD - 1)), [wuB[wi], h2B]) for k in range(KD)])
                        P.op("act", copy_fn("act", upf[ui][:, lo:hi], pb[pi][:, 0:N]), reads=[pbB[pi]], writes=[upfB[ui]])
                    gp = 4 + f % 2
                    P.mm_group(pbB[gp], [(lambda e, gp=gp, k=k, wi=wi: e.matmul(pb[gp][:, 0:GW], lhsT=wg[wi][:, k, :], rhs=h2[:, k, 1:GW + 1], start=(k == 0), stop=(k == KD - 1)), [wgB[wi], h2B]) for k in range(KD)])
                    if gi == 0:
                        P.op("dve", lambda e, ui=ui: e.tensor_scalar(out=upf[ui][:, 0:1], in0=upf[ui][:, 0:1], scalar1=mk[:, 3:4], scalar2=None, op0=ALU.mult), reads=[cB], writes=[upfB[ui]])
                    if gi == NGR - 1:
                        P.op("dve", lambda e, ui=ui: e.tensor_scalar(out=upf[ui][:, GW + 1:GW + 2], in0=upf[ui][:, GW + 1:GW + 2], scalar1=mk[:, 0:1], scalar2=None, op0=ALU.mult), reads=[cB], writes=[upfB[ui]])
                    P.op("dve", lambda e, ui=ui, f=f: e.tensor_scalar(out=c1[:], in0=upf[ui][:, 1:GW + 1], scalar1=cw[:, 1, f:f + 1], scalar2=cbias[:, f:f + 1], op0=ALU.mult, op1=ALU.add), reads=[upfB[ui], cB], writes=[c1B])
                    P.op("dve", lambda e, ui=ui, f=f: e.scalar_tensor_tensor(out=c1[:], in0=upf[ui][:, 0:GW], scalar=cw[:, 0, f:f + 1], in1=c1[:], op0=ALU.mult, op1=ALU.add), reads=[upfB[ui], cB], writes=[c1B])
                    P.op("dve", lambda e, ui=ui, f=f: e.scalar_tensor_tensor(out=c1[:], in0=upf[ui][:, 2:GW + 2], scalar=cw[:, 2, f:f + 1], in1=c1[:], op0=ALU.mult, op1=ALU.add), reads=[upfB[ui], cB], writes=[c1B])
                    P.op("act", lambda e: e.activation(out=sil[:], in_=c1[:], func=AF.Silu), reads=[c1B], writes=[silB])
                    P.op("dve", lambda e, gp=gp, f=f: e.tensor_tensor(out=actT[:, f, :], in0=pb[gp][:, 0:GW], in1=sil[:], op=ALU.mult), reads=[pbB[gp], silB], writes=[actB[f]])
                P.flush()
                st1.close()
                st2 = ExitStack()

                def sb2(name, shape, dt):
                    return st2.enter_context(nc.sbuf_tensor(f"E{seq}{gi}_{name}", list(shape), dt))
                wd = [sb2(f"wd{i}", [128, 512], BF16) for i in range(12)]
                wdB = [Buf() for _ in range(12)]
                xr = [sb2(f"xr{i}", [128, 512], F32) for i in range(2)]
                xrB = [Buf() for _ in range(2)]
                xo = [sb2(f"xo{i}", [128, 512], F32) for i in range(2)]
                xoB = [Buf() for _ in range(2)]
                junk = sb2("junk", [128, 512], BF16)
                junkB = Buf()
                ssq = sb2("ssq", [128, 4, NCG + 4], F32)
                ssqB = [Buf() for _ in range(4)]
                gfs = [sb2(f"gfs{i}", [128, 512], F32) for i in range(2)]
                gfsB = [Buf() for _ in range(2)]
                wdc = 0
                xc = 0
                for cgi in range(NCG):
                    st_ = (cgi % 2) * 4
                    cs = slice(cgi * 512, (cgi + 1) * 512)
                    for f in range(FT):
                        wi = wdc % 12
                        wdc += 1
                        P.op("pool", lambda e, wi=wi, f=f, cs=cs: e.dma_start(out=wd[wi][:], in_=b.w_down[f * 128:(f + 1) * 128, cs]), writes=[wdB[wi]], dma=True)
                        for i in range(4):
                            P.mm(lambda e, i=i, f=f, wi=wi, st_=st_: e.matmul(pb[st_ + i][:, 0:512], lhsT=actT[:, f, i * 128:(i + 1) * 128], rhs=wd[wi][:], start=(f == 0), stop=(f == FT - 1)), reads=[wdB[wi], actB[f]], ps=pbB[st_ + i], first=(f == 0), last=(f == FT - 1))
                    for i in range(4):
                        xi = xc % 2
                        xc += 1
                        ca = a0 + i * 128
                        P.op("sync", lambda e, xi=xi, ca=ca, cs=cs: e.dma_start(out=xr[xi][:], in_=b.x1[seq][ca:ca + 128, cs]), writes=[xrB[xi]], dma=True)
                        P.op("dve", lambda e, xi=xi, i=i, st_=st_: e.tensor_tensor(out=xo[xi][:], in0=pb[st_ + i][:, 0:512], in1=xr[xi][:], op=ALU.add), reads=[pbB[st_ + i], xrB[xi]], writes=[xoB[xi]])
                        P.op("act", lambda e, xi=xi, i=i, cgi=cgi: e.activation(out=junk[:], in_=xo[xi][:], func=AF.Square, accum_out=ssq[:, i, cgi:cgi + 1]), reads=[xoB[xi]], writes=[junkB, ssqB[i]])
                        P.op("sync", lambda e, xi=xi, ca=ca, cs=cs: e.dma_start(out=b.x2[seq][ca - 1:ca + 127, cs], in_=xo[xi][:]), reads=[xoB[xi]], writes=[x2B[i][cgi]], dma=True)
                for i in range(4):
                    P.op("dve", lambda e, i=i: e.tensor_reduce(out=ssq[:, i, NCG:NCG + 1], in_=ssq[:, i, 0:NCG], axis=mybir.AxisListType.X, op=ALU.add), reads=[ssqB[i]], writes=[ssqB[i]])
                    P.op("act", lambda e, i=i: e.activation(out=ssq[:, i, NCG + 1:NCG + 2], in_=ssq[:, i, NCG:NCG + 1], func=AF.Ln, scale=1.0 / D, bias=cfg.EPS), reads=[ssqB[i]], writes=[ssqB[i]])
                    P.op("act", lambda e, i=i: e.activation(out=ssq[:, i, NCG + 2:NCG + 3], in_=ssq[:, i, NCG + 1:NCG + 2], func=AF.Exp, scale=-0.5), reads=[ssqB[i]], writes=[ssqB[i]])
                    for cgi in range(NCG):
                        xi = xc % 2
                        xc += 1
                        cs = slice(cgi * 512, (cgi + 1) * 512)
                        r0 = a0 - 1 + i * 128
                        P.op("sync", lambda e, xi=xi, r0=r0, cs=cs: e.dma_start(out=xr[xi][:], in_=b.x2[seq][r0:r0 + 128, cs]), reads=[x2B[i][cgi]], writes=[xrB[xi]], dma=True)
                        P.op("sync", lambda e, xi=xi, cgi=cgi: e.dma_start(out=gfs[xi][:], in_=bcast_rows(b.g_fin, 128, 512, off=cgi * 512)), writes=[gfsB[xi]], dma=True)
                        P.op("dve", lambda e, xi=xi, i=i, cs=cs: e.scalar_tensor_tensor(out=xo[xi][:], in0=xr[xi][:], scalar=ssq[:, i, NCG + 2:NCG + 3], in1=gfs[xi][:], op0=ALU.mult, op1=ALU.mult), reads=[xrB[xi], ssqB[i], gfsB[xi]], writes=[xoB[xi]])
                        P.op("sync", lambda e, xi=xi, r0=r0, cs=cs: e.dma_start(out=b.y[seq][r0:r0 + 128, cs], in_=xo[xi][:]), reads=[xoB[xi]], dma=True)
                P.flush()
                st2.close()


def MM(out, lhsT, rhs, start, stop):
    return lambda e: e.matmul(out, lhsT=lhsT, rhs=rhs, start=start, stop=stop)


def rot_exps(cfg):
    s = set()
    for k in range(1, 8):
        s.add(8 * k)
        s.add(64 * k)
    for T in cfg.T:
        rq = T // 512
        for k in range(1, rq):
            s.add(512 * k)
        s.add(T)
        s.add(2 * T)
    return sorted(s)


def exp_vector(cfg):
    ev = list(range(0, 8)) + list(range(7, -1, -1)) + list(range(1, 9)) + list(range(8, 0, -1)) + [-i for i in range(8)]
    ev += rot_exps(cfg)
    assert len(ev) <= 64
    return ev


def sb3(t, off, dims):
    row = 1
    for d in list(t.shape)[1:]:
        row *= d
    return bass.AP(t, off, [[row, 128]] + [list(d) for d in dims])


def phase_B(b):
    nc, cfg, P = b.nc, b.cfg, b.P
    G, UT = cfg.G, cfg.UT
    ROT = rot_exps(cfg)
    NR = len(ROT)
    ROTI = {e: i for i, e in enumerate(ROT)}
    EV = exp_vector(cfg)
    NE = len(EV)
    RC0 = 40
    with ExitStack() as st:
        def sb(name, shape, dt):
            return st.enter_context(nc.sbuf_tensor(f"B_{name}", list(shape), dt))

        def pst(name, shape, dt):
            return st.enter_context(nc.psum_tensor(f"B_{name}", list(shape), dt))
        sel = sb("sel", [128, 64, 128], BF16)
        selo = sb("selo", [128, 64, 128], BF16)
        cstf = sb("cstf", [128, 4, 128], F32)
        identb = sb("identb", [128, 128], BF16)
        sig = sb("sig", [128, 4], F32)
        evb = sb("evb", [128, 64], F32)
        mk = sb("mk", [128, 32], F32)
        cB = Buf()
        are = sb("are", [128, 16], F32)
        aim = sb("aim", [128, 16], F32)
        ldt = sb("ldt", [128, 16], F32)
        sm = sb("sm", [128, 12, 16], F32)
        Bx1 = sb("Bx1", [128, 16, 16], F32)
        Bx2 = sb("Bx2", [128, 16, 16], F32)
        Cx1 = sb("Cx1", [128, 16, 16], F32)
        Cx2 = sb("Cx2", [128, 16, 16], F32)
        M1B = sb("M1B", [128, 16, 16], F32)
        M2B = sb("M2B", [128, 16, 16], F32)
        tb1 = sb("tb1", [128, 16, 16], F32)
        tb2 = sb("tb2", [128, 16, 16], F32)
        dvec = sb("dvec", [128, 8], F32)
        LRt = sb("LRt", [128, 16, NE], F32)
        ANG = sb("ANG", [128, 16, NE], F32)
        ANGf = sb("ANGf", [128, 16, NE], F32)
        ANGi = sb("ANGi", [128, 16, NE], I32)
        PT1 = sb("PT1", [128, 16, NE], F32)
        PT2 = sb("PT2", [128, 16, NE], F32)
        NPT2 = sb("NPT2", [128, 16, NE], F32)
        tabB = Buf()
        inB = Buf()
        gtA = [sb(f"gtA{i}", [128, 8, 16], F32) for i in range(4)]
        gtB = [sb(f"gtB{i}", [128, 8, 16], F32) for i in range(4)]
        gtAB = [Buf() for _ in range(4)]
        gtBB = [Buf() for _ in range(4)]
        r2t = sb("r2t", [128, NR, 128], BF16)
        r2B = Buf()
        Wxn = sb("Wxn", [128, 8, 16], BF16)
        WxnB = Buf()
        Phi = sb("Phi", [128, 2, 128], BF16)
        Psi = sb("Psi", [128, 2, 128], BF16)
        PhB = Buf()
        ttA = sb("ttA", [128, 128], F32)
        ttBt = sb("ttBt", [128, 128], F32)
        ttAB, ttBB = Buf(), Buf()
        WxT = [[sb(f"WxT{p}{d}", [128, 128], BF16) for d in range(2)] for p in range(2)]
        Wy = [[sb(f"Wy{p}{d}", [128, 8, 16], BF16) for d in range(2)] for p in range(2)]
        rot = [[sb(f"rot{p}{d}", [128, NR, 128], BF16) for d in range(2)] for p in range(2)]
        TT = [sb(f"TT{p}", [128, 128], BF16) for p in range(2)]
        matB = [Buf() for _ in range(2)]
        Lmax = max(cfg.L)
        NCmax = Lmax // 8
        uld = sb("uld", [128, Lmax], BF16)
        uldB = Buf()
        uT8 = [sb(f"uT8{s}", [128, 8, cfg.L[s] // 8], BF16) for s in range(2)]
        uTB = [Buf() for _ in range(2)]
        U = [sb(f"U{s}", [128, 8, cfg.L[s] // 8], BF16) for s in range(2)]
        UB = [[Buf() for _ in range(8)] for _ in range(2)]
        XL = [[[sb(f"X{s_}{d_}{l_}", [128, max(4, (cfg.L[s_] // 8) // (8 ** l_))], BF16) for l_ in range(3)] + [sb(f"X{s_}{d_}q", [128, 4], BF16)] for d_ in range(2)] for s_ in range(2)]
        XLB = [[[Buf() for _ in range(4)] for d_ in range(2)] for s_ in range(2)]
        PL = [[[sb(f"P{s_}{d_}{l_}", [128, max(4, (cfg.L[s_] // 8) // (8 ** l_))], BF16) for l_ in range(3)] + [sb(f"P{s_}{d_}q", [128, 4], BF16)] for d_ in range(2)] for s_ in range(2)]
        PLB = [[[Buf() for _ in range(4)] for d_ in range(2)] for s_ in range(2)]
        XM = [[sb(f"Xm{s_}{d_}", [128, 3, 4], BF16) for d_ in range(2)] for s_ in range(2)]
        XMB = [[Buf() for d_ in range(2)] for s_ in range(2)]
        NYmax = max(cfg.T) // 8 + 2
        Ys = [sb(f"Ys{s}", [128, 8, cfg.T[s] // 8 + 2], BF16) for s in range(2)]
        YsB = [[Buf() for _ in range(8)] for _ in range(2)]
        yst = [sb(f"yst{s}", [128, cfg.T[s] + 2], F32) for s in range(2)]
        ystB = [Buf() for _ in range(2)]
        pA = [pst(f"pA{i}", [128, 512], F32) for i in range(2)]
        pAB = [Buf() for _ in range(2)]
        pS = [pst(f"pS{i}", [128, 512], F32) for i in range(2)]
        pSB = [Buf() for _ in range(2)]
        pC = [pst(f"pC{i}", [128, 512], F32) for i in range(2)]
        pCB = [Buf() for _ in range(2)]
        pY = pst("pY", [128, 512], F32)
        pYB = Buf()
        pT = pst("pT", [128, 8, 128], BF16)
        pTB = Buf()
        cnt = {"a": 0, "s": 0, "ev": 0, "rt": 0}

        def evac_eng():
            cnt["ev"] += 1
            return "act" if cnt["ev"] % 2 else "dve"

        P.op("pool", lambda e: e.dma_start(out=sel[:], in_=b.sel_in), writes=[cB], dma=True)
        P.op("pool", lambda e: e.dma_start(out=selo[:], in_=b.selo_in), writes=[cB], dma=True)
        P.op("sync", lambda e: e.dma_start(out=cstf[:], in_=b.cst[:, 0:4, :]), writes=[cB], dma=True)
        P.op("pool", lambda e: e.dma_start(out=identb[:], in_=b.ident_in), writes=[cB], dma=True)
        P.op("sync", lambda e: e.dma_start(out=sig[:], in_=b.sig), writes=[cB], dma=True)
        P.op("sync", lambda e: e.dma_start(out=evb[:], in_=bcast_rows(b.evec, 128, 64)), writes=[cB], dma=True)
        P.op("sync", lambda e: e.dma_start(out=mk[:], in_=bcast_rows(b.masks, 128, 32)), writes=[cB], dma=True)
        pswap, maskf, maskb, identf = cstf[:, 0, :], cstf[:, 1, :], cstf[:, 2, :], cstf[:, 3, :]
        SIG, NSIG = sig[:, 0:1], sig[:, 1:2]

        def bc_e(tab):
            return tab[:, :].unsqueeze(2).broadcast_to([128, 16, NE])

        evbc = evb[:, 0:NE].unsqueeze(1).broadcast_to([128, 16, NE])

        def dv(fn, reads, writes):
            return P.op("dve", fn, reads=reads, writes=writes)

        for k in range(UT):
            g0 = k * 8
            for d in range(2):
                P.op("sync", lambda e, d=d, g0=g0: e.dma_start(out=are[:, d * 8:(d + 1) * 8], in_=b.are_h[:, d * G + g0:d * G + g0 + 8]), writes=[inB], dma=True)
                P.op("sync", lambda e, d=d, g0=g0: e.dma_start(out=aim[:, d * 8:(d + 1) * 8], in_=b.aim_h[:, d * G + g0:d * G + g0 + 8]), writes=[inB], dma=True)
                P.op("sync", lambda e, d=d, g0=g0: e.dma_start(out=ldt[:, d * 8:(d + 1) * 8], in_=bcast_rows(b.log_dt, 128, 8, off=d * G + g0)), writes=[inB], dma=True)
                for (dst, src) in ((Bx1, b.bx1_h), (Bx2, b.bx2_h), (Cx1, b.cx1_h), (Cx2, b.cx2_h)):
                    P.op("sync", lambda e, d=d, g0=g0, dst=dst, src=src: e.dma_start(out=dst[:, d * 8:(d + 1) * 8, :], in_=src[:, d * G + g0:d * G + g0 + 8, :]), writes=[inB], dma=True)
            P.op("sync", lambda e, g0=g0: e.dma_start(out=dvec[:], in_=b.dsk_h[:, g0:g0 + 8]), writes=[inB], dma=True)
            DT, LR, TH = sm[:, 0, :], sm[:, 1, :], sm[:, 2, :]
            P.op("act", lambda e: e.activation(out=DT, in_=ldt[:], func=AF.Exp), reads=[inB], writes=[tabB])
            dv(lambda e: e.tensor_tensor(out=LR, in0=are[:], in1=DT, op=ALU.mult), [inB, tabB], [tabB])
            dv(lambda e: e.tensor_tensor(out=TH, in0=aim[:], in1=DT, op=ALU.mult), [inB, tabB], [tabB])
            dv(lambda e: e.tensor_tensor(out=LRt[:], in0=bc_e(sm[:, 1, :]), in1=evbc, op=ALU.mult), [tabB, cB], [tabB])
            P.op("act", lambda e: e.activation(out=LRt[:], in_=LRt[:], func=AF.Exp), reads=[tabB], writes=[tabB])
            for which in range(2):
                dv(lambda e: e.tensor_tensor(out=ANG[:], in0=bc_e(sm[:, 2, :]), in1=evbc, op=ALU.mult), [tabB, cB], [tabB])
                if which == 1:
                    dv(lambda e: e.tensor_scalar(out=ANG[:], in0=ANG[:], scalar1=PI / 2, scalar2=None, op0=ALU.add), [tabB], [tabB])
                range_reduce(P, "dve", ANG[:], ANGf[:], ANGi[:], tabB, tabB, 16 * NE)
                P.op("act", lambda e: e.activation(out=ANG[:], in_=ANG[:], func=AF.Sin), reads=[tabB], writes=[tabB])
                if which == 0:
                    dv(lambda e: e.scalar_tensor_tensor(out=PT2[:], in0=LRt[:], scalar=SIG, in1=ANG[:], op0=ALU.mult, op1=ALU.mult), [tabB, cB], [tabB])
                    dv(lambda e: e.tensor_scalar(out=NPT2[:], in0=PT2[:], scalar1=-1.0, scalar2=None, op0=ALU.mult), [tabB], [tabB])
                else:
                    dv(lambda e: e.tensor_tensor(out=PT1[:], in0=LRt[:], in1=ANG[:], op=ALU.mult), [tabB], [tabB])
            i1 = 16
            NRr, NI, DEN, ZR, ZI, T0, T1 = (sm[:, j, :] for j in range(3, 10))
            dv(lambda e: e.tensor_scalar(out=NRr, in0=PT1[:, :, i1], scalar1=-1.0, scalar2=None, op0=ALU.add), [tabB], [tabB])
            dv(lambda e: e.tensor_scalar(out=NI, in0=PT2[:, :, i1], scalar1=SIG, scalar2=None, op0=ALU.mult), [tabB, cB], [tabB])
            dv(lambda e: e.tensor_tensor(out=DEN, in0=are[:], in1=are[:], op=ALU.mult), [inB], [tabB])
            dv(lambda e: e.tensor_tensor(out=T0, in0=aim[:], in1=aim[:], op=ALU.mult), [inB], [tabB])
            dv(lambda e: e.tensor_tensor(out=DEN, in0=DEN, in1=T0, op=ALU.add), [tabB], [tabB])
            dv(lambda e: e.reciprocal(out=DEN, in_=DEN), [tabB], [tabB])
            dv(lambda e: e.tensor_tensor(out=T0, in0=NRr, in1=are[:], op=ALU.mult), [tabB, inB], [tabB])
            dv(lambda e: e.tensor_tensor(out=T1, in0=NI, in1=aim[:], op=ALU.mult), [tabB, inB], [tabB])
            dv(lambda e: e.tensor_tensor(out=T0, in0=T0, in1=T1, op=ALU.add), [tabB], [tabB])
            dv(lambda e: e.tensor_tensor(out=ZR, in0=T0, in1=DEN, op=ALU.mult), [tabB], [tabB])
            dv(lambda e: e.tensor_tensor(out=T0, in0=NI, in1=are[:], op=ALU.mult), [tabB, inB], [tabB])
            dv(lambda e: e.tensor_tensor(out=T1, in0=NRr, in1=aim[:], op=ALU.mult), [tabB, inB], [tabB])
            dv(lambda e: e.tensor_tensor(out=T0, in0=T0, in1=T1, op=ALU.subtract), [tabB], [tabB])
            dv(lambda e: e.tensor_tensor(out=ZI, in0=T0, in1=DEN, op=ALU.mult), [tabB], [tabB])
            dv(lambda e: e.tensor_scalar(out=ZI, in0=ZI, scalar1=SIG, scalar2=None, op0=ALU.mult), [tabB, cB], [tabB])

            def bc_h(v):
                return v.unsqueeze(2).broadcast_to([128, 16, 16])
            dv(lambda e: e.tensor_tensor(out=tb1[:], in0=Bx1[:], in1=bc_h(ZR), op=ALU.mult), [tabB, inB], [tabB])
            dv(lambda e: e.tensor_tensor(out=tb2[:], in0=Bx2[:], in1=bc_h(ZI), op=ALU.mult), [tabB, inB], [tabB])
            dv(lambda e: e.tensor_tensor(out=M1B[:], in0=tb1[:], in1=tb2[:], op=ALU.add), [tabB], [tabB])
            dv(lambda e: e.tensor_tensor(out=tb1[:], in0=Bx2[:], in1=bc_h(ZR), op=ALU.mult), [tabB, inB], [tabB])
            dv(lambda e: e.tensor_tensor(out=tb2[:], in0=Bx1[:], in1=bc_h(ZI), op=ALU.mult), [tabB, inB], [tabB])
            dv(lambda e: e.tensor_tensor(out=M2B[:], in0=tb1[:], in1=tb2[:], op=ALU.subtract), [tabB], [tabB])
            dv(lambda e: e.tensor_scalar(out=Cx1[:], in0=Cx1[:], scalar1=NSIG, scalar2=None, op0=ALU.mult), [inB, cB], [tabB, inB])
            dv(lambda e: e.tensor_scalar(out=Cx2[:], in0=Cx2[:], scalar1=NSIG, scalar2=None, op0=ALU.mult), [inB, cB], [tabB, inB])
            for s in range(2):
                L = cfg.L[s]
                NC = L // 8
                P.op("sync", lambda e, s=s, k=k, L=L: e.dma_start(out=uld[:, 0:L], in_=b.uT[s][k * 128:(k + 1) * 128, :]), writes=[uldB], dma=True)
                P.op("pool", lambda e, s=s, L=L: e.tensor_copy(out=uT8[s][:], in_=uld[:, 0:L].rearrange("p (c t) -> p t c", t=8)), reads=[uldB], writes=[uTB[s]])
                for gl in range(8):
                    for (c0, c1) in splits(NC, 512):
                        N = c1 - c0
                        a = cnt["a"] % 2
                        cnt["a"] += 1
                        P.mm_group(pAB[a], [(MM(pA[a][:, 0:N], sel[:, gl * 8 + t, :], uT8[s][:, t, c0:c1], t == 0, t == 7), [uTB[s], cB]) for t in range(8)])
                        eng = evac_eng()
                        P.op(eng, copy_fn(eng, U[s][:, gl, c0:c1], pA[a][:, 0:N]), reads=[pAB[a]], writes=[UB[s][gl]])
            for gl in range(8):
                par = gl % 2

                def outer_mul(slot, j0, T1tab, T2tab, gd):
                    p1 = PT1[:, gd, j0:j0 + 8].unsqueeze(2).broadcast_to([128, 8, 16])
                    p2 = PT2[:, gd, j0:j0 + 8].unsqueeze(2).broadcast_to([128, 8, 16])
                    m1 = T1tab[:, gd, :].unsqueeze(1).broadcast_to([128, 8, 16])
                    m2 = T2tab[:, gd, :].unsqueeze(1).broadcast_to([128, 8, 16])
                    dv(lambda e: e.tensor_tensor(out=gtA[slot][:], in0=p1, in1=m1, op=ALU.mult), [tabB], [gtAB[slot]])
                    dv(lambda e: e.tensor_tensor(out=gtB[slot][:], in0=p2, in1=m2, op=ALU.mult), [tabB], [gtBB[slot]])

                def outer_add(slot, out3, outB):
                    dv(lambda e: e.tensor_tensor(out=out3, in0=gtA[slot][:], in1=gtB[slot][:], op=ALU.add), [gtAB[slot], gtBB[slot]], [outB])
                for d in range(2):
                    gd = d * 8 + gl
                    outer_mul(0, 8 if d == 0 else 0, M1B, M2B, gd)
                    outer_mul(1, 16 if d == 0 else 24, Cx1, Cx2, gd)
                    outer_mul(2, 0 if d == 0 else 32, Cx1, Cx2, gd)
                    outer_mul(3, 32 if d == 0 else 0, M1B, M2B, gd)
                    idb = cstf[:, 3, :].unsqueeze(1).broadcast_to([128, NR, 128])
                    psb = cstf[:, 0, :].unsqueeze(1).broadcast_to([128, NR, 128])
                    s1b = PT1[:, gd, RC0:RC0 + NR].unsqueeze(2).broadcast_to([128, NR, 128])
                    s2b = NPT2[:, gd, RC0:RC0 + NR].unsqueeze(2).broadcast_to([128, NR, 128])
                    dv(lambda e, par=par, d=d, idb=idb, s1b=s1b: e.tensor_tensor(out=rot[par][d][:], in0=idb, in1=s1b, op=ALU.mult), [tabB, cB], [matB[par]])
                    dv(lambda e, psb=psb, s2b=s2b: e.tensor_tensor(out=r2t[:], in0=psb, in1=s2b, op=ALU.mult), [tabB, cB], [r2B])
                    outer_add(0, Wxn[:], WxnB)
                    P.mm_group(pTB, [(lambda e: e.transpose(out=pT[:, 0, :], in_=Wxn[:].rearrange("p a b -> p (a b)"), identity=identb[:]), [WxnB, cB])])
                    P.op("act", copy_fn("act", WxT[par][d][:], pT[:, 0, :]), reads=[pTB], writes=[matB[par]])
                    outer_add(1, Wy[par][d][:], matB[par])
                    outer_add(2, Phi[:, d, :].rearrange("p (a b) -> p a b", a=8), PhB)
                    outer_add(3, Psi[:, d, :].rearrange("p (a b) -> p a b", a=8), PhB)
                    dv(lambda e, par=par, d=d: e.tensor_tensor(out=rot[par][d][:], in0=rot[par][d][:], in1=r2t[:], op=ALU.add), [r2B], [matB[par]])
                a = cnt["a"] % 2
                cnt["a"] += 1
                P.mm_group(pAB[a], [(MM(pA[a][:, 0:128], Psi[:, 0, :], Phi[:, 0, :], True, True), [PhB]), (MM(pA[a][:, 128:256], Psi[:, 1, :], Phi[:, 1, :], True, True), [PhB])])
                dv(lambda e, a=a: e.tensor_tensor(out=ttA[:], in0=pA[a][:, 0:128], in1=maskf, op=ALU.mult), [pAB[a], cB], [ttAB])
                dv(lambda e, a=a: e.tensor_tensor(out=ttBt[:], in0=pA[a][:, 128:256], in1=maskb, op=ALU.mult), [pAB[a], cB], [ttBB])
                dv(lambda e: e.tensor_tensor(out=ttA[:], in0=ttA[:], in1=ttBt[:], op=ALU.add), [ttBB], [ttAB])
                dv(lambda e, gl=gl, par=par: e.scalar_tensor_tensor(out=TT[par][:], in0=identf, scalar=dvec[:, gl:gl + 1], in1=ttA[:], op0=ALU.mult, op1=ALU.add), [ttAB, inB, cB], [matB[par]])

                ring = [pS[0], pS[1], pA[0], pA[1]]
                ringB = [pSB[0], pSB[1], pAB[0], pAB[1]]

                def tree_gen(s, d, gl=gl, par=par):
                    L, T = cfg.L[s], cfg.T[s]
                    NC = L // 8
                    rq = T // 512
                    radices = [8, 8] + ([rq] if rq > 1 else [])
                    nlev = len(radices)
                    n = [NC]
                    for r_ in radices:
                        n.append(n[-1] // r_)
                    assert n[-1] == 4
                    ue = [8]
                    for r_ in radices:
                        ue.append(ue[-1] * r_)
                    assert ue[-1] == T
                    fwd = d == 0
                    mB = matB[par]

                    def rotap(ex):
                        if ex == 0:
                            return identb[:]
                        return rot[par][d][:, ROTI[ex], :]
                    Xlev = [XL[s][d][i] for i in range(nlev)] + [XL[s][d][3]]
                    XBl = [XLB[s][d][i] for i in range(nlev)] + [XLB[s][d][3]]
                    Plev = [PL[s][d][i] for i in range(nlev)] + [PL[s][d][3]]
                    PBl = [PLB[s][d][i] for i in range(nlev)] + [PLB[s][d][3]]
                    Xm, XmB = XM[s][d], XMB[s][d]

                    def bank():
                        si = cnt["s"] % 4
                        cnt["s"] += 1
                        return ring[si], ringB[si]
                    for (c0, c1) in splits(NC, 512):
                        N = c1 - c0
                        bk_, bkB = bank()
                        P.mm_group(bkB, [(MM(bk_[:, 0:N], WxT[par][d][:], U[s][:, gl, c0:c1], True, True), [mB, UB[s][gl]])])
                        eng = evac_eng()
                        P.op(eng, copy_fn(eng, sb3(Xlev[0], c0 // 8, [[1, N // 8], [NC // 8, 8]]), bk_[:, 0:N].rearrange("p (j r) -> p j r", r=8)), reads=[bkB], writes=[XBl[0]])
                        yield
                    for lev, rad in enumerate(radices):
                        nn = n[lev + 1]
                        bk_, bkB = bank()
                        mms = []
                        for r in range(rad):
                            kk = (rad - 1 - r) if fwd else r
                            w_ = n[lev] // rad
                            mms.append((MM(bk_[:, 0:nn], rotap(ue[lev] * kk), Xlev[lev][:, r * w_:(r + 1) * w_], r == 0, r == rad - 1), [mB, XBl[lev], cB]))
                        P.mm_group(bkB, mms)
                        eng = evac_eng()
                        if lev + 1 < nlev:
                            rad2 = radices[lev + 1]
                            dstx = sb3(Xlev[lev + 1], 0, [[1, nn // rad2], [nn // rad2, rad2]])
                            srcx = bk_[:, 0:nn].rearrange("p (j r) -> p j r", r=rad2)
                        else:
                            dstx, srcx = Xlev[lev + 1][:, 0:nn], bk_[:, 0:nn]
                        P.op(eng, copy_fn(eng, dstx, srcx), reads=[bkB], writes=[XBl[lev + 1]])
                        yield
                    X4 = Xlev[nlev]
                    mb0 = 4 if fwd else 16
                    for i in range(1, 4):
                        mo = mb0 + (i - 1) * 4
                        if fwd:
                            dv(lambda e, i=i, mo=mo: e.tensor_tensor(out=Xm[:, i - 1, i:4], in0=X4[:, 0:4 - i], in1=mk[:, mo + i:mo + 4], op=ALU.mult), [XBl[nlev], cB], [XmB])
                            dv(lambda e, i=i, mo=mo: e.tensor_tensor(out=Xm[:, i - 1, 0:i], in0=X4[:, 4 - i:4], in1=mk[:, mo:mo + i], op=ALU.mult), [XBl[nlev], cB], [XmB])
                        else:
                            dv(lambda e, i=i, mo=mo: e.tensor_tensor(out=Xm[:, i - 1, 0:4 - i], in0=X4[:, i:4], in1=mk[:, mo:mo + 4 - i], op=ALU.mult), [XBl[nlev], cB], [XmB])
                            dv(lambda e, i=i, mo=mo: e.tensor_tensor(out=Xm[:, i - 1, 4 - i:4], in0=X4[:, 0:i], in1=mk[:, mo + 4 - i:mo + 4], op=ALU.mult), [XBl[nlev], cB], [XmB])
                    yield
                    bk_, bkB = bank()
                    P.mm_group(bkB, [(MM(bk_[:, 0:4], rotap(T * (i - 1)), Xm[:, i - 1, :], i == 1, i == 3), [mB, XmB, cB]) for i in range(1, 4)])
                    eng = evac_eng()
                    P.op(eng, copy_fn(eng, Plev[nlev][:, 0:4], bk_[:, 0:4]), reads=[bkB], writes=[PBl[nlev]])
                    yield
                    for lev in range(nlev - 1, -1, -1):
                        rad = radices[lev]
                        nn = n[lev + 1]
                        big = rad * nn > 512
                        if big:
                            banks, bB = pC, pCB
                            per = 4
                        else:
                            bk_, bkB = bank()
                            banks, bB = [bk_], [bkB]
                            per = rad
                        mms_by_bank = {}
                        for r in range(rad):
                            bk = r // per
                            o0 = (r % per) * nn
                            terms = [(rotap(ue[lev] * (r if fwd else rad - 1 - r)), Plev[lev + 1][:, 0:nn], PBl[lev + 1])]
                            rr = range(0, r) if fwd else range(r + 1, rad)
                            for r2 in rr:
                                kk = (r - 1 - r2) if fwd else (r2 - 1 - r)
                                w_ = n[lev] // rad
                                terms.append((rotap(ue[lev] * kk), Xlev[lev][:, r2 * w_:(r2 + 1) * w_], XBl[lev]))
                            for ti_, (lh, rh, rb) in enumerate(terms):
                                mms_by_bank.setdefault(bk, []).append((MM(banks[bk][:, o0:o0 + nn], lh, rh, ti_ == 0, ti_ == len(terms) - 1), [mB, rb, cB]))
                        for bk, mms in mms_by_bank.items():
                            P.mm_group(bB[bk], mms)
                            r0 = bk * per
                            cnt_r = min(per, rad - r0)
                            dst = sb3(Plev[lev], r0, [[1, cnt_r], [rad, nn]])
                            src = banks[bk][:, 0:cnt_r * nn].rearrange("p (r j) -> p r j", r=cnt_r)
                            eng = evac_eng()
                            P.op(eng, copy_fn(eng, dst, src), reads=[bB[bk]], writes=[PBl[lev]])
                            yield

                gens = [tree_gen(s_, d_) for s_ in range(2) for d_ in range(2)]
                while gens:
                    for g_ in list(gens):
                        try:
                            next(g_)
                        except StopIteration:
                            gens.remove(g_)
                for s in range(2):
                    L, T = cfg.L[s], cfg.T[s]
                    NC = L // 8
                    NYo = T // 8 + 1
                    terms = [(TT[par][:], lambda c0, c1, s=s, gl=gl: U[s][:, gl, c0:c1], UB[s][gl]),
                             (Wy[par][0][:].rearrange("p a b -> p (a b)"), lambda c0, c1, s=s: PL[s][0][0][:, c0:c1], PLB[s][0][0]),
                             (Wy[par][1][:].rearrange("p a b -> p (a b)"), lambda c0, c1, s=s: PL[s][1][0][:, c0:c1], PLB[s][1][0])]
                    mms = []
                    for ti_, (lh, rf, rb) in enumerate(terms):
                        mms.append((MM(pY[:, 0:1], lh, rf(NC - 1, NC), ti_ == 0, ti_ == 2), [matB[par], rb]))
                    for ti_, (lh, rf, rb) in enumerate(terms):
                        mms.append((MM(pY[:, 1:1 + NYo], lh, rf(0, NYo), ti_ == 0, ti_ == 2), [matB[par], rb]))
                    P.mm_group(pYB, mms)
                    eng = evac_eng()
                    P.op(eng, copy_fn(eng, Ys[s][:, gl, 0:NYo + 1], pY[:, 0:NYo + 1]), reads=[pYB], writes=[YsB[s][gl]])
            for s in range(2):
                T = cfg.T[s]
                NY = T // 8 + 2
                for t in range(8):
                    a = cnt["a"] % 2
                    cnt["a"] += 1
                    P.mm_group(pAB[a], [(MM(pA[a][:, 0:NY], selo[:, gl * 8 + t, :], Ys[s][:, gl, 0:NY], gl == 0, gl == 7), [YsB[s][gl], cB]) for gl in range(8)])
                    dst = sb3(yst[s], 1 + t, [[8, T // 8]])
                    P.op("act", copy_fn("act", dst, pA[a][:, 1:1 + T // 8]), reads=[pAB[a]], writes=[ystB[s]])
                    if t == 7:
                        P.op("act", copy_fn("act", yst[s][:, 0:1], pA[a][:, 0:1]), reads=[pAB[a]], writes=[ystB[s]])
                    if t == 0:
                        P.op("act", copy_fn("act", yst[s][:, T + 1:T + 2], pA[a][:, T // 8 + 1:T // 8 + 2]), reads=[pAB[a]], writes=[ystB[s]])
                P.op("sync", lambda e, s=s, k=k: e.dma_start(out=b.yT[s][k * 128:(k + 1) * 128, :], in_=yst[s][:]), reads=[ystB[s]], dma=True)
        P.flush()


def phase_H(b):
    nc, cfg, P = b.nc, b.cfg, b.P
    UT = cfg.UT
    with ExitStack() as st:
        def sb(name, shape, dt):
            return st.enter_context(nc.sbuf_tensor(f"H_{name}", list(shape), dt))

        def pst(name, shape, dt):
            return st.enter_context(nc.psum_tensor(f"H_{name}", list(shape), dt))
        yv = sb("yv", [128, UT, 512], F32)
        yB = Buf()
        gT = sb("gT", [128, UT, 512], BF16)
        gB = Buf()
        w1 = sb("w1", [128, 512], F32)
        w2 = sb("w2", [128, 512], F32)
        w1B, w2B = Buf(), Buf()
        sg = [sb(f"sg{i}", [128, 512], F32) for i in range(2)]
        sgB = [Buf() for _ in range(2)]
        wt = [sb(f"wt{i}", [128, UT, 128], BF16) for i in range(2)]
        wtB = [Buf() for _ in range(2)]
        stg = [sb(f"stg{i}", [128, 512], BF16) for i in range(2)]
        stgB = [Buf() for _ in range(2)]
        acc = [pst(f"acc{i}", [128, 512], F32) for i in range(2)]
        accB = [Buf() for _ in range(2)]
        gv = b.glu_w.rearrange("(k p) c -> p k c", p=128)
        c = 0
        for seq in range(2):
            TC = cfg.T[seq] + 2
            yTv = b.yT[seq].rearrange("(k p) c -> p k c", p=128)
            mrs = b.mraw[seq]
            for (c0, c1) in splits(TC, 512):
                N = c1 - c0
                P.op("sync", lambda e, c0=c0, c1=c1, N=N, yTv=yTv: e.dma_start(out=yv[:, :, 0:N], in_=yTv[:, :, c0:c1]), writes=[yB], dma=True)
                for k in range(UT):
                    yk = yv[:, k, 0:N]
                    P.op("dve", lambda e, yk=yk, N=N: e.tensor_tensor(out=w1[:, 0:N], in0=yk, in1=yk, op=ALU.mult), reads=[yB], writes=[w1B])
                    P.op("dve", lambda e, N=N: e.tensor_scalar(out=w1[:, 0:N], in0=w1[:, 0:N], scalar1=0.044715, scalar2=1.0, op0=ALU.mult, op1=ALU.add), reads=[w1B], writes=[w1B])
                    P.op("dve", lambda e, yk=yk, N=N: e.tensor_tensor(out=w1[:, 0:N], in0=w1[:, 0:N], in1=yk, op=ALU.mult), reads=[w1B, yB], writes=[w1B])
                    P.op("act", lambda e, N=N: e.activation(out=w2[:, 0:N], in_=w1[:, 0:N], func=AF.Sigmoid, scale=1.5957691216057308), reads=[w1B], writes=[w2B])
                    P.op("dve", lambda e, yk=yk, N=N, k=k: e.tensor_tensor(out=gT[:, k, 0:N], in0=w2[:, 0:N], in1=yk, op=ALU.mult), reads=[w2B, yB], writes=[gB])
                for j in range(UT):
                    wi = c % 2
                    c += 1
                    P.op("pool", lambda e, wi=wi, j=j: e.dma_start(out=wt[wi][:], in_=gv[:, :, j * 128:(j + 1) * 128]), writes=[wtB[wi]], dma=True)
                    P.mm_group(accB[wi], [(MM(acc[wi][:, 0:N], wt[wi][:, k, :], gT[:, k, 0:N], k == 0, k == UT - 1), [wtB[wi], gB]) for k in range(UT)])
                    P.op("act", lambda e, wi=wi, N=N: e.activation(out=sg[wi][:, 0:N], in_=acc[wi][:, 0:N], func=AF.Sigmoid), reads=[accB[wi]], writes=[sgB[wi]])
                    P.op("dve", lambda e, wi=wi, N=N, j=j: e.tensor_tensor(out=stg[wi][:, 0:N], in0=sg[wi][:, 0:N], in1=gT[:, j, 0:N], op=ALU.mult), reads=[sgB[wi], gB], writes=[stgB[wi]])
                    P.op("sync", lambda e, wi=wi, N=N, j=j, c0=c0, c1=c1, mrs=mrs: e.dma_start(out=mrs[j * 128:(j + 1) * 128, c0:c1], in_=stg[wi][:, 0:N]), reads=[stgB[wi]], dma=True)
        P.flush()


ROPE_THETA = 10000.0

def consts(cfg):
    c = {}
    c["ident"] = np.eye(128, dtype=np.float32)
    inv = 1.0 / (ROPE_THETA ** (np.arange(0, 64, 2, dtype=np.float32) / 64)).astype(np.float32)
    c["invf"] = np.concatenate([inv, inv]).reshape(64, 1).astype(np.float32)
    c["sgn"] = np.concatenate([-np.ones(32), np.ones(32)]).reshape(64, 1).astype(np.float32)
    sel = np.zeros((128, 64, 128), np.float32)
    for gl in range(8):
        for t in range(8):
            for h in range(16):
                sel[gl * 16 + h, gl * 8 + t, t * 16 + h] = 1.0
    c["sel"] = sel
    c["selo"] = np.ascontiguousarray(sel.transpose(2, 1, 0))
    cst = np.zeros((128, 8, 128), np.float32)
    for k in range(128):
        cst[k, 0, (k + 64) % 128] = 1.0
    s_idx = np.arange(128) // 16
    cst[:, 1, :] = (s_idx[None, :] >= s_idx[:, None]).astype(np.float32)
    cst[:, 2, :] = (s_idx[None, :] <= s_idx[:, None]).astype(np.float32)
    cst[:, 3, :] = np.eye(128)
    c["cst"] = cst
    sig = np.zeros((128, 4), np.float32)
    sig[:64, 0] = -1.0; sig[64:, 0] = 1.0
    sig[:, 1] = -sig[:, 0]
    c["sigv"] = sig
    return c

def host_prepare(cfg, inputs):
    f = lambda a: np.ascontiguousarray(np.asarray(a, dtype=np.float32))
    cs = consts(cfg)
    shared = dict(cs)
    shared["w_in"] = f(inputs["w_in"][0])
    shared["g_mix"] = f(inputs["norm_mix_g"][0]).reshape(1, -1)
    shared["g_ffn"] = f(inputs["norm_ffn_g"][0]).reshape(1, -1)
    shared["g_fin"] = f(inputs["norm_final_g"]).reshape(1, -1)
    shared["gq"] = f(np.asarray(inputs["q_norm_g"][0]).reshape(cfg.QT, 128).T)
    shared["gkv"] = f(np.asarray(inputs["kv_norm_g"][0]).reshape(cfg.KT, 128).T)
    gmo = np.concatenate([np.asarray(inputs["ssm_out_norm_g"][0]), np.asarray(inputs["attn_out_norm_g"][0])])
    shared["gmo"] = f(gmo.reshape(cfg.MT, 128).T)
    shared["w_q_up"] = f(inputs["w_q_up"][0])
    shared["w_kv_up"] = f(inputs["w_kv_up"][0])
    shared["w_out"] = f(inputs["w_out"][0])
    shared["w_up"] = f(inputs["w_ffn_up"][0])
    shared["w_gate"] = f(inputs["w_ffn_gate"][0])
    shared["w_down"] = f(inputs["w_ffn_down"][0])
    cw = np.asarray(inputs["ffn_conv_w"][0])
    shared["convw"] = f(cw.reshape(3, cfg.FT, 128).transpose(2, 0, 1))
    shared["convb"] = f(np.asarray(inputs["ffn_conv_b"][0]).reshape(cfg.FT, 128).T)
    shared["glu_w"] = f(inputs["ssm_glu_w"][0])
    G = cfg.G
    are = np.asarray(inputs["ssm_a_re"][0]); aim = np.asarray(inputs["ssm_a_im"][0])
    def st2(x):
        y = x.transpose(2, 0, 1).reshape(64, 2 * G)
        return f(np.concatenate([y, y], 0))
    shared["are_h"] = st2(are); shared["aim_h"] = st2(aim)
    bre = np.asarray(inputs["ssm_b_re"][0]); bim = np.asarray(inputs["ssm_b_im"][0])
    br = bre.transpose(2, 0, 1, 3).reshape(64, 2 * G, 16); bi = bim.transpose(2, 0, 1, 3).reshape(64, 2 * G, 16)
    shared["bx1_h"] = f(np.concatenate([br, bi], 0)); shared["bx2_h"] = f(np.concatenate([bi, br], 0))
    cre = np.asarray(inputs["ssm_c_re"][0]); cim = np.asarray(inputs["ssm_c_im"][0])
    cr = cre.transpose(3, 0, 1, 2).reshape(64, 2 * G, 16); ci = cim.transpose(3, 0, 1, 2).reshape(64, 2 * G, 16)
    shared["cx1_h"] = f(np.concatenate([cr, ci], 0)); shared["cx2_h"] = f(np.concatenate([ci, cr], 0))
    dsk = np.asarray(inputs["ssm_d"][0])
    shared["dsk_h"] = f(np.tile(dsk.T, (8, 1)))
    shared["log_dt"] = f(inputs["ssm_log_dt"][0]).reshape(1, -1)
    ev = exp_vector(cfg)
    evv = np.zeros((1, 64), np.float32); evv[0, :len(ev)] = ev
    shared["evec"] = evv
    maps = []
    xp = np.asarray(inputs["x_prompt"]); xs = np.asarray(inputs["x_sample"])
    for c in range(8):
        si, q = c // 4, c % 4
        m = dict(shared)
        for nm, xx, s in (("p", xp, 0), ("s", xs, 1)):
            L, T = cfg.L[s], cfg.T[s]
            m["x" + nm] = f(np.roll(xx[si], -q * T, axis=0))
            m["pos" + nm] = ((np.arange(L) + q * T) % L).astype(np.float32).reshape(1, L)
        mk = np.zeros((1, 32), np.float32)
        lm = np.array([0.0 if j == (3 - q) else 1.0 for j in range(4)], np.float32)
        mk[0, 0:4] = lm
        for j in range(4):
            for i in range(1, 4):
                mk[0, 4 + (i - 1) * 4 + j] = np.prod([lm[(j - l) % 4] for l in range(1, i + 1)])
                mk[0, 16 + (i - 1) * 4 + j] = np.prod([lm[(j + l) % 4] for l in range(0, i)])
        m["masks"] = mk
        maps.append(m)
    return maps


_PHASES = "ABHCDGE"


def build_program(cfg, debug=False):
    nc = bass.Bass("TRN2", target_bir_lowering=False)
    b = declare(nc, cfg, debug=debug)
    g = globals()
    for ph in _PHASES:
        if ph == "A":
            phase_A(b, 0)
            phase_A(b, 1)
        else:
            g["phase_" + ph](b)
    return nc


def kernel(**inputs):
    cfg = FULL
    nc = build_program(cfg)
    maps = host_prepare(cfg, inputs)
    res = run_bass_kernel_spmd(nc, maps, core_ids=list(range(8)))
    yp = np.zeros((2, cfg.L[0], cfg.D), np.float32)
    ys = np.zeros((2, cfg.L[1], cfg.D), np.float32)
    for c in range(8):
        si, q = c // 4, c % 4
        r = res.results[c]
        yp[si, q * cfg.T[0]:(q + 1) * cfg.T[0]] = np.asarray(r["yp"])
        ys[si, q * cfg.T[1]:(q + 1) * cfg.T[1]] = np.asarray(r["ys"])
    return (yp, ys)
```

```python
from contextlib import ExitStack
import numpy as np
from concourse.bass_utils import run_bass_kernel_spmd
import concourse.bass as bass
import concourse.mybir as mybir

F32 = mybir.dt.float32
BF16 = mybir.dt.bfloat16
I32 = mybir.dt.int32
ALU = mybir.AluOpType
AF = mybir.ActivationFunctionType
TWO_PI = 6.283185307179586
PI = 3.141592653589793


class Buf:
    __slots__ = ("w", "r", "name")

    def __init__(self, name=""):
        self.w = None
        self.r = {}
        self.name = name


class Prog:
    NDMA = 12

    def __init__(self, nc, phase_sem):
        self.nc = nc
        self.phase_sem = phase_sem
        self.phase_idx = 0
        self.sems = {}
        self.cnt = {}
        self.dma_rr = {"sync": 0, "act": 0, "pool": 0}
        for eng in ("act", "pool", "dve", "pe"):
            self._sem(eng)
        for qn in ("sync", "act", "pool"):
            for i in range(self.NDMA):
                self._sem(f"d{qn}{i}")
        self._reset()

    def _reset(self):
        self.q = {k: [] for k in ("sync", "act", "pool", "dve", "pe")}

    def _sem(self, name):
        if name not in self.sems:
            self.sems[name] = self.nc.alloc_semaphore(f"s_{name}")
            self.cnt[name] = 0
        return self.sems[name]

    def op(self, eng, fn, reads=(), writes=(), waits=(), dma=False, track=True):
        w = {}

        def add(ev):
            if ev is None:
                return
            s, c = ev
            if w.get(s, 0) < c:
                w[s] = c
        for ev in waits:
            add(ev)
        for b in reads:
            add(b.w)
        for b in writes:
            add(b.w)
            for s, c in b.r.items():
                add((s, c))
        ev = None
        inc = 0
        sname = None
        if track:
            if dma:
                i = self.dma_rr[eng]
                self.dma_rr[eng] = (i + 1) % self.NDMA
                sname = f"d{eng}{i}"
                if self.cnt[sname] > 0:
                    add((sname, self.cnt[sname]))
                inc = 16
            else:
                sname = eng
                inc = 1
            self.cnt[sname] += inc
            ev = (sname, self.cnt[sname])
        self.q[eng].append((fn, tuple(w.items()), sname, inc))
        if ev is not None:
            for b in reads:
                if b.r.get(ev[0], 0) < ev[1]:
                    b.r[ev[0]] = ev[1]
            for b in writes:
                b.w = ev
                b.r = {}
        return ev

    def mm_group(self, psbuf, mms, extra_reads=()):
        n = len(mms)
        allreads = []
        for i, (fn, reads) in enumerate(mms):
            allreads.extend(reads)
            last = i == n - 1
            w = {}

            def add(ev, w=w):
                if ev is None:
                    return
                s, c = ev
                if w.get(s, 0) < c:
                    w[s] = c
            for b in reads:
                add(b.w)
            if i == 0:
                add(psbuf.w)
                for s, c in psbuf.r.items():
                    add((s, c))
            if last:
                self.cnt["pe"] += 1
                ev = ("pe", self.cnt["pe"])
                self.q["pe"].append((fn, tuple(w.items()), "pe", 1))
            else:
                self.q["pe"].append((fn, tuple(w.items()), None, 0))
        for b in allreads:
            if b.r.get("pe", 0) < ev[1]:
                b.r["pe"] = ev[1]
        psbuf.w = ev
        psbuf.r = {}
        return ev

    def mm(self, fn, reads=(), ps=None, first=False, last=False):
        w = {}

        def add(ev):
            if ev is None:
                return
            s_, c = ev
            if w.get(s_, 0) < c:
                w[s_] = c
        for b_ in reads:
            add(b_.w)
        if first and ps is not None:
            add(ps.w)
            for s_, c in ps.r.items():
                add((s_, c))
        if last:
            self.cnt["pe"] += 1
            ev = ("pe", self.cnt["pe"])
            self.q["pe"].append((fn, tuple(w.items()), "pe", 1))
            for b_ in reads:
                if b_.r.get("pe", 0) < ev[1]:
                    b_.r["pe"] = ev[1]
            ps.w = ev
            ps.r = {}
            return ev
        self.cnt["pe"] += 1
        ev = ("pe", self.cnt["pe"])
        self.q["pe"].append((fn, tuple(w.items()), "pe", 1))
        for b_ in reads:
            if b_.r.get("pe", 0) < ev[1]:
                b_.r["pe"] = ev[1]
        return None

    def flush(self):
        nc = self.nc
        pidx = self.phase_idx
        with nc.Block() as block:
            def run(engname):
                def body(e):
                    seen = {}
                    if pidx > 0:
                        e.wait_ge(self.phase_sem, pidx)
                    for fn, waits, sname, inc in self.q[engname]:
                        for (s, v) in waits:
                            if seen.get(s, 0) >= v:
                                continue
                            seen[s] = v
                            e.wait_ge(self.sems[s], v)
                        ins = fn(e)
                        if sname is not None:
                            ins.then_inc(self.sems[sname], inc)
                    if engname == "sync":
                        for s, c in self.cnt.items():
                            if c > 0 and seen.get(s, 0) < c:
                                e.wait_ge(self.sems[s], c)
                        e.nop().then_inc(self.phase_sem, 1)
                return body
            block.sync(run("sync"))
            block.scalar(run("act"))
            block.gpsimd(run("pool"))
            block.vector(run("dve"))
            block.tensor(run("pe"))
        self.phase_idx += 1
        self._reset()


def splits(n, maxw):
    k = (n + maxw - 1) // maxw
    base, rem = divmod(n, k)
    out = []
    lo = 0
    for i in range(k):
        wdt = base + (1 if i < rem else 0)
        out.append((lo, lo + wdt))
        lo += wdt
    return out


class Cfg:
    def __init__(self, D, G, H, QL, KVL, DFF, Lp, Ls):
        self.D, self.G, self.H, self.QL, self.KVL, self.DFF = D, G, H, QL, KVL, DFF
        self.SW = 16 * G
        self.AW = 128 * H
        self.KD = D // 128
        self.UT = self.SW // 128
        self.QT = QL // 128
        self.KT = KVL // 128
        self.HT = H
        self.MT = self.UT + self.HT
        self.FT = DFF // 128
        self.INC = self.SW + QL + KVL + 64
        self.L = [Lp, Ls]
        self.T = [Lp // 4, Ls // 4]
        self.QKD = 192
        self.EPS = 1e-6


FULL = Cfg(4096, 128, 16, 896, 512, 11008, 8192, 4096)


class B:
    pass


def own_ranges(L, T, lo, hi):
    out = []
    a, b = max(lo, 0), min(hi, T + 1)
    if a < b:
        out.append((a, b, a + 1))
    if lo <= L - 1 < hi:
        out.append((L - 1, L, 0))
    return out


def declare(nc, cfg, debug):
    b = B()
    b.nc, b.cfg = nc, cfg
    D = cfg.D
    kind_s = "ExternalOutput" if debug else "Internal"

    def inp(name, shape, dt=F32):
        return nc.dram_tensor(name, list(shape), dt, kind="ExternalInput").ap()

    def scr(name, shape, dt):
        return nc.dram_tensor(name, list(shape), dt, kind=kind_s).ap()
    b.x = [inp("xp", [cfg.L[0], D]), inp("xs", [cfg.L[1], D])]
    b.pos = [inp("posp", [1, cfg.L[0]]), inp("poss", [1, cfg.L[1]])]
    b.masks = inp("masks", [1, 32])
    b.w_in = inp("w_in", [D, cfg.INC])
    b.g_mix = inp("g_mix", [1, D])
    b.g_ffn = inp("g_ffn", [1, D])
    b.g_fin = inp("g_fin", [1, D])
    b.gq = inp("gq", [128, cfg.QT])
    b.gkv = inp("gkv", [128, cfg.KT])
    b.gmo = inp("gmo", [128, cfg.MT])
    b.w_q_up = inp("w_q_up", [cfg.QL, cfg.H * 192])
    b.w_kv_up = inp("w_kv_up", [cfg.KVL, cfg.H * 256])
    b.w_out = inp("w_out", [cfg.SW + cfg.AW, D])
    b.w_up = inp("w_up", [D, cfg.DFF])
    b.w_gate = inp("w_gate", [D, cfg.DFF])
    b.w_down = inp("w_down", [cfg.DFF, D])
    b.convw = inp("convw", [128, 3, cfg.FT])
    b.convb = inp("convb", [128, cfg.FT])
    b.glu_w = inp("glu_w", [cfg.SW, cfg.SW])
    b.invf = inp("invf", [64, 1])
    b.sgn = inp("sgn", [64, 1])
    b.ident_in = inp("ident", [128, 128])
    b.log_dt = inp("log_dt", [1, 2 * cfg.G])
    b.sel_in = inp("sel", [128, 64, 128])
    b.selo_in = inp("selo", [128, 64, 128])
    b.cst = inp("cst", [128, 8, 128])
    b.sig = inp("sigv", [128, 4])
    b.evec = inp("evec", [1, 64])
    b.are_h = inp("are_h", [128, 2 * cfg.G])
    b.aim_h = inp("aim_h", [128, 2 * cfg.G])
    b.bx1_h = inp("bx1_h", [128, 2 * cfg.G, 16])
    b.bx2_h = inp("bx2_h", [128, 2 * cfg.G, 16])
    b.cx1_h = inp("cx1_h", [128, 2 * cfg.G, 16])
    b.cx2_h = inp("cx2_h", [128, 2 * cfg.G, 16])
    b.dsk_h = inp("dsk_h", [128, cfg.G])
    b.y = [nc.dram_tensor("yp", [cfg.T[0], D], F32, kind="ExternalOutput").ap(),
           nc.dram_tensor("ys", [cfg.T[1], D], F32, kind="ExternalOutput").ap()]
    b.uT, b.kvn, b.kr, b.rtab, b.qn, b.mraw, b.x1, b.h2T, b.x2, b.yT = [], [], [], [], [], [], [], [], [], []
    for s in range(2):
        L, T = cfg.L[s], cfg.T[s]
        b.uT.append(scr(f"uT{s}", [cfg.SW, L], BF16))
        b.kvn.append(scr(f"kvn{s}", [cfg.KVL, L], BF16))
        b.kr.append(scr(f"kr{s}", [64, L], BF16))
        b.rtab.append(scr(f"rtab{s}", [2, 64, L], F32))
        b.qn.append(scr(f"qn{s}", [cfg.QL, T + 2], BF16))
        b.mraw.append(scr(f"mraw{s}", [cfg.SW + cfg.AW, T + 2], BF16))
        b.x1.append(scr(f"x1{s}", [T + 2, D], F32))
        b.h2T.append(scr(f"h2T{s}", [D, T + 2], BF16))
        b.x2.append(scr(f"x2{s}", [T, D], F32))
        b.yT.append(scr(f"yT{s}", [cfg.SW, T + 2], F32))
    b.phase_sem = nc.alloc_semaphore("phase")
    b.P = Prog(nc, b.phase_sem)
    return b


def bcast_rows(ap_row, nparts, n, off=0):
    return bass.AP(ap_row.tensor, ap_row.offset + off, [[0, nparts], [1, n]])


def range_reduce(P, eng, ang, tmpf, tmpi, angB, tmpB, n):
    P.op(eng, lambda e: e.tensor_scalar(out=tmpf, in0=ang, scalar1=1.0 / TWO_PI, scalar2=None, op0=ALU.mult), reads=[angB], writes=[tmpB])
    P.op(eng, lambda e: e.tensor_copy(out=tmpi, in_=tmpf), reads=[tmpB], writes=[tmpB])
    P.op(eng, lambda e: e.tensor_copy(out=tmpf, in_=tmpi), reads=[tmpB], writes=[tmpB])
    P.op(eng, lambda e: e.scalar_tensor_tensor(out=ang, in0=tmpf, scalar=-TWO_PI, in1=ang, op0=ALU.mult, op1=ALU.add), reads=[tmpB], writes=[angB])
    P.op(eng, lambda e: e.tensor_scalar(out=tmpf, in0=ang, scalar1=PI, scalar2=TWO_PI, op0=ALU.is_gt, op1=ALU.mult), reads=[angB], writes=[tmpB])
    P.op(eng, lambda e: e.tensor_tensor(out=ang, in0=ang, in1=tmpf, op=ALU.subtract), reads=[tmpB], writes=[angB])
    P.op(eng, lambda e: e.tensor_scalar(out=tmpf, in0=ang, scalar1=-PI, scalar2=TWO_PI, op0=ALU.is_lt, op1=ALU.mult), reads=[angB], writes=[tmpB])
    P.op(eng, lambda e: e.tensor_tensor(out=ang, in0=ang, in1=tmpf, op=ALU.add), reads=[tmpB], writes=[angB])


def phase_A(b, seq):
    nc, cfg, P = b.nc, b.cfg, b.P
    L, T, D, KD = cfg.L[seq], cfg.T[seq], cfg.D, cfg.KD
    GS = 512
    NG = L // GS
    x = b.x[seq]
    o1, o2, o3 = cfg.SW, cfg.SW + cfg.QL, cfg.SW + cfg.QL + cfg.KVL
    NLAT = max(cfg.QT, cfg.KT)
    with ExitStack() as st:
        def sb(name, shape, dt):
            return st.enter_context(nc.sbuf_tensor(f"A{seq}_{name}", list(shape), dt))

        def pst(name, shape, dt):
            return st.enter_context(nc.psum_tensor(f"A{seq}_{name}", list(shape), dt))
        xt = [sb(f"xt{i}", [128, D], F32) for i in range(2)]
        xtB = [Buf() for _ in range(2)]
        xn = [sb(f"xn{i}", [128, D], BF16) for i in range(2)]
        xnB = [Buf() for _ in range(2)]
        ss = sb("ss", [128, 8], F32)
        ssB = Buf()
        hT = sb("hT", [128, KD, GS], BF16)
        hTB = [Buf() for _ in range(4)]
        gbc = sb("gbc", [128, D], F32)
        gbcB = Buf()
        wt = [sb(f"wt{i}", [128, KD, 128], BF16) for i in range(3)]
        wtB = [Buf() for _ in range(3)]
        wtr = sb("wtr", [128, KD, 64], BF16)
        wtrs = sb("wtrs", [128, KD, 64], BF16)
        wtrB = Buf()
        ident = sb("ident", [128, 128], BF16)
        ones = sb("ones", [128, 128], BF16)
        cB = Buf()
        gq = sb("gq", [128, cfg.QT], F32)
        gkv = sb("gkv", [128, cfg.KT], F32)
        invf = sb("invf", [64, 1], F32)
        sgn = sb("sgn", [64, 1], F32)
        stg = [sb(f"stg{i}", [128, GS], BF16) for i in range(4)]
        stgB = [Buf() for _ in range(4)]
        ltmp = sb("ltmp", [128, NLAT, GS], F32)
        ltmpB = [Buf() for _ in range(NLAT)]
        lsq = sb("lsq", [128, NLAT, GS], BF16)
        lsqB = [Buf() for _ in range(NLAT)]
        rk = sb("rk", [128, GS], F32)
        rkB = Buf()
        posb = sb("posb", [64, GS], F32)
        posB = Buf()
        ang = sb("ang", [64, GS], F32)
        angB = Buf()
        rtf = sb("rtf", [64, GS], F32)
        rti = sb("rti", [64, GS], I32)
        rtB = Buf()
        cosT = sb("cosT", [64, GS], F32)
        sinT = sb("sinT", [64, GS], F32)
        cosB, sinB = Buf(), Buf()
        t1 = sb("t1", [64, GS], F32)
        t2 = sb("t2", [64, GS], F32)
        t1B, t2B = Buf(), Buf()
        tp = [pst(f"tp{i}", [128, 8, 128], BF16) for i in range(2)]
        tpB = [Buf() for _ in range(2)]
        acc = [pst(f"acc{i}", [128, GS], F32) for i in range(3)]
        accB = [Buf() for _ in range(3)]
        ssp = pst("ssp", [128, GS], F32)
        sspB = Buf()

        P.op("pool", lambda e: e.dma_start(out=ident[:], in_=b.ident_in), writes=[cB], dma=True)
        P.op("pool", lambda e: e.memset(ones[:], 1.0), writes=[cB])
        P.op("sync", lambda e: e.dma_start(out=gbc[:], in_=bcast_rows(b.g_mix, 128, D)), writes=[gbcB], dma=True)
        P.op("sync", lambda e: e.dma_start(out=gq[:], in_=b.gq), writes=[cB], dma=True)
        P.op("sync", lambda e: e.dma_start(out=gkv[:], in_=b.gkv), writes=[cB], dma=True)
        P.op("sync", lambda e: e.dma_start(out=invf[:], in_=b.invf), writes=[cB], dma=True)
        P.op("sync", lambda e: e.dma_start(out=sgn[:], in_=b.sgn), writes=[cB], dma=True)
        wv = b.w_in.rearrange("(k p) c -> p k c", p=128)
        P.op("pool", lambda e: e.dma_start(out=wtr[:], in_=wv[:, :, o3:o3 + 64]), writes=[wtrB], dma=True)
        P.op("pool", lambda e: e.dma_start(out=wtrs[:, :, 0:32], in_=wv[:, :, o3 + 32:o3 + 64]), writes=[wtrB], dma=True)
        P.op("pool", lambda e: e.dma_start(out=wtrs[:, :, 32:64], in_=wv[:, :, o3:o3 + 32]), writes=[wtrB], dma=True)

        cnt = {"tile": 0, "wt": 0, "acc": 0, "stg": 0, "ev": 0}

        def evac_eng():
            cnt["ev"] += 1
            return "act" if cnt["ev"] % 2 else "dve"

        def copy_op(eng, out, in_):
            if eng == "act":
                return lambda e: e.activation(out=out, in_=in_, func=AF.Copy)
            return lambda e: e.tensor_copy(out=out, in_=in_)

        def load_w(c0, M):
            i = cnt["wt"] % 3
            cnt["wt"] += 1
            P.op("pool", lambda e: e.dma_start(out=wt[i][:, :, 0:M], in_=wv[:, :, c0:c0 + M]), writes=[wtB[i]], dma=True)
            return i

        def proj(wtile, wB, M, lo, hi):
            a = cnt["acc"] % 3
            cnt["acc"] += 1
            N = hi - lo
            mms = []
            for k in range(KD):
                mms.append((lambda e, k=k: e.matmul(acc[a][0:M, 0:N], lhsT=wtile[:, k, 0:M], rhs=hT[:, k, lo:hi],
                                                    start=(k == 0), stop=(k == KD - 1)), [wB] + hTB if k == 0 else []))
            P.mm_group(accB[a], mms)
            return a

        def store(dst, src_fn_eng, N, M=128):
            i = cnt["stg"] % 4
            cnt["stg"] += 1
            src_fn_eng(stg[i][0:M, 0:N], stgB[i])
            P.op("sync", lambda e: e.dma_start(out=dst, in_=stg[i][0:M, 0:N], allow_slow_non_contiguous=(N < 16)), reads=[stgB[i]], dma=True)

        def latent(kind, ntile, c_base, gain, nfeat, lo, hi, dst, dcol):
            N = hi - lo
            for t in range(ntile):
                wi = load_w(c_base + t * 128, 128)
                a = proj(wt[wi], wtB[wi], 128, lo, hi)
                P.op("act", lambda e, a=a, t=t: e.activation(out=ltmp[:, t, 0:N], in_=acc[a][:, 0:N], func=AF.Copy), reads=[accB[a]], writes=[ltmpB[t]])
                P.op("act", lambda e, a=a, t=t: e.activation(out=lsq[:, t, 0:N], in_=acc[a][:, 0:N], func=AF.Square), reads=[accB[a]], writes=[lsqB[t]])
            P.mm_group(sspB, [(lambda e, t=t: e.matmul(ssp[:, 0:N], lhsT=ones[:], rhs=lsq[:, t, 0:N], start=(t == 0), stop=(t == ntile - 1)), [lsqB[t], cB]) for t in range(ntile)])
            P.op("act", lambda e: e.activation(out=rk[:, 0:N], in_=ssp[:, 0:N], func=AF.Ln, scale=1.0 / nfeat, bias=cfg.EPS), reads=[sspB], writes=[rkB])
            P.op("act", lambda e: e.activation(out=rk[:, 0:N], in_=rk[:, 0:N], func=AF.Exp, scale=-0.5), reads=[rkB], writes=[rkB])
            for t in range(ntile):
                def prod(sap, sB, t=t):
                    P.op("dve", lambda e: e.scalar_tensor_tensor(out=sap, in0=ltmp[:, t, 0:N], scalar=gain[:, t:t + 1], in1=rk[:, 0:N], op0=ALU.mult, op1=ALU.mult), reads=[ltmpB[t], rkB, cB], writes=[sB])
                store(dst[t * 128:(t + 1) * 128, dcol:dcol + N], prod, N)

        for g in range(NG):
            for tt in range(4):
                ti = cnt["tile"]
                cnt["tile"] += 1
                bi = ti % 2
                r0 = g * GS + tt * 128
                P.op("sync", lambda e, bi=bi, r0=r0: e.dma_start(out=xt[bi][:], in_=x[r0:r0 + 128, :]), writes=[xtB[bi]], dma=True)
                P.op("act", lambda e, bi=bi: e.activation(out=xn[bi][:], in_=xt[bi][:], func=AF.Square, accum_out=ss[:, 0:1]), reads=[xtB[bi]], writes=[xnB[bi], ssB])
                P.op("act", lambda e: e.activation(out=ss[:, 1:2], in_=ss[:, 0:1], func=AF.Ln, scale=1.0 / D, bias=cfg.EPS), reads=[ssB], writes=[ssB])
                P.op("act", lambda e: e.activation(out=ss[:, 2:3], in_=ss[:, 1:2], func=AF.Exp, scale=-0.5), reads=[ssB], writes=[ssB])
                P.op("dve", lambda e, bi=bi: e.scalar_tensor_tensor(out=xn[bi][:], in0=xt[bi][:], scalar=ss[:, 2:3], in1=gbc[:], op0=ALU.mult, op1=ALU.mult), reads=[xtB[bi], ssB, gbcB], writes=[xnB[bi]])
                TB = min(8, KD)
                for j in range(KD // TB):
                    pb = (ti * (KD // TB) + j) % 2
                    P.mm_group(tpB[pb], [(lambda e, pb=pb, bi=bi, k=k: e.transpose(out=tp[pb][:, k % TB, :], in_=xn[bi][:, k * 128:(k + 1) * 128], identity=ident[:]), [xnB[bi], cB]) for k in range(j * TB, j * TB + TB)])
                    eng = "act"
                    P.op(eng, copy_op(eng, hT[:, j * TB:j * TB + TB, tt * 128:(tt + 1) * 128], tp[pb][:, 0:TB, :]), reads=[tpB[pb]], writes=[hTB[tt]])
            for j in range(cfg.UT):
                wi = load_w(j * 128, 128)
                a = proj(wt[wi], wtB[wi], 128, 0, GS)
                eng = evac_eng()

                def prod(sap, sB, a=a, eng=eng):
                    P.op(eng, copy_op(eng, sap, acc[a][:, 0:GS]), reads=[accB[a]], writes=[sB])
                store(b.uT[seq][j * 128:(j + 1) * 128, g * GS:(g + 1) * GS], prod, GS)
            latent("kv", cfg.KT, o2, gkv, cfg.KVL, 0, GS, b.kvn[seq], g * GS)
            P.op("sync", lambda e, g=g: e.dma_start(out=posb[:], in_=bcast_rows(b.pos[seq], 64, GS, off=g * GS)), writes=[posB], dma=True)
            for which in range(2):
                if which == 0:
                    P.op("dve", lambda e: e.tensor_scalar(out=ang[:], in0=posb[:], scalar1=invf[:, 0:1], scalar2=None, op0=ALU.mult), reads=[posB, cB], writes=[angB])
                else:
                    P.op("dve", lambda e: e.tensor_scalar(out=ang[:], in0=posb[:], scalar1=invf[:, 0:1], scalar2=PI / 2, op0=ALU.mult, op1=ALU.add), reads=[posB, cB], writes=[angB])
                range_reduce(P, "dve", ang[:], rtf[:], rti[:], angB, rtB, GS)
                if which == 0:
                    P.op("act", lambda e: e.activation(out=sinT[:], in_=ang[:], func=AF.Sin), reads=[angB], writes=[sinB])
                    P.op("dve", lambda e: e.tensor_scalar(out=sinT[:], in0=sinT[:], scalar1=sgn[:, 0:1], scalar2=None, op0=ALU.mult), reads=[sinB, cB], writes=[sinB])
                else:
                    P.op("act", lambda e: e.activation(out=cosT[:], in_=ang[:], func=AF.Sin), reads=[angB], writes=[cosB])
            P.op("sync", lambda e, g=g: e.dma_start(out=b.rtab[seq][0, :, g * GS:(g + 1) * GS], in_=cosT[:]), reads=[cosB], dma=True)
            P.op("sync", lambda e, g=g: e.dma_start(out=b.rtab[seq][1, :, g * GS:(g + 1) * GS], in_=sinT[:]), reads=[sinB], dma=True)
            a1 = proj(wtr, wtrB, 64, 0, GS)
            a2 = proj(wtrs, wtrB, 64, 0, GS)
            P.op("dve", lambda e, a1=a1: e.tensor_tensor(out=t1[:], in0=acc[a1][0:64, :], in1=cosT[:], op=ALU.mult), reads=[accB[a1], cosB], writes=[t1B])
            P.op("dve", lambda e, a2=a2: e.tensor_tensor(out=t2[:], in0=acc[a2][0:64, :], in1=sinT[:], op=ALU.mult), reads=[accB[a2], sinB], writes=[t2B])

            def prod(sap, sB):
                P.op("dve", lambda e: e.tensor_tensor(out=sap, in0=t1[:], in1=t2[:], op=ALU.add), reads=[t1B, t2B], writes=[sB])
            store(b.kr[seq][:, g * GS:(g + 1) * GS], prod, GS, M=64)
            for (tlo, thi, clo) in own_ranges(L, T, g * GS, (g + 1) * GS):
                latent("q", cfg.QT, o1, gq, cfg.QL, tlo - g * GS, thi - g * GS, b.qn[seq], clo)
        P.flush()


def copy_fn(eng, out, in_, scale=None):
    if eng == "act":
        if scale is None:
            return lambda e: e.activation(out=out, in_=in_, func=AF.Copy)
        return lambda e: e.activation(out=out, in_=in_, func=AF.Copy, scale=scale)
    if scale is None:
        return lambda e: e.tensor_copy(out=out, in_=in_)
    return lambda e: e.tensor_scalar(out=out, in0=in_, scalar1=scale, scalar2=None, op0=ALU.mult)


def phase_C(b):
    for seq in range(2):
        attn_seq(b, seq)


def attn_seq(b, seq):
    nc, cfg, P = b.nc, b.cfg, b.P
    L, T = cfg.L[seq], cfg.T[seq]
    TC = T + 2
    KT, QT, H = cfg.KT, cfg.QT, cfg.H
    NKT, NKB = L // 128, L // 512
    qtiles = splits(TC, 512)
    SCALE = 192.0 ** -0.5
    with ExitStack() as st:
        def sb(name, shape, dt):
            return st.enter_context(nc.sbuf_tensor(f"C{seq}_{name}", list(shape), dt))

        def pst(name, shape, dt):
            return st.enter_context(nc.psum_tensor(f"C{seq}_{name}", list(shape), dt))
        kvnT = sb("kvnT", [128, KT, L], BF16)
        krT = sb("krT", [64, L], BF16)
        qnT = sb("qnT", [128, QT, TC], BF16)
        cosq = sb("cosq", [64, TC], BF16)
        sinq = sb("sinq", [64, TC], BF16)
        inB = Buf()
        Kh = sb("Kh", [128, L], BF16)
        Vh = sb("Vh", [128, L], BF16)
        KhB = [Buf() for _ in range(NKB)]
        VhB = [Buf() for _ in range(NKB)]
        qhn = sb("qhn", [128, TC], BF16)
        qhr = sb("qhr", [64, TC], BF16)
        qhB = [Buf() for _ in qtiles]
        wkv = [sb(f"wkv{i}", [128, KT, 256], BF16) for i in range(2)]
        wq = [sb(f"wq{i}", [128, QT, 192], BF16) for i in range(2)]
        wqs = [sb(f"wqs{i}", [128, QT, 64], BF16) for i in range(2)]
        wB = [Buf() for _ in range(2)]
        PT = [sb(f"PT{i}", [128, 512], BF16) for i in range(3)]
        PTB = [Buf() for _ in range(3)]
        rden = sb("rden", [128, 512], F32)
        rdB = Buf()
        ostg = [sb(f"ostg{i}", [128, 512], BF16) for i in range(2)]
        ostgB = [Buf() for _ in range(2)]
        t1 = sb("t1", [64, 512], F32)
        t2 = sb("t2", [64, 512], F32)
        t1B, t2B = Buf(), Buf()
        ones = sb("ones", [128, 128], BF16)
        cB = Buf()
        S = [pst(f"S{i}", [128, 512], F32) for i in range(2)]
        SB = [Buf() for _ in range(2)]
        O = [pst(f"O{i}", [128, 512], F32) for i in range(2)]
        OB = [Buf() for _ in range(2)]
        DEN = [pst(f"DEN{i}", [128, 512], F32) for i in range(2)]
        DENB = [Buf() for _ in range(2)]
        acc = [pst(f"acc{i}", [128, 512], F32) for i in range(2)]
        accB = [Buf() for _ in range(2)]
        cnt = {"acc": 0, "ev": 0, "s": 0, "pt": 0, "o": 0, "st": 0}

        def evac_eng():
            cnt["ev"] += 1
            return "act" if cnt["ev"] % 2 else "dve"

        P.op("pool", lambda e: e.memset(ones[:], 1.0), writes=[cB])
        P.op("sync", lambda e: e.dma_start(out=kvnT[:], in_=b.kvn[seq].rearrange("(k p) l -> p k l", p=128)), writes=[inB], dma=True)
        P.op("sync", lambda e: e.dma_start(out=krT[:], in_=b.kr[seq]), writes=[inB], dma=True)
        P.op("sync", lambda e: e.dma_start(out=qnT[:], in_=b.qn[seq].rearrange("(k p) c -> p k c", p=128)), writes=[inB], dma=True)
        for tab, dst in ((0, cosq), (1, sinq)):
            P.op("pool", lambda e, tab=tab, dst=dst: e.dma_start(out=dst[:, 0:1], in_=b.rtab[seq][tab, :, L - 1:L], allow_slow_non_contiguous=True), writes=[inB], dma=True)
            P.op("pool", lambda e, tab=tab, dst=dst: e.dma_start(out=dst[:, 1:TC], in_=b.rtab[seq][tab, :, 0:T + 1]), writes=[inB], dma=True)
        wkvv = b.w_kv_up.rearrange("(k p) c -> p k c", p=128)
        wqv = b.w_q_up.rearrange("(k p) c -> p k c", p=128)
        for h in range(H):
            wi = h % 2
            P.op("pool", lambda e, wi=wi, h=h: e.dma_start(out=wkv[wi][:], in_=wkvv[:, :, h * 256:(h + 1) * 256]), writes=[wB[wi]], dma=True)
            P.op("pool", lambda e, wi=wi, h=h: e.dma_start(out=wq[wi][:], in_=wqv[:, :, h * 192:(h + 1) * 192]), writes=[wB[wi]], dma=True)
            P.op("pool", lambda e, wi=wi, h=h: e.dma_start(out=wqs[wi][:, :, 0:32], in_=wqv[:, :, h * 192 + 160:h * 192 + 192]), writes=[wB[wi]], dma=True)
            P.op("pool", lambda e, wi=wi, h=h: e.dma_start(out=wqs[wi][:, :, 32:64], in_=wqv[:, :, h * 192 + 128:h * 192 + 160]), writes=[wB[wi]], dma=True)
            for kb in range(NKB):
                a = cnt["acc"] % 2
                cnt["acc"] += 1
                P.mm_group(accB[a], [(lambda e, a=a, kt=kt, kb=kb, wi=wi: e.matmul(acc[a][:, 0:512], lhsT=wkv[wi][:, kt, 0:128], rhs=kvnT[:, kt, kb * 512:(kb + 1) * 512], start=(kt == 0), stop=(kt == KT - 1)), [wB[wi], inB]) for kt in range(KT)])
                eng = evac_eng()
                P.op(eng, copy_fn(eng, Kh[:, kb * 512:(kb + 1) * 512], acc[a][:, 0:512]), reads=[accB[a]], writes=[KhB[kb]])
                a = cnt["acc"] % 2
                cnt["acc"] += 1
                mms = []
                for i in range(4):
                    for kt in range(KT):
                        k0 = kb * 512 + i * 128
                        mms.append((lambda e, a=a, kt=kt, i=i, k0=k0, wi=wi: e.matmul(acc[a][:, i * 128:(i + 1) * 128], lhsT=kvnT[:, kt, k0:k0 + 128], rhs=wkv[wi][:, kt, 128:256], start=(kt == 0), stop=(kt == KT - 1)), [wB[wi], inB]))
                P.mm_group(accB[a], mms)
                eng = evac_eng()
                P.op(eng, copy_fn(eng, Vh[:, kb * 512:(kb + 1) * 512], acc[a][:, 0:512]), reads=[accB[a]], writes=[VhB[kb]])
            for qi, (c0, c1) in enumerate(qtiles):
                N = c1 - c0
                a = cnt["acc"] % 2
                cnt["acc"] += 1
                P.mm_group(accB[a], [(lambda e, a=a, qt=qt, wi=wi, c0=c0, c1=c1, N=N: e.matmul(acc[a][:, 0:N], lhsT=wq[wi][:, qt, 0:128], rhs=qnT[:, qt, c0:c1], start=(qt == 0), stop=(qt == QT - 1)), [wB[wi], inB]) for qt in range(QT)])
                P.op("act", copy_fn("act", qhn[:, c0:c1], acc[a][:, 0:N], scale=SCALE), reads=[accB[a]], writes=[qhB[qi]])
                a1 = cnt["acc"] % 2
                cnt["acc"] += 1
                P.mm_group(accB[a1], [(lambda e, a1=a1, qt=qt, wi=wi, c0=c0, c1=c1, N=N: e.matmul(acc[a1][0:64, 0:N], lhsT=wq[wi][:, qt, 128:192], rhs=qnT[:, qt, c0:c1], start=(qt == 0), stop=(qt == QT - 1)), [wB[wi], inB]) for qt in range(QT)])
                P.op("dve", lambda e, a1=a1, c0=c0, c1=c1, N=N: e.scalar_tensor_tensor(out=t1[:, 0:N], in0=acc[a1][0:64, 0:N], scalar=SCALE, in1=cosq[:, c0:c1], op0=ALU.mult, op1=ALU.mult), reads=[accB[a1], inB], writes=[t1B])
                a2 = cnt["acc"] % 2
                cnt["acc"] += 1
                P.mm_group(accB[a2], [(lambda e, a2=a2, qt=qt, wi=wi, c0=c0, c1=c1, N=N: e.matmul(acc[a2][0:64, 0:N], lhsT=wqs[wi][:, qt, 0:64], rhs=qnT[:, qt, c0:c1], start=(qt == 0), stop=(qt == QT - 1)), [wB[wi], inB]) for qt in range(QT)])
                P.op("dve", lambda e, a2=a2, c0=c0, c1=c1, N=N: e.scalar_tensor_tensor(out=t2[:, 0:N], in0=acc[a2][0:64, 0:N], scalar=SCALE, in1=sinq[:, c0:c1], op0=ALU.mult, op1=ALU.mult), reads=[accB[a2], inB], writes=[t2B])
                P.op("dve", lambda e, c0=c0, c1=c1, N=N: e.tensor_tensor(out=qhr[:, c0:c1], in0=t1[:, 0:N], in1=t2[:, 0:N], op=ALU.add), reads=[t1B, t2B], writes=[qhB[qi]])
            for qi, (c0, c1) in enumerate(qtiles):
                N = c1 - c0
                ob = cnt["o"] % 2
                cnt["o"] += 1
                sis = [(cnt["s"] + kt) % 2 for kt in range(NKT)]
                pis = [(cnt["pt"] + kt) % 3 for kt in range(NKT)]
                cnt["s"] += NKT
                cnt["pt"] += NKT

                def emit_S(kt, N=N, c0=c0, c1=c1, qi=qi, sis=sis):
                    si, k0, kb = sis[kt], kt * 128, kt // 4
                    P.mm_group(SB[si], [
                        (lambda e, si=si, k0=k0: e.matmul(S[si][:, 0:N], lhsT=Kh[:, k0:k0 + 128], rhs=qhn[:, c0:c1], start=True, stop=False), [KhB[kb], qhB[qi]]),
                        (lambda e, si=si, k0=k0: e.matmul(S[si][:, 0:N], lhsT=krT[:, k0:k0 + 128], rhs=qhr[:, c0:c1], start=False, stop=True), [inB]),
                    ])
                emit_S(0)
                for kt in range(NKT):
                    if kt + 1 < NKT:
                        emit_S(kt + 1)
                    si, pi, k0, kb = sis[kt], pis[kt], kt * 128, kt // 4
                    P.op("act", lambda e, si=si, pi=pi, N=N: e.activation(out=PT[pi][:, 0:N], in_=S[si][:, 0:N], func=AF.Exp), reads=[SB[si]], writes=[PTB[pi]])
                    first, last = kt == 0, kt == NKT - 1
                    P.mm(lambda e, pi=pi, k0=k0, first=first, last=last, N=N, ob=ob: e.matmul(O[ob][:, 0:N], lhsT=Vh[:, k0:k0 + 128], rhs=PT[pi][:, 0:N], start=first, stop=last), reads=[VhB[kb], PTB[pi]], ps=OB[ob], first=first, last=last)
                    P.mm(lambda e, pi=pi, first=first, last=last, N=N, ob=ob: e.matmul(DEN[ob][:, 0:N], lhsT=ones[:], rhs=PT[pi][:, 0:N], start=first, stop=last), reads=[cB, PTB[pi]], ps=DENB[ob], first=first, last=last)
                P.op("dve", lambda e, ob=ob, N=N: e.reciprocal(out=rden[:, 0:N], in_=DEN[ob][:, 0:N]), reads=[DENB[ob]], writes=[rdB])
                oi = cnt["st"] % 2
                cnt["st"] += 1
                P.op("dve", lambda e, ob=ob, oi=oi, N=N: e.tensor_tensor(out=ostg[oi][:, 0:N], in0=O[ob][:, 0:N], in1=rden[:, 0:N], op=ALU.mult), reads=[OB[ob], rdB], writes=[ostgB[oi]])
                r0 = cfg.SW + h * 128
                P.op("sync", lambda e, oi=oi, r0=r0, c0=c0, c1=c1, N=N: e.dma_start(out=b.mraw[seq][r0:r0 + 128, c0:c1], in_=ostg[oi][:, 0:N]), reads=[ostgB[oi]], dma=True)
        P.flush()


def phase_D(b):
    nc, cfg, P = b.nc, b.cfg, b.P
    D, MT, UT = cfg.D, cfg.MT, cfg.UT
    NCG = D // 512
    for seq in range(2):
        L, T = cfg.L[seq], cfg.T[seq]
        TC = T + 2
        for gi, (g0, g1) in enumerate(splits(TC, 1026)):
            n = g1 - g0
            with ExitStack() as st:
                def sb(name, shape, dt):
                    return st.enter_context(nc.sbuf_tensor(f"D{seq}{gi}_{name}", list(shape), dt))

                def pst(name, shape, dt):
                    return st.enter_context(nc.psum_tensor(f"D{seq}{gi}_{name}", list(shape), dt))
                mr = sb("mr", [128, MT, n], BF16)
                mrB = [Buf() for _ in range(MT)]
                sq = [sb(f"sq{i}", [128, 512], BF16) for i in range(2)]
                sqB = [Buf() for _ in range(2)]
                rs = sb("rs", [128, 2, n], F32)
                rsB = [Buf() for _ in range(2)]
                wo = [sb(f"wo{i}", [128, MT, 512], BF16) for i in range(2)]
                woB = [Buf() for _ in range(2)]
                xr = [sb(f"xr{i}", [128, 512], F32) for i in range(2)]
                xrB = [Buf() for _ in range(2)]
                xo = [sb(f"xo{i}", [128, 512], F32) for i in range(2)]
                xoB = [Buf() for _ in range(2)]
                gmo = sb("gmo", [128, MT], F32)
                ones = sb("ones", [128, 128], BF16)
                cB = Buf()
                ssp = pst("ssp", [128, 512], F32)
                sspB = Buf()
                acc = [pst(f"acc{i}", [128, 512], F32) for i in range(3)]
                accB = [Buf() for _ in range(3)]
                cnt = {"sq": 0, "acc": 0, "x": 0}
                P.op("pool", lambda e: e.memset(ones[:], 1.0), writes=[cB])
                P.op("sync", lambda e: e.dma_start(out=gmo[:], in_=b.gmo), writes=[cB], dma=True)
                mv = b.mraw[seq].rearrange("(k p) c -> p k c", p=128)
                for k in range(MT):
                    P.op("sync", lambda e, k=k: e.dma_start(out=mr[:, k, :], in_=mv[:, k, g0:g1]), writes=[mrB[k]], dma=True)
                for half, (k0, k1) in enumerate(((0, UT), (UT, MT))):
                    width = (k1 - k0) * 128
                    for (c0, c1) in splits(n, 512):
                        N = c1 - c0
                        for k in range(k0, k1):
                            i = cnt["sq"] % 2
                            cnt["sq"] += 1
                            P.op("act", lambda e, i=i, k=k, c0=c0, c1=c1, N=N: e.activation(out=sq[i][:, 0:N], in_=mr[:, k, c0:c1], func=AF.Square), reads=[mrB[k]], writes=[sqB[i]])
                            P.mm(lambda e, i=i, N=N, k=k, k0=k0, k1=k1: e.matmul(ssp[:, 0:N], lhsT=ones[:], rhs=sq[i][:, 0:N], start=(k == k0), stop=(k == k1 - 1)), reads=[sqB[i], cB], ps=sspB, first=(k == k0), last=(k == k1 - 1))
                        P.op("act", lambda e, half=half, c0=c0, c1=c1, N=N, width=width: e.activation(out=rs[:, half, c0:c1], in_=ssp[:, 0:N], func=AF.Ln, scale=1.0 / width, bias=cfg.EPS), reads=[sspB], writes=[rsB[half]])
                        P.op("act", lambda e, half=half, c0=c0, c1=c1: e.activation(out=rs[:, half, c0:c1], in_=rs[:, half, c0:c1], func=AF.Exp, scale=-0.5), reads=[rsB[half]], writes=[rsB[half]])
                for k in range(MT):
                    half = 0 if k < UT else 1
                    P.op("dve", lambda e, k=k, half=half: e.scalar_tensor_tensor(out=mr[:, k, :], in0=mr[:, k, :], scalar=gmo[:, k:k + 1], in1=rs[:, half, :], op0=ALU.mult, op1=ALU.mult), reads=[rsB[half], cB], writes=[mrB[k]])
                wov = b.w_out.rearrange("(k p) c -> p k c", p=128)
                for cgi in range(NCG):
                    wi = cgi % 2
                    P.op("pool", lambda e, wi=wi, cgi=cgi: e.dma_start(out=wo[wi][:], in_=wov[:, :, cgi * 512:(cgi + 1) * 512]), writes=[woB[wi]], dma=True)
                    for (t0, t1_) in splits(n, 128):
                        m = t1_ - t0
                        ca, cb = g0 + t0, g0 + t1_
                        a = cnt["acc"] % 3
                        cnt["acc"] += 1
                        P.mm_group(accB[a], [(lambda e, a=a, k=k, wi=wi, t0=t0, t1_=t1_, m=m: e.matmul(acc[a][0:m, 0:512], lhsT=mr[:, k, t0:t1_], rhs=wo[wi][:, k, :], start=(k == 0), stop=(k == MT - 1)), [mrB[k], woB[wi]]) for k in range(MT)])
                        xi = cnt["x"] % 2
                        cnt["x"] += 1
                        cs = slice(cgi * 512, (cgi + 1) * 512)
                        if ca == 0:
                            P.op("sync", lambda e, xi=xi, cs=cs: e.dma_start(out=xr[xi][0:1, :], in_=b.x[seq][L - 1:L, cs]), writes=[xrB[xi]], dma=True)
                            if m > 1:
                                P.op("sync", lambda e, xi=xi, cs=cs, m=m, cb=cb: e.dma_start(out=xr[xi][1:m, :], in_=b.x[seq][0:cb - 1, cs]), writes=[xrB[xi]], dma=True)
                        else:
                            P.op("sync", lambda e, xi=xi, cs=cs, m=m, ca=ca, cb=cb: e.dma_start(out=xr[xi][0:m, :], in_=b.x[seq][ca - 1:cb - 1, cs]), writes=[xrB[xi]], dma=True)
                        P.op("dve", lambda e, xi=xi, a=a, m=m: e.tensor_tensor(out=xo[xi][0:m, :], in0=acc[a][0:m, 0:512], in1=xr[xi][0:m, :], op=ALU.add), reads=[accB[a], xrB[xi]], writes=[xoB[xi]])
                        P.op("sync", lambda e, xi=xi, m=m, ca=ca, cb=cb, cs=cs: e.dma_start(out=b.x1[seq][ca:cb, cs], in_=xo[xi][0:m, :]), reads=[xoB[xi]], dma=True)
                P.flush()


def phase_G(b):
    nc, cfg, P = b.nc, b.cfg, b.P
    D, KD = cfg.D, cfg.KD
    TB = min(8, KD)
    with ExitStack() as st:
        def sb(name, shape, dt):
            return st.enter_context(nc.sbuf_tensor(f"G_{name}", list(shape), dt))

        def pst(name, shape, dt):
            return st.enter_context(nc.psum_tensor(f"G_{name}", list(shape), dt))
        xt = [sb(f"xt{i}", [128, D], F32) for i in range(2)]
        xtB = [Buf() for _ in range(2)]
        xn = [sb(f"xn{i}", [128, D], BF16) for i in range(2)]
        xnB = [Buf() for _ in range(2)]
        ss = sb("ss", [128, 8], F32)
        ssB = Buf()
        gbc = sb("gbc", [128, D], F32)
        ident = sb("ident", [128, 128], BF16)
        cB = Buf()
        hs = [sb(f"hs{i}", [128, KD, 128], BF16) for i in range(2)]
        hsB = [Buf() for _ in range(2)]
        tp = [pst(f"tp{i}", [128, 8, 128], BF16) for i in range(2)]
        tpB = [Buf() for _ in range(2)]
        P.op("pool", lambda e: e.dma_start(out=ident[:], in_=b.ident_in), writes=[cB], dma=True)
        P.op("sync", lambda e: e.dma_start(out=gbc[:], in_=bcast_rows(b.g_ffn, 128, D)), writes=[cB], dma=True)
        ti = 0
        pbc = 0
        for seq in range(2):
            TC = cfg.T[seq] + 2
            hv = b.h2T[seq].rearrange("(k p) c -> p k c", p=128)
            for (r0, r1) in splits(TC, 128):
                m = r1 - r0
                bi = ti % 2
                ti += 1
                P.op("sync", lambda e, bi=bi, r0=r0, r1=r1, m=m, x1s=b.x1[seq]: e.dma_start(out=xt[bi][0:m, :], in_=x1s[r0:r1, :]), writes=[xtB[bi]], dma=True)
                P.op("act", lambda e, bi=bi, m=m: e.activation(out=xn[bi][0:m, :], in_=xt[bi][0:m, :], func=AF.Square, accum_out=ss[0:m, 0:1]), reads=[xtB[bi]], writes=[xnB[bi], ssB])
                P.op("act", lambda e, m=m: e.activation(out=ss[0:m, 1:2], in_=ss[0:m, 0:1], func=AF.Ln, scale=1.0 / D, bias=cfg.EPS), reads=[ssB], writes=[ssB])
                P.op("act", lambda e, m=m: e.activation(out=ss[0:m, 2:3], in_=ss[0:m, 1:2], func=AF.Exp, scale=-0.5), reads=[ssB], writes=[ssB])
                P.op("dve", lambda e, bi=bi, m=m: e.scalar_tensor_tensor(out=xn[bi][0:m, :], in0=xt[bi][0:m, :], scalar=ss[0:m, 2:3], in1=gbc[0:m, :], op0=ALU.mult, op1=ALU.mult), reads=[xtB[bi], ssB, cB], writes=[xnB[bi]])
                for j in range(KD // TB):
                    pb = pbc % 2
                    pbc += 1
                    P.mm_group(tpB[pb], [(lambda e, pb=pb, bi=bi, k=k, m=m: e.transpose(out=tp[pb][:, k % TB, 0:m], in_=xn[bi][0:m, k * 128:(k + 1) * 128], identity=ident[0:m, 0:m]), [xnB[bi], cB]) for k in range(j * TB, j * TB + TB)])
                    P.op("act", copy_fn("act", hs[bi][:, j * TB:j * TB + TB, 0:m], tp[pb][:, 0:TB, 0:m]), reads=[tpB[pb]], writes=[hsB[bi]])
                P.op("sync", lambda e, bi=bi, r0=r0, r1=r1, m=m, hv=hv: e.dma_start(out=hv[:, :, r0:r1], in_=hs[bi][:, :, 0:m], allow_slow_non_contiguous=(m < 16)), reads=[hsB[bi]], dma=True)
        P.flush()


def phase_E(b):
    nc, cfg, P = b.nc, b.cfg, b.P
    D, KD, FT = cfg.D, cfg.KD, cfg.FT
    NCG = D // 512
    GW = 512
    for seq in range(2):
        T = cfg.T[seq]
        NGR = T // GW
        for gi in range(NGR):
            a0 = 1 + gi * GW
            with ExitStack() as st:
                def sb(name, shape, dt):
                    return st.enter_context(nc.sbuf_tensor(f"E{seq}{gi}_{name}", list(shape), dt))

                def pst(name, shape, dt):
                    return st.enter_context(nc.psum_tensor(f"E{seq}{gi}_{name}", list(shape), dt))
                h2 = sb("h2", [128, KD, GW + 2], BF16)
                h2B = Buf()
                actT = sb("actT", [128, FT, GW], BF16)
                actB = [Buf() for _ in range(FT)]
                cw = sb("cw", [128, 3, FT], F32)
                cbias = sb("cbias", [128, FT], F32)
                mk = sb("mk", [128, 32], F32)
                cB = Buf()
                pb = [pst(f"pb{i}", [128, 512], F32) for i in range(8)]
                pbB = [Buf() for _ in range(8)]
                x2B = [[Buf() for _ in range(NCG)] for _ in range(4)]
                st1 = ExitStack()

                def sb1(name, shape, dt):
                    return st1.enter_context(nc.sbuf_tensor(f"E{seq}{gi}_{name}", list(shape), dt))
                wu = [sb1(f"wu{i}", [128, KD, 128], BF16) for i in range(2)]
                wg = [sb1(f"wg{i}", [128, KD, 128], BF16) for i in range(2)]
                wuB = [Buf() for _ in range(2)]
                wgB = [Buf() for _ in range(2)]
                upf = [sb1(f"upf{i}", [128, GW + 2], F32) for i in range(2)]
                upfB = [Buf() for _ in range(2)]
                c1 = sb1("c1", [128, GW], F32)
                c1B = Buf()
                sil = sb1("sil", [128, GW], F32)
                silB = Buf()
                P.op("sync", lambda e: e.dma_start(out=cw[:], in_=b.convw), writes=[cB], dma=True)
                P.op("sync", lambda e: e.dma_start(out=cbias[:], in_=b.convb), writes=[cB], dma=True)
                P.op("sync", lambda e: e.dma_start(out=mk[:], in_=bcast_rows(b.masks, 128, 32)), writes=[cB], dma=True)
                hv = b.h2T[seq].rearrange("(k p) c -> p k c", p=128)
                P.op("sync", lambda e, a0=a0, hv=hv: e.dma_start(out=h2[:], in_=hv[:, :, a0 - 1:a0 + GW + 1]), writes=[h2B], dma=True)
                wuv = b.w_up.rearrange("(k p) c -> p k c", p=128)
                wgv = b.w_gate.rearrange("(k p) c -> p k c", p=128)
                halves = splits(GW + 2, 512)
                for f in range(FT):
                    wi = f % 2
                    P.op("pool", lambda e, wi=wi, f=f: e.dma_start(out=wu[wi][:], in_=wuv[:, :, f * 128:(f + 1) * 128]), writes=[wuB[wi]], dma=True)
                    P.op("pool", lambda e, wi=wi, f=f: e.dma_start(out=wg[wi][:], in_=wgv[:, :, f * 128:(f + 1) * 128]), writes=[wgB[wi]], dma=True)
                    ui = f % 2
                    for hi_, (lo, hi) in enumerate(halves):
                        pi = (f % 2) * 2 + hi_
                        N = hi - lo
                        P.mm_group(pbB[pi], [(lambda e, pi=pi, k=k, wi=wi, lo=lo, hi=hi, N=N: e.matmul(pb[pi][:, 0:N], lhsT=wu[wi][:, k, :], rhs=h2[:, k, lo:hi], start=(k == 0), stop=(k == KD - 1)), [wuB[wi], h2B]) for k in range(KD)])
                        P.op("act", copy_fn("act", upf[ui][:, lo:hi], pb[pi][:, 0:N]), reads=[pbB[pi]], writes=[upfB[ui]])
                    gp = 4 + f % 2
                    P.mm_group(pbB[gp], [(lambda e, gp=gp, k=k, wi=wi: e.matmul(pb[gp][:, 0:GW], lhsT=wg[wi][:, k, :], rhs=h2[:, k, 1:GW + 1], start=(k == 0), stop=(k == KD - 1)), [wgB[wi], h2B]) for k in range(KD)])
                    if gi == 0:
                        P.op("dve", lambda e, ui=ui: e.tensor_scalar(out=upf[ui][:, 0:1], in0=upf[ui][:, 0:1], scalar1=mk[:, 3:4], scalar2=None, op0=ALU.mult), reads=[cB], writes=[upfB[ui]])
                    if gi == NGR - 1:
                        P.op("dve", lambda e, ui=ui: e.tensor_scalar(out=upf[ui][:, GW + 1:GW + 2], in0=upf[ui][:, GW + 1:GW + 2], scalar1=mk[:, 0:1], scalar2=None, op0=ALU.mult), reads=[cB], writes=[upfB[ui]])
                    P.op("dve", lambda e, ui=ui, f=f: e.tensor_scalar(out=c1[:], in0=upf[ui][:, 1:GW + 1], scalar1=cw[:, 1, f:f + 1], scalar2=cbias[:, f:f + 1], op0=ALU.mult, op1=ALU.add), reads=[upfB[ui], cB], writes=[c1B])
                    P.op("dve", lambda e, ui=ui, f=f: e.scalar_tensor_tensor(out=c1[:], in0=upf[ui][:, 0:GW], scalar=cw[:, 0, f:f + 1], in1=c1[:], op0=ALU.mult, op1=ALU.add), reads=[upfB[ui], cB], writes=[c1B])
                    P.op("dve", lambda e, ui=ui, f=f: e.scalar_tensor_tensor(out=c1[:], in0=upf[ui][:, 2:GW + 2], scalar=cw[:, 2, f:f + 1], in1=c1[:], op0=ALU.mult, op1=ALU.add), reads=[upfB[ui], cB], writes=[c1B])
                    P.op("act", lambda e: e.activation(out=sil[:], in_=c1[:], func=AF.Silu), reads=[c1B], writes=[silB])
                    P.op("dve", lambda e, gp=gp, f=f: e.tensor_tensor(out=actT[:, f, :], in0=pb[gp][:, 0:GW], in1=sil[:], op=ALU.mult), reads=[pbB[gp], silB], writes=[actB[f]])
                P.flush()
                st1.close()
                st2 = ExitStack()

                def sb2(name, shape, dt):
                    return st2.enter_context(nc.sbuf_tensor(f"E{seq}{gi}_{name}", list(shape), dt))
                wd = [sb2(f"wd{i}", [128, 512], BF16) for i in range(12)]
                wdB = [Buf() for _ in range(12)]
                xr = [sb2(f"xr{i}", [128, 512], F32) for i in range(2)]
                xrB = [Buf() for _ in range(2)]
                xo = [sb2(f"xo{i}", [128, 512], F32) for i in range(2)]
                xoB = [Buf() for _ in range(2)]
                junk = sb2("junk", [128, 512], BF16)
                junkB = Buf()
                ssq = sb2("ssq", [128, 4, NCG + 4], F32)
                ssqB = [Buf() for _ in range(4)]
                gfs = [sb2(f"gfs{i}", [128, 512], F32) for i in range(2)]
                gfsB = [Buf() for _ in range(2)]
                wdc = 0
                xc = 0
                for cgi in range(NCG):
                    st_ = (cgi % 2) * 4
                    cs = slice(cgi * 512, (cgi + 1) * 512)
                    for f in range(FT):
                        wi = wdc % 12
                        wdc += 1
                        P.op("pool", lambda e, wi=wi, f=f, cs=cs: e.dma_start(out=wd[wi][:], in_=b.w_down[f * 128:(f + 1) * 128, cs]), writes=[wdB[wi]], dma=True)
                        for i in range(4):
                            P.mm(lambda e, i=i, f=f, wi=wi, st_=st_: e.matmul(pb[st_ + i][:, 0:512], lhsT=actT[:, f, i * 128:(i + 1) * 128], rhs=wd[wi][:], start=(f == 0), stop=(f == FT - 1)), reads=[wdB[wi], actB[f]], ps=pbB[st_ + i], first=(f == 0), last=(f == FT - 1))
                    for i in range(4):
                        xi = xc % 2
                        xc += 1
                        ca = a0 + i * 128
                        P.op("sync", lambda e, xi=xi, ca=ca, cs=cs: e.dma_start(out=xr[xi][:], in_=b.x1[seq][ca:ca + 128, cs]), writes=[xrB[xi]], dma=True)
                        P.op("dve", lambda e, xi=xi, i=i, st_=st_: e.tensor_tensor(out=xo[xi][:], in0=pb[st_ + i][:, 0:512], in1=xr[xi][:], op=ALU.add), reads=[pbB[st_ + i], xrB[xi]], writes=[xoB[xi]])
                        P.op("act", lambda e, xi=xi, i=i, cgi=cgi: e.activation(out=junk[:], in_=xo[xi][:], func=AF.Square, accum_out=ssq[:, i, cgi:cgi + 1]), reads=[xoB[xi]], writes=[junkB, ssqB[i]])
                        P.op("sync", lambda e, xi=xi, ca=ca, cs=cs: e.dma_start(out=b.x2[seq][ca - 1:ca + 127, cs], in_=xo[xi][:]), reads=[xoB[xi]], writes=[x2B[i][cgi]], dma=True)
                for i in range(4):
                    P.op("dve", lambda e, i=i: e.tensor_reduce(out=ssq[:, i, NCG:NCG + 1], in_=ssq[:, i, 0:NCG], axis=mybir.AxisListType.X, op=ALU.add), reads=[ssqB[i]], writes=[ssqB[i]])
                    P.op("act", lambda e, i=i: e.activation(out=ssq[:, i, NCG + 1:NCG + 2], in_=ssq[:, i, NCG:NCG + 1], func=AF.Ln, scale=1.0 / D, bias=cfg.EPS), reads=[ssqB[i]], writes=[ssqB[i]])
                    P.op("act", lambda e, i=i: e.activation(out=ssq[:, i, NCG + 2:NCG + 3], in_=ssq[:, i, NCG + 1:NCG + 2], func=AF.Exp, scale=-0.5), reads=[ssqB[i]], writes=[ssqB[i]])
                    for cgi in range(NCG):
                        xi = xc % 2
                        xc += 1
                        cs = slice(cgi * 512, (cgi + 1) * 512)
                        r0 = a0 - 1 + i * 128
                        P.op("sync", lambda e, xi=xi, r0=r0, cs=cs: e.dma_start(out=xr[xi][:], in_=b.x2[seq][r0:r0 + 128, cs]), reads=[x2B[i][cgi]], writes=[xrB[xi]], dma=True)
                        P.op("sync", lambda e, xi=xi, cgi=cgi: e.dma_start(out=gfs[xi][:], in_=bcast_rows(b.g_fin, 128, 512, off=cgi * 512)), writes=[gfsB[xi]], dma=True)
                        P.op("dve", lambda e, xi=xi, i=i, cs=cs: e.scalar_tensor_tensor(out=xo[xi][:], in0=xr[xi][:], scalar=ssq[:, i, NCG + 2:NCG + 3], in1=gfs[xi][:], op0=ALU.mult, op1=ALU.mult), reads=[xrB[xi], ssqB[i], gfsB[xi]], writes=[xoB[xi]])
                        P.op("sync", lambda e, xi=xi, r0=r0, cs=cs: e.dma_start(out=b.y[seq][r0:r0 + 128, cs], in_=xo[xi][:]), reads=[xoB[xi]], dma=True)
                P.flush()
                st2.close()


def MM(out, lhsT, rhs, start, stop):
    return lambda e: e.matmul(out, lhsT=lhsT, rhs=rhs, start=start, stop=stop)


def rot_exps(cfg):
    s = set()
    for k in range(1, 8):
        s.add(8 * k)
        s.add(64 * k)
    for T in cfg.T:
        rq = T // 512
        for k in range(1, rq):
            s.add(512 * k)
        s.add(T)
        s.add(2 * T)
    return sorted(s)


def exp_vector(cfg):
    ev = list(range(0, 8)) + list(range(7, -1, -1)) + list(range(1, 9)) + list(range(8, 0, -1)) + [-i for i in range(8)]
    ev += rot_exps(cfg)
    assert len(ev) <= 64
    return ev


def sb3(t, off, dims):
    row = 1
    for d in list(t.shape)[1:]:
        row *= d
    return bass.AP(t, off, [[row, 128]] + [list(d) for d in dims])


def phase_B(b):
    nc, cfg, P = b.nc, b.cfg, b.P
    G, UT = cfg.G, cfg.UT
    ROT = rot_exps(cfg)
    NR = len(ROT)
    ROTI = {e: i for i, e in enumerate(ROT)}
    EV = exp_vector(cfg)
    NE = len(EV)
    RC0 = 40
    with ExitStack() as st:
        def sb(name, shape, dt):
            return st.enter_context(nc.sbuf_tensor(f"B_{name}", list(shape), dt))

        def pst(name, shape, dt):
            return st.enter_context(nc.psum_tensor(f"B_{name}", list(shape), dt))
        sel = sb("sel", [128, 64, 128], BF16)
        selo = sb("selo", [128, 64, 128], BF16)
        cstf = sb("cstf", [128, 4, 128], F32)
        identb = sb("identb", [128, 128], BF16)
        sig = sb("sig", [128, 4], F32)
        evb = sb("evb", [128, 64], F32)
        mk = sb("mk", [128, 32], F32)
        cB = Buf()
        are = sb("are", [128, 16], F32)
        aim = sb("aim", [128, 16], F32)
        ldt = sb("ldt", [128, 16], F32)
        sm = sb("sm", [128, 12, 16], F32)
        Bx1 = sb("Bx1", [128, 16, 16], F32)
        Bx2 = sb("Bx2", [128, 16, 16], F32)
        Cx1 = sb("Cx1", [128, 16, 16], F32)
        Cx2 = sb("Cx2", [128, 16, 16], F32)
        M1B = sb("M1B", [128, 16, 16], F32)
        M2B = sb("M2B", [128, 16, 16], F32)
        tb1 = sb("tb1", [128, 16, 16], F32)
        tb2 = sb("tb2", [128, 16, 16], F32)
        dvec = sb("dvec", [128, 8], F32)
        LRt = sb("LRt", [128, 16, NE], F32)
        ANG = sb("ANG", [128, 16, NE], F32)
        ANGf = sb("ANGf", [128, 16, NE], F32)
        ANGi = sb("ANGi", [128, 16, NE], I32)
        PT1 = sb("PT1", [128, 16, NE], F32)
        PT2 = sb("PT2", [128, 16, NE], F32)
        NPT2 = sb("NPT2", [128, 16, NE], F32)
        tabB = Buf()
        inB = Buf()
        gtA = [sb(f"gtA{i}", [128, 8, 16], F32) for i in range(4)]
        gtB = [sb(f"gtB{i}", [128, 8, 16], F32) for i in range(4)]
        gtAB = [Buf() for _ in range(4)]
        gtBB = [Buf() for _ in range(4)]
        r2t = sb("r2t", [128, NR, 128], BF16)
        r2B = Buf()
        Wxn = sb("Wxn", [128, 8, 16], BF16)
        WxnB = Buf()
        Phi = sb("Phi", [128, 2, 128], BF16)
        Psi = sb("Psi", [128, 2, 128], BF16)
        PhB = Buf()
        ttA = sb("ttA", [128, 128], F32)
        ttBt = sb("ttBt", [128, 128], F32)
        ttAB, ttBB = Buf(), Buf()
        WxT = [[sb(f"WxT{p}{d}", [128, 128], BF16) for d in range(2)] for p in range(2)]
        Wy = [[sb(f"Wy{p}{d}", [128, 8, 16], BF16) for d in range(2)] for p in range(2)]
        rot = [[sb(f"rot{p}{d}", [128, NR, 128], BF16) for d in range(2)] for p in range(2)]
        TT = [sb(f"TT{p}", [128, 128], BF16) for p in range(2)]
        matB = [Buf() for _ in range(2)]
        Lmax = max(cfg.L)
        NCmax = Lmax // 8
        uld = sb("uld", [128, Lmax], BF16)
        uldB = Buf()
        uT8 = [sb(f"uT8{s}", [128, 8, cfg.L[s] // 8], BF16) for s in range(2)]
        uTB = [Buf() for _ in range(2)]
        U = [sb(f"U{s}", [128, 8, cfg.L[s] // 8], BF16) for s in range(2)]
        UB = [[Buf() for _ in range(8)] for _ in range(2)]
        XL = [[[sb(f"X{s_}{d_}{l_}", [128, max(4, (cfg.L[s_] // 8) // (8 ** l_))], BF16) for l_ in range(3)] + [sb(f"X{s_}{d_}q", [128, 4], BF16)] for d_ in range(2)] for s_ in range(2)]
        XLB = [[[Buf() for _ in range(4)] for d_ in range(2)] for s_ in range(2)]
        PL = [[[sb(f"P{s_}{d_}{l_}", [128, max(4, (cfg.L[s_] // 8) // (8 ** l_))], BF16) for l_ in range(3)] + [sb(f"P{s_}{d_}q", [128, 4], BF16)] for d_ in range(2)] for s_ in range(2)]
        PLB = [[[Buf() for _ in range(4)] for d_ in range(2)] for s_ in range(2)]
        XM = [[sb(f"Xm{s_}{d_}", [128, 3, 4], BF16) for d_ in range(2)] for s_ in range(2)]
        XMB = [[Buf() for d_ in range(2)] for s_ in range(2)]
        NYmax = max(cfg.T) // 8 + 2
        Ys = [sb(f"Ys{s}", [128, 8, cfg.T[s] // 8 + 2], BF16) for s in range(2)]
        YsB = [[Buf() for _ in range(8)] for _ in range(2)]
        yst = [sb(f"yst{s}", [128, cfg.T[s] + 2], F32) for s in range(2)]
        ystB = [Buf() for _ in range(2)]
        pA = [pst(f"pA{i}", [128, 512], F32) for i in range(2)]
        pAB = [Buf() for _ in range(2)]
        pS = [pst(f"pS{i}", [128, 512], F32) for i in range(2)]
        pSB = [Buf() for _ in range(2)]
        pC = [pst(f"pC{i}", [128, 512], F32) for i in range(2)]
        pCB = [Buf() for _ in range(2)]
        pY = pst("pY", [128, 512], F32)
        pYB = Buf()
        pT = pst("pT", [128, 8, 128], BF16)
        pTB = Buf()
        cnt = {"a": 0, "s": 0, "ev": 0, "rt": 0}

        def evac_eng():
            cnt["ev"] += 1
            return "act" if cnt["ev"] % 2 else "dve"

        P.op("pool", lambda e: e.dma_start(out=sel[:], in_=b.sel_in), writes=[cB], dma=True)
        P.op("pool", lambda e: e.dma_start(out=selo[:], in_=b.selo_in), writes=[cB], dma=True)
        P.op("sync", lambda e: e.dma_start(out=cstf[:], in_=b.cst[:, 0:4, :]), writes=[cB], dma=True)
        P.op("pool", lambda e: e.dma_start(out=identb[:], in_=b.ident_in), writes=[cB], dma=True)
        P.op("sync", lambda e: e.dma_start(out=sig[:], in_=b.sig), writes=[cB], dma=True)
        P.op("sync", lambda e: e.dma_start(out=evb[:], in_=bcast_rows(b.evec, 128, 64)), writes=[cB], dma=True)
        P.op("sync", lambda e: e.dma_start(out=mk[:], in_=bcast_rows(b.masks, 128, 32)), writes=[cB], dma=True)
        pswap, maskf, maskb, identf = cstf[:, 0, :], cstf[:, 1, :], cstf[:, 2, :], cstf[:, 3, :]
        SIG, NSIG = sig[:, 0:1], sig[:, 1:2]

        def bc_e(tab):
            return tab[:, :].unsqueeze(2).broadcast_to([128, 16, NE])

        evbc = evb[:, 0:NE].unsqueeze(1).broadcast_to([128, 16, NE])

        def dv(fn, reads, writes):
            return P.op("dve", fn, reads=reads, writes=writes)

        for k in range(UT):
            g0 = k * 8
            for d in range(2):
                P.op("sync", lambda e, d=d, g0=g0: e.dma_start(out=are[:, d * 8:(d + 1) * 8], in_=b.are_h[:, d * G + g0:d * G + g0 + 8]), writes=[inB], dma=True)
                P.op("sync", lambda e, d=d, g0=g0: e.dma_start(out=aim[:, d * 8:(d + 1) * 8], in_=b.aim_h[:, d * G + g0:d * G + g0 + 8]), writes=[inB], dma=True)
                P.op("sync", lambda e, d=d, g0=g0: e.dma_start(out=ldt[:, d * 8:(d + 1) * 8], in_=bcast_rows(b.log_dt, 128, 8, off=d * G + g0)), writes=[inB], dma=True)
                for (dst, src) in ((Bx1, b.bx1_h), (Bx2, b.bx2_h), (Cx1, b.cx1_h), (Cx2, b.cx2_h)):
                    P.op("sync", lambda e, d=d, g0=g0, dst=dst, src=src: e.dma_start(out=dst[:, d * 8:(d + 1) * 8, :], in_=src[:, d * G + g0:d * G + g0 + 8, :]), writes=[inB], dma=True)
            P.op("sync", lambda e, g0=g0: e.dma_start(out=dvec[:], in_=b.dsk_h[:, g0:g0 + 8]), writes=[inB], dma=True)
            DT, LR, TH = sm[:, 0, :], sm[:, 1, :], sm[:, 2, :]
            P.op("act", lambda e: e.activation(out=DT, in_=ldt[:], func=AF.Exp), reads=[inB], writes=[tabB])
            dv(lambda e: e.tensor_tensor(out=LR, in0=are[:], in1=DT, op=ALU.mult), [inB, tabB], [tabB])
            dv(lambda e: e.tensor_tensor(out=TH, in0=aim[:], in1=DT, op=ALU.mult), [inB, tabB], [tabB])
            dv(lambda e: e.tensor_tensor(out=LRt[:], in0=bc_e(sm[:, 1, :]), in1=evbc, op=ALU.mult), [tabB, cB], [tabB])
            P.op("act", lambda e: e.activation(out=LRt[:], in_=LRt[:], func=AF.Exp), reads=[tabB], writes=[tabB])
            for which in range(2):
                dv(lambda e: e.tensor_tensor(out=ANG[:], in0=bc_e(sm[:, 2, :]), in1=evbc, op=ALU.mult), [tabB, cB], [tabB])
                if which == 1:
                    dv(lambda e: e.tensor_scalar(out=ANG[:], in0=ANG[:], scalar1=PI / 2, scalar2=None, op0=ALU.add), [tabB], [tabB])
                range_reduce(P, "dve", ANG[:], ANGf[:], ANGi[:], tabB, tabB, 16 * NE)
                P.op("act", lambda e: e.activation(out=ANG[:], in_=ANG[:], func=AF.Sin), reads=[tabB], writes=[tabB])
                if which == 0:
                    dv(lambda e: e.scalar_tensor_tensor(out=PT2[:], in0=LRt[:], scalar=SIG, in1=ANG[:], op0=ALU.mult, op1=ALU.mult), [tabB, cB], [tabB])
                    dv(lambda e: e.tensor_scalar(out=NPT2[:], in0=PT2[:], scalar1=-1.0, scalar2=None, op0=ALU.mult), [tabB], [tabB])
                else:
                    dv(lambda e: e.tensor_tensor(out=PT1[:], in0=LRt[:], in1=ANG[:], op=ALU.mult), [tabB], [tabB])
            i1 = 16
            NRr, NI, DEN, ZR, ZI, T0, T1 = (sm[:, j, :] for j in range(3, 10))
            dv(lambda e: e.tensor_scalar(out=NRr, in0=PT1[:, :, i1], scalar1=-1.0, scalar2=None, op0=ALU.add), [tabB], [tabB])
            dv(lambda e: e.tensor_scalar(out=NI, in0=PT2[:, :, i1], scalar1=SIG, scalar2=None, op0=ALU.mult), [tabB, cB], [tabB])
            dv(lambda e: e.tensor_tensor(out=DEN, in0=are[:], in1=are[:], op=ALU.mult), [inB], [tabB])
            dv(lambda e: e.tensor_tensor(out=T0, in0=aim[:], in1=aim[:], op=ALU.mult), [inB], [tabB])
            dv(lambda e: e.tensor_tensor(out=DEN, in0=DEN, in1=T0, op=ALU.add), [tabB], [tabB])
            dv(lambda e: e.reciprocal(out=DEN, in_=DEN), [tabB], [tabB])
            dv(lambda e: e.tensor_tensor(out=T0, in0=NRr, in1=are[:], op=ALU.mult), [tabB, inB], [tabB])
            dv(lambda e: e.tensor_tensor(out=T1, in0=NI, in1=aim[:], op=ALU.mult), [tabB, inB], [tabB])
            dv(lambda e: e.tensor_tensor(out=T0, in0=T0, in1=T1, op=ALU.add), [tabB], [tabB])
            dv(lambda e: e.tensor_tensor(out=ZR, in0=T0, in1=DEN, op=ALU.mult), [tabB], [tabB])
            dv(lambda e: e.tensor_tensor(out=T0, in0=NI, in1=are[:], op=ALU.mult), [tabB, inB], [tabB])
            dv(lambda e: e.tensor_tensor(out=T1, in0=NRr, in1=aim[:], op=ALU.mult), [tabB, inB], [tabB])
            dv(lambda e: e.tensor_tensor(out=T0, in0=T0, in1=T1, op=ALU.subtract), [tabB], [tabB])
            dv(lambda e: e.tensor_tensor(out=ZI, in0=T0, in1=DEN, op=ALU.mult), [tabB], [tabB])
            dv(lambda e: e.tensor_scalar(out=ZI, in0=ZI, scalar1=SIG, scalar2=None, op0=ALU.mult), [tabB, cB], [tabB])

            def bc_h(v):
                return v.unsqueeze(2).broadcast_to([128, 16, 16])
            dv(lambda e: e.tensor_tensor(out=tb1[:], in0=Bx1[:], in1=bc_h(ZR), op=ALU.mult), [tabB, inB], [tabB])
            dv(lambda e: e.tensor_tensor(out=tb2[:], in0=Bx2[:], in1=bc_h(ZI), op=ALU.mult), [tabB, inB], [tabB])
            dv(lambda e: e.tensor_tensor(out=M1B[:], in0=tb1[:], in1=tb2[:], op=ALU.add), [tabB], [tabB])
            dv(lambda e: e.tensor_tensor(out=tb1[:], in0=Bx2[:], in1=bc_h(ZR), op=ALU.mult), [tabB, inB], [tabB])
            dv(lambda e: e.tensor_tensor(out=tb2[:], in0=Bx1[:], in1=bc_h(ZI), op=ALU.mult), [tabB, inB], [tabB])
            dv(lambda e: e.tensor_tensor(out=M2B[:], in0=tb1[:], in1=tb2[:], op=ALU.subtract), [tabB], [tabB])
            dv(lambda e: e.tensor_scalar(out=Cx1[:], in0=Cx1[:], scalar1=NSIG, scalar2=None, op0=ALU.mult), [inB, cB], [tabB, inB])
            dv(lambda e: e.tensor_scalar(out=Cx2[:], in0=Cx2[:], scalar1=NSIG, scalar2=None, op0=ALU.mult), [inB, cB], [tabB, inB])
            for s in range(2):
                L = cfg.L[s]
                NC = L // 8
                P.op("sync", lambda e, s=s, k=k, L=L: e.dma_start(out=uld[:, 0:L], in_=b.uT[s][k * 128:(k + 1) * 128, :]), writes=[uldB], dma=True)
                P.op("act", lambda e, s=s, L=L: e.activation(out=uT8[s][:], in_=uld[:, 0:L].rearrange("p (c t) -> p t c", t=8), func=AF.Copy), reads=[uldB], writes=[uTB[s]])
                for gl in range(8):
                    for (c0, c1) in splits(NC, 512):
                        N = c1 - c0
                        a = cnt["a"] % 2
                        cnt["a"] += 1
                        P.mm_group(pAB[a], [(MM(pA[a][:, 0:N], sel[:, gl * 8 + t, :], uT8[s][:, t, c0:c1], t == 0, t == 7), [uTB[s], cB]) for t in range(8)])
                        eng = evac_eng()
                        P.op(eng, copy_fn(eng, U[s][:, gl, c0:c1], pA[a][:, 0:N]), reads=[pAB[a]], writes=[UB[s][gl]])
            for gl in range(8):
                par = gl % 2

                def outer_mul(slot, j0, T1tab, T2tab, gd):
                    p1 = PT1[:, gd, j0:j0 + 8].unsqueeze(2).broadcast_to([128, 8, 16])
                    p2 = PT2[:, gd, j0:j0 + 8].unsqueeze(2).broadcast_to([128, 8, 16])
                    m1 = T1tab[:, gd, :].unsqueeze(1).broadcast_to([128, 8, 16])
                    m2 = T2tab[:, gd, :].unsqueeze(1).broadcast_to([128, 8, 16])
                    dv(lambda e: e.tensor_tensor(out=gtA[slot][:], in0=p1, in1=m1, op=ALU.mult), [tabB], [gtAB[slot]])
                    dv(lambda e: e.tensor_tensor(out=gtB[slot][:], in0=p2, in1=m2, op=ALU.mult), [tabB], [gtBB[slot]])

                def outer_add(slot, out3, outB):
                    dv(lambda e: e.tensor_tensor(out=out3, in0=gtA[slot][:], in1=gtB[slot][:], op=ALU.add), [gtAB[slot], gtBB[slot]], [outB])
                for d in range(2):
                    gd = d * 8 + gl
                    outer_mul(0, 8 if d == 0 else 0, M1B, M2B, gd)
                    outer_mul(1, 16 if d == 0 else 24, Cx1, Cx2, gd)
                    outer_mul(2, 0 if d == 0 else 32, Cx1, Cx2, gd)
                    outer_mul(3, 32 if d == 0 else 0, M1B, M2B, gd)
                    idb = cstf[:, 3, :].unsqueeze(1).broadcast_to([128, NR, 128])
                    psb = cstf[:, 0, :].unsqueeze(1).broadcast_to([128, NR, 128])
                    s1b = PT1[:, gd, RC0:RC0 + NR].unsqueeze(2).broadcast_to([128, NR, 128])
                    s2b = NPT2[:, gd, RC0:RC0 + NR].unsqueeze(2).broadcast_to([128, NR, 128])
                    dv(lambda e, par=par, d=d, idb=idb, s1b=s1b: e.tensor_tensor(out=rot[par][d][:], in0=idb, in1=s1b, op=ALU.mult), [tabB, cB], [matB[par]])
                    dv(lambda e, psb=psb, s2b=s2b: e.tensor_tensor(out=r2t[:], in0=psb, in1=s2b, op=ALU.mult), [tabB, cB], [r2B])
                    outer_add(0, Wxn[:], WxnB)
                    P.mm_group(pTB, [(lambda e: e.transpose(out=pT[:, 0, :], in_=Wxn[:].rearrange("p a b -> p (a b)"), identity=identb[:]), [WxnB, cB])])
                    P.op("act", copy_fn("act", WxT[par][d][:], pT[:, 0, :]), reads=[pTB], writes=[matB[par]])
                    outer_add(1, Wy[par][d][:], matB[par])
                    outer_add(2, Phi[:, d, :].rearrange("p (a b) -> p a b", a=8), PhB)
                    outer_add(3, Psi[:, d, :].rearrange("p (a b) -> p a b", a=8), PhB)
                    dv(lambda e, par=par, d=d: e.tensor_tensor(out=rot[par][d][:], in0=rot[par][d][:], in1=r2t[:], op=ALU.add), [r2B], [matB[par]])
                a = cnt["a"] % 2
                cnt["a"] += 1
                P.mm_group(pAB[a], [(MM(pA[a][:, 0:128], Psi[:, 0, :], Phi[:, 0, :], True, True), [PhB]), (MM(pA[a][:, 128:256], Psi[:, 1, :], Phi[:, 1, :], True, True), [PhB])])
                dv(lambda e, a=a: e.tensor_tensor(out=ttA[:], in0=pA[a][:, 0:128], in1=maskf, op=ALU.mult), [pAB[a], cB], [ttAB])
                dv(lambda e, a=a: e.tensor_tensor(out=ttBt[:], in0=pA[a][:, 128:256], in1=maskb, op=ALU.mult), [pAB[a], cB], [ttBB])
                dv(lambda e: e.tensor_tensor(out=ttA[:], in0=ttA[:], in1=ttBt[:], op=ALU.add), [ttBB], [ttAB])
                dv(lambda e, gl=gl, par=par: e.scalar_tensor_tensor(out=TT[par][:], in0=identf, scalar=dvec[:, gl:gl + 1], in1=ttA[:], op0=ALU.mult, op1=ALU.add), [ttAB, inB, cB], [matB[par]])

                ring = [pS[0], pS[1], pA[0], pA[1]]
                ringB = [pSB[0], pSB[1], pAB[0], pAB[1]]

                def tree_gen(s, d, gl=gl, par=par):
                    L, T = cfg.L[s], cfg.T[s]
                    NC = L // 8
                    rq = T // 512
                    radices = [8, 8] + ([rq] if rq > 1 else [])
                    nlev = len(radices)
                    n = [NC]
                    for r_ in radices:
                        n.append(n[-1] // r_)
                    assert n[-1] == 4
                    ue = [8]
                    for r_ in radices:
                        ue.append(ue[-1] * r_)
                    assert ue[-1] == T
                    fwd = d == 0
                    mB = matB[par]

                    def rotap(ex):
                        if ex == 0:
                            return identb[:]
                        return rot[par][d][:, ROTI[ex], :]
                    Xlev = [XL[s][d][i] for i in range(nlev)] + [XL[s][d][3]]
                    XBl = [XLB[s][d][i] for i in range(nlev)] + [XLB[s][d][3]]
                    Plev = [PL[s][d][i] for i in range(nlev)] + [PL[s][d][3]]
                    PBl = [PLB[s][d][i] for i in range(nlev)] + [PLB[s][d][3]]
                    Xm, XmB = XM[s][d], XMB[s][d]

                    def bank():
                        si = cnt["s"] % 4
                        cnt["s"] += 1
                        return ring[si], ringB[si]
                    for (c0, c1) in splits(NC, 512):
                        N = c1 - c0
                        bk_, bkB = bank()
                        P.mm_group(bkB, [(MM(bk_[:, 0:N], WxT[par][d][:], U[s][:, gl, c0:c1], True, True), [mB, UB[s][gl]])])
                        eng = evac_eng()
                        P.op(eng, copy_fn(eng, sb3(Xlev[0], c0 // 8, [[1, N // 8], [NC // 8, 8]]), bk_[:, 0:N].rearrange("p (j r) -> p j r", r=8)), reads=[bkB], writes=[XBl[0]])
                        yield
                    for lev, rad in enumerate(radices):
                        nn = n[lev + 1]
                        bk_, bkB = bank()
                        mms = []
                        for r in range(rad):
                            kk = (rad - 1 - r) if fwd else r
                            w_ = n[lev] // rad
                            mms.append((MM(bk_[:, 0:nn], rotap(ue[lev] * kk), Xlev[lev][:, r * w_:(r + 1) * w_], r == 0, r == rad - 1), [mB, XBl[lev], cB]))
                        P.mm_group(bkB, mms)
                        eng = evac_eng()
                        if lev + 1 < nlev:
                            rad2 = radices[lev + 1]
                            dstx = sb3(Xlev[lev + 1], 0, [[1, nn // rad2], [nn // rad2, rad2]])
                            srcx = bk_[:, 0:nn].rearrange("p (j r) -> p j r", r=rad2)
                        else:
                            dstx, srcx = Xlev[lev + 1][:, 0:nn], bk_[:, 0:nn]
                        P.op(eng, copy_fn(eng, dstx, srcx), reads=[bkB], writes=[XBl[lev + 1]])
                        yield
                    X4 = Xlev[nlev]
                    mb0 = 4 if fwd else 16
                    for i in range(1, 4):
                        mo = mb0 + (i - 1) * 4
                        if fwd:
                            dv(lambda e, i=i, mo=mo: e.tensor_tensor(out=Xm[:, i - 1, i:4], in0=X4[:, 0:4 - i], in1=mk[:, mo + i:mo + 4], op=ALU.mult), [XBl[nlev], cB], [XmB])
                            dv(lambda e, i=i, mo=mo: e.tensor_tensor(out=Xm[:, i - 1, 0:i], in0=X4[:, 4 - i:4], in1=mk[:, mo:mo + i], op=ALU.mult), [XBl[nlev], cB], [XmB])
                        else:
                            dv(lambda e, i=i, mo=mo: e.tensor_tensor(out=Xm[:, i - 1, 0:4 - i], in0=X4[:, i:4], in1=mk[:, mo:mo + 4 - i], op=ALU.mult), [XBl[nlev], cB], [XmB])
                            dv(lambda e, i=i, mo=mo: e.tensor_tensor(out=Xm[:, i - 1, 4 - i:4], in0=X4[:, 0:i], in1=mk[:, mo + 4 - i:mo + 4], op=ALU.mult), [XBl[nlev], cB], [XmB])
                    yield
                    bk_, bkB = bank()
                    P.mm_group(bkB, [(MM(bk_[:, 0:4], rotap(T * (i - 1)), Xm[:, i - 1, :], i == 1, i == 3), [mB, XmB, cB]) for i in range(1, 4)])
                    eng = evac_eng()
                    P.op(eng, copy_fn(eng, Plev[nlev][:, 0:4], bk_[:, 0:4]), reads=[bkB], writes=[PBl[nlev]])
                    yield
                    for lev in range(nlev - 1, -1, -1):
                        rad = radices[lev]
                        nn = n[lev + 1]
                        big = rad * nn > 512
                        if big:
                            banks, bB = pC, pCB
                            per = 4
                        else:
                            bk_, bkB = bank()
                            banks, bB = [bk_], [bkB]
                            per = rad
                        mms_by_bank = {}
                        for r in range(rad):
                            bk = r // per
                            o0 = (r % per) * nn
                            terms = [(rotap(ue[lev] * (r if fwd else rad - 1 - r)), Plev[lev + 1][:, 0:nn], PBl[lev + 1])]
                            rr = range(0, r) if fwd else range(r + 1, rad)
                            for r2 in rr:
                                kk = (r - 1 - r2) if fwd else (r2 - 1 - r)
                                w_ = n[lev] // rad
                                terms.append((rotap(ue[lev] * kk), Xlev[lev][:, r2 * w_:(r2 + 1) * w_], XBl[lev]))
                            for ti_, (lh, rh, rb) in enumerate(terms):
                                mms_by_bank.setdefault(bk, []).append((MM(banks[bk][:, o0:o0 + nn], lh, rh, ti_ == 0, ti_ == len(terms) - 1), [mB, rb, cB]))
                        for bk, mms in mms_by_bank.items():
                            P.mm_group(bB[bk], mms)
                            r0 = bk * per
                            cnt_r = min(per, rad - r0)
                            dst = sb3(Plev[lev], r0, [[1, cnt_r], [rad, nn]])
                            src = banks[bk][:, 0:cnt_r * nn].rearrange("p (r j) -> p r j", r=cnt_r)
                            eng = evac_eng()
                            P.op(eng, copy_fn(eng, dst, src), reads=[bB[bk]], writes=[PBl[lev]])
                            yield

                gens = [tree_gen(s_, d_) for s_ in range(2) for d_ in range(2)]
                while gens:
                    for g_ in list(gens):
                        try:
                            next(g_)
                        except StopIteration:
                            gens.remove(g_)
                for s in range(2):
                    L, T = cfg.L[s], cfg.T[s]
                    NC = L // 8
                    NYo = T // 8 + 1
                    terms = [(TT[par][:], lambda c0, c1, s=s, gl=gl: U[s][:, gl, c0:c1], UB[s][gl]),
                             (Wy[par][0][:].rearrange("p a b -> p (a b)"), lambda c0, c1, s=s: PL[s][0][0][:, c0:c1], PLB[s][0][0]),
                             (Wy[par][1][:].rearrange("p a b -> p (a b)"), lambda c0, c1, s=s: PL[s][1][0][:, c0:c1], PLB[s][1][0])]
                    mms = []
                    for ti_, (lh, rf, rb) in enumerate(terms):
                        mms.append((MM(pY[:, 0:1], lh, rf(NC - 1, NC), ti_ == 0, ti_ == 2), [matB[par], rb]))
                    for ti_, (lh, rf, rb) in enumerate(terms):
                        mms.append((MM(pY[:, 1:1 + NYo], lh, rf(0, NYo), ti_ == 0, ti_ == 2), [matB[par], rb]))
                    P.mm_group(pYB, mms)
                    eng = evac_eng()
                    P.op(eng, copy_fn(eng, Ys[s][:, gl, 0:NYo + 1], pY[:, 0:NYo + 1]), reads=[pYB], writes=[YsB[s][gl]])
            for s in range(2):
                T = cfg.T[s]
                NY = T // 8 + 2
                for t in range(8):
                    a = cnt["a"] % 2
                    cnt["a"] += 1
                    P.mm_group(pAB[a], [(MM(pA[a][:, 0:NY], selo[:, gl * 8 + t, :], Ys[s][:, gl, 0:NY], gl == 0, gl == 7), [YsB[s][gl], cB]) for gl in range(8)])
                    dst = sb3(yst[s], 1 + t, [[8, T // 8]])
                    P.op("act", copy_fn("act", dst, pA[a][:, 1:1 + T // 8]), reads=[pAB[a]], writes=[ystB[s]])
                    if t == 7:
                        P.op("act", copy_fn("act", yst[s][:, 0:1], pA[a][:, 0:1]), reads=[pAB[a]], writes=[ystB[s]])
                    if t == 0:
                        P.op("act", copy_fn("act", yst[s][:, T + 1:T + 2], pA[a][:, T // 8 + 1:T // 8 + 2]), reads=[pAB[a]], writes=[ystB[s]])
                P.op("sync", lambda e, s=s, k=k: e.dma_start(out=b.yT[s][k * 128:(k + 1) * 128, :], in_=yst[s][:]), reads=[ystB[s]], dma=True)
        P.flush()


def phase_H(b):
    nc, cfg, P = b.nc, b.cfg, b.P
    UT = cfg.UT
    with ExitStack() as st:
        def sb(name, shape, dt):
            return st.enter_context(nc.sbuf_tensor(f"H_{name}", list(shape), dt))

        def pst(name, shape, dt):
            return st.enter_context(nc.psum_tensor(f"H_{name}", list(shape), dt))
        yv = sb("yv", [128, UT, 512], F32)
        yB = Buf()
        gT = sb("gT", [128, UT, 512], BF16)
        gB = Buf()
        w1 = sb("w1", [128, 512], F32)
        w2 = sb("w2", [128, 512], F32)
        w1B, w2B = Buf(), Buf()
        sg = [sb(f"sg{i}", [128, 512], F32) for i in range(2)]
        sgB = [Buf() for _ in range(2)]
        wt = [sb(f"wt{i}", [128, UT, 128], BF16) for i in range(2)]
        wtB = [Buf() for _ in range(2)]
        stg = [sb(f"stg{i}", [128, 512], BF16) for i in range(2)]
        stgB = [Buf() for _ in range(2)]
        acc = [pst(f"acc{i}", [128, 512], F32) for i in range(2)]
        accB = [Buf() for _ in range(2)]
        gv = b.glu_w.rearrange("(k p) c -> p k c", p=128)
        c = 0
        for seq in range(2):
            TC = cfg.T[seq] + 2
            yTv = b.yT[seq].rearrange("(k p) c -> p k c", p=128)
            mrs = b.mraw[seq]
            for (c0, c1) in splits(TC, 512):
                N = c1 - c0
                P.op("sync", lambda e, c0=c0, c1=c1, N=N, yTv=yTv: e.dma_start(out=yv[:, :, 0:N], in_=yTv[:, :, c0:c1]), writes=[yB], dma=True)
                for k in range(UT):
                    yk = yv[:, k, 0:N]
                    P.op("dve", lambda e, yk=yk, N=N: e.tensor_tensor(out=w1[:, 0:N], in0=yk, in1=yk, op=ALU.mult), reads=[yB], writes=[w1B])
                    P.op("dve", lambda e, N=N: e.tensor_scalar(out=w1[:, 0:N], in0=w1[:, 0:N], scalar1=0.044715, scalar2=1.0, op0=ALU.mult, op1=ALU.add), reads=[w1B], writes=[w1B])
                    P.op("dve", lambda e, yk=yk, N=N: e.tensor_tensor(out=w1[:, 0:N], in0=w1[:, 0:N], in1=yk, op=ALU.mult), reads=[w1B, yB], writes=[w1B])
                    P.op("act", lambda e, N=N: e.activation(out=w2[:, 0:N], in_=w1[:, 0:N], func=AF.Sigmoid, scale=1.5957691216057308), reads=[w1B], writes=[w2B])
                    P.op("dve", lambda e, yk=yk, N=N, k=k: e.tensor_tensor(out=gT[:, k, 0:N], in0=w2[:, 0:N], in1=yk, op=ALU.mult), reads=[w2B, yB], writes=[gB])
                for j in range(UT):
                    wi = c % 2
                    c += 1
                    P.op("pool", lambda e, wi=wi, j=j: e.dma_start(out=wt[wi][:], in_=gv[:, :, j * 128:(j + 1) * 128]), writes=[wtB[wi]], dma=True)
                    P.mm_group(accB[wi], [(MM(acc[wi][:, 0:N], wt[wi][:, k, :], gT[:, k, 0:N], k == 0, k == UT - 1), [wtB[wi], gB]) for k in range(UT)])
                    P.op("act", lambda e, wi=wi, N=N: e.activation(out=sg[wi][:, 0:N], in_=acc[wi][:, 0:N], func=AF.Sigmoid), reads=[accB[wi]], writes=[sgB[wi]])
                    P.op("dve", lambda e, wi=wi, N=N, j=j: e.tensor_tensor(out=stg[wi][:, 0:N], in0=sg[wi][:, 0:N], in1=gT[:, j, 0:N], op=ALU.mult), reads=[sgB[wi], gB], writes=[stgB[wi]])
                    P.op("sync", lambda e, wi=wi, N=N, j=j, c0=c0, c1=c1, mrs=mrs: e.dma_start(out=mrs[j * 128:(j + 1) * 128, c0:c1], in_=stg[wi][:, 0:N]), reads=[stgB[wi]], dma=True)
        P.flush()


ROPE_THETA = 10000.0

def consts(cfg):
    c = {}
    c["ident"] = np.eye(128, dtype=np.float32)
    inv = 1.0 / (ROPE_THETA ** (np.arange(0, 64, 2, dtype=np.float32) / 64)).astype(np.float32)
    c["invf"] = np.concatenate([inv, inv]).reshape(64, 1).astype(np.float32)
    c["sgn"] = np.concatenate([-np.ones(32), np.ones(32)]).reshape(64, 1).astype(np.float32)
    sel = np.zeros((128, 64, 128), np.float32)
    for gl in range(8):
        for t in range(8):
            for h in range(16):
                sel[gl * 16 + h, gl * 8 + t, t * 16 + h] = 1.0
    c["sel"] = sel
    c["selo"] = np.ascontiguousarray(sel.transpose(2, 1, 0))
    cst = np.zeros((128, 8, 128), np.float32)
    for k in range(128):
        cst[k, 0, (k + 64) % 128] = 1.0
    s_idx = np.arange(128) // 16
    cst[:, 1, :] = (s_idx[None, :] >= s_idx[:, None]).astype(np.float32)
    cst[:, 2, :] = (s_idx[None, :] <= s_idx[:, None]).astype(np.float32)
    cst[:, 3, :] = np.eye(128)
    c["cst"] = cst
    sig = np.zeros((128, 4), np.float32)
    sig[:64, 0] = -1.0; sig[64:, 0] = 1.0
    sig[:, 1] = -sig[:, 0]
    c["sigv"] = sig
    return c

def host_prepare(cfg, inputs):
    f = lambda a: np.ascontiguousarray(np.asarray(a, dtype=np.float32))
    cs = consts(cfg)
    shared = dict(cs)
    shared["w_in"] = f(inputs["w_in"][0])
    shared["g_mix"] = f(inputs["norm_mix_g"][0]).reshape(1, -1)
    shared["g_ffn"] = f(inputs["norm_ffn_g"][0]).reshape(1, -1)
    shared["g_fin"] = f(inputs["norm_final_g"]).reshape(1, -1)
    shared["gq"] = f(np.asarray(inputs["q_norm_g"][0]).reshape(cfg.QT, 128).T)
    shared["gkv"] = f(np.asarray(inputs["kv_norm_g"][0]).reshape(cfg.KT, 128).T)
    gmo = np.concatenate([np.asarray(inputs["ssm_out_norm_g"][0]), np.asarray(inputs["attn_out_norm_g"][0])])
    shared["gmo"] = f(gmo.reshape(cfg.MT, 128).T)
    shared["w_q_up"] = f(inputs["w_q_up"][0])
    shared["w_kv_up"] = f(inputs["w_kv_up"][0])
    shared["w_out"] = f(inputs["w_out"][0])
    shared["w_up"] = f(inputs["w_ffn_up"][0])
    shared["w_gate"] = f(inputs["w_ffn_gate"][0])
    shared["w_down"] = f(inputs["w_ffn_down"][0])
    cw = np.asarray(inputs["ffn_conv_w"][0])
    shared["convw"] = f(cw.reshape(3, cfg.FT, 128).transpose(2, 0, 1))
    shared["convb"] = f(np.asarray(inputs["ffn_conv_b"][0]).reshape(cfg.FT, 128).T)
    shared["glu_w"] = f(inputs["ssm_glu_w"][0])
    G = cfg.G
    are = np.asarray(inputs["ssm_a_re"][0]); aim = np.asarray(inputs["ssm_a_im"][0])
    def st2(x):
        y = x.transpose(2, 0, 1).reshape(64, 2 * G)
        return f(np.concatenate([y, y], 0))
    shared["are_h"] = st2(are); shared["aim_h"] = st2(aim)
    bre = np.asarray(inputs["ssm_b_re"][0]); bim = np.asarray(inputs["ssm_b_im"][0])
    br = bre.transpose(2, 0, 1, 3).reshape(64, 2 * G, 16); bi = bim.transpose(2, 0, 1, 3).reshape(64, 2 * G, 16)
    shared["bx1_h"] = f(np.concatenate([br, bi], 0)); shared["bx2_h"] = f(np.concatenate([bi, br], 0))
    cre = np.asarray(inputs["ssm_c_re"][0]); cim = np.asarray(inputs["ssm_c_im"][0])
    cr = cre.transpose(3, 0, 1, 2).reshape(64, 2 * G, 16); ci = cim.transpose(3, 0, 1, 2).reshape(64, 2 * G, 16)
    shared["cx1_h"] = f(np.concatenate([cr, ci], 0)); shared["cx2_h"] = f(np.concatenate([ci, cr], 0))
    dsk = np.asarray(inputs["ssm_d"][0])
    shared["dsk_h"] = f(np.tile(dsk.T, (8, 1)))
    shared["log_dt"] = f(inputs["ssm_log_dt"][0]).reshape(1, -1)
    ev = exp_vector(cfg)
    evv = np.zeros((1, 64), np.float32); evv[0, :len(ev)] = ev
    shared["evec"] = evv
    maps = []
    xp = np.asarray(inputs["x_prompt"]); xs = np.asarray(inputs["x_sample"])
    for c in range(8):
        si, q = c // 4, c % 4
        m = dict(shared)
        for nm, xx, s in (("p", xp, 0), ("s", xs, 1)):
            L, T = cfg.L[s], cfg.T[s]
            m["x" + nm] = f(np.roll(xx[si], -q * T, axis=0))
            m["pos" + nm] = ((np.arange(L) + q * T) % L).astype(np.float32).reshape(1, L)
        mk = np.zeros((1, 32), np.float32)
        lm = np.array([0.0 if j == (3 - q) else 1.0 for j in range(4)], np.float32)
        mk[0, 0:4] = lm
        for j in range(4):
            for i in range(1, 4):
                mk[0, 4 + (i - 1) * 4 + j] = np.prod([lm[(j - l) % 4] for l in range(1, i + 1)])
                mk[0, 16 + (i - 1) * 4 + j] = np.prod([lm[(j + l) % 4] for l in range(0, i)])
        m["masks"] = mk
        maps.append(m)
    return maps


_PHASES = "ABHCDGE"


def build_program(cfg, debug=False):
    nc = bass.Bass("TRN2", target_bir_lowering=False)
    b = declare(nc, cfg, debug=debug)
    g = globals()
    for ph in _PHASES:
        if ph == "A":
            phase_A(b, 0)
            phase_A(b, 1)
        else:
            g["phase_" + ph](b)
    return nc


def kernel(**inputs):
    cfg = FULL
    nc = build_program(cfg)
    maps = host_prepare(cfg, inputs)
    res = run_bass_kernel_spmd(nc, maps, core_ids=list(range(8)))
    yp = np.zeros((2, cfg.L[0], cfg.D), np.float32)
    ys = np.zeros((2, cfg.L[1], cfg.D), np.float32)
    for c in range(8):
        si, q = c // 4, c % 4
        r = res.results[c]
        yp[si, q * cfg.T[0]:(q + 1) * cfg.T[0]] = np.asarray(r["yp"])
        ys[si, q * cfg.T[1]:(q + 1) * cfg.T[1]] = np.asarray(r["ys"])
    return (yp, ys)
```
